# Optimizing a Trainium2 kernel written in Bass

```python
import jax, jax.numpy as jnp
from jax import lax
import numpy as np

D_MODEL = 2048
BATCH = 2
SEQ = 4096
DEPTH = 2
DEC_BATCH = 32
DEC_SEQ = 8
PAST_LEN = 16384
PAGE_SIZE = 128

N_A_LAYERS = DEPTH // 2
N_B_LAYERS = DEPTH - N_A_LAYERS
CONV_WIDTH = 31
HEAD_DIM = 64
N_HEADS = D_MODEL // HEAD_DIM
N_KV_HEADS = 8
ROT_DIM = HEAD_DIM // 4
ROPE_THETA = 500000.0
WINDOW = 128
BLOCK = WINDOW
D_FF = 4 * D_MODEL
EPS = 1e-6

kernel_name = 'yoco_conformer_conv_swa_sink_decoder_step'

F32 = jnp.float32


def _rms(x, g):
    xf = x.astype(F32)
    y = xf * lax.rsqrt(jnp.mean(xf * xf, axis=-1, keepdims=True) + EPS)
    return (y * g.astype(F32)).astype(x.dtype)


def _layernorm(x, g, b):
    xf = x.astype(F32)
    mu = jnp.mean(xf, axis=-1, keepdims=True)
    var = jnp.mean(jnp.square(xf - mu), axis=-1, keepdims=True)
    return ((xf - mu) * lax.rsqrt(var + EPS) * g.astype(F32) + b.astype(F32)).astype(x.dtype)


def _rope(x, pos):
    half = ROT_DIM // 2
    inv = ROPE_THETA ** (-jnp.arange(half, dtype=F32) / half)
    ang = pos.astype(F32)[:, None] * inv[None, :]
    cos = jnp.cos(ang)[None, :, None, :]
    sin = jnp.sin(ang)[None, :, None, :]
    xr = x[..., :ROT_DIM].astype(F32)
    x1, x2 = xr[..., :half], xr[..., half:]
    rot = jnp.concatenate([x1 * cos - x2 * sin, x2 * cos + x1 * sin], axis=-1).astype(x.dtype)
    return jnp.concatenate([rot, x[..., ROT_DIM:]], axis=-1)


def _conv_module(h, ctx, w_pw1, b_pw1, w_dw, b_dw, ln_g, ln_b, w_pw2, b_pw2):
    u = h @ w_pw1 + b_pw1
    a, gate = u[..., :D_MODEL], u[..., D_MODEL:]
    u = a * jax.nn.sigmoid(gate)
    full = jnp.concatenate([ctx.astype(u.dtype), u], axis=1)
    c = lax.conv_general_dilated(full, w_dw[:, None, :].astype(u.dtype), (1,), 'VALID',
                                 dimension_numbers=('NWC', 'WIO', 'NWC'),
                                 feature_group_count=D_MODEL) + b_dw
    c = jax.nn.silu(_layernorm(c, ln_g, ln_b))
    out = c @ w_pw2 + b_pw2
    return out, full[:, -(CONV_WIDTH - 1):]


def _block_attend(q, k, v, q_pos, k_pos, sinks):
    b, nb, bq, h, hd = q.shape
    kvh = k.shape[3]
    g = h // kvh
    qg = q.reshape(b, nb, bq, kvh, g, hd)
    s = jnp.einsum('bnqkgd,bnskd->bnkgqs', qg, k).astype(F32) * (hd ** -0.5)
    rel = q_pos[:, :, None] - k_pos[:, None, :]
    ok = (rel >= 0) & (rel < WINDOW) & (k_pos[:, None, :] >= 0)
    s = jnp.where(ok[None, :, None, None], s, -jnp.inf)
    sink = sinks.astype(F32).reshape(kvh, g)[None, None, :, :, None, None]
    m = jnp.maximum(jnp.max(s, axis=-1, keepdims=True), sink)
    p = jnp.exp(s - m)
    denom = jnp.sum(p, axis=-1, keepdims=True) + jnp.exp(sink - m)
    p = (p / denom).astype(v.dtype)
    o = jnp.einsum('bnkgqs,bnskd->bnqkgd', p, v)
    return o.reshape(b, nb, bq, h, hd)


def _swa_prompt(q, k, v, sinks):
    b, t, h, hd = q.shape
    kvh = k.shape[2]
    nb = t // BLOCK
    qb = q.reshape(b, nb, BLOCK, h, hd)
    kb = k.reshape(b, nb, BLOCK, kvh, hd)
    vb = v.reshape(b, nb, BLOCK, kvh, hd)
    pad = jnp.zeros_like(kb[:, :1])
    kk = jnp.concatenate([jnp.concatenate([pad, kb[:, :-1]], axis=1), kb], axis=2)
    vv = jnp.concatenate([jnp.concatenate([pad, vb[:, :-1]], axis=1), vb], axis=2)
    starts = jnp.arange(nb, dtype=jnp.int32) * BLOCK
    q_pos = starts[:, None] + jnp.arange(BLOCK, dtype=jnp.int32)[None, :]
    k_pos = starts[:, None] + jnp.arange(-BLOCK, BLOCK, dtype=jnp.int32)[None, :]
    return _block_attend(qb, kk, vv, q_pos, k_pos, sinks).reshape(b, t, h, hd)


def _swa_sample(q, k_new, v_new, k_buf, v_buf, sinks):
    s = q.shape[1]
    L = k_buf.shape[1]
    kk = jnp.concatenate([k_buf.astype(k_new.dtype), k_new], axis=1)[:, None]
    vv = jnp.concatenate([v_buf.astype(v_new.dtype), v_new], axis=1)[:, None]
    q_pos = (PAST_LEN + jnp.arange(s, dtype=jnp.int32))[None]
    k_pos = (PAST_LEN - L + jnp.arange(L + s, dtype=jnp.int32))[None]
    return _block_attend(q[:, None], kk, vv, q_pos, k_pos, sinks)[:, 0]


def _trunk(x, pos, conv_ctx, k_buf, v_buf, p):
    b, t, _ = x.shape
    conv_new = []
    k = v = None
    for l in range(DEPTH):
        if l < N_A_LAYERS:
            h = _rms(x, p['norm_mix'][l])
            out, ctx = _conv_module(h, conv_ctx[l], p['w_pw1'][l], p['b_pw1'][l], p['w_dw'][l],
                                    p['b_dw'][l], p['conv_ln_g'][l], p['conv_ln_b'][l],
                                    p['w_pw2'][l], p['b_pw2'][l])
            conv_new.append(ctx)
            x = x + out
        else:
            j = l - N_A_LAYERS
            if l == N_A_LAYERS:
                hk = _rms(x, p['kv_norm'])
                k = _rope((hk @ p['w_k']).reshape(b, t, N_KV_HEADS, HEAD_DIM), pos)
                v = (hk @ p['w_v']).reshape(b, t, N_KV_HEADS, HEAD_DIM)
            hq = _rms(x, p['norm_mix'][l])
            q = _rope((hq @ p['w_q'][j]).reshape(b, t, N_HEADS, HEAD_DIM), pos)
            if k_buf is None:
                o = _swa_prompt(q, k, v, p['sinks'][j])
            else:
                o = _swa_sample(q, k, v, k_buf, v_buf, p['sinks'][j])
            x = x + o.reshape(b, t, N_HEADS * HEAD_DIM) @ p['w_o'][j]
        h = _rms(x, p['norm_mlp'][l])
        x = x + jnp.square(jax.nn.relu(h @ p['w_up'][l])) @ p['w_down'][l]
    return _rms(x, p['final_norm']), jnp.stack(conv_new), k, v


def setup_inputs(seed: int = 0) -> dict:
    key = jax.random.key(seed)
    ks = jax.random.split(key, 32)
    n = lambda i, shape, s: jax.random.normal(ks[i], shape, F32) * s
    win_rows = min(WINDOW, PAST_LEN)
    return {
        'x_prompt': n(0, (BATCH, SEQ, D_MODEL), 1.0),
        'x_sample': n(1, (DEC_BATCH, DEC_SEQ, D_MODEL), 1.0),
        'state_conv': n(2, (N_A_LAYERS, DEC_BATCH, CONV_WIDTH - 1, D_MODEL), 0.5),
        'cache_k': n(3, (DEC_BATCH, win_rows, N_KV_HEADS, HEAD_DIM), 1.0),
        'cache_v': n(4, (DEC_BATCH, win_rows, N_KV_HEADS, HEAD_DIM), 1.0),
        'norm_mix': 1.0 + n(5, (DEPTH, D_MODEL), 0.02),
        'w_pw1': n(6, (N_A_LAYERS, D_MODEL, 2 * D_MODEL), D_MODEL ** -0.5),
        'b_pw1': n(7, (N_A_LAYERS, 2 * D_MODEL), 0.02),
        'w_dw': n(8, (N_A_LAYERS, CONV_WIDTH, D_MODEL), CONV_WIDTH ** -0.5),
        'b_dw': n(9, (N_A_LAYERS, D_MODEL), 0.02),
        'conv_ln_g': 1.0 + n(10, (N_A_LAYERS, D_MODEL), 0.02),
        'conv_ln_b': n(11, (N_A_LAYERS, D_MODEL), 0.02),
        'w_pw2': n(12, (N_A_LAYERS, D_MODEL, D_MODEL), D_MODEL ** -0.5),
        'b_pw2': n(13, (N_A_LAYERS, D_MODEL), 0.02),
        'kv_norm': 1.0 + n(14, (D_MODEL,), 0.02),
        'w_k': n(15, (D_MODEL, N_KV_HEADS * HEAD_DIM), D_MODEL ** -0.5),
        'w_v': n(16, (D_MODEL, N_KV_HEADS * HEAD_DIM), D_MODEL ** -0.5),
        'w_q': n(17, (N_B_LAYERS, D_MODEL, N_HEADS * HEAD_DIM), D_MODEL ** -0.5),
        'w_o': n(18, (N_B_LAYERS, N_HEADS * HEAD_DIM, D_MODEL), (N_HEADS * HEAD_DIM) ** -0.5),
        'sinks': n(19, (N_B_LAYERS, N_HEADS), 0.5),
        'norm_mlp': 1.0 + n(20, (DEPTH, D_MODEL), 0.02),
        'w_up': n(21, (DEPTH, D_MODEL, D_FF), D_MODEL ** -0.5),
        'w_down': n(22, (DEPTH, D_FF, D_MODEL), D_FF ** -0.5),
        'final_norm': 1.0 + n(23, (D_MODEL,), 0.02),
    }


def reference(x_prompt, x_sample, state_conv, cache_k, cache_v, norm_mix, w_pw1, b_pw1, w_dw, b_dw,
              conv_ln_g, conv_ln_b, w_pw2, b_pw2, kv_norm, w_k, w_v, w_q, w_o, sinks, norm_mlp,
              w_up, w_down, final_norm):
    p = dict(norm_mix=norm_mix, w_pw1=w_pw1, b_pw1=b_pw1, w_dw=w_dw, b_dw=b_dw,
             conv_ln_g=conv_ln_g, conv_ln_b=conv_ln_b, w_pw2=w_pw2, b_pw2=b_pw2,
             kv_norm=kv_norm, w_k=w_k, w_v=w_v, w_q=w_q, w_o=w_o, sinks=sinks,
             norm_mlp=norm_mlp, w_up=w_up, w_down=w_down, final_norm=final_norm)
    zero_ctx = jnp.zeros((N_A_LAYERS, x_prompt.shape[0], CONV_WIDTH - 1, D_MODEL), x_prompt.dtype)
    pos_p = jnp.arange(x_prompt.shape[1], dtype=jnp.int32)
    y_prompt, conv_prompt, k_p, v_p = _trunk(x_prompt, pos_p, zero_ctx, None, None, p)
    pos_s = PAST_LEN + jnp.arange(x_sample.shape[1], dtype=jnp.int32)
    y_sample, conv_sample, k_s, v_s = _trunk(x_sample, pos_s, state_conv, cache_k, cache_v, p)
    L = cache_k.shape[1]
    win_k_prompt = k_p[:, -WINDOW:]
    win_v_prompt = v_p[:, -WINDOW:]
    win_k_sample = jnp.concatenate([cache_k, k_s.astype(cache_k.dtype)], axis=1)[:, -L:]
    win_v_sample = jnp.concatenate([cache_v, v_s.astype(cache_v.dtype)], axis=1)[:, -L:]
    return (y_prompt, y_sample, conv_prompt, conv_sample, win_k_prompt, win_v_prompt, win_k_sample, win_v_sample)
```

```python
import contextlib
import os
import numpy as np
import concourse.bass as bass
import concourse.mybir as mybir
from concourse.bass_utils import run_bass_kernel_spmd

F32 = mybir.dt.float32
BF16 = mybir.dt.bfloat16
AF = mybir.ActivationFunctionType
ALU = mybir.AluOpType

ENGS = ("pe", "act", "dve", "pool", "sp")


def C(name, *a, **k):
    return (name, a, k)


class Op:
    __slots__ = ("eng", "fn", "deps", "signal", "sem", "val", "pos", "dma")

    def __init__(self, eng, fn, dma=None):
        self.eng = eng
        self.fn = fn
        self.deps = []
        self.signal = False
        self.sem = None
        self.val = 0
        self.pos = 0
        self.dma = dma


class _Stop(Exception):
    pass


STOP = float(os.environ.get("MK_STOP", "99"))


def checkpoint(k):
    if STOP <= k:
        raise _Stop()


class Prog:
    SAFE_DIST = 10 ** 9

    def __init__(self):
        self.streams = {e: [] for e in ENGS}
        self.last_w = {}
        self.readers = {}
        self.last_op = {}
        self.pending_fence = {e: [] for e in ENGS}
        self.dma_pending = []

    def add(self, eng, fn, reads=(), writes=(), dma=None):
        op = Op(eng, fn, dma=dma)
        if dma is not None:
            op.signal = True
        deps = set()
        for r in reads:
            w = self.last_w.get(r)
            if w is not None:
                deps.add(w)
        for wkey in writes:
            w = self.last_w.get(wkey)
            if w is not None:
                deps.add(w)
            for rd in self.readers.get(wkey, ()):
                deps.add(rd)
        for f in self.pending_fence[eng]:
            deps.add(f)
        self.pending_fence[eng] = []
        op.pos = len(self.streams[eng])
        keep = []
        latest = {}
        for d in deps:
            if d.dma is None and d.eng == eng:
                if eng == "pe" or eng == "sp":
                    continue
                if op.pos - d.pos >= self.SAFE_DIST:
                    continue
            if d.dma is None:
                cur = latest.get(d.eng)
                if cur is None or d.pos > cur.pos:
                    latest[d.eng] = d
            else:
                keep.append(d)
        keep.extend(latest.values())
        op.deps = keep
        for d in keep:
            d.signal = True
        self.streams[eng].append(op)
        if dma is None:
            self.last_op[eng] = op
        elif not dma.startswith("slot"):
            self.dma_pending.append(op)
        for r in reads:
            self.readers.setdefault(r, []).append(op)
        for wkey in writes:
            self.last_w[wkey] = op
            self.readers[wkey] = []
        return op

    def fence(self):
        lasts = [o for o in self.last_op.values()]
        dmas = self.dma_pending
        self.dma_pending = []
        for e in ENGS:
            self.pending_fence[e] = [o for o in lasts if o.eng != e] + dmas

    def finalize(self, final_ops):
        for o in final_ops:
            o.signal = True
        counters = {}
        for e in ENGS:
            for op in self.streams[e]:
                if not op.signal:
                    continue
                if op.dma is not None:
                    key = ("dma", op.dma)
                    counters[key] = counters.get(key, 0) + 16
                else:
                    key = ("eng", e)
                    counters[key] = counters.get(key, 0) + 1
                op.sem = key
                op.val = counters[key]
        self.sem_keys = list(counters.keys())
        self.final_ops = final_ops

    def emit(self, nc):
        with contextlib.ExitStack() as st:
            sems = {}
            for i, k in enumerate(self.sem_keys):
                sems[k] = st.enter_context(nc.semaphore("s%d" % i))
            block = st.enter_context(nc.Block())
            prog = self

            def run(ename, e, is_last=False):
                waited = {}
                for op in prog.streams[ename]:
                    for d in op.deps:
                        if waited.get(d.sem, 0) < d.val:
                            e.wait_ge(sems[d.sem], d.val)
                            waited[d.sem] = d.val
                    name_, a_, k_ = op.fn
                    ins = getattr(e, name_)(*a_, **k_)
                    if op.signal:
                        ins.then_inc(sems[op.sem], 16 if op.dma is not None else 1)
                if is_last:
                    for d in prog.final_ops:
                        if waited.get(d.sem, 0) < d.val:
                            e.wait_ge(sems[d.sem], d.val)
                            waited[d.sem] = d.val

            @block.sync
            def _(e):
                run("sp", e, is_last=True)

            @block.scalar
            def _(e):
                run("act", e)

            @block.vector
            def _(e):
                run("dve", e)

            @block.gpsimd
            def _(e):
                run("pool", e)

            @block.tensor
            def _(e):
                run("pe", e)


NCORE = 8
D = 2048
KC = 16
TE = 1214
T_PRE, T_HALO, T_MAIN, T_SMP = 0, 30, 158, 1182
NTOK = TE - 30
NQ = TE - 158
SEGS = [0, 30, 158, 592, 622, 670, 1214]
NS = 4
CW = 31
NDV = 8
EPS = 1e-6
UW = 742

V_NMIX0, V_NMLP0, V_KVN, V_NMIX1, V_NMLP1, V_FIN = 0, 16, 32, 48, 64, 80
V_BA, V_BG, V_BDW, V_LNG, V_LNB, V_BPW2, V_SINK = 96, 112, 128, 144, 160, 176, 192
V_FLAG = 208
V_WDW = 212
NV = V_WDW + 16 * CW


def seg_keys(name, kc, a, b):
    return [(name, kc, i) for i in range(len(SEGS) - 1) if SEGS[i] < b and SEGS[i + 1] > a]


def tiles(a, b):
    out = []
    t = a
    while t < b:
        n = min(512, b - t)
        out.append((t, n))
        t += n
    return out


def weight_order():
    order = []
    for _pass in range(2):
        for oc in range(16):
            order.append(("pw1a", oc))
            order.append(("pw1g", oc))
        for oc in range(16):
            order.append(("pw2", oc))
    for g in range(8):
        for fc in range(8):
            order.append(("up", 0, g * 8 + fc))
        for ocp in range(8):
            order.append(("down", 0, g, ocp))
    for c in range(4):
        order.append(("wk", c))
    for _grp in range(2):
        for c in range(4):
            order.append(("wv", c))
    for _pass in range(2):
        for oc in range(16):
            order.append(("wq", oc))
        for oc in range(16):
            order.append(("wo", oc))
    for g in range(8):
        for fc in range(8):
            order.append(("up", 1, g * 8 + fc))
        for ocp in range(8):
            order.append(("down", 1, g, ocp))
    return order


def unique_tiles():
    seen = {}
    for nm in weight_order():
        if nm not in seen:
            seen[nm] = len(seen)
    return seen


def build():
    nc = bass.Bass("TRN2", target_bir_lowering=False)
    tidx = unique_tiles()
    NT = len(tidx)
    order = weight_order()

    def din(name, shape):
        return nc.dram_tensor(name, shape, F32, kind="ExternalInput").ap()

    def dout(name, shape):
        return nc.dram_tensor(name, shape, F32, kind="ExternalOutput").ap()

    xT = din("xT", [128, KC, TE])
    wst = din("wst", [NT, 128, 2048])
    ctxT = din("ctxT", [128, KC, 4, 30])
    ckT = din("ckT", [128, 4, 512])
    sc_raw = din("sc_raw", [4, 30, D])
    ck_raw = din("ck_raw", [4, 128, 512])
    cv_raw = din("cv_raw", [4, 128, 512])
    vecs = din("vecs", [128, NV])
    tab = din("tab", [128, 2 * NTOK])
    mask4 = din("mask4", [128, 512])
    masks = din("masks", [128, 32])
    mask40 = din("mask40", [128, 512])
    perm = din("perm", [128, 128])
    ident = din("ident", [128, 128])
    maskns = din("maskns", [128, 64])

    yT = dout("yT", [128, KC, NQ])
    uT_out = dout("uT_out", [128, KC * 62])
    conv_old = dout("conv_old", [4, 22, D])
    kT_out = dout("kT_out", [128, 4 * 160])
    v_out = dout("v_out", [128, 512])
    v_new = dout("v_new", [4, 8, 512])
    k_old = dout("k_old", [4, 120, 512])
    v_old = dout("v_old", [4, 120, 512])

    P = Prog()
    outs = []
    with contextlib.ExitStack() as st:
        def sb(name, cols, dt):
            return st.enter_context(nc.sbuf_tensor(name, [128, cols], dt))

        X = sb("X", KC * TE, F32)
        H = sb("H", KC * TE, BF16)
        RING = sb("RING", NS * 2048, BF16)
        VEC = sb("VEC", NV, F32)
        TAB = sb("TAB", 2 * NTOK, F32)
        TR0 = sb("TR0", TE, F32)
        TR1 = sb("TR1", TE, F32)
        SQ = sb("SQ", 4 * 512, BF16)
        ONESB = sb("ONESB", 128, BF16)
        ONE1 = sb("ONE1", 64, BF16)
        PERMF = sb("PERMF", 128, F32)
        PERMB = sb("PERMB", 128, BF16)
        MASK4 = sb("MASK4", 512, BF16)
        MASKS = sb("MASKS", 32, BF16)
        MASK40 = sb("MASK40", 512, BF16)
        ES = sb("ES", 16, F32)
        EPSC = sb("EPSC", 1, F32)
        IDENT = sb("IDENT", 128, BF16)
        MASKNS = sb("MASKNS", 64, BF16)
        RBYTES = 44288
        R = sb("R", RBYTES // 4, F32)
        banks = [st.enter_context(nc.psum_tensor("ps%d" % i, [128, 512], F32)) for i in range(8)]

        def carve(off, ncols, dt):
            sz = 4 if dt == F32 else 2
            assert off % 4 == 0 and (ncols * sz) % 4 == 0
            assert off + ncols * sz <= RBYTES, (off, ncols, sz)
            v = R[:, off // 4: off // 4 + (ncols * sz) // 4]
            return v if dt == F32 else v.bitcast(BF16)

        o = 0
        SIG = [carve(o + i * 2488, 622, F32) for i in range(2)]; o += 2 * 2488
        U = [carve(o + i * UW * 4, UW, F32) for i in range(2)]; o += 2 * UW * 4
        ACCD = [carve(o + i * 2368, 592, F32) for i in range(2)]; o += 2 * 2368
        UB = [carve(o + i * 1488, 744, BF16) for i in range(2)]; o += 2 * 1488
        NDG = 32
        DG = carve(o, NDG * 128, BF16); o += NDG * 256
        UC = carve(o, 16 * 30, F32); o += 16 * 30 * 4
        LNT = [carve(o + i * 2368, 592, F32) for i in range(2)]; o += 2 * 2368
        USAVE = carve(o, 16 * 62, F32); o += 16 * 62 * 4
        assert o <= RBYTES, o
        o = 0
        HID = carve(o, 8 * NTOK, BF16); o += 8 * NTOK * 2
        RT = [carve(o + i * 2048, 512, F32) for i in range(3)]; o += 3 * 2048
        YST = [carve(o + i * 2048, 512, F32) for i in range(2)]; o += 2 * 2048
        assert o <= RBYTES, o
        KTW = NTOK + 512
        o = 0
        KT = carve(o, 4 * KTW, BF16); o += 4 * KTW * 2
        VB = carve(o, 13 * 512, BF16); o += 13 * 512 * 2
        VN = carve(o, 4 * 512, BF16); o += 4 * 512 * 2
        o_alias = o
        QB = [carve(o + i * 1088, 544, BF16) for i in range(2)]; o += 2 * 1088
        QT = [carve(o + i * 1088, 544, BF16) for i in range(2)]; o += 2 * 1088
        KOUT = carve(o_alias, 4 * 160, F32)
        VST = [carve(o_alias + 2560 + i * 1024, 256, F32) for i in range(2)]
        assert o_alias + 2560 + 2048 <= o + 256
        o = max(o, o_alias + 2560 + 2048)
        NEV = 3
        EB = [carve(o + i * 1024, 512, BF16) for i in range(NEV)]; o += NEV * 1024
        PT = [carve(o + i * 1024, 512, BF16) for i in range(NEV)]; o += NEV * 1024
        DEN = [carve(o + i * 512, 128, F32) for i in range(NEV)]; o += NEV * 512
        assert o <= RBYTES, o

        def Xap(kc, a, b):
            return X[:, kc * TE + a: kc * TE + b]

        def Hap(kc, a, b):
            return H[:, kc * TE + a: kc * TE + b]

        def vcol(c):
            return VEC[:, c:c + 1]

        bank_ctr = [0]
        dgc = [0]

        def bank():
            b = bank_ctr[0] % 8
            bank_ctr[0] += 1
            return b

        wstate = {"issued": 0, "consumed": 0}

        def issue_next():
            n = wstate["issued"]
            if n >= len(order):
                return
            s = n % NS
            ti = tidx[order[n]]
            P.add("pool", C("dma_start", out=RING[:, s * 2048:(s + 1) * 2048], in_=wst[ti]),
                  writes=[("slot", s)], dma="slot%d" % s)
            wstate["issued"] = n + 1

        def get_tile(name):
            n = wstate["consumed"]
            assert order[n] == name, (order[n], name)
            s = n % NS
            wstate["consumed"] = n + 1
            return s

        def slot_ap(s, a, b):
            return RING[:, s * 2048 + a: s * 2048 + b]

        for k0 in range(0, KC, 4):
            wk = []
            for kc in range(k0, k0 + 4):
                wk += seg_keys("X", kc, 0, TE)
            P.add("sp", C("dma_start",
                out=X[:, k0 * TE:(k0 + 4) * TE].rearrange("p (k t) -> p k t", t=TE), in_=xT[:, k0:k0 + 4, :]),
                writes=wk, dma="xl%d" % k0)
        P.add("sp", C("dma_start", out=VEC[:], in_=vecs[:]), writes=["VEC"], dma="vec")
        P.add("sp", C("dma_start", out=TAB[:], in_=tab[:]), writes=["TAB"], dma="tab")
        P.add("sp", C("dma_start", out=PERMF[:], in_=perm[:]), writes=["PERMF"], dma="permf")
        P.add("pool", C("dma_start", out=PERMB[:], in_=perm[:]), writes=["PERMB"], dma="permb")
        P.add("pool", C("dma_start", out=MASK4[:], in_=mask4[:]), writes=["MASK4"], dma="mask4")
        P.add("pool", C("dma_start", out=IDENT[:], in_=ident[:]), writes=["IDENT"], dma="ident")
        P.add("pool", C("dma_start", out=MASKNS[:], in_=maskns[:]), writes=["MASKNS"], dma="maskns")
        P.add("pool", C("dma_start", out=MASKS[:], in_=masks[:]), writes=["MASKS"], dma="masks")
        P.add("pool", C("dma_start", out=MASK40[:], in_=mask40[:]), writes=["MASK4"], dma="mask40")
        for _ in range(NS):
            issue_next()
        P.add("dve", C("memset", ONESB[:], 1.0 / D), writes=["ONESB"])
        P.add("dve", C("memset", ONE1[:], 1.0), writes=["ONE1"])
        P.add("dve", C("memset", EPSC[:], EPS), writes=["EPSC"])
        P.add("act", C("activation", out=ES[:], in_=VEC[:, V_SINK:V_SINK + 16], func=AF.Exp),
              reads=["VEC"], writes=["ES"])
        outs.append(P.add("sp", C("dma_start", out=conv_old[:], in_=sc_raw[:, 8:30, :]), dma="d2d0"))
        outs.append(P.add("sp", C("dma_start", out=k_old[:], in_=ck_raw[:, 8:128, :]), dma="d2d1"))
        outs.append(P.add("sp", C("dma_start", out=v_old[:], in_=cv_raw[:, 8:128, :]), dma="d2d2"))

        def rstd_from_bank(b, n, rs_ap, rs_keys):
            P.add("act", C("activation", out=rs_ap, in_=banks[b][:, :n], func=AF.Ln, bias=EPSC[:, 0:1]),
                  reads=[("ps", b), "EPSC"], writes=rs_keys)
            P.add("act", C("activation", out=rs_ap, in_=rs_ap, func=AF.Exp, scale=-0.5), reads=rs_keys, writes=rs_keys)

        def rms_stats(a, b):
            for (t0, n) in tiles(a, b):
                bk = bank()
                for kc in range(KC):
                    sq = kc % 4
                    P.add("act", C("activation",
                        out=SQ[:, sq * 512: sq * 512 + n], in_=Xap(kc, t0, t0 + n), func=AF.Square),
                        reads=seg_keys("X", kc, t0, t0 + n), writes=[("SQ", sq)])
                    P.add("pe", C("matmul",
                        banks[bk][:, :n], lhsT=ONESB[:], rhs=SQ[:, sq * 512: sq * 512 + n],
                        start=(kc == 0), stop=(kc == KC - 1)),
                        reads=[("SQ", sq), "ONESB"], writes=[("ps", bk)])
                rstd_from_bank(bk, n, TR0[:, t0:t0 + n], seg_keys("RS", 0, t0, t0 + n))

        def rms_to_h(a, b, gcol):
            rms_stats(a, b)
            for kc in range(KC):
                for (t0, n) in tiles(a, b):
                    P.add("dve", C("scalar_tensor_tensor",
                        out=Hap(kc, t0, t0 + n), in0=Xap(kc, t0, t0 + n), scalar=vcol(gcol + kc),
                        in1=TR0[:, t0:t0 + n], op0=ALU.mult, op1=ALU.mult),
                        reads=seg_keys("X", kc, t0, t0 + n) + seg_keys("RS", 0, t0, t0 + n) + ["VEC"],
                        writes=seg_keys("H", kc, t0, t0 + n))

        def proj_group(slot, lhs_off, lhs_stride, nk, rhs_fn, rhs_keys_fn, tl):
            bks = [bank() for _ in tl]
            for k in range(nk):
                for i, (t0, n) in enumerate(tl):
                    P.add("pe", C("matmul",
                        banks[bks[i]][:, :n], lhsT=slot_ap(slot, lhs_off + k * lhs_stride, lhs_off + k * lhs_stride + 128),
                        rhs=rhs_fn(k, t0, n), start=(k == 0), stop=(k == nk - 1)),
                        reads=[("slot", slot)] + rhs_keys_fn(k, t0, n), writes=[("ps", bks[i])])
            return bks

        def h_rhs(k, t0, n):
            return Hap(k, t0, t0 + n)

        def h_keys(k, t0, n):
            return seg_keys("H", k, t0, t0 + n)

        try:
            checkpoint(0)
            def conv_pass(pa):
                if pa == 0:
                    ga, gb, oa, ob = 0, 622, 30, 622
                    cbase = 622
                else:
                    ga, gb, oa, ob = 622, 1214, 622, 1214
                    cbase = 0
                rms_to_h(ga, gb, V_NMIX0)
                checkpoint(1)
                tl = tiles(ga, gb)

                def cap(oc, a, b):
                    return Hap(oc, cbase + a, cbase + b)

                def ckeys(oc, a, b):
                    return seg_keys("H", oc, cbase + a, cbase + b)

                UWID = 622 if pa == 0 else UW

                def stage1(oc):
                    u = oc % 2
                    sA = get_tile(("pw1a", oc))
                    bA = proj_group(sA, 0, 128, KC, h_rhs, h_keys, tl)
                    sG = get_tile(("pw1g", oc))
                    bG = proj_group(sG, 0, 128, KC, h_rhs, h_keys, tl)
                    issue_next()
                    issue_next()
                    if pa == 1:
                        P.add("act", C("activation", out=U[u][:, 0:30], in_=UC[:, oc * 30:(oc + 1) * 30], func=AF.Copy),
                              reads=[("UC", oc)], writes=[("U", u)])
                        P.add("sp", C("dma_start",
                            out=U[u][:, 590:742].rearrange("p (s c) -> p s c", c=38)[:, :, 0:30], in_=ctxT[:, oc, :, :]),
                            writes=[("U", u)], dma="ctx%d" % u)
                    for i, (t0, n) in enumerate(tl):
                        g0 = t0 - ga
                        P.add("act", C("activation",
                            out=SIG[u][:, g0:g0 + n], in_=banks[bG[i]][:, :n], func=AF.Sigmoid, bias=vcol(V_BG + oc)),
                            reads=[("ps", bG[i]), "VEC"], writes=[("SIG", u)])
                        npr = min(t0 + n, T_SMP) - t0
                        if npr > 0:
                            uc0 = t0 if pa == 0 else 30 + (t0 - 622)
                            P.add("dve", C("scalar_tensor_tensor",
                                out=U[u][:, uc0:uc0 + npr], in0=banks[bA[i]][:, :npr], scalar=vcol(V_BA + oc),
                                in1=SIG[u][:, g0:g0 + npr], op0=ALU.add, op1=ALU.mult),
                                reads=[("ps", bA[i]), ("SIG", u), "VEC"], writes=[("U", u)])
                        if npr < n:
                            assert n - npr == 32
                            P.add("dve", C("scalar_tensor_tensor",
                                out=U[u][:, 590:742].rearrange("p (s c) -> p s c", c=38)[:, :, 30:38],
                                in0=banks[bA[i]][:, npr:npr + 32].rearrange("p (s c) -> p s c", c=8), scalar=vcol(V_BA + oc),
                                in1=SIG[u][:, g0 + npr:g0 + npr + 32].rearrange("p (s c) -> p s c", c=8),
                                op0=ALU.add, op1=ALU.mult),
                                reads=[("ps", bA[i]), ("SIG", u), "VEC"], writes=[("U", u)])
                    if pa == 0:
                        P.add("dve", C("tensor_single_scalar", out=U[u][:, 0:158], in_=U[u][:, 0:158],
                                       scalar=vcol(V_FLAG), op=ALU.mult),
                              reads=[("U", u), "VEC"], writes=[("U", u)])
                        P.add("act", C("activation", out=UC[:, oc * 30:(oc + 1) * 30], in_=U[u][:, 592:622], func=AF.Copy),
                              reads=[("U", u)], writes=[("UC", oc)])
                    else:
                        P.add("act", C("activation", out=USAVE[:, oc * 62: oc * 62 + 30], in_=U[u][:, 560:590], func=AF.Copy),
                              reads=[("U", u)], writes=["USAVE"])
                        P.add("act", C("activation",
                            out=USAVE[:, oc * 62 + 30: oc * 62 + 62].rearrange("p (s c) -> p s c", c=8),
                            in_=U[u][:, 590:742].rearrange("p (s c) -> p s c", c=38)[:, :, 30:38], func=AF.Copy),
                            reads=[("U", u)], writes=["USAVE"])
                    if NDV < CW:
                        P.add("act", C("activation", out=UB[u][:, 0:UWID], in_=U[u][:, 0:UWID], func=AF.Copy),
                              reads=[("U", u)], writes=[("UB", u)])

                def stage2a(oc):
                    dgs = []
                    for j in range(NDV, CW):
                        dg = dgc[0] % NDG
                        dgc[0] += 1
                        P.add("act", C("activation", out=DG[:, dg * 128:(dg + 1) * 128], in_=IDENT[:, :], func=AF.Copy,
                                       scale=vcol(V_WDW + oc * CW + j)),
                              reads=["IDENT", "VEC"], writes=[("DG", dg)])
                        dgs.append(dg)
                    return dgs

                def stage2(oc, dgs):
                    u = oc % 2
                    if pa == 0:
                        parts = [(lambda acc: acc[:, 0:512], lambda X_, j: X_[:, j:j + 512], lambda: cap(oc, 0, 512), lambda bk: banks[bk][:, 0:512]),
                                 (lambda acc: acc[:, 512:592], lambda X_, j: X_[:, 512 + j:592 + j], lambda: cap(oc, 512, 592), lambda bk: banks[bk][:, 0:80])]
                    else:
                        parts = [(lambda acc: acc[:, 0:512], lambda X_, j: X_[:, j:j + 512], lambda: cap(oc, 0, 512), lambda bk: banks[bk][:, 0:512]),
                                 (lambda acc: acc[:, 512:560], lambda X_, j: X_[:, 512 + j:560 + j], lambda: cap(oc, 512, 560), lambda bk: banks[bk][:, 0:48]),
                                 (lambda acc: acc[:, 560:592].rearrange("p (s c) -> p s c", c=8),
                                  lambda X_, j: X_[:, 590:742].rearrange("p (s c) -> p s c", c=38)[:, :, j:j + 8],
                                  lambda: cap(oc, 560, 592).rearrange("p (s c) -> p s c", c=8),
                                  lambda bk: banks[bk][:, 0:32].rearrange("p (s c) -> p s c", c=8))]
                    akey = ("ACC", "dve", u)
                    cb = []
                    if NDV < CW:
                        cb = [bank() for _ in parts]
                        for j in range(NDV, CW):
                            dg = dgs[j - NDV]
                            for pi, (accv, xv, cv, pv) in enumerate(parts):
                                P.add("pe", C("matmul", pv(cb[pi]), lhsT=DG[:, dg * 128:(dg + 1) * 128], rhs=xv(UB[u], j),
                                              start=(j == NDV), stop=(j == CW - 1)),
                                      reads=[("DG", dg), ("UB", u)], writes=[("ps", cb[pi])])
                    for j in range(NDV):
                        wcol = vcol(V_WDW + oc * CW + j)
                        for pi, (accv, xv, cv, pv) in enumerate(parts):
                            if j == 0:
                                P.add("dve", C("tensor_scalar", out=accv(ACCD[u]), in0=xv(U[u], j), scalar1=wcol,
                                               scalar2=vcol(V_BDW + oc), op0=ALU.mult, op1=ALU.add),
                                      reads=[("U", u), "VEC"], writes=[akey])
                            else:
                                P.add("dve", C("scalar_tensor_tensor", out=accv(ACCD[u]), in0=xv(U[u], j), scalar=wcol,
                                               in1=accv(ACCD[u]), op0=ALU.mult, op1=ALU.add),
                                      reads=[("U", u), "VEC", akey], writes=[akey])
                    for pi, (accv, xv, cv, pv) in enumerate(parts):
                        if NDV == 0:
                            P.add("dve", C("tensor_single_scalar", out=cv(), in_=pv(cb[pi]), scalar=vcol(V_BDW + oc), op=ALU.add),
                                  reads=[("ps", cb[pi]), "VEC"], writes=ckeys(oc, 0, 592))
                        elif NDV == CW:
                            P.add("dve", C("tensor_copy", out=cv(), in_=accv(ACCD[u])), reads=[akey], writes=ckeys(oc, 0, 592))
                        else:
                            P.add("dve", C("tensor_tensor", out=cv(), in0=pv(cb[pi]), in1=accv(ACCD[u]), op=ALU.add),
                                  reads=[("ps", cb[pi]), akey], writes=ckeys(oc, 0, 592))

                stage1(0)
                for oc in range(16):
                    dgs = stage2a(oc)
                    if oc + 1 < 16:
                        stage1(oc + 1)
                    stage2(oc, dgs)
                ltl = tiles(0, 592)
                bmu = [bank() for _ in ltl]
                bvar = [bank() for _ in ltl]
                for oc in range(16):
                    for i, (c0, n) in enumerate(ltl):
                        sq = (2 * oc + i) % 4
                        P.add("act", C("activation",
                            out=SQ[:, sq * 512: sq * 512 + n], in_=cap(oc, c0, c0 + n), func=AF.Square),
                            reads=ckeys(oc, c0, c0 + n), writes=[("SQ", sq)])
                        P.add("pe", C("matmul",
                            banks[bmu[i]][:, :n], lhsT=ONESB[:], rhs=cap(oc, c0, c0 + n), start=(oc == 0), stop=(oc == 15)),
                            reads=ckeys(oc, c0, c0 + n) + ["ONESB"], writes=[("ps", bmu[i])])
                        P.add("pe", C("matmul",
                            banks[bvar[i]][:, :n], lhsT=ONESB[:], rhs=SQ[:, sq * 512: sq * 512 + n], start=(oc == 0), stop=(oc == 15)),
                            reads=[("SQ", sq), "ONESB"], writes=[("ps", bvar[i])])
                for i, (c0, n) in enumerate(ltl):
                    P.add("act", C("activation", out=TR1[:, c0:c0 + n], in_=banks[bmu[i]][:, :n], func=AF.Copy),
                          reads=[("ps", bmu[i])], writes=[("MU", i)])
                    P.add("dve", C("tensor_tensor", out=LNT[0][:, c0:c0 + n], in0=TR1[:, c0:c0 + n], in1=TR1[:, c0:c0 + n], op=ALU.mult),
                          reads=[("MU", i)], writes=[("LNT", 0)])
                    P.add("dve", C("tensor_tensor", out=LNT[0][:, c0:c0 + n], in0=banks[bvar[i]][:, :n], in1=LNT[0][:, c0:c0 + n], op=ALU.subtract),
                          reads=[("ps", bvar[i]), ("LNT", 0)], writes=[("LNT", 0)])
                    P.add("act", C("activation", out=TR0[:, c0:c0 + n], in_=LNT[0][:, c0:c0 + n], func=AF.Ln, bias=EPSC[:, 0:1]),
                          reads=[("LNT", 0), "EPSC"], writes=seg_keys("RS", 0, c0, c0 + n))
                    P.add("act", C("activation", out=TR0[:, c0:c0 + n], in_=TR0[:, c0:c0 + n], func=AF.Exp, scale=-0.5),
                          reads=seg_keys("RS", 0, c0, c0 + n), writes=seg_keys("RS", 0, c0, c0 + n))
                rs_keys = seg_keys("RS", 0, 0, 592)
                for oc in range(16):
                    a = oc % 2
                    P.add("dve", C("tensor_tensor", out=LNT[a][:, 0:592], in0=cap(oc, 0, 592), in1=TR1[:, 0:592], op=ALU.subtract),
                          reads=ckeys(oc, 0, 592) + [("MU", 0), ("MU", 1)], writes=[("LNT", a)])
                    P.add("dve", C("tensor_tensor", out=LNT[a][:, 0:592], in0=LNT[a][:, 0:592], in1=TR0[:, 0:592], op=ALU.mult),
                          reads=[("LNT", a)] + rs_keys, writes=[("LNT", a)])
                    P.add("act", C("activation", out=Hap(oc, oa, ob), in_=LNT[a][:, 0:592], func=AF.Silu,
                                                                   bias=vcol(V_LNB + oc), scale=vcol(V_LNG + oc)),
                          reads=[("LNT", a), "VEC"], writes=seg_keys("H", oc, oa, ob))
                otl = tiles(oa, ob)
                for oc in range(16):
                    s2 = get_tile(("pw2", oc))
                    bks = proj_group(s2, 0, 128, KC, h_rhs, h_keys, otl)
                    issue_next()
                    for i, (t0, n) in enumerate(otl):
                        P.add("dve", C("scalar_tensor_tensor",
                            out=Xap(oc, t0, t0 + n), in0=banks[bks[i]][:, :n], scalar=vcol(V_BPW2 + oc),
                            in1=Xap(oc, t0, t0 + n), op0=ALU.add, op1=ALU.add),
                            reads=[("ps", bks[i]), "VEC"] + seg_keys("X", oc, t0, t0 + n), writes=seg_keys("X", oc, t0, t0 + n))

            conv_pass(0)
            checkpoint(2)
            conv_pass(1)
            checkpoint(3)
            outs.append(P.add("sp", C("dma_start", out=uT_out[:], in_=USAVE[:]), reads=["USAVE"], dma="usave"))
            P.fence()

            def mlp(l, a, b, gcol):
                rms_to_h(a, b, gcol)
                tl = tiles(a, b)
                for g in range(8):
                    for fc in range(8):
                        s = get_tile(("up", l, g * 8 + fc))
                        bks = proj_group(s, 0, 128, KC, h_rhs, h_keys, tl)
                        issue_next()
                        for i, (t0, n) in enumerate(tl):
                            P.add("act", C("activation", out=RT[i][:, :n], in_=banks[bks[i]][:, :n], func=AF.Relu),
                                  reads=[("ps", bks[i])], writes=[("RT", i)])
                        for i, (t0, n) in enumerate(tl):
                            P.add("act", C("activation",
                                out=HID[:, fc * NTOK + (t0 - a): fc * NTOK + (t0 - a) + n], in_=RT[i][:, :n], func=AF.Square),
                                reads=[("RT", i)], writes=[("HID", fc, i)])
                    for ocp in range(8):
                        s = get_tile(("down", l, g, ocp))
                        for o2 in range(2):
                            oc = 2 * ocp + o2
                            bks = proj_group(s, o2 * 128, 256, 8,
                                             lambda k, t0, n: HID[:, k * NTOK + (t0 - a): k * NTOK + (t0 - a) + n],
                                             lambda k, t0, n: [("HID", k, i) for i in range(3)], tl)
                            for i, (t0, n) in enumerate(tl):
                                P.add("dve", C("tensor_tensor",
                                    out=Xap(oc, t0, t0 + n), in0=banks[bks[i]][:, :n], in1=Xap(oc, t0, t0 + n), op=ALU.add),
                                    reads=[("ps", bks[i])] + seg_keys("X", oc, t0, t0 + n), writes=seg_keys("X", oc, t0, t0 + n))
                        issue_next()

            mlp(0, 30, TE, V_NMLP0)
            P.fence()
            checkpoint(4)

            P.add("pool", C("dma_start",
                out=KT[:].rearrange("p (c w) -> p c w", w=KTW)[:, :, NTOK:NTOK + 512],
                in_=ckT[:]), writes=["KTC"], dma="ktc")
            P.add("pool", C("dma_start",
                out=VB[:, 9 * 512:13 * 512].rearrange("p (s f) -> p s f", f=512),
                in_=cv_raw[:].rearrange("s k f -> k s f")), writes=["VBC"], dma="vbc")
            rms_to_h(30, TE, V_KVN)
            ktl = tiles(30, TE)
            CT = lambda t0, n: TAB[:, t0 - 30: t0 - 30 + n]
            ST = lambda t0, n: TAB[:, NTOK + t0 - 30: NTOK + t0 - 30 + n]
            for c in range(4):
                s = get_tile(("wk", c))
                bks = proj_group(s, 0, 128, KC, h_rhs, h_keys, ktl)
                issue_next()
                for i, (t0, n) in enumerate(ktl):
                    kf = TR1[:, t0 - 30: t0 - 30 + n]
                    kk = [("KF", i)]
                    P.add("act", C("activation", out=kf, in_=banks[bks[i]][:, :n], func=AF.Copy),
                          reads=[("ps", bks[i])], writes=kk)
                    b2 = bank()
                    P.add("pe", C("matmul", banks[b2][:, :n], lhsT=PERMF[:], rhs=kf, start=True, stop=True),
                          reads=kk + ["PERMF"], writes=[("ps", b2)])
                    tv = TR0[:, t0 - 30: t0 - 30 + n]
                    tk = seg_keys("RS", 0, t0 - 30, t0 - 30 + n)
                    P.add("dve", C("tensor_tensor", out=tv, in0=banks[b2][:, :n], in1=ST(t0, n), op=ALU.mult),
                          reads=[("ps", b2), "TAB"], writes=tk)
                    P.add("dve", C("tensor_tensor", out=kf, in0=kf, in1=CT(t0, n), op=ALU.mult),
                          reads=kk + ["TAB"], writes=kk)
                    P.add("dve", C("tensor_tensor", out=kf, in0=kf, in1=tv, op=ALU.add),
                          reads=kk + tk, writes=kk)
                    P.add("act", C("activation",
                        out=KT[:, c * KTW + t0 - 30: c * KTW + t0 - 30 + n], in_=kf, func=AF.Copy),
                        reads=kk, writes=[("KT", c, i)])
                    if i == 2:
                        assert t0 == 1054 and n == 160
                        P.add("act", C("activation", out=KOUT[:, c * 160:(c + 1) * 160], in_=kf, func=AF.Copy),
                              reads=kk, writes=["KOUT"])
            outs.append(P.add("sp", C("dma_start", out=kT_out[:], in_=KOUT[:]), reads=["KOUT"], dma="kout"))
            checkpoint(5)
            vgroups = [[("blk", bi) for bi in range(6)], [("blk", bi) for bi in range(6, 9)] + [("smp", s_) for s_ in range(4)]]
            vst_ctr = [0]
            for grp in vgroups:
                gb_ = [bank() for _ in grp]
                for c in range(4):
                    s = get_tile(("wv", c))
                    for gi, (kind, idx) in enumerate(grp):
                        for kk_ in range(4):
                            kc = c * 4 + kk_
                            if kind == "blk":
                                t0 = 30 + 128 * idx
                                P.add("pe", C("matmul",
                                    banks[gb_[gi]][:, :], lhsT=Hap(kc, t0, t0 + 128), rhs=slot_ap(s, kk_ * 512, kk_ * 512 + 512),
                                    start=(kc == 0), stop=(kc == KC - 1)),
                                    reads=[("slot", s)] + seg_keys("H", kc, t0, t0 + 128), writes=[("ps", gb_[gi])])
                            else:
                                t0 = T_SMP + 8 * idx
                                P.add("pe", C("matmul",
                                    banks[gb_[gi]][0:8, :], lhsT=Hap(kc, t0, t0 + 8), rhs=slot_ap(s, kk_ * 512, kk_ * 512 + 512),
                                    start=(kc == 0), stop=(kc == KC - 1)),
                                    reads=[("slot", s)] + seg_keys("H", kc, t0, t0 + 8), writes=[("ps", gb_[gi])])
                    issue_next()
                for gi, (kind, idx) in enumerate(grp):
                    bkk = gb_[gi]
                    if kind == "blk":
                        P.add("act", C("activation", out=VB[:, idx * 512:(idx + 1) * 512], in_=banks[bkk][:, :], func=AF.Copy),
                              reads=[("ps", bkk)], writes=[("VB", idx)])
                        if idx == 8:
                            for hh in range(2):
                                P.add("act", C("activation", out=VST[hh][:, :], in_=banks[bkk][:, hh * 256:(hh + 1) * 256], func=AF.Copy),
                                      reads=[("ps", bkk)], writes=[("VST", hh)])
                                outs.append(P.add("sp", C("dma_start", out=v_out[:, hh * 256:(hh + 1) * 256], in_=VST[hh][:, :]),
                                                  reads=[("VST", hh)], dma="vst%d" % hh))
                    else:
                        P.add("act", C("activation", out=VN[0:8, idx * 512:(idx + 1) * 512], in_=banks[bkk][0:8, :], func=AF.Copy),
                              reads=[("ps", bkk)], writes=[("VN", idx)])
                        for hh in range(2):
                            P.add("act", C("activation", out=VST[hh][0:8, :], in_=banks[bkk][0:8, hh * 256:(hh + 1) * 256], func=AF.Copy),
                                  reads=[("ps", bkk)], writes=[("VST", hh)])
                            outs.append(P.add("sp", C("dma_start", out=v_new[idx, :, hh * 256:(hh + 1) * 256], in_=VST[hh][0:8, :]),
                                              reads=[("VST", hh)], dma="vst%d" % hh))

            def attn_pass(pa):
                if pa == 0:
                    qa, qb, obase = 158, 670, 670
                    blocks = [0, 1, 2, 3]
                    smp = []
                else:
                    qa, qb, obase = 670, 1214, 30
                    blocks = [4, 5, 6, 7]
                    smp = [0, 1, 2, 3]
                nq = qb - qa
                rms_to_h(qa, qb, V_NMIX1)
                qtl = tiles(qa, qb)

                def oT(oc, a, n):
                    return Hap(oc, obase + a, obase + a + n)

                def oT_keys(oc, a, n):
                    return seg_keys("H", oc, obase + a, obase + a + n)

                ev = [0]

                def stage_q(oc):
                    u = oc % 2
                    s = get_tile(("wq", oc))
                    bks = proj_group(s, 0, 128, KC, h_rhs, h_keys, qtl)
                    issue_next()
                    for i, (t0, n) in enumerate(qtl):
                        off = t0 - qa
                        P.add("act", C("activation", out=QB[u][:, off:off + n], in_=banks[bks[i]][:, :n], func=AF.Copy),
                              reads=[("ps", bks[i])], writes=[("QB", u, i), "KOUT", ("VST", 0), ("VST", 1)])
                        b2 = bank()
                        P.add("pe", C("matmul", banks[b2][:, :n], lhsT=PERMB[:], rhs=QB[u][:, off:off + n], start=True, stop=True),
                              reads=[("QB", u, i), "PERMB"], writes=[("ps", b2)])
                        t1 = TR1[:, off:off + n]
                        t2 = TR1[:, 600 + off:600 + off + n]
                        P.add("dve", C("tensor_tensor", out=t1, in0=banks[b2][:, :n], in1=ST(t0, n), op=ALU.mult),
                              reads=[("ps", b2), "TAB"], writes=[("QTMP", 0, i)])
                        P.add("dve", C("tensor_tensor", out=t2, in0=QB[u][:, off:off + n], in1=CT(t0, n), op=ALU.mult),
                              reads=[("QB", u, i), "TAB"], writes=[("QTMP", 1, i)])
                        P.add("dve", C("tensor_tensor", out=QT[u][:, off:off + n], in0=t1, in1=t2, op=ALU.add),
                              reads=[("QTMP", 0, i), ("QTMP", 1, i)], writes=[("QT", u, i)])

                def unit_scores(oc, unit):
                    c = oc // 4
                    u = oc % 2
                    qt_keys = [("QT", u, i) for i in range(len(qtl))]
                    v = ev[0] % NEV
                    ev[0] += 1
                    bSh = [sbank(), sbank()]
                    if unit[0] == "blk":
                        m = unit[1]
                        tb = 158 + 128 * m
                        off = tb - qa
                        kprev = c * KTW + (tb - 128 - 30)
                        kown = c * KTW + (tb - 30)
                        for hd in range(2):
                            pr0, pr1 = hd * 64, hd * 64 + 64
                            for kh, kcol in enumerate((kprev, kown)):
                                P.add("pe", C("matmul", banks[bSh[hd]][:, kh * 128: kh * 128 + 128],
                                              lhsT=KT[pr0:pr1, kcol:kcol + 128], rhs=QT[u][pr0:pr1, off:off + 128], start=True, stop=True),
                                      reads=[("KT", c, 0), ("KT", c, 1), ("KT", c, 2)] + qt_keys, writes=[("ps", bSh[hd])])
                        for hd in range(2):
                            P.add("act", C("activation", out=PT[v][:, hd * 256:(hd + 1) * 256], in_=banks[bSh[hd]][:, 0:256], func=AF.Exp, scale=0.125),
                                  reads=[("ps", bSh[hd])], writes=[("PT", v)])
                        M4 = MASK40 if m == 0 else MASK4
                        P.add("dve", C("tensor_tensor", out=EB[v][:, :], in0=PT[v][:, :], in1=M4[:, :], op=ALU.mult),
                              reads=[("PT", v), "MASK4"], writes=[("EB", v)])
                    else:
                        for hd in range(2):
                            pr0, pr1 = hd * 64, hd * 64 + 64
                            P.add("pe", C("matmul", banks[bSh[hd]][:, 0:32], lhsT=IDENT[:, :], rhs=MASKNS[:, 0:32], start=True, stop=False),
                                  reads=["IDENT", "MASKNS"], writes=[("ps", bSh[hd])])
                            for sq_ in range(4):
                                off = (T_SMP - qa) + 8 * sq_
                                kcache = c * KTW + NTOK + sq_ * 128
                                P.add("pe", C("matmul", banks[bSh[hd]][:, sq_ * 8: sq_ * 8 + 8],
                                              lhsT=KT[pr0:pr1, kcache:kcache + 128], rhs=QT[u][pr0:pr1, off:off + 8], start=False, stop=(sq_ == 3)),
                                      reads=["KTC"] + qt_keys, writes=[("ps", bSh[hd])])
                            P.add("pe", C("matmul", banks[bSh[hd]][0:8, 32:64], lhsT=IDENT[0:8, 0:8], rhs=MASKNS[0:8, 32:64], start=True, stop=False),
                                  reads=["IDENT", "MASKNS"], writes=[("ps", bSh[hd])])
                            for sq_ in range(4):
                                off = (T_SMP - qa) + 8 * sq_
                                knew = c * KTW + (T_SMP + 8 * sq_ - 30)
                                P.add("pe", C("matmul", banks[bSh[hd]][0:8, 32 + sq_ * 8: 40 + sq_ * 8],
                                              lhsT=KT[pr0:pr1, knew:knew + 8], rhs=QT[u][pr0:pr1, off:off + 8], start=False, stop=(sq_ == 3)),
                                      reads=[("KT", c, 2)] + qt_keys, writes=[("ps", bSh[hd])])
                        for hd in range(2):
                            P.add("act", C("activation", out=EB[v][:, hd * 64: hd * 64 + 32], in_=banks[bSh[hd]][:, 0:32], func=AF.Exp, scale=0.125),
                                  reads=[("ps", bSh[hd])], writes=[("EB", v)])
                            P.add("act", C("activation", out=EB[v][0:8, hd * 64 + 32: hd * 64 + 64], in_=banks[bSh[hd]][0:8, 32:64], func=AF.Exp, scale=0.125),
                                  reads=[("ps", bSh[hd])], writes=[("EB", v)])
                    return v

                def unit_pv(oc, unit, v):
                    c = oc // 4
                    kva, kvb = 2 * c, 2 * c + 1
                    bO = obank()
                    if unit[0] == "blk":
                        m = unit[1]
                        off = 158 + 128 * m - qa
                        w = 128
                        for hd, kv in enumerate((kva, kvb)):
                            pr0, pr1 = hd * 64, hd * 64 + 64
                            for kh, vblk in enumerate((m, m + 1)):
                                P.add("pe", C("matmul", banks[bO][pr0:pr1, 0:128], lhsT=VB[:, vblk * 512 + kv * 64: vblk * 512 + kv * 64 + 64],
                                              rhs=EB[v][:, hd * 256 + kh * 128: hd * 256 + kh * 128 + 128], start=(kh == 0), stop=(kh == 1)),
                                      reads=[("VB", vblk), ("EB", v)], writes=[("ps", bO)])
                            for kh in range(2):
                                P.add("pe", C("matmul", banks[bO][pr0:pr1, 128:256], lhsT=ONE1[:, 0:64],
                                              rhs=EB[v][:, hd * 256 + kh * 128: hd * 256 + kh * 128 + 128], start=(kh == 0), stop=(kh == 1)),
                                      reads=["ONE1", ("EB", v)], writes=[("ps", bO)])
                    else:
                        off = T_SMP - qa
                        w = 32
                        for hd, kv in enumerate((kva, kvb)):
                            pr0, pr1 = hd * 64, hd * 64 + 64
                            for sq_ in range(4):
                                rc = EB[v][:, hd * 64 + sq_ * 8: hd * 64 + sq_ * 8 + 8]
                                rn = EB[v][0:8, hd * 64 + 32 + sq_ * 8: hd * 64 + 40 + sq_ * 8]
                                P.add("pe", C("matmul", banks[bO][pr0:pr1, sq_ * 8: sq_ * 8 + 8],
                                              lhsT=VB[:, (9 + sq_) * 512 + kv * 64: (9 + sq_) * 512 + kv * 64 + 64], rhs=rc, start=True, stop=False),
                                      reads=["VBC", ("EB", v)], writes=[("ps", bO)])
                                P.add("pe", C("matmul", banks[bO][pr0:pr1, sq_ * 8: sq_ * 8 + 8],
                                              lhsT=VN[0:8, sq_ * 512 + kv * 64: sq_ * 512 + kv * 64 + 64], rhs=rn, start=False, stop=True),
                                      reads=[("VN", sq_), ("EB", v)], writes=[("ps", bO)])
                                P.add("pe", C("matmul", banks[bO][pr0:pr1, 128 + sq_ * 8: 136 + sq_ * 8], lhsT=ONE1[:, 0:64], rhs=rc, start=True, stop=False),
                                      reads=["ONE1", ("EB", v)], writes=[("ps", bO)])
                                P.add("pe", C("matmul", banks[bO][pr0:pr1, 128 + sq_ * 8: 136 + sq_ * 8], lhsT=ONE1[0:8, 0:64], rhs=rn, start=False, stop=True),
                                      reads=["ONE1", ("EB", v)], writes=[("ps", bO)])
                    P.add("act", C("activation", out=DEN[v][:, 0:w], in_=banks[bO][:, 128:128 + w], func=AF.Ln, bias=ES[:, oc:oc + 1]),
                          reads=[("ps", bO), "ES"], writes=[("DEN", v)])
                    P.add("act", C("activation", out=DEN[v][:, 0:w], in_=DEN[v][:, 0:w], func=AF.Exp, scale=-1.0),
                          reads=[("DEN", v)], writes=[("DEN", v)])
                    P.add("dve", C("tensor_tensor", out=oT(oc, off, w), in0=banks[bO][:, 0:w], in1=DEN[v][:, 0:w], op=ALU.mult),
                          reads=[("ps", bO), ("DEN", v)], writes=oT_keys(oc, off, w))

                units = [("blk", m) for m in blocks] + ([("smp", 0)] if smp else [])
                seq = [(oc, un) for oc in range(16) for un in units]
                sctr = [0]
                octr = [0]

                def sbank():
                    b_ = sctr[0] % 6
                    sctr[0] += 1
                    return b_

                def obank():
                    b_ = 6 + octr[0] % 2
                    octr[0] += 1
                    return b_

                stage_q(0)
                pend = []
                for idx, (oc, un) in enumerate(seq):
                    if un is units[0] and oc + 1 < 16:
                        stage_q(oc + 1)
                    v = unit_scores(oc, un)
                    pend.append((oc, un, v))
                    if len(pend) > 2:
                        unit_pv(*pend.pop(0))
                while pend:
                    unit_pv(*pend.pop(0))
                otl = [(t0 - qa, n) for (t0, n) in qtl]
                for oc in range(16):
                    s = get_tile(("wo", oc))
                    bks = proj_group(s, 0, 128, KC,
                                     lambda k, o0, n: oT(k, o0, n), lambda k, o0, n: oT_keys(k, o0, n), otl)
                    issue_next()
                    for i, (o0, n) in enumerate(otl):
                        t0 = qa + o0
                        P.add("dve", C("tensor_tensor",
                            out=Xap(oc, t0, t0 + n), in0=banks[bks[i]][:, :n], in1=Xap(oc, t0, t0 + n), op=ALU.add),
                            reads=[("ps", bks[i])] + seg_keys("X", oc, t0, t0 + n), writes=seg_keys("X", oc, t0, t0 + n))

            checkpoint(6)
            attn_pass(0)
            checkpoint(7)
            attn_pass(1)
            checkpoint(8)
            P.fence()

            mlp(1, 158, TE, V_NMLP1)
            checkpoint(9)

            rms_stats(158, TE)
            yc = [0]
            for (t0, n) in tiles(158, TE):
                for kc in range(KC):
                    yb = yc[0] % 2
                    yc[0] += 1
                    P.add("dve", C("scalar_tensor_tensor",
                        out=YST[yb][:, :n], in0=Xap(kc, t0, t0 + n), scalar=vcol(V_FIN + kc),
                        in1=TR0[:, t0:t0 + n], op0=ALU.mult, op1=ALU.mult),
                        reads=seg_keys("X", kc, t0, t0 + n) + seg_keys("RS", 0, t0, t0 + n) + ["VEC"], writes=[("YST", yb)])
                    outs.append(P.add("sp", C("dma_start",
                        out=yT[:, kc, t0 - 158: t0 - 158 + n], in_=YST[yb][:, :n]), reads=[("YST", yb)], dma="yst%d" % yb))

            assert wstate["consumed"] == len(order), (wstate, len(order))
        except _Stop:
            pass
        P.finalize(outs)
        P.emit(nc)
    return nc


def _tile_kc(W, c0):
    return np.ascontiguousarray(W[:, c0:c0 + 128].reshape(16, 128, 128).transpose(1, 0, 2).reshape(128, 2048))


def _col(v):
    return np.ascontiguousarray(np.asarray(v, np.float32).reshape(16, 128).T)


_NC_CACHE = {}


def prepare(x_prompt, x_sample, state_conv, cache_k, cache_v, norm_mix, w_pw1, b_pw1, w_dw, b_dw,
           conv_ln_g, conv_ln_b, w_pw2, b_pw2, kv_norm, w_k, w_v, w_q, w_o, sinks, norm_mlp,
           w_up, w_down, final_norm):
    f = lambda a: np.asarray(a, dtype=np.float32)
    x_prompt, x_sample, state_conv, cache_k, cache_v = map(f, (x_prompt, x_sample, state_conv, cache_k, cache_v))
    w_pw1, w_pw2, w_k, w_v, w_q, w_o, w_up, w_down = map(f, (w_pw1, w_pw2, w_k, w_v, w_q, w_o, w_up, w_down))
    norm_mix, norm_mlp, b_pw1, w_dw, b_dw = map(f, (norm_mix, norm_mlp, b_pw1, w_dw, b_dw))
    conv_ln_g, conv_ln_b, b_pw2, kv_norm, sinks, final_norm = map(f, (conv_ln_g, conv_ln_b, b_pw2, kv_norm, sinks, final_norm))

    tidx = unique_tiles()

    qperm = np.zeros(2048, np.int64)
    head_of = np.zeros((16, 2), np.int64)
    for oc in range(16):
        c, i = oc // 4, oc % 4
        for hd in range(2):
            h = 8 * c + 4 * hd + i
            head_of[oc, hd] = h
            qperm[oc * 128 + hd * 64: oc * 128 + hd * 64 + 64] = h * 64 + np.arange(64)
    wq_p = w_q[0][:, qperm]
    wo_p = w_o[0][qperm, :]

    wst = np.empty((len(tidx), 128, 2048), np.float32)
    for nm, ti in tidx.items():
        kind = nm[0]
        if kind == "pw1a":
            wst[ti] = _tile_kc(w_pw1[0], nm[1] * 128)
        elif kind == "pw1g":
            wst[ti] = _tile_kc(w_pw1[0], 2048 + nm[1] * 128)
        elif kind == "pw2":
            wst[ti] = _tile_kc(w_pw2[0], nm[1] * 128)
        elif kind == "up":
            wst[ti] = _tile_kc(w_up[nm[1]], nm[2] * 128)
        elif kind == "down":
            l, g, ocp = nm[1], nm[2], nm[3]
            wst[ti] = w_down[l][g * 1024:(g + 1) * 1024, ocp * 256:(ocp + 1) * 256].reshape(8, 128, 256).transpose(1, 0, 2).reshape(128, 2048)
        elif kind == "wk":
            wst[ti] = _tile_kc(w_k, nm[1] * 128)
        elif kind == "wv":
            c = nm[1]
            wst[ti] = w_v[c * 512:(c + 1) * 512, :].reshape(4, 128, 512).transpose(1, 0, 2).reshape(128, 2048)
        elif kind == "wq":
            wst[ti] = _tile_kc(wq_p, nm[1] * 128)
        elif kind == "wo":
            wst[ti] = _tile_kc(wo_p, nm[1] * 128)
        else:
            raise AssertionError(nm)

    vec = np.zeros((128, NV), np.float32)
    vec[:, V_NMIX0:V_NMIX0 + 16] = _col(norm_mix[0])
    vec[:, V_NMLP0:V_NMLP0 + 16] = _col(norm_mlp[0])
    vec[:, V_KVN:V_KVN + 16] = _col(kv_norm)
    vec[:, V_NMIX1:V_NMIX1 + 16] = _col(norm_mix[1])
    vec[:, V_NMLP1:V_NMLP1 + 16] = _col(norm_mlp[1])
    vec[:, V_FIN:V_FIN + 16] = _col(final_norm)
    vec[:, V_BA:V_BA + 16] = _col(b_pw1[0][:2048])
    vec[:, V_BG:V_BG + 16] = _col(b_pw1[0][2048:])
    vec[:, V_BDW:V_BDW + 16] = _col(b_dw[0])
    vec[:, V_LNG:V_LNG + 16] = _col(conv_ln_g[0])
    vec[:, V_LNB:V_LNB + 16] = _col(conv_ln_b[0])
    vec[:, V_BPW2:V_BPW2 + 16] = _col(b_pw2[0])
    for oc in range(16):
        vec[0:64, V_SINK + oc] = sinks[0][head_of[oc, 0]]
        vec[64:128, V_SINK + oc] = sinks[0][head_of[oc, 1]]
    vec[:, V_WDW:] = w_dw[0].T.reshape(16, 128, CW).transpose(1, 0, 2).reshape(128, 16 * CW)

    mask_prev = (np.arange(128)[:, None] > np.arange(128)[None, :]).astype(np.float32)
    mask_own = (np.arange(128)[:, None] <= np.arange(128)[None, :]).astype(np.float32)
    m4 = np.concatenate([mask_prev, mask_own, mask_prev, mask_own], axis=1)
    m4_first = m4.copy()
    m4_first[:, 0:128] = 0.0
    m4_first[:, 256:384] = 0.0
    ms = np.zeros((128, 32), np.float32)
    ms[:, 0:8] = mask_prev[:, 0:8]
    ms[:, 8:16] = mask_prev[:, 0:8]
    ms[0:8, 16:24] = mask_own[0:8, 0:8]
    ms[0:8, 24:32] = mask_own[0:8, 0:8]
    NEG = np.float32(-240000.0)
    mn = np.concatenate([(1.0 - mask_prev) * NEG, (1.0 - mask_own) * NEG], axis=1).astype(np.float32)
    mn_first = mn.copy()
    mn_first[:, 0:128] = NEG
    mns = np.zeros((128, 64), np.float32)
    for s_ in range(4):
        mns[:, s_ * 8:(s_ + 1) * 8] = (1.0 - mask_prev[:, 0:8]) * NEG
        mns[0:8, 32 + s_ * 8: 40 + s_ * 8] = (1.0 - mask_own[0:8, 0:8]) * NEG
    eye = np.eye(128, dtype=np.float32)
    pm = np.zeros((128, 128), np.float32)
    sgn = np.zeros(128, np.float32)
    fidx = np.full(128, -1, np.int64)
    for m_ in range(128):
        d = m_ % 64
        if d < 8:
            pm[m_ + 8, m_] = 1.0
            sgn[m_] = -1.0
            fidx[m_] = d
        elif d < 16:
            pm[m_ - 8, m_] = 1.0
            sgn[m_] = 1.0
            fidx[m_] = d - 8
    inv = (np.float32(500000.0) ** (-np.arange(8, dtype=np.float32) / np.float32(8))).astype(np.float32)

    in_maps = []
    for core in range(NCORE):
        b, j = core // 4, core % 4
        xe = np.zeros((TE, D), np.float32)
        if j > 0:
            xe[0:158] = x_prompt[b, 1024 * j - 158: 1024 * j]
        xe[158:1182] = x_prompt[b, 1024 * j: 1024 * (j + 1)]
        xe[1182:1214] = x_sample[4 * core: 4 * core + 4].reshape(32, D)
        xTc = np.ascontiguousarray(xe.T.reshape(16, 128, TE).transpose(1, 0, 2))
        sc = state_conv[0, 4 * core: 4 * core + 4]
        ctxTc = np.ascontiguousarray(sc.transpose(2, 0, 1).reshape(16, 128, 4, 30).transpose(1, 0, 2, 3))
        ck = cache_k[4 * core: 4 * core + 4].reshape(4, 128, 512)
        cv = cache_v[4 * core: 4 * core + 4].reshape(4, 128, 512)
        ckTc = np.ascontiguousarray(ck.transpose(2, 0, 1).reshape(4, 128, 4 * 128).transpose(1, 0, 2))
        pos = np.zeros(NTOK, np.float32)
        pos[0:1152] = (1024 * j - 128) + np.arange(1152)
        pos[1152:1184] = 16384 + (np.arange(32) % 8)
        tabc = np.zeros((128, 2 * NTOK), np.float32)
        tabc[:, 0:NTOK] = 1.0
        for m_ in range(128):
            if fidx[m_] >= 0:
                ang = (pos * inv[fidx[m_]]).astype(np.float32)
                tabc[m_, 0:NTOK] = np.cos(ang).astype(np.float32)
                tabc[m_, NTOK:] = sgn[m_] * np.sin(ang).astype(np.float32)
        v = vec.copy()
        v[:, V_FLAG] = 0.0 if j == 0 else 1.0
        in_maps.append({
            "xT": xTc, "wst": wst, "ctxT": ctxTc, "ckT": ckTc,
            "sc_raw": np.ascontiguousarray(sc), "ck_raw": np.ascontiguousarray(ck), "cv_raw": np.ascontiguousarray(cv),
            "vecs": v, "tab": tabc, "mask4": m4, "masks": ms, "perm": pm,
            "mask40": (m4_first if j == 0 else m4), "ident": eye, "maskns": mns,
        })

    return in_maps


def kernel(**inputs):
    in_maps = prepare(**inputs)
    if "nc" not in _NC_CACHE:
        _NC_CACHE["nc"] = build()
    nc = _NC_CACHE["nc"]
    res = run_bass_kernel_spmd(nc, in_maps, core_ids=list(range(NCORE)))
    return assemble(res.results)


def assemble(R_):

    y_prompt = np.zeros((2, 4096, D), np.float32)
    y_sample = np.zeros((32, 8, D), np.float32)
    conv_prompt = np.zeros((1, 2, 30, D), np.float32)
    conv_sample = np.zeros((1, 32, 30, D), np.float32)
    win_k_prompt = np.zeros((2, 128, 8, 64), np.float32)
    win_v_prompt = np.zeros((2, 128, 8, 64), np.float32)
    win_k_sample = np.zeros((32, 128, 8, 64), np.float32)
    win_v_sample = np.zeros((32, 128, 8, 64), np.float32)
    for core in range(NCORE):
        b, j = core // 4, core % 4
        r = R_[core]
        y = np.transpose(r["yT"], (2, 1, 0)).reshape(NQ, D)
        y_prompt[b, 1024 * j: 1024 * (j + 1)] = y[0:1024]
        y_sample[4 * core: 4 * core + 4] = y[1024:1056].reshape(4, 8, D)
        uo = r["uT_out"].reshape(128, 16, 62)
        ut = np.transpose(uo, (2, 1, 0)).reshape(62, D)
        kt = np.transpose(r["kT_out"].reshape(128, 4, 160), (2, 1, 0)).reshape(160, 512)
        if j == 3:
            conv_prompt[0, b] = ut[0:30]
            win_k_prompt[b] = kt[0:128].reshape(128, 8, 64)
            win_v_prompt[b] = r["v_out"].reshape(128, 8, 64)
        for s in range(4):
            q = 4 * core + s
            conv_sample[0, q, 0:22] = r["conv_old"][s]
            conv_sample[0, q, 22:30] = ut[30 + 8 * s: 30 + 8 * s + 8]
            win_k_sample[q, 0:120] = r["k_old"][s].reshape(120, 8, 64)
            win_k_sample[q, 120:128] = kt[128 + 8 * s: 128 + 8 * s + 8].reshape(8, 8, 64)
            win_v_sample[q, 0:120] = r["v_old"][s].reshape(120, 8, 64)
            win_v_sample[q, 120:128] = r["v_new"][s].reshape(8, 8, 64)
    return (y_prompt, y_sample, conv_prompt, conv_sample, win_k_prompt, win_v_prompt, win_k_sample, win_v_sample)
```

```python
import contextlib
import os
import numpy as np
import concourse.bass as bass
import concourse.mybir as mybir
from concourse.bass_utils import run_bass_kernel_spmd

F32 = mybir.dt.float32
BF16 = mybir.dt.bfloat16
AF = mybir.ActivationFunctionType
ALU = mybir.AluOpType

ENGS = ("pe", "act", "dve", "pool", "sp")


def C(name, *a, **k):
    return (name, a, k)


class Op:
    __slots__ = ("eng", "fn", "deps", "signal", "sem", "val", "pos", "dma")

    def __init__(self, eng, fn, dma=None):
        self.eng = eng
        self.fn = fn
        self.deps = []
        self.signal = False
        self.sem = None
        self.val = 0
        self.pos = 0
        self.dma = dma


class _Stop(Exception):
    pass


STOP = float(os.environ.get("MK_STOP", "99"))


def checkpoint(k):
    if STOP <= k:
        raise _Stop()


class Prog:
    SAFE_DIST = 10 ** 9

    def __init__(self):
        self.streams = {e: [] for e in ENGS}
        self.last_w = {}
        self.readers = {}
        self.last_op = {}
        self.pending_fence = {e: [] for e in ENGS}
        self.dma_pending = []

    def add(self, eng, fn, reads=(), writes=(), dma=None):
        op = Op(eng, fn, dma=dma)
        if dma is not None:
            op.signal = True
        deps = set()
        for r in reads:
            w = self.last_w.get(r)
            if w is not None:
                deps.add(w)
        for wkey in writes:
            w = self.last_w.get(wkey)
            if w is not None:
                deps.add(w)
            for rd in self.readers.get(wkey, ()):
                deps.add(rd)
        for f in self.pending_fence[eng]:
            deps.add(f)
        self.pending_fence[eng] = []
        op.pos = len(self.streams[eng])
        keep = []
        latest = {}
        for d in deps:
            if d.dma is None and d.eng == eng:
                if eng == "pe" or eng == "sp":
                    continue
                if op.pos - d.pos >= self.SAFE_DIST:
                    continue
            if d.dma is None:
                cur = latest.get(d.eng)
                if cur is None or d.pos > cur.pos:
                    latest[d.eng] = d
            else:
                keep.append(d)
        keep.extend(latest.values())
        op.deps = keep
        for d in keep:
            d.signal = True
        self.streams[eng].append(op)
        if dma is None:
            self.last_op[eng] = op
        elif not dma.startswith("slot"):
            self.dma_pending.append(op)
        for r in reads:
            self.readers.setdefault(r, []).append(op)
        for wkey in writes:
            self.last_w[wkey] = op
            self.readers[wkey] = []
        return op

    def fence(self):
        lasts = [o for o in self.last_op.values()]
        dmas = self.dma_pending
        self.dma_pending = []
        for e in ENGS:
            self.pending_fence[e] = [o for o in lasts if o.eng != e] + dmas

    def finalize(self, final_ops):
        for o in final_ops:
            o.signal = True
        counters = {}
        for e in ENGS:
            for op in self.streams[e]:
                if not op.signal:
                    continue
                if op.dma is not None:
                    key = ("dma", op.dma)
                    counters[key] = counters.get(key, 0) + 16
                else:
                    key = ("eng", e)
                    counters[key] = counters.get(key, 0) + 1
                op.sem = key
                op.val = counters[key]
        self.sem_keys = list(counters.keys())
        self.final_ops = final_ops

    def emit(self, nc):
        with contextlib.ExitStack() as st:
            sems = {}
            for i, k in enumerate(self.sem_keys):
                sems[k] = st.enter_context(nc.semaphore("s%d" % i))
            block = st.enter_context(nc.Block())
            prog = self

            def run(ename, e, is_last=False):
                waited = {}
                for op in prog.streams[ename]:
                    for d in op.deps:
                        if waited.get(d.sem, 0) < d.val:
                            e.wait_ge(sems[d.sem], d.val)
                            waited[d.sem] = d.val
                    name_, a_, k_ = op.fn
                    ins = getattr(e, name_)(*a_, **k_)
                    if op.signal:
                        ins.then_inc(sems[op.sem], 16 if op.dma is not None else 1)
                if is_last:
                    for d in prog.final_ops:
                        if waited.get(d.sem, 0) < d.val:
                            e.wait_ge(sems[d.sem], d.val)
                            waited[d.sem] = d.val

            @block.sync
            def _(e):
                run("sp", e, is_last=True)

            @block.scalar
            def _(e):
                run("act", e)

            @block.vector
            def _(e):
                run("dve", e)

            @block.gpsimd
            def _(e):
                run("pool", e)

            @block.tensor
            def _(e):
                run("pe", e)


NCORE = 8
D = 2048
KC = 16
TE = 1214
T_PRE, T_HALO, T_MAIN, T_SMP = 0, 30, 158, 1182
NTOK = TE - 30
NQ = TE - 158
SEGS = [0, 30, 158, 592, 622, 670, 1214]
NS = 4
CW = 31
NDV = 8
EPS = 1e-6
UW = 742

V_NMIX0, V_NMLP0, V_KVN, V_NMIX1, V_NMLP1, V_FIN = 0, 16, 32, 48, 64, 80
V_BA, V_BG, V_BDW, V_LNG, V_LNB, V_BPW2, V_SINK = 96, 112, 128, 144, 160, 176, 192
V_FLAG = 208
V_WDW = 212
NV = V_WDW + 16 * CW


def seg_keys(name, kc, a, b):
    return [(name, kc, i) for i in range(len(SEGS) - 1) if SEGS[i] < b and SEGS[i + 1] > a]


def tiles(a, b):
    out = []
    t = a
    while t < b:
        n = min(512, b - t)
        out.append((t, n))
        t += n
    return out


def weight_order():
    order = []
    for _pass in range(2):
        for oc in range(16):
            order.append(("pw1a", oc))
            order.append(("pw1g", oc))
        for oc in range(16):
            order.append(("pw2", oc))
    for g in range(8):
        for fc in range(8):
            order.append(("up", 0, g * 8 + fc))
        for ocp in range(8):
            order.append(("down", 0, g, ocp))
    for c in range(4):
        order.append(("wk", c))
    for _grp in range(2):
        for c in range(4):
            order.append(("wv", c))
    for _pass in range(2):
        for oc in range(16):
            order.append(("wq", oc))
        for oc in range(16):
            order.append(("wo", oc))
    for g in range(8):
        for fc in range(8):
            order.append(("up", 1, g * 8 + fc))
        for ocp in range(8):
            order.append(("down", 1, g, ocp))
    return order


def unique_tiles():
    seen = {}
    for nm in weight_order():
        if nm not in seen:
            seen[nm] = len(seen)
    return seen


def build():
    nc = bass.Bass("TRN2", target_bir_lowering=False)
    tidx = unique_tiles()
    NT = len(tidx)
    order = weight_order()

    def din(name, shape):
        return nc.dram_tensor(name, shape, F32, kind="ExternalInput").ap()

    def dout(name, shape):
        return nc.dram_tensor(name, shape, F32, kind="ExternalOutput").ap()

    xT = din("xT", [128, KC, TE])
    wst = din("wst", [NT, 128, 2048])
    ctxT = din("ctxT", [128, KC, 4, 30])
    ckT = din("ckT", [128, 4, 512])
    sc_raw = din("sc_raw", [4, 30, D])
    ck_raw = din("ck_raw", [4, 128, 512])
    cv_raw = din("cv_raw", [4, 128, 512])
    vecs = din("vecs", [128, NV])
    tab = din("tab", [128, 2 * NTOK])
    mask4 = din("mask4", [128, 512])
    masks = din("masks", [128, 32])
    mask40 = din("mask40", [128, 512])
    perm = din("perm", [128, 128])
    ident = din("ident", [128, 128])
    maskns = din("maskns", [128, 64])

    yT = dout("yT", [128, KC, NQ])
    uT_out = dout("uT_out", [128, KC * 62])
    conv_old = dout("conv_old", [4, 22, D])
    kT_out = dout("kT_out", [128, 4 * 160])
    v_out = dout("v_out", [128, 512])
    v_new = dout("v_new", [4, 8, 512])
    k_old = dout("k_old", [4, 120, 512])
    v_old = dout("v_old", [4, 120, 512])

    P = Prog()
    outs = []
    with contextlib.ExitStack() as st:
        def sb(name, cols, dt):
            return st.enter_context(nc.sbuf_tensor(name, [128, cols], dt))

        X = sb("X", KC * TE, F32)
        H = sb("H", KC * TE, BF16)
        RING = sb("RING", NS * 2048, BF16)
        VEC = sb("VEC", NV, F32)
        TAB = sb("TAB", 2 * NTOK, F32)
        TR0 = sb("TR0", TE, F32)
        TR1 = sb("TR1", TE, F32)
        SQ = sb("SQ", 4 * 512, BF16)
        ONESB = sb("ONESB", 128, BF16)
        ONE1 = sb("ONE1", 64, BF16)
        PERMF = sb("PERMF", 128, F32)
        PERMB = sb("PERMB", 128, BF16)
        MASK4 = sb("MASK4", 512, BF16)
        MASKS = sb("MASKS", 32, BF16)
        MASK40 = sb("MASK40", 512, BF16)
        ES = sb("ES", 16, F32)
        EPSC = sb("EPSC", 1, F32)
        IDENT = sb("IDENT", 128, BF16)
        MASKNS = sb("MASKNS", 64, BF16)
        RBYTES = 44288
        R = sb("R", RBYTES // 4, F32)
        banks = [st.enter_context(nc.psum_tensor("ps%d" % i, [128, 512], F32)) for i in range(8)]

        def carve(off, ncols, dt):
            sz = 4 if dt == F32 else 2
            assert off % 4 == 0 and (ncols * sz) % 4 == 0
            assert off + ncols * sz <= RBYTES, (off, ncols, sz)
            v = R[:, off // 4: off // 4 + (ncols * sz) // 4]
            return v if dt == F32 else v.bitcast(BF16)

        o = 0
        SIG = [carve(o + i * 2488, 622, F32) for i in range(2)]; o += 2 * 2488
        U = [carve(o + i * UW * 4, UW, F32) for i in range(2)]; o += 2 * UW * 4
        ACCD = [carve(o + i * 2368, 592, F32) for i in range(2)]; o += 2 * 2368
        UB = [carve(o + i * 1488, 744, BF16) for i in range(2)]; o += 2 * 1488
        NDG = 32
        DG = carve(o, NDG * 128, BF16); o += NDG * 256
        UC = carve(o, 16 * 30, F32); o += 16 * 30 * 4
        LNT = [carve(o + i * 2368, 592, F32) for i in range(2)]; o += 2 * 2368
        USAVE = carve(o, 16 * 62, F32); o += 16 * 62 * 4
        assert o <= RBYTES, o
        o = 0
        HID = carve(o, 8 * NTOK, BF16); o += 8 * NTOK * 2
        RT = [carve(o + i * 2048, 512, F32) for i in range(3)]; o += 3 * 2048
        YST = [carve(o + i * 2048, 512, F32) for i in range(2)]; o += 2 * 2048
        assert o <= RBYTES, o
        KTW = NTOK + 512
        o = 0
        KT = carve(o, 4 * KTW, BF16); o += 4 * KTW * 2
        VB = carve(o, 13 * 512, BF16); o += 13 * 512 * 2
        VN = carve(o, 4 * 512, BF16); o += 4 * 512 * 2
        o_alias = o
        QB = [carve(o + i * 1088, 544, BF16) for i in range(2)]; o += 2 * 1088
        QT = [carve(o + i * 1088, 544, BF16) for i in range(2)]; o += 2 * 1088
        KOUT = carve(o_alias, 4 * 160, F32)
        VST = [carve(o_alias + 2560 + i * 1024, 256, F32) for i in range(2)]
        assert o_alias + 2560 + 2048 <= o + 256
        o = max(o, o_alias + 2560 + 2048)
        NEV = 3
        EB = [carve(o + i * 1024, 512, BF16) for i in range(NEV)]; o += NEV * 1024
        PT = [carve(o + i * 1024, 512, BF16) for i in range(NEV)]; o += NEV * 1024
        DEN = [carve(o + i * 512, 128, F32) for i in range(NEV)]; o += NEV * 512
        assert o <= RBYTES, o

        def Xap(kc, a, b):
            return X[:, kc * TE + a: kc * TE + b]

        def Hap(kc, a, b):
            return H[:, kc * TE + a: kc * TE + b]

        def vcol(c):
            return VEC[:, c:c + 1]

        bank_ctr = [0]
        dgc = [0]

        def bank():
            b = bank_ctr[0] % 8
            bank_ctr[0] += 1
            return b

        wstate = {"issued": 0, "consumed": 0}

        def issue_next():
            n = wstate["issued"]
            if n >= len(order):
                return
            s = n % NS
            ti = tidx[order[n]]
            P.add("pool", C("dma_start", out=RING[:, s * 2048:(s + 1) * 2048], in_=wst[ti]),
                  writes=[("slot", s)], dma="slot%d" % s)
            wstate["issued"] = n + 1

        def get_tile(name):
            n = wstate["consumed"]
            assert order[n] == name, (order[n], name)
            s = n % NS
            wstate["consumed"] = n + 1
            return s

        def slot_ap(s, a, b):
            return RING[:, s * 2048 + a: s * 2048 + b]

        for k0 in range(0, KC, 4):
            wk = []
            for kc in range(k0, k0 + 4):
                wk += seg_keys("X", kc, 0, TE)
            P.add("sp", C("dma_start",
                out=X[:, k0 * TE:(k0 + 4) * TE].rearrange("p (k t) -> p k t", t=TE), in_=xT[:, k0:k0 + 4, :]),
                writes=wk, dma="xl%d" % k0)
        P.add("sp", C("dma_start", out=VEC[:], in_=vecs[:]), writes=["VEC"], dma="vec")
        P.add("sp", C("dma_start", out=TAB[:], in_=tab[:]), writes=["TAB"], dma="tab")
        P.add("sp", C("dma_start", out=PERMF[:], in_=perm[:]), writes=["PERMF"], dma="permf")
        P.add("pool", C("dma_start", out=PERMB[:], in_=perm[:]), writes=["PERMB"], dma="permb")
        P.add("pool", C("dma_start", out=MASK4[:], in_=mask4[:]), writes=["MASK4"], dma="mask4")
        P.add("pool", C("dma_start", out=IDENT[:], in_=ident[:]), writes=["IDENT"], dma="ident")
        P.add("pool", C("dma_start", out=MASKNS[:], in_=maskns[:]), writes=["MASKNS"], dma="maskns")
        P.add("pool", C("dma_start", out=MASKS[:], in_=masks[:]), writes=["MASKS"], dma="masks")
        P.add("pool", C("dma_start", out=MASK40[:], in_=mask40[:]), writes=["MASK4"], dma="mask40")
        for _ in range(NS):
            issue_next()
        P.add("dve", C("memset", ONESB[:], 1.0 / D), writes=["ONESB"])
        P.add("dve", C("memset", ONE1[:], 1.0), writes=["ONE1"])
        P.add("dve", C("memset", EPSC[:], EPS), writes=["EPSC"])
        P.add("act", C("activation", out=ES[:], in_=VEC[:, V_SINK:V_SINK + 16], func=AF.Exp),
              reads=["VEC"], writes=["ES"])
        outs.append(P.add("sp", C("dma_start", out=conv_old[:], in_=sc_raw[:, 8:30, :]), dma="d2d0"))
        outs.append(P.add("sp", C("dma_start", out=k_old[:], in_=ck_raw[:, 8:128, :]), dma="d2d1"))
        outs.append(P.add("sp", C("dma_start", out=v_old[:], in_=cv_raw[:, 8:128, :]), dma="d2d2"))

        def rstd_from_bank(b, n, rs_ap, rs_keys):
            P.add("act", C("activation", out=rs_ap, in_=banks[b][:, :n], func=AF.Ln, bias=EPSC[:, 0:1]),
                  reads=[("ps", b), "EPSC"], writes=rs_keys)
            P.add("act", C("activation", out=rs_ap, in_=rs_ap, func=AF.Exp, scale=-0.5), reads=rs_keys, writes=rs_keys)

        def rms_stats(a, b):
            for (t0, n) in tiles(a, b):
                bk = bank()
                for kc in range(KC):
                    sq = kc % 4
                    P.add("act", C("activation",
                        out=SQ[:, sq * 512: sq * 512 + n], in_=Xap(kc, t0, t0 + n), func=AF.Square),
                        reads=seg_keys("X", kc, t0, t0 + n), writes=[("SQ", sq)])
                    P.add("pe", C("matmul",
                        banks[bk][:, :n], lhsT=ONESB[:], rhs=SQ[:, sq * 512: sq * 512 + n],
                        start=(kc == 0), stop=(kc == KC - 1)),
                        reads=[("SQ", sq), "ONESB"], writes=[("ps", bk)])
                rstd_from_bank(bk, n, TR0[:, t0:t0 + n], seg_keys("RS", 0, t0, t0 + n))

        def rms_to_h(a, b, gcol):
            rms_stats(a, b)
            for kc in range(KC):
                for (t0, n) in tiles(a, b):
                    P.add("dve", C("scalar_tensor_tensor",
                        out=Hap(kc, t0, t0 + n), in0=Xap(kc, t0, t0 + n), scalar=vcol(gcol + kc),
                        in1=TR0[:, t0:t0 + n], op0=ALU.mult, op1=ALU.mult),
                        reads=seg_keys("X", kc, t0, t0 + n) + seg_keys("RS", 0, t0, t0 + n) + ["VEC"],
                        writes=seg_keys("H", kc, t0, t0 + n))

        def proj_group(slot, lhs_off, lhs_stride, nk, rhs_fn, rhs_keys_fn, tl):
            bks = [bank() for _ in tl]
            for k in range(nk):
                for i, (t0, n) in enumerate(tl):
                    P.add("pe", C("matmul",
                        banks[bks[i]][:, :n], lhsT=slot_ap(slot, lhs_off + k * lhs_stride, lhs_off + k * lhs_stride + 128),
                        rhs=rhs_fn(k, t0, n), start=(k == 0), stop=(k == nk - 1)),
                        reads=[("slot", slot)] + rhs_keys_fn(k, t0, n), writes=[("ps", bks[i])])
            return bks

        def h_rhs(k, t0, n):
            return Hap(k, t0, t0 + n)

        def h_keys(k, t0, n):
            return seg_keys("H", k, t0, t0 + n)

        try:
            checkpoint(0)
            def conv_pass(pa):
                if pa == 0:
                    ga, gb, oa, ob = 0, 622, 30, 622
                    cbase = 622
                else:
                    ga, gb, oa, ob = 622, 1214, 622, 1214
                    cbase = 0
                rms_to_h(ga, gb, V_NMIX0)
                checkpoint(1)
                tl = tiles(ga, gb)

                def cap(oc, a, b):
                    return Hap(oc, cbase + a, cbase + b)

                def ckeys(oc, a, b):
                    return seg_keys("H", oc, cbase + a, cbase + b)

                UWID = 622 if pa == 0 else UW

                def stage1(oc):
                    u = oc % 2
                    sA = get_tile(("pw1a", oc))
                    bA = proj_group(sA, 0, 128, KC, h_rhs, h_keys, tl)
                    sG = get_tile(("pw1g", oc))
                    bG = proj_group(sG, 0, 128, KC, h_rhs, h_keys, tl)
                    issue_next()
                    issue_next()
                    if pa == 1:
                        P.add("act", C("activation", out=U[u][:, 0:30], in_=UC[:, oc * 30:(oc + 1) * 30], func=AF.Copy),
                              reads=[("UC", oc)], writes=[("UX", u)])
                        P.add("sp", C("dma_start",
                            out=U[u][:, 590:742].rearrange("p (s c) -> p s c", c=38)[:, :, 0:30], in_=ctxT[:, oc, :, :]),
                            writes=[("UX", u)], dma="ctx%d" % u)
                    for i, (t0, n) in enumerate(tl):
                        g0 = t0 - ga
                        P.add("act", C("activation",
                            out=SIG[u][:, g0:g0 + n], in_=banks[bG[i]][:, :n], func=AF.Sigmoid, bias=vcol(V_BG + oc)),
                            reads=[("ps", bG[i]), "VEC"], writes=[("SIG", u)])
                        npr = min(t0 + n, T_SMP) - t0
                        if npr > 0:
                            uc0 = t0 if pa == 0 else 30 + (t0 - 622)
                            P.add("dve", C("scalar_tensor_tensor",
                                out=U[u][:, uc0:uc0 + npr], in0=banks[bA[i]][:, :npr], scalar=vcol(V_BA + oc),
                                in1=SIG[u][:, g0:g0 + npr], op0=ALU.add, op1=ALU.mult),
                                reads=[("ps", bA[i]), ("SIG", u), "VEC"], writes=[("U", u)])
                        if npr < n:
                            assert n - npr == 32
                            P.add("dve", C("scalar_tensor_tensor",
                                out=U[u][:, 590:742].rearrange("p (s c) -> p s c", c=38)[:, :, 30:38],
                                in0=banks[bA[i]][:, npr:npr + 32].rearrange("p (s c) -> p s c", c=8), scalar=vcol(V_BA + oc),
                                in1=SIG[u][:, g0 + npr:g0 + npr + 32].rearrange("p (s c) -> p s c", c=8),
                                op0=ALU.add, op1=ALU.mult),
                                reads=[("ps", bA[i]), ("SIG", u), "VEC"], writes=[("U", u)])
                    if pa == 0:
                        P.add("dve", C("tensor_single_scalar", out=U[u][:, 0:158], in_=U[u][:, 0:158],
                                       scalar=vcol(V_FLAG), op=ALU.mult),
                              reads=[("U", u), "VEC"], writes=[("U", u)])
                        P.add("act", C("activation", out=UC[:, oc * 30:(oc + 1) * 30], in_=U[u][:, 592:622], func=AF.Copy),
                              reads=[("U", u)], writes=[("UC", oc)])
                    else:
                        P.add("act", C("activation", out=USAVE[:, oc * 62: oc * 62 + 30], in_=U[u][:, 560:590], func=AF.Copy),
                              reads=[("U", u)], writes=["USAVE"])
                        P.add("act", C("activation",
                            out=USAVE[:, oc * 62 + 30: oc * 62 + 62].rearrange("p (s c) -> p s c", c=8),
                            in_=U[u][:, 590:742].rearrange("p (s c) -> p s c", c=38)[:, :, 30:38], func=AF.Copy),
                            reads=[("U", u)], writes=["USAVE"])
                    if NDV < CW:
                        P.add("act", C("activation", out=UB[u][:, 0:UWID], in_=U[u][:, 0:UWID], func=AF.Copy),
                              reads=[("U", u), ("UX", u)], writes=[("UB", u)])

                def stage2a(oc):
                    dgs = []
                    for j in range(NDV, CW):
                        dg = dgc[0] % NDG
                        dgc[0] += 1
                        P.add("act", C("activation", out=DG[:, dg * 128:(dg + 1) * 128], in_=IDENT[:, :], func=AF.Copy,
                                       scale=vcol(V_WDW + oc * CW + j)),
                              reads=["IDENT", "VEC"], writes=[("DG", dg)])
                        dgs.append(dg)
                    return dgs

                def stage2(oc, dgs):
                    u = oc % 2
                    if pa == 0:
                        parts = [(lambda acc: acc[:, 0:512], lambda X_, j: X_[:, j:j + 512], lambda: cap(oc, 0, 512), lambda bk: banks[bk][:, 0:512]),
                                 (lambda acc: acc[:, 512:592], lambda X_, j: X_[:, 512 + j:592 + j], lambda: cap(oc, 512, 592), lambda bk: banks[bk][:, 0:80])]
                    else:
                        parts = [(lambda acc: acc[:, 0:512], lambda X_, j: X_[:, j:j + 512], lambda: cap(oc, 0, 512), lambda bk: banks[bk][:, 0:512]),
                                 (lambda acc: acc[:, 512:560], lambda X_, j: X_[:, 512 + j:560 + j], lambda: cap(oc, 512, 560), lambda bk: banks[bk][:, 0:48]),
                                 (lambda acc: acc[:, 560:592].rearrange("p (s c) -> p s c", c=8),
                                  lambda X_, j: X_[:, 590:742].rearrange("p (s c) -> p s c", c=38)[:, :, j:j + 8],
                                  lambda: cap(oc, 560, 592).rearrange("p (s c) -> p s c", c=8),
                                  lambda bk: banks[bk][:, 0:32].rearrange("p (s c) -> p s c", c=8))]
                    akey = ("ACC", "dve", u)
                    cb = []
                    if NDV < CW:
                        cb = [bank() for _ in parts]
                        for j in range(NDV, CW):
                            dg = dgs[j - NDV]
                            for pi, (accv, xv, cv, pv) in enumerate(parts):
                                P.add("pe", C("matmul", pv(cb[pi]), lhsT=DG[:, dg * 128:(dg + 1) * 128], rhs=xv(UB[u], j),
                                              start=(j == NDV), stop=(j == CW - 1)),
                                      reads=[("DG", dg), ("UB", u)], writes=[("ps", cb[pi])])
                    for j in range(NDV):
                        wcol = vcol(V_WDW + oc * CW + j)
                        for pi, (accv, xv, cv, pv) in enumerate(parts):
                            if j == 0:
                                P.add("dve", C("tensor_scalar", out=accv(ACCD[u]), in0=xv(U[u], j), scalar1=wcol,
                                               scalar2=vcol(V_BDW + oc), op0=ALU.mult, op1=ALU.add),
                                      reads=[("U", u), ("UX", u), "VEC"], writes=[akey])
                            else:
                                P.add("dve", C("scalar_tensor_tensor", out=accv(ACCD[u]), in0=xv(U[u], j), scalar=wcol,
                                               in1=accv(ACCD[u]), op0=ALU.mult, op1=ALU.add),
                                      reads=[("U", u), ("UX", u), "VEC", akey], writes=[akey])
                    for pi, (accv, xv, cv, pv) in enumerate(parts):
                        if NDV == 0:
                            P.add("dve", C("tensor_single_scalar", out=cv(), in_=pv(cb[pi]), scalar=vcol(V_BDW + oc), op=ALU.add),
                                  reads=[("ps", cb[pi]), "VEC"], writes=ckeys(oc, 0, 592))
                        elif NDV == CW:
                            P.add("dve", C("tensor_copy", out=cv(), in_=accv(ACCD[u])), reads=[akey], writes=ckeys(oc, 0, 592))
                        else:
                            P.add("dve", C("tensor_tensor", out=cv(), in0=pv(cb[pi]), in1=accv(ACCD[u]), op=ALU.add),
                                  reads=[("ps", cb[pi]), akey], writes=ckeys(oc, 0, 592))

                stage1(0)
                for oc in range(16):
                    dgs = stage2a(oc)
                    if oc + 1 < 16:
                        stage1(oc + 1)
                    stage2(oc, dgs)
                ltl = tiles(0, 592)
                bmu = [bank() for _ in ltl]
                bvar = [bank() for _ in ltl]
                for oc in range(16):
                    for i, (c0, n) in enumerate(ltl):
                        sq = (2 * oc + i) % 4
                        P.add("act", C("activation",
                            out=SQ[:, sq * 512: sq * 512 + n], in_=cap(oc, c0, c0 + n), func=AF.Square),
                            reads=ckeys(oc, c0, c0 + n), writes=[("SQ", sq)])
                        P.add("pe", C("matmul",
                            banks[bmu[i]][:, :n], lhsT=ONESB[:], rhs=cap(oc, c0, c0 + n), start=(oc == 0), stop=(oc == 15)),
                            reads=ckeys(oc, c0, c0 + n) + ["ONESB"], writes=[("ps", bmu[i])])
                        P.add("pe", C("matmul",
                            banks[bvar[i]][:, :n], lhsT=ONESB[:], rhs=SQ[:, sq * 512: sq * 512 + n], start=(oc == 0), stop=(oc == 15)),
                            reads=[("SQ", sq), "ONESB"], writes=[("ps", bvar[i])])
                for i, (c0, n) in enumerate(ltl):
                    P.add("act", C("activation", out=TR1[:, c0:c0 + n], in_=banks[bmu[i]][:, :n], func=AF.Copy),
                          reads=[("ps", bmu[i])], writes=[("MU", i)])
                    P.add("dve", C("tensor_tensor", out=LNT[0][:, c0:c0 + n], in0=TR1[:, c0:c0 + n], in1=TR1[:, c0:c0 + n], op=ALU.mult),
                          reads=[("MU", i)], writes=[("LNT", 0)])
                    P.add("dve", C("tensor_tensor", out=LNT[0][:, c0:c0 + n], in0=banks[bvar[i]][:, :n], in1=LNT[0][:, c0:c0 + n], op=ALU.subtract),
                          reads=[("ps", bvar[i]), ("LNT", 0)], writes=[("LNT", 0)])
                    P.add("act", C("activation", out=TR0[:, c0:c0 + n], in_=LNT[0][:, c0:c0 + n], func=AF.Ln, bias=EPSC[:, 0:1]),
                          reads=[("LNT", 0), "EPSC"], writes=seg_keys("RS", 0, c0, c0 + n))
                    P.add("act", C("activation", out=TR0[:, c0:c0 + n], in_=TR0[:, c0:c0 + n], func=AF.Exp, scale=-0.5),
                          reads=seg_keys("RS", 0, c0, c0 + n), writes=seg_keys("RS", 0, c0, c0 + n))
                rs_keys = seg_keys("RS", 0, 0, 592)
                for oc in range(16):
                    a = oc % 2
                    P.add("dve", C("tensor_tensor", out=LNT[a][:, 0:592], in0=cap(oc, 0, 592), in1=TR1[:, 0:592], op=ALU.subtract),
                          reads=ckeys(oc, 0, 592) + [("MU", 0), ("MU", 1)], writes=[("LNT", a)])
                    P.add("dve", C("tensor_tensor", out=LNT[a][:, 0:592], in0=LNT[a][:, 0:592], in1=TR0[:, 0:592], op=ALU.mult),
                          reads=[("LNT", a)] + rs_keys, writes=[("LNT", a)])
                    P.add("act", C("activation", out=Hap(oc, oa, ob), in_=LNT[a][:, 0:592], func=AF.Silu,
                                                                   bias=vcol(V_LNB + oc), scale=vcol(V_LNG + oc)),
                          reads=[("LNT", a), "VEC"], writes=seg_keys("H", oc, oa, ob))
                otl = tiles(oa, ob)
                for oc in range(16):
                    s2 = get_tile(("pw2", oc))
                    bks = proj_group(s2, 0, 128, KC, h_rhs, h_keys, otl)
                    issue_next()
                    for i, (t0, n) in enumerate(otl):
                        P.add("dve", C("scalar_tensor_tensor",
                            out=Xap(oc, t0, t0 + n), in0=banks[bks[i]][:, :n], scalar=vcol(V_BPW2 + oc),
                            in1=Xap(oc, t0, t0 + n), op0=ALU.add, op1=ALU.add),
                            reads=[("ps", bks[i]), "VEC"] + seg_keys("X", oc, t0, t0 + n), writes=seg_keys("X", oc, t0, t0 + n))

            conv_pass(0)
            checkpoint(2)
            conv_pass(1)
            checkpoint(3)
            outs.append(P.add("sp", C("dma_start", out=uT_out[:], in_=USAVE[:]), reads=["USAVE"], dma="usave"))
            P.fence()

            def mlp(l, a, b, gcol):
                rms_to_h(a, b, gcol)
                tl = tiles(a, b)
                for g in range(8):
                    for fc in range(8):
                        s = get_tile(("up", l, g * 8 + fc))
                        bks = proj_group(s, 0, 128, KC, h_rhs, h_keys, tl)
                        issue_next()
                        for i, (t0, n) in enumerate(tl):
                            P.add("act", C("activation", out=RT[i][:, :n], in_=banks[bks[i]][:, :n], func=AF.Relu),
                                  reads=[("ps", bks[i])], writes=[("RT", i)])
                        for i, (t0, n) in enumerate(tl):
                            P.add("act", C("activation",
                                out=HID[:, fc * NTOK + (t0 - a): fc * NTOK + (t0 - a) + n], in_=RT[i][:, :n], func=AF.Square),
                                reads=[("RT", i)], writes=[("HID", fc, i)])
                    for ocp in range(8):
                        s = get_tile(("down", l, g, ocp))
                        for o2 in range(2):
                            oc = 2 * ocp + o2
                            bks = proj_group(s, o2 * 128, 256, 8,
                                             lambda k, t0, n: HID[:, k * NTOK + (t0 - a): k * NTOK + (t0 - a) + n],
                                             lambda k, t0, n: [("HID", k, i) for i in range(3)], tl)
                            for i, (t0, n) in enumerate(tl):
                                P.add("dve", C("tensor_tensor",
                                    out=Xap(oc, t0, t0 + n), in0=banks[bks[i]][:, :n], in1=Xap(oc, t0, t0 + n), op=ALU.add),
                                    reads=[("ps", bks[i])] + seg_keys("X", oc, t0, t0 + n), writes=seg_keys("X", oc, t0, t0 + n))
                        issue_next()

            mlp(0, 30, TE, V_NMLP0)
            P.fence()
            checkpoint(4)

            P.add("pool", C("dma_start",
                out=KT[:].rearrange("p (c w) -> p c w", w=KTW)[:, :, NTOK:NTOK + 512],
                in_=ckT[:]), writes=["KTC"], dma="ktc")
            P.add("pool", C("dma_start",
                out=VB[:, 9 * 512:13 * 512].rearrange("p (s f) -> p s f", f=512),
                in_=cv_raw[:].rearrange("s k f -> k s f")), writes=["VBC"], dma="vbc")
            rms_to_h(30, TE, V_KVN)
            ktl = tiles(30, TE)
            CT = lambda t0, n: TAB[:, t0 - 30: t0 - 30 + n]
            ST = lambda t0, n: TAB[:, NTOK + t0 - 30: NTOK + t0 - 30 + n]
            for c in range(4):
                s = get_tile(("wk", c))
                bks = proj_group(s, 0, 128, KC, h_rhs, h_keys, ktl)
                issue_next()
                for i, (t0, n) in enumerate(ktl):
                    kf = TR1[:, t0 - 30: t0 - 30 + n]
                    kk = [("KF", i)]
                    P.add("act", C("activation", out=kf, in_=banks[bks[i]][:, :n], func=AF.Copy),
                          reads=[("ps", bks[i])], writes=kk)
                    b2 = bank()
                    P.add("pe", C("matmul", banks[b2][:, :n], lhsT=PERMF[:], rhs=kf, start=True, stop=True),
                          reads=kk + ["PERMF"], writes=[("ps", b2)])
                    tv = TR0[:, t0 - 30: t0 - 30 + n]
                    tk = seg_keys("RS", 0, t0 - 30, t0 - 30 + n)
                    P.add("dve", C("tensor_tensor", out=tv, in0=banks[b2][:, :n], in1=ST(t0, n), op=ALU.mult),
                          reads=[("ps", b2), "TAB"], writes=tk)
                    P.add("dve", C("tensor_tensor", out=kf, in0=kf, in1=CT(t0, n), op=ALU.mult),
                          reads=kk + ["TAB"], writes=kk)
                    P.add("dve", C("tensor_tensor", out=kf, in0=kf, in1=tv, op=ALU.add),
                          reads=kk + tk, writes=kk)
                    P.add("act", C("activation",
                        out=KT[:, c * KTW + t0 - 30: c * KTW + t0 - 30 + n], in_=kf, func=AF.Copy),
                        reads=kk, writes=[("KT", c, i)])
                    if i == 2:
                        assert t0 == 1054 and n == 160
                        P.add("act", C("activation", out=KOUT[:, c * 160:(c + 1) * 160], in_=kf, func=AF.Copy),
                              reads=kk, writes=["KOUT"])
            outs.append(P.add("sp", C("dma_start", out=kT_out[:], in_=KOUT[:]), reads=["KOUT"], dma="kout"))
            checkpoint(5)
            vgroups = [[("blk", bi) for bi in range(6)], [("blk", bi) for bi in range(6, 9)] + [("smp", s_) for s_ in range(4)]]
            vst_ctr = [0]
            for grp in vgroups:
                gb_ = [bank() for _ in grp]
                for c in range(4):
                    s = get_tile(("wv", c))
                    for gi, (kind, idx) in enumerate(grp):
                        for kk_ in range(4):
                            kc = c * 4 + kk_
                            if kind == "blk":
                                t0 = 30 + 128 * idx
                                P.add("pe", C("matmul",
                                    banks[gb_[gi]][:, :], lhsT=Hap(kc, t0, t0 + 128), rhs=slot_ap(s, kk_ * 512, kk_ * 512 + 512),
                                    start=(kc == 0), stop=(kc == KC - 1)),
                                    reads=[("slot", s)] + seg_keys("H", kc, t0, t0 + 128), writes=[("ps", gb_[gi])])
                            else:
                                t0 = T_SMP + 8 * idx
                                P.add("pe", C("matmul",
                                    banks[gb_[gi]][0:8, :], lhsT=Hap(kc, t0, t0 + 8), rhs=slot_ap(s, kk_ * 512, kk_ * 512 + 512),
                                    start=(kc == 0), stop=(kc == KC - 1)),
                                    reads=[("slot", s)] + seg_keys("H", kc, t0, t0 + 8), writes=[("ps", gb_[gi])])
                    issue_next()
                for gi, (kind, idx) in enumerate(grp):
                    bkk = gb_[gi]
                    if kind == "blk":
                        P.add("act", C("activation", out=VB[:, idx * 512:(idx + 1) * 512], in_=banks[bkk][:, :], func=AF.Copy),
                              reads=[("ps", bkk)], writes=[("VB", idx)])
                        if idx == 8:
                            for hh in range(2):
                                P.add("act", C("activation", out=VST[hh][:, :], in_=banks[bkk][:, hh * 256:(hh + 1) * 256], func=AF.Copy),
                                      reads=[("ps", bkk)], writes=[("VST", hh)])
                                outs.append(P.add("sp", C("dma_start", out=v_out[:, hh * 256:(hh + 1) * 256], in_=VST[hh][:, :]),
                                                  reads=[("VST", hh)], dma="vst%d" % hh))
                    else:
                        P.add("act", C("activation", out=VN[0:8, idx * 512:(idx + 1) * 512], in_=banks[bkk][0:8, :], func=AF.Copy),
                              reads=[("ps", bkk)], writes=[("VN", idx)])
                        for hh in range(2):
                            P.add("act", C("activation", out=VST[hh][0:8, :], in_=banks[bkk][0:8, hh * 256:(hh + 1) * 256], func=AF.Copy),
                                  reads=[("ps", bkk)], writes=[("VST", hh)])
                            outs.append(P.add("sp", C("dma_start", out=v_new[idx, :, hh * 256:(hh + 1) * 256], in_=VST[hh][0:8, :]),
                                              reads=[("VST", hh)], dma="vst%d" % hh))

            def attn_pass(pa):
                if pa == 0:
                    qa, qb, obase = 158, 670, 670
                    blocks = [0, 1, 2, 3]
                    smp = []
                else:
                    qa, qb, obase = 670, 1214, 30
                    blocks = [4, 5, 6, 7]
                    smp = [0, 1, 2, 3]
                nq = qb - qa
                rms_to_h(qa, qb, V_NMIX1)
                qtl = tiles(qa, qb)

                def oT(oc, a, n):
                    return Hap(oc, obase + a, obase + a + n)

                def oT_keys(oc, a, n):
                    return seg_keys("H", oc, obase + a, obase + a + n)

                ev = [0]

                def stage_q(oc):
                    u = oc % 2
                    s = get_tile(("wq", oc))
                    bks = proj_group(s, 0, 128, KC, h_rhs, h_keys, qtl)
                    issue_next()
                    for i, (t0, n) in enumerate(qtl):
                        off = t0 - qa
                        P.add("act", C("activation", out=QB[u][:, off:off + n], in_=banks[bks[i]][:, :n], func=AF.Copy),
                              reads=[("ps", bks[i])], writes=[("QB", u, i), "KOUT", ("VST", 0), ("VST", 1)])
                        b2 = bank()
                        P.add("pe", C("matmul", banks[b2][:, :n], lhsT=PERMB[:], rhs=QB[u][:, off:off + n], start=True, stop=True),
                              reads=[("QB", u, i), "PERMB"], writes=[("ps", b2)])
                        t1 = TR1[:, off:off + n]
                        t2 = TR1[:, 600 + off:600 + off + n]
                        P.add("dve", C("tensor_tensor", out=t1, in0=banks[b2][:, :n], in1=ST(t0, n), op=ALU.mult),
                              reads=[("ps", b2), "TAB"], writes=[("QTMP", 0, i)])
                        P.add("dve", C("tensor_tensor", out=t2, in0=QB[u][:, off:off + n], in1=CT(t0, n), op=ALU.mult),
                              reads=[("QB", u, i), "TAB"], writes=[("QTMP", 1, i)])
                        P.add("dve", C("tensor_tensor", out=QT[u][:, off:off + n], in0=t1, in1=t2, op=ALU.add),
                              reads=[("QTMP", 0, i), ("QTMP", 1, i)], writes=[("QT", u, i)])

                def unit_scores(oc, unit):
                    c = oc // 4
                    u = oc % 2
                    qt_keys = [("QT", u, i) for i in range(len(qtl))]
                    v = ev[0] % NEV
                    ev[0] += 1
                    bSh = [sbank(), sbank()]
                    if unit[0] == "blk":
                        m = unit[1]
                        tb = 158 + 128 * m
                        off = tb - qa
                        kprev = c * KTW + (tb - 128 - 30)
                        kown = c * KTW + (tb - 30)
                        for hd in range(2):
                            pr0, pr1 = hd * 64, hd * 64 + 64
                            for kh, kcol in enumerate((kprev, kown)):
                                P.add("pe", C("matmul", banks[bSh[hd]][:, kh * 128: kh * 128 + 128],
                                              lhsT=KT[pr0:pr1, kcol:kcol + 128], rhs=QT[u][pr0:pr1, off:off + 128], start=True, stop=True),
                                      reads=[("KT", c, 0), ("KT", c, 1), ("KT", c, 2)] + qt_keys, writes=[("ps", bSh[hd])])
                        for hd in range(2):
                            P.add("act", C("activation", out=PT[v][:, hd * 256:(hd + 1) * 256], in_=banks[bSh[hd]][:, 0:256], func=AF.Exp, scale=0.125),
                                  reads=[("ps", bSh[hd])], writes=[("PT", v)])
                        M4 = MASK40 if m == 0 else MASK4
                        P.add("dve", C("tensor_tensor", out=EB[v][:, :], in0=PT[v][:, :], in1=M4[:, :], op=ALU.mult),
                              reads=[("PT", v), "MASK4"], writes=[("EB", v)])
                    else:
                        for hd in range(2):
                            pr0, pr1 = hd * 64, hd * 64 + 64
                            P.add("pe", C("matmul", banks[bSh[hd]][:, 0:32], lhsT=IDENT[:, :], rhs=MASKNS[:, 0:32], start=True, stop=False),
                                  reads=["IDENT", "MASKNS"], writes=[("ps", bSh[hd])])
                            for sq_ in range(4):
                                off = (T_SMP - qa) + 8 * sq_
                                kcache = c * KTW + NTOK + sq_ * 128
                                P.add("pe", C("matmul", banks[bSh[hd]][:, sq_ * 8: sq_ * 8 + 8],
                                              lhsT=KT[pr0:pr1, kcache:kcache + 128], rhs=QT[u][pr0:pr1, off:off + 8], start=False, stop=(sq_ == 3)),
                                      reads=["KTC"] + qt_keys, writes=[("ps", bSh[hd])])
                            P.add("pe", C("matmul", banks[bSh[hd]][0:8, 32:64], lhsT=IDENT[0:8, 0:8], rhs=MASKNS[0:8, 32:64], start=True, stop=False),
                                  reads=["IDENT", "MASKNS"], writes=[("ps", bSh[hd])])
                            for sq_ in range(4):
                                off = (T_SMP - qa) + 8 * sq_
                                knew = c * KTW + (T_SMP + 8 * sq_ - 30)
                                P.add("pe", C("matmul", banks[bSh[hd]][0:8, 32 + sq_ * 8: 40 + sq_ * 8],
                                              lhsT=KT[pr0:pr1, knew:knew + 8], rhs=QT[u][pr0:pr1, off:off + 8], start=False, stop=(sq_ == 3)),
                                      reads=[("KT", c, 2)] + qt_keys, writes=[("ps", bSh[hd])])
                        for hd in range(2):
                            P.add("act", C("activation", out=EB[v][:, hd * 64: hd * 64 + 32], in_=banks[bSh[hd]][:, 0:32], func=AF.Exp, scale=0.125),
                                  reads=[("ps", bSh[hd])], writes=[("EB", v)])
                            P.add("act", C("activation", out=EB[v][0:8, hd * 64 + 32: hd * 64 + 64], in_=banks[bSh[hd]][0:8, 32:64], func=AF.Exp, scale=0.125),
                                  reads=[("ps", bSh[hd])], writes=[("EB", v)])
                    return v

                def unit_pv(oc, unit, v):
                    c = oc // 4
                    kva, kvb = 2 * c, 2 * c + 1
                    bO = obank()
                    if unit[0] == "blk":
                        m = unit[1]
                        off = 158 + 128 * m - qa
                        w = 128
                        for hd, kv in enumerate((kva, kvb)):
                            pr0, pr1 = hd * 64, hd * 64 + 64
                            for kh, vblk in enumerate((m, m + 1)):
                                P.add("pe", C("matmul", banks[bO][pr0:pr1, 0:128], lhsT=VB[:, vblk * 512 + kv * 64: vblk * 512 + kv * 64 + 64],
                                              rhs=EB[v][:, hd * 256 + kh * 128: hd * 256 + kh * 128 + 128], start=(kh == 0), stop=(kh == 1)),
                                      reads=[("VB", vblk), ("EB", v)], writes=[("ps", bO)])
                            for kh in range(2):
                                P.add("pe", C("matmul", banks[bO][pr0:pr1, 128:256], lhsT=ONE1[:, 0:64],
                                              rhs=EB[v][:, hd * 256 + kh * 128: hd * 256 + kh * 128 + 128], start=(kh == 0), stop=(kh == 1)),
                                      reads=["ONE1", ("EB", v)], writes=[("ps", bO)])
                    else:
                        off = T_SMP - qa
                        w = 32
                        for hd, kv in enumerate((kva, kvb)):
                            pr0, pr1 = hd * 64, hd * 64 + 64
                            for sq_ in range(4):
                                rc = EB[v][:, hd * 64 + sq_ * 8: hd * 64 + sq_ * 8 + 8]
                                rn = EB[v][0:8, hd * 64 + 32 + sq_ * 8: hd * 64 + 40 + sq_ * 8]
                                P.add("pe", C("matmul", banks[bO][pr0:pr1, sq_ * 8: sq_ * 8 + 8],
                                              lhsT=VB[:, (9 + sq_) * 512 + kv * 64: (9 + sq_) * 512 + kv * 64 + 64], rhs=rc, start=True, stop=False),
                                      reads=["VBC", ("EB", v)], writes=[("ps", bO)])
                                P.add("pe", C("matmul", banks[bO][pr0:pr1, sq_ * 8: sq_ * 8 + 8],
                                              lhsT=VN[0:8, sq_ * 512 + kv * 64: sq_ * 512 + kv * 64 + 64], rhs=rn, start=False, stop=True),
                                      reads=[("VN", sq_), ("EB", v)], writes=[("ps", bO)])
                                P.add("pe", C("matmul", banks[bO][pr0:pr1, 128 + sq_ * 8: 136 + sq_ * 8], lhsT=ONE1[:, 0:64], rhs=rc, start=True, stop=False),
                                      reads=["ONE1", ("EB", v)], writes=[("ps", bO)])
                                P.add("pe", C("matmul", banks[bO][pr0:pr1, 128 + sq_ * 8: 136 + sq_ * 8], lhsT=ONE1[0:8, 0:64], rhs=rn, start=False, stop=True),
                                      reads=["ONE1", ("EB", v)], writes=[("ps", bO)])
                    P.add("act", C("activation", out=DEN[v][:, 0:w], in_=banks[bO][:, 128:128 + w], func=AF.Ln, bias=ES[:, oc:oc + 1]),
                          reads=[("ps", bO), "ES"], writes=[("DEN", v)])
                    P.add("act", C("activation", out=DEN[v][:, 0:w], in_=DEN[v][:, 0:w], func=AF.Exp, scale=-1.0),
                          reads=[("DEN", v)], writes=[("DEN", v)])
                    P.add("dve", C("tensor_tensor", out=oT(oc, off, w), in0=banks[bO][:, 0:w], in1=DEN[v][:, 0:w], op=ALU.mult),
                          reads=[("ps", bO), ("DEN", v)], writes=oT_keys(oc, off, w))

                units = [("blk", m) for m in blocks] + ([("smp", 0)] if smp else [])
                seq = [(oc, un) for oc in range(16) for un in units]
                sctr = [0]
                octr = [0]

                def sbank():
                    b_ = sctr[0] % 6
                    sctr[0] += 1
                    return b_

                def obank():
                    b_ = 6 + octr[0] % 2
                    octr[0] += 1
                    return b_

                stage_q(0)
                pend = []
                for idx, (oc, un) in enumerate(seq):
                    if un is units[0] and oc + 1 < 16:
                        stage_q(oc + 1)
                    v = unit_scores(oc, un)
                    pend.append((oc, un, v))
                    if len(pend) > 2:
                        unit_pv(*pend.pop(0))
                while pend:
                    unit_pv(*pend.pop(0))
                otl = [(t0 - qa, n) for (t0, n) in qtl]
                for oc in range(16):
                    s = get_tile(("wo", oc))
                    bks = proj_group(s, 0, 128, KC,
                                     lambda k, o0, n: oT(k, o0, n), lambda k, o0, n: oT_keys(k, o0, n), otl)
                    issue_next()
                    for i, (o0, n) in enumerate(otl):
                        t0 = qa + o0
                        P.add("dve", C("tensor_tensor",
                            out=Xap(oc, t0, t0 + n), in0=banks[bks[i]][:, :n], in1=Xap(oc, t0, t0 + n), op=ALU.add),
                            reads=[("ps", bks[i])] + seg_keys("X", oc, t0, t0 + n), writes=seg_keys("X", oc, t0, t0 + n))

            checkpoint(6)
            attn_pass(0)
            checkpoint(7)
            attn_pass(1)
            checkpoint(8)
            P.fence()

            mlp(1, 158, TE, V_NMLP1)
            checkpoint(9)

            rms_stats(158, TE)
            yc = [0]
            for (t0, n) in tiles(158, TE):
                for kc in range(KC):
                    yb = yc[0] % 2
                    yc[0] += 1
                    P.add("dve", C("scalar_tensor_tensor",
                        out=YST[yb][:, :n], in0=Xap(kc, t0, t0 + n), scalar=vcol(V_FIN + kc),
                        in1=TR0[:, t0:t0 + n], op0=ALU.mult, op1=ALU.mult),
                        reads=seg_keys("X", kc, t0, t0 + n) + seg_keys("RS", 0, t0, t0 + n) + ["VEC"], writes=[("YST", yb)])
                    outs.append(P.add("sp", C("dma_start",
                        out=yT[:, kc, t0 - 158: t0 - 158 + n], in_=YST[yb][:, :n]), reads=[("YST", yb)], dma="yst%d" % yb))

            assert wstate["consumed"] == len(order), (wstate, len(order))
        except _Stop:
            pass
        P.finalize(outs)
        P.emit(nc)
    return nc


def _tile_kc(W, c0):
    return np.ascontiguousarray(W[:, c0:c0 + 128].reshape(16, 128, 128).transpose(1, 0, 2).reshape(128, 2048))


def _col(v):
    return np.ascontiguousarray(np.asarray(v, np.float32).reshape(16, 128).T)


_NC_CACHE = {}


def prepare(x_prompt, x_sample, state_conv, cache_k, cache_v, norm_mix, w_pw1, b_pw1, w_dw, b_dw,
           conv_ln_g, conv_ln_b, w_pw2, b_pw2, kv_norm, w_k, w_v, w_q, w_o, sinks, norm_mlp,
           w_up, w_down, final_norm):
    f = lambda a: np.asarray(a, dtype=np.float32)
    x_prompt, x_sample, state_conv, cache_k, cache_v = map(f, (x_prompt, x_sample, state_conv, cache_k, cache_v))
    w_pw1, w_pw2, w_k, w_v, w_q, w_o, w_up, w_down = map(f, (w_pw1, w_pw2, w_k, w_v, w_q, w_o, w_up, w_down))
    norm_mix, norm_mlp, b_pw1, w_dw, b_dw = map(f, (norm_mix, norm_mlp, b_pw1, w_dw, b_dw))
    conv_ln_g, conv_ln_b, b_pw2, kv_norm, sinks, final_norm = map(f, (conv_ln_g, conv_ln_b, b_pw2, kv_norm, sinks, final_norm))

    tidx = unique_tiles()

    qperm = np.zeros(2048, np.int64)
    head_of = np.zeros((16, 2), np.int64)
    for oc in range(16):
        c, i = oc // 4, oc % 4
        for hd in range(2):
            h = 8 * c + 4 * hd + i
            head_of[oc, hd] = h
            qperm[oc * 128 + hd * 64: oc * 128 + hd * 64 + 64] = h * 64 + np.arange(64)
    wq_p = w_q[0][:, qperm]
    wo_p = w_o[0][qperm, :]

    wst = np.empty((len(tidx), 128, 2048), np.float32)
    for nm, ti in tidx.items():
        kind = nm[0]
        if kind == "pw1a":
            wst[ti] = _tile_kc(w_pw1[0], nm[1] * 128)
        elif kind == "pw1g":
            wst[ti] = _tile_kc(w_pw1[0], 2048 + nm[1] * 128)
        elif kind == "pw2":
            wst[ti] = _tile_kc(w_pw2[0], nm[1] * 128)
        elif kind == "up":
            wst[ti] = _tile_kc(w_up[nm[1]], nm[2] * 128)
        elif kind == "down":
            l, g, ocp = nm[1], nm[2], nm[3]
            wst[ti] = w_down[l][g * 1024:(g + 1) * 1024, ocp * 256:(ocp + 1) * 256].reshape(8, 128, 256).transpose(1, 0, 2).reshape(128, 2048)
        elif kind == "wk":
            wst[ti] = _tile_kc(w_k, nm[1] * 128)
        elif kind == "wv":
            c = nm[1]
            wst[ti] = w_v[c * 512:(c + 1) * 512, :].reshape(4, 128, 512).transpose(1, 0, 2).reshape(128, 2048)
        elif kind == "wq":
            wst[ti] = _tile_kc(wq_p, nm[1] * 128)
        elif kind == "wo":
            wst[ti] = _tile_kc(wo_p, nm[1] * 128)
        else:
            raise AssertionError(nm)

    vec = np.zeros((128, NV), np.float32)
    vec[:, V_NMIX0:V_NMIX0 + 16] = _col(norm_mix[0])
    vec[:, V_NMLP0:V_NMLP0 + 16] = _col(norm_mlp[0])
    vec[:, V_KVN:V_KVN + 16] = _col(kv_norm)
    vec[:, V_NMIX1:V_NMIX1 + 16] = _col(norm_mix[1])
    vec[:, V_NMLP1:V_NMLP1 + 16] = _col(norm_mlp[1])
    vec[:, V_FIN:V_FIN + 16] = _col(final_norm)
    vec[:, V_BA:V_BA + 16] = _col(b_pw1[0][:2048])
    vec[:, V_BG:V_BG + 16] = _col(b_pw1[0][2048:])
    vec[:, V_BDW:V_BDW + 16] = _col(b_dw[0])
    vec[:, V_LNG:V_LNG + 16] = _col(conv_ln_g[0])
    vec[:, V_LNB:V_LNB + 16] = _col(conv_ln_b[0])
    vec[:, V_BPW2:V_BPW2 + 16] = _col(b_pw2[0])
    for oc in range(16):
        vec[0:64, V_SINK + oc] = sinks[0][head_of[oc, 0]]
        vec[64:128, V_SINK + oc] = sinks[0][head_of[oc, 1]]
    vec[:, V_WDW:] = w_dw[0].T.reshape(16, 128, CW).transpose(1, 0, 2).reshape(128, 16 * CW)

    mask_prev = (np.arange(128)[:, None] > np.arange(128)[None, :]).astype(np.float32)
    mask_own = (np.arange(128)[:, None] <= np.arange(128)[None, :]).astype(np.float32)
    m4 = np.concatenate([mask_prev, mask_own, mask_prev, mask_own], axis=1)
    m4_first = m4.copy()
    m4_first[:, 0:128] = 0.0
    m4_first[:, 256:384] = 0.0
    ms = np.zeros((128, 32), np.float32)
    ms[:, 0:8] = mask_prev[:, 0:8]
    ms[:, 8:16] = mask_prev[:, 0:8]
    ms[0:8, 16:24] = mask_own[0:8, 0:8]
    ms[0:8, 24:32] = mask_own[0:8, 0:8]
    NEG = np.float32(-240000.0)
    mn = np.concatenate([(1.0 - mask_prev) * NEG, (1.0 - mask_own) * NEG], axis=1).astype(np.float32)
    mn_first = mn.copy()
    mn_first[:, 0:128] = NEG
    mns = np.zeros((128, 64), np.float32)
    for s_ in range(4):
        mns[:, s_ * 8:(s_ + 1) * 8] = (1.0 - mask_prev[:, 0:8]) * NEG
        mns[0:8, 32 + s_ * 8: 40 + s_ * 8] = (1.0 - mask_own[0:8, 0:8]) * NEG
    eye = np.eye(128, dtype=np.float32)
    pm = np.zeros((128, 128), np.float32)
    sgn = np.zeros(128, np.float32)
    fidx = np.full(128, -1, np.int64)
    for m_ in range(128):
        d = m_ % 64
        if d < 8:
            pm[m_ + 8, m_] = 1.0
            sgn[m_] = -1.0
            fidx[m_] = d
        elif d < 16:
            pm[m_ - 8, m_] = 1.0
            sgn[m_] = 1.0
            fidx[m_] = d - 8
    inv = (np.float32(500000.0) ** (-np.arange(8, dtype=np.float32) / np.float32(8))).astype(np.float32)

    in_maps = []
    for core in range(NCORE):
        b, j = core // 4, core % 4
        xe = np.zeros((TE, D), np.float32)
        if j > 0:
            xe[0:158] = x_prompt[b, 1024 * j - 158: 1024 * j]
        xe[158:1182] = x_prompt[b, 1024 * j: 1024 * (j + 1)]
        xe[1182:1214] = x_sample[4 * core: 4 * core + 4].reshape(32, D)
        xTc = np.ascontiguousarray(xe.T.reshape(16, 128, TE).transpose(1, 0, 2))
        sc = state_conv[0, 4 * core: 4 * core + 4]
        ctxTc = np.ascontiguousarray(sc.transpose(2, 0, 1).reshape(16, 128, 4, 30).transpose(1, 0, 2, 3))
        ck = cache_k[4 * core: 4 * core + 4].reshape(4, 128, 512)
        cv = cache_v[4 * core: 4 * core + 4].reshape(4, 128, 512)
        ckTc = np.ascontiguousarray(ck.transpose(2, 0, 1).reshape(4, 128, 4 * 128).transpose(1, 0, 2))
        pos = np.zeros(NTOK, np.float32)
        pos[0:1152] = (1024 * j - 128) + np.arange(1152)
        pos[1152:1184] = 16384 + (np.arange(32) % 8)
        tabc = np.zeros((128, 2 * NTOK), np.float32)
        tabc[:, 0:NTOK] = 1.0
        for m_ in range(128):
            if fidx[m_] >= 0:
                ang = (pos * inv[fidx[m_]]).astype(np.float32)
                tabc[m_, 0:NTOK] = np.cos(ang).astype(np.float32)
                tabc[m_, NTOK:] = sgn[m_] * np.sin(ang).astype(np.float32)
        v = vec.copy()
        v[:, V_FLAG] = 0.0 if j == 0 else 1.0
        in_maps.append({
            "xT": xTc, "wst": wst, "ctxT": ctxTc, "ckT": ckTc,
            "sc_raw": np.ascontiguousarray(sc), "ck_raw": np.ascontiguousarray(ck), "cv_raw": np.ascontiguousarray(cv),
            "vecs": v, "tab": tabc, "mask4": m4, "masks": ms, "perm": pm,
            "mask40": (m4_first if j == 0 else m4), "ident": eye, "maskns": mns,
        })

    return in_maps


def kernel(**inputs):
    in_maps = prepare(**inputs)
    if "nc" not in _NC_CACHE:
        _NC_CACHE["nc"] = build()
    nc = _NC_CACHE["nc"]
    res = run_bass_kernel_spmd(nc, in_maps, core_ids=list(range(NCORE)))
    return assemble(res.results)


def assemble(R_):

    y_prompt = np.zeros((2, 4096, D), np.float32)
    y_sample = np.zeros((32, 8, D), np.float32)
    conv_prompt = np.zeros((1, 2, 30, D), np.float32)
    conv_sample = np.zeros((1, 32, 30, D), np.float32)
    win_k_prompt = np.zeros((2, 128, 8, 64), np.float32)
    win_v_prompt = np.zeros((2, 128, 8, 64), np.float32)
    win_k_sample = np.zeros((32, 128, 8, 64), np.float32)
    win_v_sample = np.zeros((32, 128, 8, 64), np.float32)
    for core in range(NCORE):
        b, j = core // 4, core % 4
        r = R_[core]
        y = np.transpose(r["yT"], (2, 1, 0)).reshape(NQ, D)
        y_prompt[b, 1024 * j: 1024 * (j + 1)] = y[0:1024]
        y_sample[4 * core: 4 * core + 4] = y[1024:1056].reshape(4, 8, D)
        uo = r["uT_out"].reshape(128, 16, 62)
        ut = np.transpose(uo, (2, 1, 0)).reshape(62, D)
        kt = np.transpose(r["kT_out"].reshape(128, 4, 160), (2, 1, 0)).reshape(160, 512)
        if j == 3:
            conv_prompt[0, b] = ut[0:30]
            win_k_prompt[b] = kt[0:128].reshape(128, 8, 64)
            win_v_prompt[b] = r["v_out"].reshape(128, 8, 64)
        for s in range(4):
            q = 4 * core + s
            conv_sample[0, q, 0:22] = r["conv_old"][s]
            conv_sample[0, q, 22:30] = ut[30 + 8 * s: 30 + 8 * s + 8]
            win_k_sample[q, 0:120] = r["k_old"][s].reshape(120, 8, 64)
            win_k_sample[q, 120:128] = kt[128 + 8 * s: 128 + 8 * s + 8].reshape(8, 8, 64)
            win_v_sample[q, 0:120] = r["v_old"][s].reshape(120, 8, 64)
            win_v_sample[q, 120:128] = r["v_new"][s].reshape(8, 8, 64)
    return (y_prompt, y_sample, conv_prompt, conv_sample, win_k_prompt, win_v_prompt, win_k_sample, win_v_sample)
```

```python
import contextlib
import os
import numpy as np
import concourse.bass as bass
import concourse.mybir as mybir
from concourse.bass_utils import run_bass_kernel_spmd

F32 = mybir.dt.float32
BF16 = mybir.dt.bfloat16
AF = mybir.ActivationFunctionType
ALU = mybir.AluOpType

ENGS = ("pe", "act", "dve", "pool", "sp")


def C(name, *a, **k):
    return (name, a, k)


class Op:
    __slots__ = ("eng", "fn", "deps", "signal", "sem", "val", "pos", "dma")

    def __init__(self, eng, fn, dma=None):
        self.eng = eng
        self.fn = fn
        self.deps = []
        self.signal = False
        self.sem = None
        self.val = 0
        self.pos = 0
        self.dma = dma


class _Stop(Exception):
    pass


STOP = float(os.environ.get("MK_STOP", "99"))


def checkpoint(k):
    if STOP <= k:
        raise _Stop()


class Prog:
    SAFE_DIST = 10 ** 9

    def __init__(self):
        self.streams = {e: [] for e in ENGS}
        self.last_w = {}
        self.readers = {}
        self.last_op = {}
        self.pending_fence = {e: [] for e in ENGS}
        self.dma_pending = []

    def add(self, eng, fn, reads=(), writes=(), dma=None):
        op = Op(eng, fn, dma=dma)
        if dma is not None:
            op.signal = True
        deps = set()
        for r in reads:
            w = self.last_w.get(r)
            if w is not None:
                deps.add(w)
        for wkey in writes:
            w = self.last_w.get(wkey)
            if w is not None:
                deps.add(w)
            for rd in self.readers.get(wkey, ()):
                deps.add(rd)
        for f in self.pending_fence[eng]:
            deps.add(f)
        self.pending_fence[eng] = []
        op.pos = len(self.streams[eng])
        keep = []
        latest = {}
        for d in deps:
            if d.dma is None and d.eng == eng:
                if eng == "pe" or eng == "sp":
                    continue
                if op.pos - d.pos >= self.SAFE_DIST:
                    continue
            if d.dma is None:
                cur = latest.get(d.eng)
                if cur is None or d.pos > cur.pos:
                    latest[d.eng] = d
            else:
                keep.append(d)
        keep.extend(latest.values())
        op.deps = keep
        for d in keep:
            d.signal = True
        self.streams[eng].append(op)
        if dma is None:
            self.last_op[eng] = op
        elif not dma.startswith("slot"):
            self.dma_pending.append(op)
        for r in reads:
            self.readers.setdefault(r, []).append(op)
        for wkey in writes:
            self.last_w[wkey] = op
            self.readers[wkey] = []
        return op

    def fence(self):
        lasts = [o for o in self.last_op.values()]
        dmas = self.dma_pending
        self.dma_pending = []
        for e in ENGS:
            self.pending_fence[e] = [o for o in lasts if o.eng != e] + dmas

    def finalize(self, final_ops):
        for o in final_ops:
            o.signal = True
        counters = {}
        for e in ENGS:
            for op in self.streams[e]:
                if not op.signal:
                    continue
                if op.dma is not None:
                    key = ("dma", op.dma)
                    counters[key] = counters.get(key, 0) + 16
                else:
                    key = ("eng", e)
                    counters[key] = counters.get(key, 0) + 1
                op.sem = key
                op.val = counters[key]
        self.sem_keys = list(counters.keys())
        self.final_ops = final_ops

    def emit(self, nc):
        with contextlib.ExitStack() as st:
            sems = {}
            for i, k in enumerate(self.sem_keys):
                sems[k] = st.enter_context(nc.semaphore("s%d" % i))
            block = st.enter_context(nc.Block())
            prog = self

            def run(ename, e, is_last=False):
                waited = {}
                for op in prog.streams[ename]:
                    for d in op.deps:
                        if waited.get(d.sem, 0) < d.val:
                            e.wait_ge(sems[d.sem], d.val)
                            waited[d.sem] = d.val
                    name_, a_, k_ = op.fn
                    ins = getattr(e, name_)(*a_, **k_)
                    if op.signal:
                        ins.then_inc(sems[op.sem], 16 if op.dma is not None else 1)
                if is_last:
                    for d in prog.final_ops:
                        if waited.get(d.sem, 0) < d.val:
                            e.wait_ge(sems[d.sem], d.val)
                            waited[d.sem] = d.val

            @block.sync
            def _(e):
                run("sp", e, is_last=True)

            @block.scalar
            def _(e):
                run("act", e)

            @block.vector
            def _(e):
                run("dve", e)

            @block.gpsimd
            def _(e):
                run("pool", e)

            @block.tensor
            def _(e):
                run("pe", e)


NCORE = 8
D = 2048
KC = 16
TE = 1214
T_PRE, T_HALO, T_MAIN, T_SMP = 0, 30, 158, 1182
NTOK = TE - 30
NQ = TE - 158
SEGS = [0, 30, 158, 592, 622, 670, 1214]
NS = 4
CW = 31
NDV = 8
EPS = 1e-6
UW = 742

V_NMIX0, V_NMLP0, V_KVN, V_NMIX1, V_NMLP1, V_FIN = 0, 16, 32, 48, 64, 80
V_BA, V_BG, V_BDW, V_LNG, V_LNB, V_BPW2, V_SINK = 96, 112, 128, 144, 160, 176, 192
V_FLAG = 208
V_WDW = 212
NV = V_WDW + 16 * CW


def seg_keys(name, kc, a, b):
    return [(name, kc, i) for i in range(len(SEGS) - 1) if SEGS[i] < b and SEGS[i + 1] > a]


def tiles(a, b):
    out = []
    t = a
    while t < b:
        n = min(512, b - t)
        out.append((t, n))
        t += n
    return out


def weight_order():
    order = []
    for _pass in range(2):
        for oc in range(16):
            order.append(("pw1a", oc))
            order.append(("pw1g", oc))
        for oc in range(16):
            order.append(("pw2", oc))
    for g in range(8):
        for fc in range(8):
            order.append(("up", 0, g * 8 + fc))
        for ocp in range(8):
            order.append(("down", 0, g, ocp))
    for c in range(4):
        order.append(("wk", c))
    for _grp in range(2):
        for c in range(4):
            order.append(("wv", c))
    for _pass in range(2):
        for oc in range(16):
            order.append(("wq", oc))
        for oc in range(16):
            order.append(("wo", oc))
    for g in range(8):
        for fc in range(8):
            order.append(("up", 1, g * 8 + fc))
        for ocp in range(8):
            order.append(("down", 1, g, ocp))
    return order


def unique_tiles():
    seen = {}
    for nm in weight_order():
        if nm not in seen:
            seen[nm] = len(seen)
    return seen


def build():
    nc = bass.Bass("TRN2", target_bir_lowering=False)
    tidx = unique_tiles()
    NT = len(tidx)
    order = weight_order()

    def din(name, shape):
        return nc.dram_tensor(name, shape, F32, kind="ExternalInput").ap()

    def dout(name, shape):
        return nc.dram_tensor(name, shape, F32, kind="ExternalOutput").ap()

    xT = din("xT", [128, KC, TE])
    wst = din("wst", [NT, 128, 2048])
    ctxT = din("ctxT", [128, KC, 4, 30])
    ckT = din("ckT", [128, 4, 512])
    sc_raw = din("sc_raw", [4, 30, D])
    ck_raw = din("ck_raw", [4, 128, 512])
    cv_raw = din("cv_raw", [4, 128, 512])
    vecs = din("vecs", [128, NV])
    tab = din("tab", [128, 2 * NTOK])
    mask4 = din("mask4", [128, 512])
    masks = din("masks", [128, 32])
    mask40 = din("mask40", [128, 512])
    perm = din("perm", [128, 128])
    ident = din("ident", [128, 128])
    maskns = din("maskns", [128, 64])

    yT = dout("yT", [128, KC, NQ])
    uT_out = dout("uT_out", [128, KC * 62])
    conv_old = dout("conv_old", [4, 22, D])
    kT_out = dout("kT_out", [128, 4 * 160])
    v_out = dout("v_out", [128, 512])
    v_new = dout("v_new", [4, 8, 512])
    k_old = dout("k_old", [4, 120, 512])
    v_old = dout("v_old", [4, 120, 512])

    P = Prog()
    outs = []
    with contextlib.ExitStack() as st:
        def sb(name, cols, dt):
            return st.enter_context(nc.sbuf_tensor(name, [128, cols], dt))

        X = sb("X", KC * TE, F32)
        H = sb("H", KC * TE, BF16)
        RING = sb("RING", NS * 2048, BF16)
        VEC = sb("VEC", NV, F32)
        TAB = sb("TAB", 2 * NTOK, F32)
        TR0 = sb("TR0", TE, F32)
        TR1 = sb("TR1", TE, F32)
        SQ = sb("SQ", 4 * 512, BF16)
        ONESB = sb("ONESB", 128, BF16)
        ONE1 = sb("ONE1", 64, BF16)
        PERMF = sb("PERMF", 128, F32)
        PERMB = sb("PERMB", 128, BF16)
        MASK4 = sb("MASK4", 512, BF16)
        MASKS = sb("MASKS", 32, BF16)
        MASK40 = sb("MASK40", 512, BF16)
        ES = sb("ES", 16, F32)
        EPSC = sb("EPSC", 1, F32)
        IDENT = sb("IDENT", 128, BF16)
        MASKNS = sb("MASKNS", 64, BF16)
        RBYTES = 44288
        R = sb("R", RBYTES // 4, F32)
        banks = [st.enter_context(nc.psum_tensor("ps%d" % i, [128, 512], F32)) for i in range(8)]

        def carve(off, ncols, dt):
            sz = 4 if dt == F32 else 2
            assert off % 4 == 0 and (ncols * sz) % 4 == 0
            assert off + ncols * sz <= RBYTES, (off, ncols, sz)
            v = R[:, off // 4: off // 4 + (ncols * sz) // 4]
            return v if dt == F32 else v.bitcast(BF16)

        o = 0
        SIG = [carve(o + i * 2488, 622, F32) for i in range(2)]; o += 2 * 2488
        U = [carve(o + i * UW * 4, UW, F32) for i in range(2)]; o += 2 * UW * 4
        ACCD = [carve(o + i * 2368, 592, F32) for i in range(2)]; o += 2 * 2368
        UB = [carve(o + i * 1488, 744, BF16) for i in range(2)]; o += 2 * 1488
        NDG = 32
        DG = carve(o, NDG * 128, BF16); o += NDG * 256
        UC = carve(o, 16 * 30, F32); o += 16 * 30 * 4
        LNT = [carve(o + i * 2368, 592, F32) for i in range(2)]; o += 2 * 2368
        USAVE = carve(o, 16 * 62, F32); o += 16 * 62 * 4
        assert o <= RBYTES, o
        o = 0
        HID = carve(o, 8 * NTOK, BF16); o += 8 * NTOK * 2
        RT = [carve(o + i * 2048, 512, F32) for i in range(3)]; o += 3 * 2048
        YST = [carve(o + i * 2048, 512, F32) for i in range(2)]; o += 2 * 2048
        assert o <= RBYTES, o
        KTW = NTOK + 512
        o = 0
        KT = carve(o, 4 * KTW, BF16); o += 4 * KTW * 2
        VB = carve(o, 13 * 512, BF16); o += 13 * 512 * 2
        VN = carve(o, 4 * 512, BF16); o += 4 * 512 * 2
        o_alias = o
        QB = [carve(o + i * 1088, 544, BF16) for i in range(2)]; o += 2 * 1088
        QT = [carve(o + i * 1088, 544, BF16) for i in range(2)]; o += 2 * 1088
        KOUT = carve(o_alias, 4 * 160, F32)
        VST = [carve(o_alias + 2560 + i * 1024, 256, F32) for i in range(2)]
        assert o_alias + 2560 + 2048 <= o + 256
        o = max(o, o_alias + 2560 + 2048)
        NEV = 3
        EB = [carve(o + i * 1024, 512, BF16) for i in range(NEV)]; o += NEV * 1024
        PT = [carve(o + i * 1024, 512, BF16) for i in range(NEV)]; o += NEV * 1024
        DEN = [carve(o + i * 512, 128, F32) for i in range(NEV)]; o += NEV * 512
        assert o <= RBYTES, o

        def Xap(kc, a, b):
            return X[:, kc * TE + a: kc * TE + b]

        def Hap(kc, a, b):
            return H[:, kc * TE + a: kc * TE + b]

        def vcol(c):
            return VEC[:, c:c + 1]

        bank_ctr = [0]
        dgc = [0]

        def bank():
            b = bank_ctr[0] % 8
            bank_ctr[0] += 1
            return b

        wstate = {"issued": 0, "consumed": 0}

        def issue_next():
            n = wstate["issued"]
            if n >= len(order):
                return
            s = n % NS
            ti = tidx[order[n]]
            P.add("pool", C("dma_start", out=RING[:, s * 2048:(s + 1) * 2048], in_=wst[ti]),
                  writes=[("slot", s)], dma="slot%d" % s)
            wstate["issued"] = n + 1

        def get_tile(name):
            n = wstate["consumed"]
            assert order[n] == name, (order[n], name)
            s = n % NS
            wstate["consumed"] = n + 1
            return s

        def slot_ap(s, a, b):
            return RING[:, s * 2048 + a: s * 2048 + b]

        P.add("sp", C("dma_start", out=VEC[:], in_=vecs[:]), writes=["VEC"], dma="vec")
        for (ta, tb_) in ((0, 622), (622, TE)):
            for k0 in range(0, KC, 8):
                wk = []
                for kc in range(k0, k0 + 8):
                    wk += seg_keys("X", kc, ta, tb_)
                P.add("sp", C("dma_start",
                    out=X[:, k0 * TE:(k0 + 8) * TE].rearrange("p (k t) -> p k t", t=TE)[:, :, ta:tb_], in_=xT[:, k0:k0 + 8, ta:tb_]),
                    writes=wk, dma="xl%d_%d" % (k0, ta))
        P.add("sp", C("dma_start", out=TAB[:], in_=tab[:]), writes=["TAB"], dma="tab")
        P.add("sp", C("dma_start", out=PERMF[:], in_=perm[:]), writes=["PERMF"], dma="permf")
        P.add("pool", C("dma_start", out=PERMB[:], in_=perm[:]), writes=["PERMB"], dma="permb")
        P.add("pool", C("dma_start", out=MASK4[:], in_=mask4[:]), writes=["MASK4"], dma="mask4")
        P.add("pool", C("dma_start", out=IDENT[:], in_=ident[:]), writes=["IDENT"], dma="ident")
        P.add("pool", C("dma_start", out=MASKNS[:], in_=maskns[:]), writes=["MASKNS"], dma="maskns")
        P.add("pool", C("dma_start", out=MASKS[:], in_=masks[:]), writes=["MASKS"], dma="masks")
        P.add("pool", C("dma_start", out=MASK40[:], in_=mask40[:]), writes=["MASK4"], dma="mask40")
        for _ in range(NS):
            issue_next()
        P.add("dve", C("memset", ONESB[:], 1.0 / D), writes=["ONESB"])
        P.add("dve", C("memset", ONE1[:], 1.0), writes=["ONE1"])
        P.add("dve", C("memset", EPSC[:], EPS), writes=["EPSC"])
        P.add("act", C("activation", out=ES[:], in_=VEC[:, V_SINK:V_SINK + 16], func=AF.Exp),
              reads=["VEC"], writes=["ES"])
        outs.append(P.add("sp", C("dma_start", out=conv_old[:], in_=sc_raw[:, 8:30, :]), dma="d2d0"))
        outs.append(P.add("sp", C("dma_start", out=k_old[:], in_=ck_raw[:, 8:128, :]), dma="d2d1"))
        outs.append(P.add("sp", C("dma_start", out=v_old[:], in_=cv_raw[:, 8:128, :]), dma="d2d2"))

        def rstd_from_bank(b, n, rs_ap, rs_keys):
            P.add("act", C("activation", out=rs_ap, in_=banks[b][:, :n], func=AF.Ln, bias=EPSC[:, 0:1]),
                  reads=[("ps", b), "EPSC"], writes=rs_keys)
            P.add("act", C("activation", out=rs_ap, in_=rs_ap, func=AF.Exp, scale=-0.5), reads=rs_keys, writes=rs_keys)

        def rms_stats(a, b):
            for (t0, n) in tiles(a, b):
                bk = bank()
                for kc in range(KC):
                    sq = kc % 4
                    if kc % 2 == 0:
                        P.add("act", C("activation",
                            out=SQ[:, sq * 512: sq * 512 + n], in_=Xap(kc, t0, t0 + n), func=AF.Square),
                            reads=seg_keys("X", kc, t0, t0 + n), writes=[("SQ", sq)])
                    else:
                        P.add("dve", C("tensor_tensor",
                            out=SQ[:, sq * 512: sq * 512 + n], in0=Xap(kc, t0, t0 + n), in1=Xap(kc, t0, t0 + n), op=ALU.mult),
                            reads=seg_keys("X", kc, t0, t0 + n), writes=[("SQ", sq)])
                    P.add("pe", C("matmul",
                        banks[bk][:, :n], lhsT=ONESB[:], rhs=SQ[:, sq * 512: sq * 512 + n],
                        start=(kc == 0), stop=(kc == KC - 1)),
                        reads=[("SQ", sq), "ONESB"], writes=[("ps", bk)])
                rstd_from_bank(bk, n, TR0[:, t0:t0 + n], seg_keys("RS", 0, t0, t0 + n))

        def rms_to_h(a, b, gcol):
            rms_stats(a, b)
            for kc in range(KC):
                for (t0, n) in tiles(a, b):
                    P.add("dve", C("scalar_tensor_tensor",
                        out=Hap(kc, t0, t0 + n), in0=Xap(kc, t0, t0 + n), scalar=vcol(gcol + kc),
                        in1=TR0[:, t0:t0 + n], op0=ALU.mult, op1=ALU.mult),
                        reads=seg_keys("X", kc, t0, t0 + n) + seg_keys("RS", 0, t0, t0 + n) + ["VEC"],
                        writes=seg_keys("H", kc, t0, t0 + n))

        def proj_group(slot, lhs_off, lhs_stride, nk, rhs_fn, rhs_keys_fn, tl):
            bks = [bank() for _ in tl]
            for k in range(nk):
                for i, (t0, n) in enumerate(tl):
                    P.add("pe", C("matmul",
                        banks[bks[i]][:, :n], lhsT=slot_ap(slot, lhs_off + k * lhs_stride, lhs_off + k * lhs_stride + 128),
                        rhs=rhs_fn(k, t0, n), start=(k == 0), stop=(k == nk - 1)),
                        reads=[("slot", slot)] + rhs_keys_fn(k, t0, n), writes=[("ps", bks[i])])
            return bks

        def h_rhs(k, t0, n):
            return Hap(k, t0, t0 + n)

        def h_keys(k, t0, n):
            return seg_keys("H", k, t0, t0 + n)

        try:
            checkpoint(0)
            def conv_pass(pa):
                if pa == 0:
                    ga, gb, oa, ob = 0, 622, 30, 622
                    cbase = 622
                else:
                    ga, gb, oa, ob = 622, 1214, 622, 1214
                    cbase = 0
                rms_to_h(ga, gb, V_NMIX0)
                checkpoint(1)
                tl = tiles(ga, gb)

                def cap(oc, a, b):
                    return Hap(oc, cbase + a, cbase + b)

                def ckeys(oc, a, b):
                    return seg_keys("H", oc, cbase + a, cbase + b)

                UWID = 622 if pa == 0 else UW

                def stage1(oc):
                    u = oc % 2
                    sA = get_tile(("pw1a", oc))
                    bA = proj_group(sA, 0, 128, KC, h_rhs, h_keys, tl)
                    sG = get_tile(("pw1g", oc))
                    bG = proj_group(sG, 0, 128, KC, h_rhs, h_keys, tl)
                    issue_next()
                    issue_next()
                    if pa == 1:
                        P.add("act", C("activation", out=U[u][:, 0:30], in_=UC[:, oc * 30:(oc + 1) * 30], func=AF.Copy),
                              reads=[("UC", oc)], writes=[("UX", u)])
                        P.add("sp", C("dma_start",
                            out=U[u][:, 590:742].rearrange("p (s c) -> p s c", c=38)[:, :, 0:30], in_=ctxT[:, oc, :, :]),
                            writes=[("UX", u)], dma="ctx%d" % u)
                    for i, (t0, n) in enumerate(tl):
                        g0 = t0 - ga
                        P.add("act", C("activation",
                            out=SIG[u][:, g0:g0 + n], in_=banks[bG[i]][:, :n], func=AF.Sigmoid, bias=vcol(V_BG + oc)),
                            reads=[("ps", bG[i]), "VEC"], writes=[("SIG", u)])
                        npr = min(t0 + n, T_SMP) - t0
                        if npr > 0:
                            uc0 = t0 if pa == 0 else 30 + (t0 - 622)
                            P.add("dve", C("scalar_tensor_tensor",
                                out=U[u][:, uc0:uc0 + npr], in0=banks[bA[i]][:, :npr], scalar=vcol(V_BA + oc),
                                in1=SIG[u][:, g0:g0 + npr], op0=ALU.add, op1=ALU.mult),
                                reads=[("ps", bA[i]), ("SIG", u), "VEC"], writes=[("U", u)])
                        if npr < n:
                            assert n - npr == 32
                            P.add("dve", C("scalar_tensor_tensor",
                                out=U[u][:, 590:742].rearrange("p (s c) -> p s c", c=38)[:, :, 30:38],
                                in0=banks[bA[i]][:, npr:npr + 32].rearrange("p (s c) -> p s c", c=8), scalar=vcol(V_BA + oc),
                                in1=SIG[u][:, g0 + npr:g0 + npr + 32].rearrange("p (s c) -> p s c", c=8),
                                op0=ALU.add, op1=ALU.mult),
                                reads=[("ps", bA[i]), ("SIG", u), "VEC"], writes=[("U", u)])
                    if pa == 0:
                        P.add("dve", C("tensor_single_scalar", out=U[u][:, 0:158], in_=U[u][:, 0:158],
                                       scalar=vcol(V_FLAG), op=ALU.mult),
                              reads=[("U", u), "VEC"], writes=[("U", u)])
                        P.add("act", C("activation", out=UC[:, oc * 30:(oc + 1) * 30], in_=U[u][:, 592:622], func=AF.Copy),
                              reads=[("U", u)], writes=[("UC", oc)])
                    else:
                        P.add("act", C("activation", out=USAVE[:, oc * 62: oc * 62 + 30], in_=U[u][:, 560:590], func=AF.Copy),
                              reads=[("U", u)], writes=["USAVE"])
                        P.add("act", C("activation",
                            out=USAVE[:, oc * 62 + 30: oc * 62 + 62].rearrange("p (s c) -> p s c", c=8),
                            in_=U[u][:, 590:742].rearrange("p (s c) -> p s c", c=38)[:, :, 30:38], func=AF.Copy),
                            reads=[("U", u)], writes=["USAVE"])
                    if NDV < CW:
                        P.add("act", C("activation", out=UB[u][:, 0:UWID], in_=U[u][:, 0:UWID], func=AF.Copy),
                              reads=[("U", u), ("UX", u)], writes=[("UB", u)])

                def stage2a(oc):
                    dgs = []
                    for j in range(NDV, CW):
                        dg = dgc[0] % NDG
                        dgc[0] += 1
                        P.add("act", C("activation", out=DG[:, dg * 128:(dg + 1) * 128], in_=IDENT[:, :], func=AF.Copy,
                                       scale=vcol(V_WDW + oc * CW + j)),
                              reads=["IDENT", "VEC"], writes=[("DG", dg)])
                        dgs.append(dg)
                    return dgs

                def stage2(oc, dgs):
                    u = oc % 2
                    if pa == 0:
                        parts = [(lambda acc: acc[:, 0:512], lambda X_, j: X_[:, j:j + 512], lambda: cap(oc, 0, 512), lambda bk: banks[bk][:, 0:512]),
                                 (lambda acc: acc[:, 512:592], lambda X_, j: X_[:, 512 + j:592 + j], lambda: cap(oc, 512, 592), lambda bk: banks[bk][:, 0:80])]
                    else:
                        parts = [(lambda acc: acc[:, 0:512], lambda X_, j: X_[:, j:j + 512], lambda: cap(oc, 0, 512), lambda bk: banks[bk][:, 0:512]),
                                 (lambda acc: acc[:, 512:560], lambda X_, j: X_[:, 512 + j:560 + j], lambda: cap(oc, 512, 560), lambda bk: banks[bk][:, 0:48]),
                                 (lambda acc: acc[:, 560:592].rearrange("p (s c) -> p s c", c=8),
                                  lambda X_, j: X_[:, 590:742].rearrange("p (s c) -> p s c", c=38)[:, :, j:j + 8],
                                  lambda: cap(oc, 560, 592).rearrange("p (s c) -> p s c", c=8),
                                  lambda bk: banks[bk][:, 0:32].rearrange("p (s c) -> p s c", c=8))]
                    akey = ("ACC", "dve", u)
                    cb = []
                    if NDV < CW:
                        cb = [bank() for _ in parts]
                        for j in range(NDV, CW):
                            dg = dgs[j - NDV]
                            for pi, (accv, xv, cv, pv) in enumerate(parts):
                                P.add("pe", C("matmul", pv(cb[pi]), lhsT=DG[:, dg * 128:(dg + 1) * 128], rhs=xv(UB[u], j),
                                              start=(j == NDV), stop=(j == CW - 1)),
                                      reads=[("DG", dg), ("UB", u)], writes=[("ps", cb[pi])])
                    for j in range(NDV):
                        wcol = vcol(V_WDW + oc * CW + j)
                        for pi, (accv, xv, cv, pv) in enumerate(parts):
                            if j == 0:
                                P.add("dve", C("tensor_scalar", out=accv(ACCD[u]), in0=xv(U[u], j), scalar1=wcol,
                                               scalar2=vcol(V_BDW + oc), op0=ALU.mult, op1=ALU.add),
                                      reads=[("U", u), ("UX", u), "VEC"], writes=[akey])
                            else:
                                P.add("dve", C("scalar_tensor_tensor", out=accv(ACCD[u]), in0=xv(U[u], j), scalar=wcol,
                                               in1=accv(ACCD[u]), op0=ALU.mult, op1=ALU.add),
                                      reads=[("U", u), ("UX", u), "VEC", akey], writes=[akey])
                    for pi, (accv, xv, cv, pv) in enumerate(parts):
                        if NDV == 0:
                            P.add("dve", C("tensor_single_scalar", out=cv(), in_=pv(cb[pi]), scalar=vcol(V_BDW + oc), op=ALU.add),
                                  reads=[("ps", cb[pi]), "VEC"], writes=ckeys(oc, 0, 592))
                        elif NDV == CW:
                            P.add("dve", C("tensor_copy", out=cv(), in_=accv(ACCD[u])), reads=[akey], writes=ckeys(oc, 0, 592))
                        else:
                            P.add("dve", C("tensor_tensor", out=cv(), in0=pv(cb[pi]), in1=accv(ACCD[u]), op=ALU.add),
                                  reads=[("ps", cb[pi]), akey], writes=ckeys(oc, 0, 592))

                stage1(0)
                for oc in range(16):
                    dgs = stage2a(oc)
                    if oc + 1 < 16:
                        stage1(oc + 1)
                    stage2(oc, dgs)
                ltl = tiles(0, 592)
                bmu = [bank() for _ in ltl]
                bvar = [bank() for _ in ltl]
                for oc in range(16):
                    for i, (c0, n) in enumerate(ltl):
                        sq = (2 * oc + i) % 4
                        P.add("act", C("activation",
                            out=SQ[:, sq * 512: sq * 512 + n], in_=cap(oc, c0, c0 + n), func=AF.Square),
                            reads=ckeys(oc, c0, c0 + n), writes=[("SQ", sq)])
                        P.add("pe", C("matmul",
                            banks[bmu[i]][:, :n], lhsT=ONESB[:], rhs=cap(oc, c0, c0 + n), start=(oc == 0), stop=(oc == 15)),
                            reads=ckeys(oc, c0, c0 + n) + ["ONESB"], writes=[("ps", bmu[i])])
                        P.add("pe", C("matmul",
                            banks[bvar[i]][:, :n], lhsT=ONESB[:], rhs=SQ[:, sq * 512: sq * 512 + n], start=(oc == 0), stop=(oc == 15)),
                            reads=[("SQ", sq), "ONESB"], writes=[("ps", bvar[i])])
                for i, (c0, n) in enumerate(ltl):
                    P.add("act", C("activation", out=TR1[:, c0:c0 + n], in_=banks[bmu[i]][:, :n], func=AF.Copy),
                          reads=[("ps", bmu[i])], writes=[("MU", i)])
                    P.add("dve", C("tensor_tensor", out=LNT[0][:, c0:c0 + n], in0=TR1[:, c0:c0 + n], in1=TR1[:, c0:c0 + n], op=ALU.mult),
                          reads=[("MU", i)], writes=[("LNT", 0)])
                    P.add("dve", C("tensor_tensor", out=LNT[0][:, c0:c0 + n], in0=banks[bvar[i]][:, :n], in1=LNT[0][:, c0:c0 + n], op=ALU.subtract),
                          reads=[("ps", bvar[i]), ("LNT", 0)], writes=[("LNT", 0)])
                    P.add("act", C("activation", out=TR0[:, c0:c0 + n], in_=LNT[0][:, c0:c0 + n], func=AF.Ln, bias=EPSC[:, 0:1]),
                          reads=[("LNT", 0), "EPSC"], writes=seg_keys("RS", 0, c0, c0 + n))
                    P.add("act", C("activation", out=TR0[:, c0:c0 + n], in_=TR0[:, c0:c0 + n], func=AF.Exp, scale=-0.5),
                          reads=seg_keys("RS", 0, c0, c0 + n), writes=seg_keys("RS", 0, c0, c0 + n))
                rs_keys = seg_keys("RS", 0, 0, 592)
                for oc in range(16):
                    a = oc % 2
                    P.add("dve", C("tensor_tensor", out=LNT[a][:, 0:592], in0=cap(oc, 0, 592), in1=TR1[:, 0:592], op=ALU.subtract),
                          reads=ckeys(oc, 0, 592) + [("MU", 0), ("MU", 1)], writes=[("LNT", a)])
                    P.add("dve", C("tensor_tensor", out=LNT[a][:, 0:592], in0=LNT[a][:, 0:592], in1=TR0[:, 0:592], op=ALU.mult),
                          reads=[("LNT", a)] + rs_keys, writes=[("LNT", a)])
                    P.add("act", C("activation", out=Hap(oc, oa, ob), in_=LNT[a][:, 0:592], func=AF.Silu,
                                                                   bias=vcol(V_LNB + oc), scale=vcol(V_LNG + oc)),
                          reads=[("LNT", a), "VEC"], writes=seg_keys("H", oc, oa, ob))
                otl = tiles(oa, ob)
                for oc in range(16):
                    s2 = get_tile(("pw2", oc))
                    bks = proj_group(s2, 0, 128, KC, h_rhs, h_keys, otl)
                    issue_next()
                    for i, (t0, n) in enumerate(otl):
                        P.add("dve", C("scalar_tensor_tensor",
                            out=Xap(oc, t0, t0 + n), in0=banks[bks[i]][:, :n], scalar=vcol(V_BPW2 + oc),
                            in1=Xap(oc, t0, t0 + n), op0=ALU.add, op1=ALU.add),
                            reads=[("ps", bks[i]), "VEC"] + seg_keys("X", oc, t0, t0 + n), writes=seg_keys("X", oc, t0, t0 + n))

            conv_pass(0)
            checkpoint(2)
            conv_pass(1)
            checkpoint(3)
            outs.append(P.add("sp", C("dma_start", out=uT_out[:], in_=USAVE[:]), reads=["USAVE"], dma="usave"))
            P.fence()

            def mlp(l, a, b, gcol):
                rms_to_h(a, b, gcol)
                tl = tiles(a, b)
                for g in range(8):
                    for fc in range(8):
                        s = get_tile(("up", l, g * 8 + fc))
                        bks = proj_group(s, 0, 128, KC, h_rhs, h_keys, tl)
                        issue_next()
                        for i, (t0, n) in enumerate(tl):
                            P.add("act", C("activation", out=RT[i][:, :n], in_=banks[bks[i]][:, :n], func=AF.Relu),
                                  reads=[("ps", bks[i])], writes=[("RT", i)])
                        for i, (t0, n) in enumerate(tl):
                            P.add("act", C("activation",
                                out=HID[:, fc * NTOK + (t0 - a): fc * NTOK + (t0 - a) + n], in_=RT[i][:, :n], func=AF.Square),
                                reads=[("RT", i)], writes=[("HID", fc, i)])
                    for ocp in range(8):
                        s = get_tile(("down", l, g, ocp))
                        for o2 in range(2):
                            oc = 2 * ocp + o2
                            bks = proj_group(s, o2 * 128, 256, 8,
                                             lambda k, t0, n: HID[:, k * NTOK + (t0 - a): k * NTOK + (t0 - a) + n],
                                             lambda k, t0, n: [("HID", k, i) for i in range(3)], tl)
                            for i, (t0, n) in enumerate(tl):
                                P.add("dve", C("tensor_tensor",
                                    out=Xap(oc, t0, t0 + n), in0=banks[bks[i]][:, :n], in1=Xap(oc, t0, t0 + n), op=ALU.add),
                                    reads=[("ps", bks[i])] + seg_keys("X", oc, t0, t0 + n), writes=seg_keys("X", oc, t0, t0 + n))
                        issue_next()

            mlp(0, 30, TE, V_NMLP0)
            P.fence()
            checkpoint(4)

            P.add("pool", C("dma_start",
                out=KT[:].rearrange("p (c w) -> p c w", w=KTW)[:, :, NTOK:NTOK + 512],
                in_=ckT[:]), writes=["KTC"], dma="ktc")
            P.add("pool", C("dma_start",
                out=VB[:, 9 * 512:13 * 512].rearrange("p (s f) -> p s f", f=512),
                in_=cv_raw[:].rearrange("s k f -> k s f")), writes=["VBC"], dma="vbc")
            rms_to_h(30, TE, V_KVN)
            ktl = tiles(30, TE)
            CT = lambda t0, n: TAB[:, t0 - 30: t0 - 30 + n]
            ST = lambda t0, n: TAB[:, NTOK + t0 - 30: NTOK + t0 - 30 + n]
            for c in range(4):
                s = get_tile(("wk", c))
                bks = proj_group(s, 0, 128, KC, h_rhs, h_keys, ktl)
                issue_next()
                for i, (t0, n) in enumerate(ktl):
                    kf = TR1[:, t0 - 30: t0 - 30 + n]
                    kk = [("KF", i)]
                    P.add("act", C("activation", out=kf, in_=banks[bks[i]][:, :n], func=AF.Copy),
                          reads=[("ps", bks[i])], writes=kk)
                    b2 = bank()
                    P.add("pe", C("matmul", banks[b2][:, :n], lhsT=PERMF[:], rhs=kf, start=True, stop=True),
                          reads=kk + ["PERMF"], writes=[("ps", b2)])
                    tv = TR0[:, t0 - 30: t0 - 30 + n]
                    tk = seg_keys("RS", 0, t0 - 30, t0 - 30 + n)
                    P.add("dve", C("tensor_tensor", out=tv, in0=banks[b2][:, :n], in1=ST(t0, n), op=ALU.mult),
                          reads=[("ps", b2), "TAB"], writes=tk)
                    P.add("dve", C("tensor_tensor", out=kf, in0=kf, in1=CT(t0, n), op=ALU.mult),
                          reads=kk + ["TAB"], writes=kk)
                    P.add("dve", C("tensor_tensor", out=kf, in0=kf, in1=tv, op=ALU.add),
                          reads=kk + tk, writes=kk)
                    P.add("act", C("activation",
                        out=KT[:, c * KTW + t0 - 30: c * KTW + t0 - 30 + n], in_=kf, func=AF.Copy),
                        reads=kk, writes=[("KT", c, i)])
                    if i == 2:
                        assert t0 == 1054 and n == 160
                        P.add("act", C("activation", out=KOUT[:, c * 160:(c + 1) * 160], in_=kf, func=AF.Copy),
                              reads=kk, writes=["KOUT"])
            outs.append(P.add("sp", C("dma_start", out=kT_out[:], in_=KOUT[:]), reads=["KOUT"], dma="kout"))
            checkpoint(5)
            vgroups = [[("blk", bi) for bi in range(6)], [("blk", bi) for bi in range(6, 9)] + [("smp", s_) for s_ in range(4)]]
            vst_ctr = [0]
            for grp in vgroups:
                gb_ = [bank() for _ in grp]
                for c in range(4):
                    s = get_tile(("wv", c))
                    for gi, (kind, idx) in enumerate(grp):
                        for kk_ in range(4):
                            kc = c * 4 + kk_
                            if kind == "blk":
                                t0 = 30 + 128 * idx
                                P.add("pe", C("matmul",
                                    banks[gb_[gi]][:, :], lhsT=Hap(kc, t0, t0 + 128), rhs=slot_ap(s, kk_ * 512, kk_ * 512 + 512),
                                    start=(kc == 0), stop=(kc == KC - 1)),
                                    reads=[("slot", s)] + seg_keys("H", kc, t0, t0 + 128), writes=[("ps", gb_[gi])])
                            else:
                                t0 = T_SMP + 8 * idx
                                P.add("pe", C("matmul",
                                    banks[gb_[gi]][0:8, :], lhsT=Hap(kc, t0, t0 + 8), rhs=slot_ap(s, kk_ * 512, kk_ * 512 + 512),
                                    start=(kc == 0), stop=(kc == KC - 1)),
                                    reads=[("slot", s)] + seg_keys("H", kc, t0, t0 + 8), writes=[("ps", gb_[gi])])
                    issue_next()
                for gi, (kind, idx) in enumerate(grp):
                    bkk = gb_[gi]
                    if kind == "blk":
                        P.add("act", C("activation", out=VB[:, idx * 512:(idx + 1) * 512], in_=banks[bkk][:, :], func=AF.Copy),
                              reads=[("ps", bkk)], writes=[("VB", idx)])
                        if idx == 8:
                            for hh in range(2):
                                P.add("act", C("activation", out=VST[hh][:, :], in_=banks[bkk][:, hh * 256:(hh + 1) * 256], func=AF.Copy),
                                      reads=[("ps", bkk)], writes=[("VST", hh)])
                                outs.append(P.add("sp", C("dma_start", out=v_out[:, hh * 256:(hh + 1) * 256], in_=VST[hh][:, :]),
                                                  reads=[("VST", hh)], dma="vst%d" % hh))
                    else:
                        P.add("act", C("activation", out=VN[0:8, idx * 512:(idx + 1) * 512], in_=banks[bkk][0:8, :], func=AF.Copy),
                              reads=[("ps", bkk)], writes=[("VN", idx)])
                        for hh in range(2):
                            P.add("act", C("activation", out=VST[hh][0:8, :], in_=banks[bkk][0:8, hh * 256:(hh + 1) * 256], func=AF.Copy),
                                  reads=[("ps", bkk)], writes=[("VST", hh)])
                            outs.append(P.add("sp", C("dma_start", out=v_new[idx, :, hh * 256:(hh + 1) * 256], in_=VST[hh][0:8, :]),
                                              reads=[("VST", hh)], dma="vst%d" % hh))

            def attn_pass(pa):
                if pa == 0:
                    qa, qb, obase = 158, 670, 670
                    blocks = [0, 1, 2, 3]
                    smp = []
                else:
                    qa, qb, obase = 670, 1214, 30
                    blocks = [4, 5, 6, 7]
                    smp = [0, 1, 2, 3]
                nq = qb - qa
                rms_to_h(qa, qb, V_NMIX1)
                qtl = tiles(qa, qb)

                def oT(oc, a, n):
                    return Hap(oc, obase + a, obase + a + n)

                def oT_keys(oc, a, n):
                    return seg_keys("H", oc, obase + a, obase + a + n)

                ev = [0]

                def stage_q(oc):
                    u = oc % 2
                    s = get_tile(("wq", oc))
                    bks = proj_group(s, 0, 128, KC, h_rhs, h_keys, qtl)
                    issue_next()
                    for i, (t0, n) in enumerate(qtl):
                        off = t0 - qa
                        P.add("act", C("activation", out=QB[u][:, off:off + n], in_=banks[bks[i]][:, :n], func=AF.Copy),
                              reads=[("ps", bks[i])], writes=[("QB", u, i), "KOUT", ("VST", 0), ("VST", 1)])
                        b2 = bank()
                        P.add("pe", C("matmul", banks[b2][:, :n], lhsT=PERMB[:], rhs=QB[u][:, off:off + n], start=True, stop=True),
                              reads=[("QB", u, i), "PERMB"], writes=[("ps", b2)])
                        t1 = TR1[:, off:off + n]
                        t2 = TR1[:, 600 + off:600 + off + n]
                        P.add("dve", C("tensor_tensor", out=t1, in0=banks[b2][:, :n], in1=ST(t0, n), op=ALU.mult),
                              reads=[("ps", b2), "TAB"], writes=[("QTMP", 0, i)])
                        P.add("dve", C("tensor_tensor", out=t2, in0=QB[u][:, off:off + n], in1=CT(t0, n), op=ALU.mult),
                              reads=[("QB", u, i), "TAB"], writes=[("QTMP", 1, i)])
                        P.add("dve", C("tensor_tensor", out=QT[u][:, off:off + n], in0=t1, in1=t2, op=ALU.add),
                              reads=[("QTMP", 0, i), ("QTMP", 1, i)], writes=[("QT", u, i)])

                def unit_scores(oc, unit):
                    c = oc // 4
                    u = oc % 2
                    qt_keys = [("QT", u, i) for i in range(len(qtl))]
                    v = ev[0] % NEV
                    ev[0] += 1
                    bSh = [sbank(), sbank()]
                    if unit[0] == "blk":
                        m = unit[1]
                        tb = 158 + 128 * m
                        off = tb - qa
                        kprev = c * KTW + (tb - 128 - 30)
                        kown = c * KTW + (tb - 30)
                        for hd in range(2):
                            pr0, pr1 = hd * 64, hd * 64 + 64
                            for kh, kcol in enumerate((kprev, kown)):
                                P.add("pe", C("matmul", banks[bSh[hd]][:, kh * 128: kh * 128 + 128],
                                              lhsT=KT[pr0:pr1, kcol:kcol + 128], rhs=QT[u][pr0:pr1, off:off + 128], start=True, stop=True),
                                      reads=[("KT", c, 0), ("KT", c, 1), ("KT", c, 2)] + qt_keys, writes=[("ps", bSh[hd])])
                        for hd in range(2):
                            P.add("act", C("activation", out=PT[v][:, hd * 256:(hd + 1) * 256], in_=banks[bSh[hd]][:, 0:256], func=AF.Exp, scale=0.125),
                                  reads=[("ps", bSh[hd])], writes=[("PT", v)])
                        M4 = MASK40 if m == 0 else MASK4
                        P.add("dve", C("tensor_tensor", out=EB[v][:, :], in0=PT[v][:, :], in1=M4[:, :], op=ALU.mult),
                              reads=[("PT", v), "MASK4"], writes=[("EB", v)])
                    else:
                        for hd in range(2):
                            pr0, pr1 = hd * 64, hd * 64 + 64
                            P.add("pe", C("matmul", banks[bSh[hd]][:, 0:32], lhsT=IDENT[:, :], rhs=MASKNS[:, 0:32], start=True, stop=False),
                                  reads=["IDENT", "MASKNS"], writes=[("ps", bSh[hd])])
                            for sq_ in range(4):
                                off = (T_SMP - qa) + 8 * sq_
                                kcache = c * KTW + NTOK + sq_ * 128
                                P.add("pe", C("matmul", banks[bSh[hd]][:, sq_ * 8: sq_ * 8 + 8],
                                              lhsT=KT[pr0:pr1, kcache:kcache + 128], rhs=QT[u][pr0:pr1, off:off + 8], start=False, stop=(sq_ == 3)),
                                      reads=["KTC"] + qt_keys, writes=[("ps", bSh[hd])])
                            P.add("pe", C("matmul", banks[bSh[hd]][0:8, 32:64], lhsT=IDENT[0:8, 0:8], rhs=MASKNS[0:8, 32:64], start=True, stop=False),
                                  reads=["IDENT", "MASKNS"], writes=[("ps", bSh[hd])])
                            for sq_ in range(4):
                                off = (T_SMP - qa) + 8 * sq_
                                knew = c * KTW + (T_SMP + 8 * sq_ - 30)
                                P.add("pe", C("matmul", banks[bSh[hd]][0:8, 32 + sq_ * 8: 40 + sq_ * 8],
                                              lhsT=KT[pr0:pr1, knew:knew + 8], rhs=QT[u][pr0:pr1, off:off + 8], start=False, stop=(sq_ == 3)),
                                      reads=[("KT", c, 2)] + qt_keys, writes=[("ps", bSh[hd])])
                        for hd in range(2):
                            P.add("act", C("activation", out=EB[v][:, hd * 64: hd * 64 + 32], in_=banks[bSh[hd]][:, 0:32], func=AF.Exp, scale=0.125),
                                  reads=[("ps", bSh[hd])], writes=[("EB", v)])
                            P.add("act", C("activation", out=EB[v][0:8, hd * 64 + 32: hd * 64 + 64], in_=banks[bSh[hd]][0:8, 32:64], func=AF.Exp, scale=0.125),
                                  reads=[("ps", bSh[hd])], writes=[("EB", v)])
                    return v

                def unit_pv(oc, unit, v):
                    c = oc // 4
                    kva, kvb = 2 * c, 2 * c + 1
                    bO = obank()
                    if unit[0] == "blk":
                        m = unit[1]
                        off = 158 + 128 * m - qa
                        w = 128
                        for hd, kv in enumerate((kva, kvb)):
                            pr0, pr1 = hd * 64, hd * 64 + 64
                            for kh, vblk in enumerate((m, m + 1)):
                                P.add("pe", C("matmul", banks[bO][pr0:pr1, 0:128], lhsT=VB[:, vblk * 512 + kv * 64: vblk * 512 + kv * 64 + 64],
                                              rhs=EB[v][:, hd * 256 + kh * 128: hd * 256 + kh * 128 + 128], start=(kh == 0), stop=(kh == 1)),
                                      reads=[("VB", vblk), ("EB", v)], writes=[("ps", bO)])
                            for kh in range(2):
                                P.add("pe", C("matmul", banks[bO][pr0:pr1, 128:256], lhsT=ONE1[:, 0:64],
                                              rhs=EB[v][:, hd * 256 + kh * 128: hd * 256 + kh * 128 + 128], start=(kh == 0), stop=(kh == 1)),
                                      reads=["ONE1", ("EB", v)], writes=[("ps", bO)])
                    else:
                        off = T_SMP - qa
                        w = 32
                        for hd, kv in enumerate((kva, kvb)):
                            pr0, pr1 = hd * 64, hd * 64 + 64
                            for sq_ in range(4):
                                rc = EB[v][:, hd * 64 + sq_ * 8: hd * 64 + sq_ * 8 + 8]
                                rn = EB[v][0:8, hd * 64 + 32 + sq_ * 8: hd * 64 + 40 + sq_ * 8]
                                P.add("pe", C("matmul", banks[bO][pr0:pr1, sq_ * 8: sq_ * 8 + 8],
                                              lhsT=VB[:, (9 + sq_) * 512 + kv * 64: (9 + sq_) * 512 + kv * 64 + 64], rhs=rc, start=True, stop=False),
                                      reads=["VBC", ("EB", v)], writes=[("ps", bO)])
                                P.add("pe", C("matmul", banks[bO][pr0:pr1, sq_ * 8: sq_ * 8 + 8],
                                              lhsT=VN[0:8, sq_ * 512 + kv * 64: sq_ * 512 + kv * 64 + 64], rhs=rn, start=False, stop=True),
                                      reads=[("VN", sq_), ("EB", v)], writes=[("ps", bO)])
                                P.add("pe", C("matmul", banks[bO][pr0:pr1, 128 + sq_ * 8: 136 + sq_ * 8], lhsT=ONE1[:, 0:64], rhs=rc, start=True, stop=False),
                                      reads=["ONE1", ("EB", v)], writes=[("ps", bO)])
                                P.add("pe", C("matmul", banks[bO][pr0:pr1, 128 + sq_ * 8: 136 + sq_ * 8], lhsT=ONE1[0:8, 0:64], rhs=rn, start=False, stop=True),
                                      reads=["ONE1", ("EB", v)], writes=[("ps", bO)])
                    P.add("act", C("activation", out=DEN[v][:, 0:w], in_=banks[bO][:, 128:128 + w], func=AF.Ln, bias=ES[:, oc:oc + 1]),
                          reads=[("ps", bO), "ES"], writes=[("DEN", v)])
                    P.add("act", C("activation", out=DEN[v][:, 0:w], in_=DEN[v][:, 0:w], func=AF.Exp, scale=-1.0),
                          reads=[("DEN", v)], writes=[("DEN", v)])
                    P.add("dve", C("tensor_tensor", out=oT(oc, off, w), in0=banks[bO][:, 0:w], in1=DEN[v][:, 0:w], op=ALU.mult),
                          reads=[("ps", bO), ("DEN", v)], writes=oT_keys(oc, off, w))

                units = [("blk", m) for m in blocks] + ([("smp", 0)] if smp else [])
                seq = [(oc, un) for oc in range(16) for un in units]
                sctr = [0]
                octr = [0]

                def sbank():
                    b_ = sctr[0] % 6
                    sctr[0] += 1
                    return b_

                def obank():
                    b_ = 6 + octr[0] % 2
                    octr[0] += 1
                    return b_

                stage_q(0)
                pend = []
                for idx, (oc, un) in enumerate(seq):
                    if un is units[0] and oc + 1 < 16:
                        stage_q(oc + 1)
                    v = unit_scores(oc, un)
                    pend.append((oc, un, v))
                    if len(pend) > 2:
                        unit_pv(*pend.pop(0))
                while pend:
                    unit_pv(*pend.pop(0))
                otl = [(t0 - qa, n) for (t0, n) in qtl]
                for oc in range(16):
                    s = get_tile(("wo", oc))
                    bks = proj_group(s, 0, 128, KC,
                                     lambda k, o0, n: oT(k, o0, n), lambda k, o0, n: oT_keys(k, o0, n), otl)
                    issue_next()
                    for i, (o0, n) in enumerate(otl):
                        t0 = qa + o0
                        P.add("dve", C("tensor_tensor",
                            out=Xap(oc, t0, t0 + n), in0=banks[bks[i]][:, :n], in1=Xap(oc, t0, t0 + n), op=ALU.add),
                            reads=[("ps", bks[i])] + seg_keys("X", oc, t0, t0 + n), writes=seg_keys("X", oc, t0, t0 + n))

            checkpoint(6)
            attn_pass(0)
            checkpoint(7)
            attn_pass(1)
            checkpoint(8)
            P.fence()

            mlp(1, 158, TE, V_NMLP1)
            checkpoint(9)

            rms_stats(158, TE)
            yc = [0]
            for (t0, n) in tiles(158, TE):
                for kc in range(KC):
                    yb = yc[0] % 2
                    yc[0] += 1
                    P.add("dve", C("scalar_tensor_tensor",
                        out=YST[yb][:, :n], in0=Xap(kc, t0, t0 + n), scalar=vcol(V_FIN + kc),
                        in1=TR0[:, t0:t0 + n], op0=ALU.mult, op1=ALU.mult),
                        reads=seg_keys("X", kc, t0, t0 + n) + seg_keys("RS", 0, t0, t0 + n) + ["VEC"], writes=[("YST", yb)])
                    outs.append(P.add("sp", C("dma_start",
                        out=yT[:, kc, t0 - 158: t0 - 158 + n], in_=YST[yb][:, :n]), reads=[("YST", yb)], dma="yst%d" % yb))

            assert wstate["consumed"] == len(order), (wstate, len(order))
        except _Stop:
            pass
        P.finalize(outs)
        P.emit(nc)
    return nc


def _tile_kc(W, c0):
    return np.ascontiguousarray(W[:, c0:c0 + 128].reshape(16, 128, 128).transpose(1, 0, 2).reshape(128, 2048))


def _col(v):
    return np.ascontiguousarray(np.asarray(v, np.float32).reshape(16, 128).T)


_NC_CACHE = {}


def prepare(x_prompt, x_sample, state_conv, cache_k, cache_v, norm_mix, w_pw1, b_pw1, w_dw, b_dw,
           conv_ln_g, conv_ln_b, w_pw2, b_pw2, kv_norm, w_k, w_v, w_q, w_o, sinks, norm_mlp,
           w_up, w_down, final_norm):
    f = lambda a: np.asarray(a, dtype=np.float32)
    x_prompt, x_sample, state_conv, cache_k, cache_v = map(f, (x_prompt, x_sample, state_conv, cache_k, cache_v))
    w_pw1, w_pw2, w_k, w_v, w_q, w_o, w_up, w_down = map(f, (w_pw1, w_pw2, w_k, w_v, w_q, w_o, w_up, w_down))
    norm_mix, norm_mlp, b_pw1, w_dw, b_dw = map(f, (norm_mix, norm_mlp, b_pw1, w_dw, b_dw))
    conv_ln_g, conv_ln_b, b_pw2, kv_norm, sinks, final_norm = map(f, (conv_ln_g, conv_ln_b, b_pw2, kv_norm, sinks, final_norm))

    tidx = unique_tiles()

    qperm = np.zeros(2048, np.int64)
    head_of = np.zeros((16, 2), np.int64)
    for oc in range(16):
        c, i = oc // 4, oc % 4
        for hd in range(2):
            h = 8 * c + 4 * hd + i
            head_of[oc, hd] = h
            qperm[oc * 128 + hd * 64: oc * 128 + hd * 64 + 64] = h * 64 + np.arange(64)
    wq_p = w_q[0][:, qperm]
    wo_p = w_o[0][qperm, :]

    wst = np.empty((len(tidx), 128, 2048), np.float32)
    for nm, ti in tidx.items():
        kind = nm[0]
        if kind == "pw1a":
            wst[ti] = _tile_kc(w_pw1[0], nm[1] * 128)
        elif kind == "pw1g":
            wst[ti] = _tile_kc(w_pw1[0], 2048 + nm[1] * 128)
        elif kind == "pw2":
            wst[ti] = _tile_kc(w_pw2[0], nm[1] * 128)
        elif kind == "up":
            wst[ti] = _tile_kc(w_up[nm[1]], nm[2] * 128)
        elif kind == "down":
            l, g, ocp = nm[1], nm[2], nm[3]
            wst[ti] = w_down[l][g * 1024:(g + 1) * 1024, ocp * 256:(ocp + 1) * 256].reshape(8, 128, 256).transpose(1, 0, 2).reshape(128, 2048)
        elif kind == "wk":
            wst[ti] = _tile_kc(w_k, nm[1] * 128)
        elif kind == "wv":
            c = nm[1]
            wst[ti] = w_v[c * 512:(c + 1) * 512, :].reshape(4, 128, 512).transpose(1, 0, 2).reshape(128, 2048)
        elif kind == "wq":
            wst[ti] = _tile_kc(wq_p, nm[1] * 128)
        elif kind == "wo":
            wst[ti] = _tile_kc(wo_p, nm[1] * 128)
        else:
            raise AssertionError(nm)

    vec = np.zeros((128, NV), np.float32)
    vec[:, V_NMIX0:V_NMIX0 + 16] = _col(norm_mix[0])
    vec[:, V_NMLP0:V_NMLP0 + 16] = _col(norm_mlp[0])
    vec[:, V_KVN:V_KVN + 16] = _col(kv_norm)
    vec[:, V_NMIX1:V_NMIX1 + 16] = _col(norm_mix[1])
    vec[:, V_NMLP1:V_NMLP1 + 16] = _col(norm_mlp[1])
    vec[:, V_FIN:V_FIN + 16] = _col(final_norm)
    vec[:, V_BA:V_BA + 16] = _col(b_pw1[0][:2048])
    vec[:, V_BG:V_BG + 16] = _col(b_pw1[0][2048:])
    vec[:, V_BDW:V_BDW + 16] = _col(b_dw[0])
    vec[:, V_LNG:V_LNG + 16] = _col(conv_ln_g[0])
    vec[:, V_LNB:V_LNB + 16] = _col(conv_ln_b[0])
    vec[:, V_BPW2:V_BPW2 + 16] = _col(b_pw2[0])
    for oc in range(16):
        vec[0:64, V_SINK + oc] = sinks[0][head_of[oc, 0]]
        vec[64:128, V_SINK + oc] = sinks[0][head_of[oc, 1]]
    vec[:, V_WDW:] = w_dw[0].T.reshape(16, 128, CW).transpose(1, 0, 2).reshape(128, 16 * CW)

    mask_prev = (np.arange(128)[:, None] > np.arange(128)[None, :]).astype(np.float32)
    mask_own = (np.arange(128)[:, None] <= np.arange(128)[None, :]).astype(np.float32)
    m4 = np.concatenate([mask_prev, mask_own, mask_prev, mask_own], axis=1)
    m4_first = m4.copy()
    m4_first[:, 0:128] = 0.0
    m4_first[:, 256:384] = 0.0
    ms = np.zeros((128, 32), np.float32)
    ms[:, 0:8] = mask_prev[:, 0:8]
    ms[:, 8:16] = mask_prev[:, 0:8]
    ms[0:8, 16:24] = mask_own[0:8, 0:8]
    ms[0:8, 24:32] = mask_own[0:8, 0:8]
    NEG = np.float32(-240000.0)
    mn = np.concatenate([(1.0 - mask_prev) * NEG, (1.0 - mask_own) * NEG], axis=1).astype(np.float32)
    mn_first = mn.copy()
    mn_first[:, 0:128] = NEG
    mns = np.zeros((128, 64), np.float32)
    for s_ in range(4):
        mns[:, s_ * 8:(s_ + 1) * 8] = (1.0 - mask_prev[:, 0:8]) * NEG
        mns[0:8, 32 + s_ * 8: 40 + s_ * 8] = (1.0 - mask_own[0:8, 0:8]) * NEG
    eye = np.eye(128, dtype=np.float32)
    pm = np.zeros((128, 128), np.float32)
    sgn = np.zeros(128, np.float32)
    fidx = np.full(128, -1, np.int64)
    for m_ in range(128):
        d = m_ % 64
        if d < 8:
            pm[m_ + 8, m_] = 1.0
            sgn[m_] = -1.0
            fidx[m_] = d
        elif d < 16:
            pm[m_ - 8, m_] = 1.0
            sgn[m_] = 1.0
            fidx[m_] = d - 8
    inv = (np.float32(500000.0) ** (-np.arange(8, dtype=np.float32) / np.float32(8))).astype(np.float32)

    in_maps = []
    for core in range(NCORE):
        b, j = core // 4, core % 4
        xe = np.zeros((TE, D), np.float32)
        if j > 0:
            xe[0:158] = x_prompt[b, 1024 * j - 158: 1024 * j]
        xe[158:1182] = x_prompt[b, 1024 * j: 1024 * (j + 1)]
        xe[1182:1214] = x_sample[4 * core: 4 * core + 4].reshape(32, D)
        xTc = np.ascontiguousarray(xe.T.reshape(16, 128, TE).transpose(1, 0, 2))
        sc = state_conv[0, 4 * core: 4 * core + 4]
        ctxTc = np.ascontiguousarray(sc.transpose(2, 0, 1).reshape(16, 128, 4, 30).transpose(1, 0, 2, 3))
        ck = cache_k[4 * core: 4 * core + 4].reshape(4, 128, 512)
        cv = cache_v[4 * core: 4 * core + 4].reshape(4, 128, 512)
        ckTc = np.ascontiguousarray(ck.transpose(2, 0, 1).reshape(4, 128, 4 * 128).transpose(1, 0, 2))
        pos = np.zeros(NTOK, np.float32)
        pos[0:1152] = (1024 * j - 128) + np.arange(1152)
        pos[1152:1184] = 16384 + (np.arange(32) % 8)
        tabc = np.zeros((128, 2 * NTOK), np.float32)
        tabc[:, 0:NTOK] = 1.0
        for m_ in range(128):
            if fidx[m_] >= 0:
                ang = (pos * inv[fidx[m_]]).astype(np.float32)
                tabc[m_, 0:NTOK] = np.cos(ang).astype(np.float32)
                tabc[m_, NTOK:] = sgn[m_] * np.sin(ang).astype(np.float32)
        v = vec.copy()
        v[:, V_FLAG] = 0.0 if j == 0 else 1.0
        in_maps.append({
            "xT": xTc, "wst": wst, "ctxT": ctxTc, "ckT": ckTc,
            "sc_raw": np.ascontiguousarray(sc), "ck_raw": np.ascontiguousarray(ck), "cv_raw": np.ascontiguousarray(cv),
            "vecs": v, "tab": tabc, "mask4": m4, "masks": ms, "perm": pm,
            "mask40": (m4_first if j == 0 else m4), "ident": eye, "maskns": mns,
        })

    return in_maps


def kernel(**inputs):
    in_maps = prepare(**inputs)
    if "nc" not in _NC_CACHE:
        _NC_CACHE["nc"] = build()
    nc = _NC_CACHE["nc"]
    res = run_bass_kernel_spmd(nc, in_maps, core_ids=list(range(NCORE)))
    return assemble(res.results)


def assemble(R_):

    y_prompt = np.zeros((2, 4096, D), np.float32)
    y_sample = np.zeros((32, 8, D), np.float32)
    conv_prompt = np.zeros((1, 2, 30, D), np.float32)
    conv_sample = np.zeros((1, 32, 30, D), np.float32)
    win_k_prompt = np.zeros((2, 128, 8, 64), np.float32)
    win_v_prompt = np.zeros((2, 128, 8, 64), np.float32)
    win_k_sample = np.zeros((32, 128, 8, 64), np.float32)
    win_v_sample = np.zeros((32, 128, 8, 64), np.float32)
    for core in range(NCORE):
        b, j = core // 4, core % 4
        r = R_[core]
        y = np.transpose(r["yT"], (2, 1, 0)).reshape(NQ, D)
        y_prompt[b, 1024 * j: 1024 * (j + 1)] = y[0:1024]
        y_sample[4 * core: 4 * core + 4] = y[1024:1056].reshape(4, 8, D)
        uo = r["uT_out"].reshape(128, 16, 62)
        ut = np.transpose(uo, (2, 1, 0)).reshape(62, D)
        kt = np.transpose(r["kT_out"].reshape(128, 4, 160), (2, 1, 0)).reshape(160, 512)
        if j == 3:
            conv_prompt[0, b] = ut[0:30]
            win_k_prompt[b] = kt[0:128].reshape(128, 8, 64)
            win_v_prompt[b] = r["v_out"].reshape(128, 8, 64)
        for s in range(4):
            q = 4 * core + s
            conv_sample[0, q, 0:22] = r["conv_old"][s]
            conv_sample[0, q, 22:30] = ut[30 + 8 * s: 30 + 8 * s + 8]
            win_k_sample[q, 0:120] = r["k_old"][s].reshape(120, 8, 64)
            win_k_sample[q, 120:128] = kt[128 + 8 * s: 128 + 8 * s + 8].reshape(8, 8, 64)
            win_v_sample[q, 0:120] = r["v_old"][s].reshape(120, 8, 64)
            win_v_sample[q, 120:128] = r["v_new"][s].reshape(8, 8, 64)
    return (y_prompt, y_sample, conv_prompt, conv_sample, win_k_prompt, win_v_prompt, win_k_sample, win_v_sample)
```

```python
import contextlib
import os
import numpy as np
import concourse.bass as bass
import concourse.mybir as mybir
from concourse.bass_utils import run_bass_kernel_spmd

F32 = mybir.dt.float32
BF16 = mybir.dt.bfloat16
AF = mybir.ActivationFunctionType
ALU = mybir.AluOpType

ENGS = ("pe", "act", "dve", "pool", "sp")


def C(name, *a, **k):
    return (name, a, k)


class Op:
    __slots__ = ("eng", "fn", "deps", "signal", "sem", "val", "pos", "dma")

    def __init__(self, eng, fn, dma=None):
        self.eng = eng
        self.fn = fn
        self.deps = []
        self.signal = False
        self.sem = None
        self.val = 0
        self.pos = 0
        self.dma = dma


class _Stop(Exception):
    pass


STOP = float(os.environ.get("MK_STOP", "99"))


def checkpoint(k):
    if STOP <= k:
        raise _Stop()


class Prog:
    SAFE_DIST = 10 ** 9

    def __init__(self):
        self.streams = {e: [] for e in ENGS}
        self.last_w = {}
        self.readers = {}
        self.last_op = {}
        self.pending_fence = {e: [] for e in ENGS}
        self.dma_pending = []

    def add(self, eng, fn, reads=(), writes=(), dma=None):
        op = Op(eng, fn, dma=dma)
        if dma is not None:
            op.signal = True
        deps = set()
        for r in reads:
            w = self.last_w.get(r)
            if w is not None:
                deps.add(w)
        for wkey in writes:
            w = self.last_w.get(wkey)
            if w is not None:
                deps.add(w)
            for rd in self.readers.get(wkey, ()):
                deps.add(rd)
        for f in self.pending_fence[eng]:
            deps.add(f)
        self.pending_fence[eng] = []
        op.pos = len(self.streams[eng])
        keep = []
        latest = {}
        for d in deps:
            if d.dma is None and d.eng == eng:
                if eng == "pe" or eng == "sp":
                    continue
                if op.pos - d.pos >= self.SAFE_DIST:
                    continue
            if d.dma is None:
                cur = latest.get(d.eng)
                if cur is None or d.pos > cur.pos:
                    latest[d.eng] = d
            else:
                keep.append(d)
        keep.extend(latest.values())
        op.deps = keep
        for d in keep:
            d.signal = True
        self.streams[eng].append(op)
        if dma is None:
            self.last_op[eng] = op
        elif not dma.startswith("slot"):
            self.dma_pending.append(op)
        for r in reads:
            self.readers.setdefault(r, []).append(op)
        for wkey in writes:
            self.last_w[wkey] = op
            self.readers[wkey] = []
        return op

    def fence(self):
        lasts = [o for o in self.last_op.values()]
        dmas = self.dma_pending
        self.dma_pending = []
        for e in ENGS:
            self.pending_fence[e] = [o for o in lasts if o.eng != e] + dmas

    def finalize(self, final_ops):
        for o in final_ops:
            o.signal = True
        counters = {}
        for e in ENGS:
            for op in self.streams[e]:
                if not op.signal:
                    continue
                if op.dma is not None:
                    key = ("dma", op.dma)
                    counters[key] = counters.get(key, 0) + 16
                else:
                    key = ("eng", e)
                    counters[key] = counters.get(key, 0) + 1
                op.sem = key
                op.val = counters[key]
        self.sem_keys = list(counters.keys())
        self.final_ops = final_ops

    def emit(self, nc):
        with contextlib.ExitStack() as st:
            sems = {}
            for i, k in enumerate(self.sem_keys):
                sems[k] = st.enter_context(nc.semaphore("s%d" % i))
            block = st.enter_context(nc.Block())
            prog = self

            def run(ename, e, is_last=False):
                waited = {}
                for op in prog.streams[ename]:
                    for d in op.deps:
                        if waited.get(d.sem, 0) < d.val:
                            e.wait_ge(sems[d.sem], d.val)
                            waited[d.sem] = d.val
                    name_, a_, k_ = op.fn
                    ins = getattr(e, name_)(*a_, **k_)
                    if op.signal:
                        ins.then_inc(sems[op.sem], 16 if op.dma is not None else 1)
                if is_last:
                    for d in prog.final_ops:
                        if waited.get(d.sem, 0) < d.val:
                            e.wait_ge(sems[d.sem], d.val)
                            waited[d.sem] = d.val

            @block.sync
            def _(e):
                run("sp", e, is_last=True)

            @block.scalar
            def _(e):
                run("act", e)

            @block.vector
            def _(e):
                run("dve", e)

            @block.gpsimd
            def _(e):
                run("pool", e)

            @block.tensor
            def _(e):
                run("pe", e)


NCORE = 8
D = 2048
KC = 16
TE = 1214
T_PRE, T_HALO, T_MAIN, T_SMP = 0, 30, 158, 1182
NTOK = TE - 30
NQ = TE - 158
SEGS = [0, 30, 158, 592, 622, 670, 1214]
NS = 4
CW = 31
NDV = 8
EPS = 1e-6
UW = 742

V_NMIX0, V_NMLP0, V_KVN, V_NMIX1, V_NMLP1, V_FIN = 0, 16, 32, 48, 64, 80
V_BA, V_BG, V_BDW, V_LNG, V_LNB, V_BPW2, V_SINK = 96, 112, 128, 144, 160, 176, 192
V_FLAG = 208
V_WDW = 212
NV = V_WDW + 16 * CW


def seg_keys(name, kc, a, b):
    return [(name, kc, i) for i in range(len(SEGS) - 1) if SEGS[i] < b and SEGS[i + 1] > a]


def tiles(a, b):
    out = []
    t = a
    while t < b:
        n = min(512, b - t)
        out.append((t, n))
        t += n
    return out


def weight_order():
    order = []
    for _pass in range(2):
        for oc in range(16):
            order.append(("pw1a", oc))
            order.append(("pw1g", oc))
        for oc in range(16):
            order.append(("pw2", oc))
    for g in range(8):
        for fc in range(8):
            order.append(("up", 0, g * 8 + fc))
        for ocp in range(8):
            order.append(("down", 0, g, ocp))
    for c in range(4):
        order.append(("wk", c))
    for _grp in range(2):
        for c in range(4):
            order.append(("wv", c))
    for _pass in range(2):
        for oc in range(16):
            order.append(("wq", oc))
        for oc in range(16):
            order.append(("wo", oc))
    for g in range(8):
        for fc in range(8):
            order.append(("up", 1, g * 8 + fc))
        for ocp in range(8):
            order.append(("down", 1, g, ocp))
    return order


def unique_tiles():
    seen = {}
    for nm in weight_order():
        if nm not in seen:
            seen[nm] = len(seen)
    return seen


def build():
    nc = bass.Bass("TRN2", target_bir_lowering=False)
    tidx = unique_tiles()
    NT = len(tidx)
    order = weight_order()

    def din(name, shape):
        return nc.dram_tensor(name, shape, F32, kind="ExternalInput").ap()

    def dout(name, shape):
        return nc.dram_tensor(name, shape, F32, kind="ExternalOutput").ap()

    xT = din("xT", [128, KC, TE])
    wst = din("wst", [NT, 128, 2048])
    ctxT = din("ctxT", [128, KC, 4, 30])
    ckT = din("ckT", [128, 4, 512])
    sc_raw = din("sc_raw", [4, 30, D])
    ck_raw = din("ck_raw", [4, 128, 512])
    cv_raw = din("cv_raw", [4, 128, 512])
    vecs = din("vecs", [128, NV])
    tab = din("tab", [128, 2 * NTOK])
    mask4 = din("mask4", [128, 512])
    masks = din("masks", [128, 32])
    mask40 = din("mask40", [128, 512])
    perm = din("perm", [128, 128])
    ident = din("ident", [128, 128])
    maskns = din("maskns", [128, 64])

    yT = dout("yT", [128, KC, NQ])
    uT_out = dout("uT_out", [128, KC * 62])
    conv_old = dout("conv_old", [4, 22, D])
    kT_out = dout("kT_out", [128, 4 * 160])
    v_out = dout("v_out", [128, 512])
    v_new = dout("v_new", [4, 8, 512])
    k_old = dout("k_old", [4, 120, 512])
    v_old = dout("v_old", [4, 120, 512])

    P = Prog()
    outs = []
    with contextlib.ExitStack() as st:
        def sb(name, cols, dt):
            return st.enter_context(nc.sbuf_tensor(name, [128, cols], dt))

        X = sb("X", KC * TE, F32)
        H = sb("H", KC * TE, BF16)
        RING = sb("RING", NS * 2048, BF16)
        VEC = sb("VEC", NV, F32)
        TAB = sb("TAB", 2 * NTOK, F32)
        TR0 = sb("TR0", TE, F32)
        TR1 = sb("TR1", TE, F32)
        SQ = sb("SQ", 4 * 512, BF16)
        ONESB = sb("ONESB", 128, BF16)
        ONE1 = sb("ONE1", 64, BF16)
        PERMF = sb("PERMF", 128, F32)
        PERMB = sb("PERMB", 128, BF16)
        MASK4 = sb("MASK4", 512, BF16)
        MASKS = sb("MASKS", 32, BF16)
        MASK40 = sb("MASK40", 512, BF16)
        ES = sb("ES", 16, F32)
        EPSC = sb("EPSC", 1, F32)
        IDENT = sb("IDENT", 128, BF16)
        MASKNS = sb("MASKNS", 64, BF16)
        RBYTES = 44288
        R = sb("R", RBYTES // 4, F32)
        banks = [st.enter_context(nc.psum_tensor("ps%d" % i, [128, 512], F32)) for i in range(8)]

        def carve(off, ncols, dt):
            sz = 4 if dt == F32 else 2
            assert off % 4 == 0 and (ncols * sz) % 4 == 0
            assert off + ncols * sz <= RBYTES, (off, ncols, sz)
            v = R[:, off // 4: off // 4 + (ncols * sz) // 4]
            return v if dt == F32 else v.bitcast(BF16)

        o = 0
        SIG = [carve(o + i * 2488, 622, F32) for i in range(2)]; o += 2 * 2488
        U = [carve(o + i * UW * 4, UW, F32) for i in range(2)]; o += 2 * UW * 4
        ACCD = [carve(o + i * 2368, 592, F32) for i in range(2)]; o += 2 * 2368
        UB = [carve(o + i * 1488, 744, BF16) for i in range(2)]; o += 2 * 1488
        NDG = 32
        DG = carve(o, NDG * 128, BF16); o += NDG * 256
        UC = carve(o, 16 * 30, F32); o += 16 * 30 * 4
        LNT = [carve(o + i * 2368, 592, F32) for i in range(2)]; o += 2 * 2368
        USAVE = carve(o, 16 * 62, F32); o += 16 * 62 * 4
        assert o <= RBYTES, o
        o = 0
        HID = carve(o, 8 * NTOK, BF16); o += 8 * NTOK * 2
        RT = [carve(o + i * 2048, 512, F32) for i in range(3)]; o += 3 * 2048
        YST = [carve(o + i * 2048, 512, F32) for i in range(2)]; o += 2 * 2048
        assert o <= RBYTES, o
        KTW = NTOK + 512
        o = 0
        KT = carve(o, 4 * KTW, BF16); o += 4 * KTW * 2
        VB = carve(o, 13 * 512, BF16); o += 13 * 512 * 2
        VN = carve(o, 4 * 512, BF16); o += 4 * 512 * 2
        o_alias = o
        QB = [carve(o + i * 1088, 544, BF16) for i in range(2)]; o += 2 * 1088
        QT = [carve(o + i * 1088, 544, BF16) for i in range(2)]; o += 2 * 1088
        KOUT = carve(o_alias, 4 * 160, F32)
        VST = [carve(o_alias + 2560 + i * 1024, 256, F32) for i in range(2)]
        assert o_alias + 2560 + 2048 <= o + 256
        o = max(o, o_alias + 2560 + 2048)
        NEV = 3
        EB = [carve(o + i * 1024, 512, BF16) for i in range(NEV)]; o += NEV * 1024
        PT = [carve(o + i * 1024, 512, BF16) for i in range(NEV)]; o += NEV * 1024
        DEN = [carve(o + i * 512, 128, F32) for i in range(NEV)]; o += NEV * 512
        assert o <= RBYTES, o

        def Xap(kc, a, b):
            return X[:, kc * TE + a: kc * TE + b]

        def Hap(kc, a, b):
            return H[:, kc * TE + a: kc * TE + b]

        def vcol(c):
            return VEC[:, c:c + 1]

        bank_ctr = [0]
        dgc = [0]

        def bank():
            b = bank_ctr[0] % 8
            bank_ctr[0] += 1
            return b

        wstate = {"issued": 0, "consumed": 0}

        def issue_next():
            n = wstate["issued"]
            if n >= len(order):
                return
            s = n % NS
            ti = tidx[order[n]]
            P.add("pool", C("dma_start", out=RING[:, s * 2048:(s + 1) * 2048], in_=wst[ti]),
                  writes=[("slot", s)], dma="slot%d" % s)
            wstate["issued"] = n + 1

        def get_tile(name):
            n = wstate["consumed"]
            assert order[n] == name, (order[n], name)
            s = n % NS
            wstate["consumed"] = n + 1
            return s

        def slot_ap(s, a, b):
            return RING[:, s * 2048 + a: s * 2048 + b]

        P.add("sp", C("dma_start", out=VEC[:], in_=vecs[:]), writes=["VEC"], dma="vec")
        for (ta, tb_) in ((0, 622), (622, TE)):
            for k0 in range(0, KC, 8):
                wk = []
                for kc in range(k0, k0 + 8):
                    wk += seg_keys("X", kc, ta, tb_)
                P.add("sp", C("dma_start",
                    out=X[:, k0 * TE:(k0 + 8) * TE].rearrange("p (k t) -> p k t", t=TE)[:, :, ta:tb_], in_=xT[:, k0:k0 + 8, ta:tb_]),
                    writes=wk, dma="xl%d_%d" % (k0, ta))
        P.add("sp", C("dma_start", out=TAB[:], in_=tab[:]), writes=["TAB"], dma="tab")
        P.add("sp", C("dma_start", out=PERMF[:], in_=perm[:]), writes=["PERMF"], dma="permf")
        P.add("pool", C("dma_start", out=PERMB[:], in_=perm[:]), writes=["PERMB"], dma="permb")
        P.add("pool", C("dma_start", out=MASK4[:], in_=mask4[:]), writes=["MASK4"], dma="mask4")
        P.add("pool", C("dma_start", out=IDENT[:], in_=ident[:]), writes=["IDENT"], dma="ident")
        P.add("pool", C("dma_start", out=MASKNS[:], in_=maskns[:]), writes=["MASKNS"], dma="maskns")
        P.add("pool", C("dma_start", out=MASKS[:], in_=masks[:]), writes=["MASKS"], dma="masks")
        P.add("pool", C("dma_start", out=MASK40[:], in_=mask40[:]), writes=["MASK4"], dma="mask40")
        for _ in range(NS):
            issue_next()
        P.add("dve", C("memset", ONESB[:], 1.0 / D), writes=["ONESB"])
        P.add("dve", C("memset", ONE1[:], 1.0), writes=["ONE1"])
        P.add("dve", C("memset", EPSC[:], EPS), writes=["EPSC"])
        P.add("act", C("activation", out=ES[:], in_=VEC[:, V_SINK:V_SINK + 16], func=AF.Exp),
              reads=["VEC"], writes=["ES"])
        outs.append(P.add("sp", C("dma_start", out=conv_old[:], in_=sc_raw[:, 8:30, :]), dma="d2d0"))
        outs.append(P.add("sp", C("dma_start", out=k_old[:], in_=ck_raw[:, 8:128, :]), dma="d2d1"))
        outs.append(P.add("sp", C("dma_start", out=v_old[:], in_=cv_raw[:, 8:128, :]), dma="d2d2"))

        def rstd_from_bank(b, n, rs_ap, rs_keys):
            P.add("act", C("activation", out=rs_ap, in_=banks[b][:, :n], func=AF.Ln, bias=EPSC[:, 0:1]),
                  reads=[("ps", b), "EPSC"], writes=rs_keys)
            P.add("act", C("activation", out=rs_ap, in_=rs_ap, func=AF.Exp, scale=-0.5), reads=rs_keys, writes=rs_keys)

        def rms_stats(a, b):
            for (t0, n) in tiles(a, b):
                bk = bank()
                for kc in range(KC):
                    sq = kc % 4
                    if kc % 2 == 0:
                        P.add("act", C("activation",
                            out=SQ[:, sq * 512: sq * 512 + n], in_=Xap(kc, t0, t0 + n), func=AF.Square),
                            reads=seg_keys("X", kc, t0, t0 + n), writes=[("SQ", sq)])
                    else:
                        P.add("dve", C("tensor_tensor",
                            out=SQ[:, sq * 512: sq * 512 + n], in0=Xap(kc, t0, t0 + n), in1=Xap(kc, t0, t0 + n), op=ALU.mult),
                            reads=seg_keys("X", kc, t0, t0 + n), writes=[("SQ", sq)])
                    P.add("pe", C("matmul",
                        banks[bk][:, :n], lhsT=ONESB[:], rhs=SQ[:, sq * 512: sq * 512 + n],
                        start=(kc == 0), stop=(kc == KC - 1)),
                        reads=[("SQ", sq), "ONESB"], writes=[("ps", bk)])
                rstd_from_bank(bk, n, TR0[:, t0:t0 + n], seg_keys("RS", 0, t0, t0 + n))

        def rms_to_h(a, b, gcol):
            rms_stats(a, b)
            for kc in range(KC):
                for (t0, n) in tiles(a, b):
                    P.add("dve", C("scalar_tensor_tensor",
                        out=Hap(kc, t0, t0 + n), in0=Xap(kc, t0, t0 + n), scalar=vcol(gcol + kc),
                        in1=TR0[:, t0:t0 + n], op0=ALU.mult, op1=ALU.mult),
                        reads=seg_keys("X", kc, t0, t0 + n) + seg_keys("RS", 0, t0, t0 + n) + ["VEC"],
                        writes=seg_keys("H", kc, t0, t0 + n))

        def proj_group(slot, lhs_off, lhs_stride, nk, rhs_fn, rhs_keys_fn, tl):
            bks = [bank() for _ in tl]
            for k in range(nk):
                for i, (t0, n) in enumerate(tl):
                    P.add("pe", C("matmul",
                        banks[bks[i]][:, :n], lhsT=slot_ap(slot, lhs_off + k * lhs_stride, lhs_off + k * lhs_stride + 128),
                        rhs=rhs_fn(k, t0, n), start=(k == 0), stop=(k == nk - 1)),
                        reads=[("slot", slot)] + rhs_keys_fn(k, t0, n), writes=[("ps", bks[i])])
            return bks

        def h_rhs(k, t0, n):
            return Hap(k, t0, t0 + n)

        def h_keys(k, t0, n):
            return seg_keys("H", k, t0, t0 + n)

        try:
            checkpoint(0)
            def conv_pass(pa):
                if pa == 0:
                    ga, gb, oa, ob = 0, 622, 30, 622
                    cbase = 622
                else:
                    ga, gb, oa, ob = 622, 1214, 622, 1214
                    cbase = 0
                rms_to_h(ga, gb, V_NMIX0)
                checkpoint(1)
                tl = tiles(ga, gb)

                def cap(oc, a, b):
                    return Hap(oc, cbase + a, cbase + b)

                def ckeys(oc, a, b):
                    return seg_keys("H", oc, cbase + a, cbase + b)

                UWID = 622 if pa == 0 else UW

                def stage1(oc):
                    u = oc % 2
                    sA = get_tile(("pw1a", oc))
                    bA = proj_group(sA, 0, 128, KC, h_rhs, h_keys, tl)
                    sG = get_tile(("pw1g", oc))
                    bG = proj_group(sG, 0, 128, KC, h_rhs, h_keys, tl)
                    issue_next()
                    issue_next()
                    if pa == 1:
                        P.add("act", C("activation", out=U[u][:, 0:30], in_=UC[:, oc * 30:(oc + 1) * 30], func=AF.Copy),
                              reads=[("UC", oc)], writes=[("UX", u)])
                        P.add("sp", C("dma_start",
                            out=U[u][:, 590:742].rearrange("p (s c) -> p s c", c=38)[:, :, 0:30], in_=ctxT[:, oc, :, :]),
                            writes=[("UX", u)], dma="ctx%d" % u)
                    for i, (t0, n) in enumerate(tl):
                        g0 = t0 - ga
                        P.add("act", C("activation",
                            out=SIG[u][:, g0:g0 + n], in_=banks[bG[i]][:, :n], func=AF.Sigmoid, bias=vcol(V_BG + oc)),
                            reads=[("ps", bG[i]), "VEC"], writes=[("SIG", u)])
                        npr = min(t0 + n, T_SMP) - t0
                        if npr > 0:
                            uc0 = t0 if pa == 0 else 30 + (t0 - 622)
                            P.add("dve", C("scalar_tensor_tensor",
                                out=U[u][:, uc0:uc0 + npr], in0=banks[bA[i]][:, :npr], scalar=vcol(V_BA + oc),
                                in1=SIG[u][:, g0:g0 + npr], op0=ALU.add, op1=ALU.mult),
                                reads=[("ps", bA[i]), ("SIG", u), "VEC"], writes=[("U", u)])
                        if npr < n:
                            assert n - npr == 32
                            P.add("dve", C("scalar_tensor_tensor",
                                out=U[u][:, 590:742].rearrange("p (s c) -> p s c", c=38)[:, :, 30:38],
                                in0=banks[bA[i]][:, npr:npr + 32].rearrange("p (s c) -> p s c", c=8), scalar=vcol(V_BA + oc),
                                in1=SIG[u][:, g0 + npr:g0 + npr + 32].rearrange("p (s c) -> p s c", c=8),
                                op0=ALU.add, op1=ALU.mult),
                                reads=[("ps", bA[i]), ("SIG", u), "VEC"], writes=[("U", u)])
                    if pa == 0:
                        P.add("dve", C("tensor_single_scalar", out=U[u][:, 0:158], in_=U[u][:, 0:158],
                                       scalar=vcol(V_FLAG), op=ALU.mult),
                              reads=[("U", u), "VEC"], writes=[("U", u)])
                        P.add("act", C("activation", out=UC[:, oc * 30:(oc + 1) * 30], in_=U[u][:, 592:622], func=AF.Copy),
                              reads=[("U", u)], writes=[("UC", oc)])
                    else:
                        P.add("act", C("activation", out=USAVE[:, oc * 62: oc * 62 + 30], in_=U[u][:, 560:590], func=AF.Copy),
                              reads=[("U", u)], writes=["USAVE"])
                        P.add("act", C("activation",
                            out=USAVE[:, oc * 62 + 30: oc * 62 + 62].rearrange("p (s c) -> p s c", c=8),
                            in_=U[u][:, 590:742].rearrange("p (s c) -> p s c", c=38)[:, :, 30:38], func=AF.Copy),
                            reads=[("U", u)], writes=["USAVE"])
                    if NDV < CW:
                        P.add("act", C("activation", out=UB[u][:, 0:UWID], in_=U[u][:, 0:UWID], func=AF.Copy),
                              reads=[("U", u), ("UX", u)], writes=[("UB", u)])

                def stage2a(oc):
                    dgs = []
                    for j in range(NDV, CW):
                        dg = dgc[0] % NDG
                        dgc[0] += 1
                        P.add("act", C("activation", out=DG[:, dg * 128:(dg + 1) * 128], in_=IDENT[:, :], func=AF.Copy,
                                       scale=vcol(V_WDW + oc * CW + j)),
                              reads=["IDENT", "VEC"], writes=[("DG", dg)])
                        dgs.append(dg)
                    return dgs

                def stage2(oc, dgs):
                    u = oc % 2
                    if pa == 0:
                        parts = [(lambda acc: acc[:, 0:512], lambda X_, j: X_[:, j:j + 512], lambda: cap(oc, 0, 512), lambda bk: banks[bk][:, 0:512]),
                                 (lambda acc: acc[:, 512:592], lambda X_, j: X_[:, 512 + j:592 + j], lambda: cap(oc, 512, 592), lambda bk: banks[bk][:, 0:80])]
                    else:
                        parts = [(lambda acc: acc[:, 0:512], lambda X_, j: X_[:, j:j + 512], lambda: cap(oc, 0, 512), lambda bk: banks[bk][:, 0:512]),
                                 (lambda acc: acc[:, 512:560], lambda X_, j: X_[:, 512 + j:560 + j], lambda: cap(oc, 512, 560), lambda bk: banks[bk][:, 0:48]),
                                 (lambda acc: acc[:, 560:592].rearrange("p (s c) -> p s c", c=8),
                                  lambda X_, j: X_[:, 590:742].rearrange("p (s c) -> p s c", c=38)[:, :, j:j + 8],
                                  lambda: cap(oc, 560, 592).rearrange("p (s c) -> p s c", c=8),
                                  lambda bk: banks[bk][:, 0:32].rearrange("p (s c) -> p s c", c=8))]
                    akey = ("ACC", "dve", u)
                    cb = []
                    if NDV < CW:
                        cb = [bank() for _ in parts]
                        for j in range(NDV, CW):
                            dg = dgs[j - NDV]
                            for pi, (accv, xv, cv, pv) in enumerate(parts):
                                P.add("pe", C("matmul", pv(cb[pi]), lhsT=DG[:, dg * 128:(dg + 1) * 128], rhs=xv(UB[u], j),
                                              start=(j == NDV), stop=(j == CW - 1)),
                                      reads=[("DG", dg), ("UB", u)], writes=[("ps", cb[pi])])
                    for j in range(NDV):
                        wcol = vcol(V_WDW + oc * CW + j)
                        for pi, (accv, xv, cv, pv) in enumerate(parts):
                            if j == 0:
                                P.add("dve", C("tensor_scalar", out=accv(ACCD[u]), in0=xv(U[u], j), scalar1=wcol,
                                               scalar2=vcol(V_BDW + oc), op0=ALU.mult, op1=ALU.add),
                                      reads=[("U", u), ("UX", u), "VEC"], writes=[akey])
                            else:
                                P.add("dve", C("scalar_tensor_tensor", out=accv(ACCD[u]), in0=xv(U[u], j), scalar=wcol,
                                               in1=accv(ACCD[u]), op0=ALU.mult, op1=ALU.add),
                                      reads=[("U", u), ("UX", u), "VEC", akey], writes=[akey])
                    for pi, (accv, xv, cv, pv) in enumerate(parts):
                        if NDV == 0:
                            P.add("dve", C("tensor_single_scalar", out=cv(), in_=pv(cb[pi]), scalar=vcol(V_BDW + oc), op=ALU.add),
                                  reads=[("ps", cb[pi]), "VEC"], writes=ckeys(oc, 0, 592))
                        elif NDV == CW:
                            P.add("dve", C("tensor_copy", out=cv(), in_=accv(ACCD[u])), reads=[akey], writes=ckeys(oc, 0, 592))
                        else:
                            P.add("dve", C("tensor_tensor", out=cv(), in0=pv(cb[pi]), in1=accv(ACCD[u]), op=ALU.add),
                                  reads=[("ps", cb[pi]), akey], writes=ckeys(oc, 0, 592))

                stage1(0)
                for oc in range(16):
                    dgs = stage2a(oc)
                    if oc + 1 < 16:
                        stage1(oc + 1)
                    stage2(oc, dgs)
                ltl = tiles(0, 592)
                bmu = [bank() for _ in ltl]
                bvar = [bank() for _ in ltl]
                for oc in range(16):
                    for i, (c0, n) in enumerate(ltl):
                        sq = (2 * oc + i) % 4
                        P.add("act", C("activation",
                            out=SQ[:, sq * 512: sq * 512 + n], in_=cap(oc, c0, c0 + n), func=AF.Square),
                            reads=ckeys(oc, c0, c0 + n), writes=[("SQ", sq)])
                        P.add("pe", C("matmul",
                            banks[bmu[i]][:, :n], lhsT=ONESB[:], rhs=cap(oc, c0, c0 + n), start=(oc == 0), stop=(oc == 15)),
                            reads=ckeys(oc, c0, c0 + n) + ["ONESB"], writes=[("ps", bmu[i])])
                        P.add("pe", C("matmul",
                            banks[bvar[i]][:, :n], lhsT=ONESB[:], rhs=SQ[:, sq * 512: sq * 512 + n], start=(oc == 0), stop=(oc == 15)),
                            reads=[("SQ", sq), "ONESB"], writes=[("ps", bvar[i])])
                for i, (c0, n) in enumerate(ltl):
                    P.add("act", C("activation", out=TR1[:, c0:c0 + n], in_=banks[bmu[i]][:, :n], func=AF.Copy),
                          reads=[("ps", bmu[i])], writes=[("MU", i)])
                    P.add("dve", C("tensor_tensor", out=LNT[0][:, c0:c0 + n], in0=TR1[:, c0:c0 + n], in1=TR1[:, c0:c0 + n], op=ALU.mult),
                          reads=[("MU", i)], writes=[("LNT", 0)])
                    P.add("dve", C("tensor_tensor", out=LNT[0][:, c0:c0 + n], in0=banks[bvar[i]][:, :n], in1=LNT[0][:, c0:c0 + n], op=ALU.subtract),
                          reads=[("ps", bvar[i]), ("LNT", 0)], writes=[("LNT", 0)])
                    P.add("act", C("activation", out=TR0[:, c0:c0 + n], in_=LNT[0][:, c0:c0 + n], func=AF.Ln, bias=EPSC[:, 0:1]),
                          reads=[("LNT", 0), "EPSC"], writes=seg_keys("RS", 0, c0, c0 + n))
                    P.add("act", C("activation", out=TR0[:, c0:c0 + n], in_=TR0[:, c0:c0 + n], func=AF.Exp, scale=-0.5),
                          reads=seg_keys("RS", 0, c0, c0 + n), writes=seg_keys("RS", 0, c0, c0 + n))
                rs_keys = seg_keys("RS", 0, 0, 592)
                for oc in range(16):
                    a = oc % 2
                    P.add("dve", C("tensor_tensor", out=LNT[a][:, 0:592], in0=cap(oc, 0, 592), in1=TR1[:, 0:592], op=ALU.subtract),
                          reads=ckeys(oc, 0, 592) + [("MU", 0), ("MU", 1)], writes=[("LNT", a)])
                    P.add("dve", C("tensor_tensor", out=LNT[a][:, 0:592], in0=LNT[a][:, 0:592], in1=TR0[:, 0:592], op=ALU.mult),
                          reads=[("LNT", a)] + rs_keys, writes=[("LNT", a)])
                    P.add("act", C("activation", out=Hap(oc, oa, ob), in_=LNT[a][:, 0:592], func=AF.Silu,
                                                                   bias=vcol(V_LNB + oc), scale=vcol(V_LNG + oc)),
                          reads=[("LNT", a), "VEC"], writes=seg_keys("H", oc, oa, ob))
                otl = tiles(oa, ob)
                for oc in range(16):
                    s2 = get_tile(("pw2", oc))
                    bks = proj_group(s2, 0, 128, KC, h_rhs, h_keys, otl)
                    issue_next()
                    for i, (t0, n) in enumerate(otl):
                        P.add("dve", C("scalar_tensor_tensor",
                            out=Xap(oc, t0, t0 + n), in0=banks[bks[i]][:, :n], scalar=vcol(V_BPW2 + oc),
                            in1=Xap(oc, t0, t0 + n), op0=ALU.add, op1=ALU.add),
                            reads=[("ps", bks[i]), "VEC"] + seg_keys("X", oc, t0, t0 + n), writes=seg_keys("X", oc, t0, t0 + n))

            conv_pass(0)
            checkpoint(2)
            conv_pass(1)
            checkpoint(3)
            outs.append(P.add("sp", C("dma_start", out=uT_out[:], in_=USAVE[:]), reads=["USAVE"], dma="usave"))
            P.fence()

            def mlp(l, a, b, gcol):
                rms_to_h(a, b, gcol)
                tl = tiles(a, b)
                for g in range(8):
                    for fc in range(8):
                        s = get_tile(("up", l, g * 8 + fc))
                        bks = proj_group(s, 0, 128, KC, h_rhs, h_keys, tl)
                        issue_next()
                        for i, (t0, n) in enumerate(tl):
                            P.add("act", C("activation", out=RT[i][:, :n], in_=banks[bks[i]][:, :n], func=AF.Relu),
                                  reads=[("ps", bks[i])], writes=[("RT", i)])
                        for i, (t0, n) in enumerate(tl):
                            P.add("act", C("activation",
                                out=HID[:, fc * NTOK + (t0 - a): fc * NTOK + (t0 - a) + n], in_=RT[i][:, :n], func=AF.Square),
                                reads=[("RT", i)], writes=[("HID", fc, i)])
                    for ocp in range(8):
                        s = get_tile(("down", l, g, ocp))
                        for o2 in range(2):
                            oc = 2 * ocp + o2
                            bks = proj_group(s, o2 * 128, 256, 8,
                                             lambda k, t0, n: HID[:, k * NTOK + (t0 - a): k * NTOK + (t0 - a) + n],
                                             lambda k, t0, n: [("HID", k, i) for i in range(3)], tl)
                            for i, (t0, n) in enumerate(tl):
                                P.add("dve", C("tensor_tensor",
                                    out=Xap(oc, t0, t0 + n), in0=banks[bks[i]][:, :n], in1=Xap(oc, t0, t0 + n), op=ALU.add),
                                    reads=[("ps", bks[i])] + seg_keys("X", oc, t0, t0 + n), writes=seg_keys("X", oc, t0, t0 + n))
                        issue_next()

            mlp(0, 30, TE, V_NMLP0)
            P.fence()
            checkpoint(4)

            P.add("pool", C("dma_start",
                out=KT[:].rearrange("p (c w) -> p c w", w=KTW)[:, :, NTOK:NTOK + 512],
                in_=ckT[:]), writes=["KTC"], dma="ktc")
            P.add("pool", C("dma_start",
                out=VB[:, 9 * 512:13 * 512].rearrange("p (s f) -> p s f", f=512),
                in_=cv_raw[:].rearrange("s k f -> k s f")), writes=["VBC"], dma="vbc")
            rms_to_h(30, TE, V_KVN)
            ktl = tiles(30, TE)
            CT = lambda t0, n: TAB[:, t0 - 30: t0 - 30 + n]
            ST = lambda t0, n: TAB[:, NTOK + t0 - 30: NTOK + t0 - 30 + n]
            for c in range(4):
                s = get_tile(("wk", c))
                bks = proj_group(s, 0, 128, KC, h_rhs, h_keys, ktl)
                issue_next()
                for i, (t0, n) in enumerate(ktl):
                    kf = TR1[:, t0 - 30: t0 - 30 + n]
                    kk = [("KF", i)]
                    P.add("act", C("activation", out=kf, in_=banks[bks[i]][:, :n], func=AF.Copy),
                          reads=[("ps", bks[i])], writes=kk)
                    b2 = bank()
                    P.add("pe", C("matmul", banks[b2][:, :n], lhsT=PERMF[:], rhs=kf, start=True, stop=True),
                          reads=kk + ["PERMF"], writes=[("ps", b2)])
                    tv = TR0[:, t0 - 30: t0 - 30 + n]
                    tk = seg_keys("RS", 0, t0 - 30, t0 - 30 + n)
                    P.add("dve", C("tensor_tensor", out=tv, in0=banks[b2][:, :n], in1=ST(t0, n), op=ALU.mult),
                          reads=[("ps", b2), "TAB"], writes=tk)
                    P.add("dve", C("tensor_tensor", out=kf, in0=kf, in1=CT(t0, n), op=ALU.mult),
                          reads=kk + ["TAB"], writes=kk)
                    P.add("dve", C("tensor_tensor", out=kf, in0=kf, in1=tv, op=ALU.add),
                          reads=kk + tk, writes=kk)
                    P.add("act", C("activation",
                        out=KT[:, c * KTW + t0 - 30: c * KTW + t0 - 30 + n], in_=kf, func=AF.Copy),
                        reads=kk, writes=[("KT", c, i)])
                    if i == 2:
                        assert t0 == 1054 and n == 160
                        P.add("act", C("activation", out=KOUT[:, c * 160:(c + 1) * 160], in_=kf, func=AF.Copy),
                              reads=kk, writes=["KOUT"])
            outs.append(P.add("sp", C("dma_start", out=kT_out[:], in_=KOUT[:]), reads=["KOUT"], dma="kout"))
            checkpoint(5)
            vgroups = [[("blk", bi) for bi in range(6)], [("blk", bi) for bi in range(6, 9)] + [("smp", s_) for s_ in range(4)]]
            vst_ctr = [0]
            for grp in vgroups:
                gb_ = [bank() for _ in grp]
                for c in range(4):
                    s = get_tile(("wv", c))
                    for gi, (kind, idx) in enumerate(grp):
                        for kk_ in range(4):
                            kc = c * 4 + kk_
                            if kind == "blk":
                                t0 = 30 + 128 * idx
                                P.add("pe", C("matmul",
                                    banks[gb_[gi]][:, :], lhsT=Hap(kc, t0, t0 + 128), rhs=slot_ap(s, kk_ * 512, kk_ * 512 + 512),
                                    start=(kc == 0), stop=(kc == KC - 1)),
                                    reads=[("slot", s)] + seg_keys("H", kc, t0, t0 + 128), writes=[("ps", gb_[gi])])
                            else:
                                t0 = T_SMP + 8 * idx
                                P.add("pe", C("matmul",
                                    banks[gb_[gi]][0:8, :], lhsT=Hap(kc, t0, t0 + 8), rhs=slot_ap(s, kk_ * 512, kk_ * 512 + 512),
                                    start=(kc == 0), stop=(kc == KC - 1)),
                                    reads=[("slot", s)] + seg_keys("H", kc, t0, t0 + 8), writes=[("ps", gb_[gi])])
                    issue_next()
                for gi, (kind, idx) in enumerate(grp):
                    bkk = gb_[gi]
                    if kind == "blk":
                        P.add("act", C("activation", out=VB[:, idx * 512:(idx + 1) * 512], in_=banks[bkk][:, :], func=AF.Copy),
                              reads=[("ps", bkk)], writes=[("VB", idx)])
                        if idx == 8:
                            for hh in range(2):
                                P.add("act", C("activation", out=VST[hh][:, :], in_=banks[bkk][:, hh * 256:(hh + 1) * 256], func=AF.Copy),
                                      reads=[("ps", bkk)], writes=[("VST", hh)])
                                outs.append(P.add("sp", C("dma_start", out=v_out[:, hh * 256:(hh + 1) * 256], in_=VST[hh][:, :]),
                                                  reads=[("VST", hh)], dma="vst%d" % hh))
                    else:
                        P.add("act", C("activation", out=VN[0:8, idx * 512:(idx + 1) * 512], in_=banks[bkk][0:8, :], func=AF.Copy),
                              reads=[("ps", bkk)], writes=[("VN", idx)])
                        for hh in range(2):
                            P.add("act", C("activation", out=VST[hh][0:8, :], in_=banks[bkk][0:8, hh * 256:(hh + 1) * 256], func=AF.Copy),
                                  reads=[("ps", bkk)], writes=[("VST", hh)])
                            outs.append(P.add("sp", C("dma_start", out=v_new[idx, :, hh * 256:(hh + 1) * 256], in_=VST[hh][0:8, :]),
                                              reads=[("VST", hh)], dma="vst%d" % hh))

            def attn_pass(pa):
                if pa == 0:
                    qa, qb, obase = 158, 670, 670
                    blocks = [0, 1, 2, 3]
                    smp = []
                else:
                    qa, qb, obase = 670, 1214, 30
                    blocks = [4, 5, 6, 7]
                    smp = [0, 1, 2, 3]
                nq = qb - qa
                rms_to_h(qa, qb, V_NMIX1)
                qtl = tiles(qa, qb)

                def oT(oc, a, n):
                    return Hap(oc, obase + a, obase + a + n)

                def oT_keys(oc, a, n):
                    return seg_keys("H", oc, obase + a, obase + a + n)

                ev = [0]

                deferred = []

                def stage_q(oc):
                    u = oc % 2
                    s = get_tile(("wq", oc))
                    bks = [6, 7][:len(qtl)]
                    for k in range(KC):
                        for i, (t0, n) in enumerate(qtl):
                            P.add("pe", C("matmul", banks[bks[i]][:, :n], lhsT=slot_ap(s, k * 128, k * 128 + 128),
                                          rhs=Hap(k, t0, t0 + n), start=(k == 0), stop=(k == KC - 1)),
                                  reads=[("slot", s)] + seg_keys("H", k, t0, t0 + n), writes=[("ps", bks[i])])
                    issue_next()
                    for i, (t0, n) in enumerate(qtl):
                        off = t0 - qa
                        P.add("act", C("activation", out=QB[u][:, off:off + n], in_=banks[bks[i]][:, :n], func=AF.Copy),
                              reads=[("ps", bks[i])], writes=[("QB", u, i), "KOUT", ("VST", 0), ("VST", 1)])

                    def rope():
                        for i, (t0, n) in enumerate(qtl):
                            off = t0 - qa
                            b2 = bks[i]
                            P.add("pe", C("matmul", banks[b2][:, :n], lhsT=PERMB[:], rhs=QB[u][:, off:off + n], start=True, stop=True),
                                  reads=[("QB", u, i), "PERMB"], writes=[("ps", b2)])
                            t1 = TR1[:, off:off + n]
                            t2 = TR1[:, 600 + off:600 + off + n]
                            P.add("dve", C("tensor_tensor", out=t1, in0=banks[b2][:, :n], in1=ST(t0, n), op=ALU.mult),
                                  reads=[("ps", b2), "TAB"], writes=[("QTMP", 0, i)])
                            P.add("dve", C("tensor_tensor", out=t2, in0=QB[u][:, off:off + n], in1=CT(t0, n), op=ALU.mult),
                                  reads=[("QB", u, i), "TAB"], writes=[("QTMP", 1, i)])
                            P.add("dve", C("tensor_tensor", out=QT[u][:, off:off + n], in0=t1, in1=t2, op=ALU.add),
                                  reads=[("QTMP", 0, i), ("QTMP", 1, i)], writes=[("QT", u, i)])
                    deferred.append(rope)

                def run_deferred():
                    while deferred:
                        deferred.pop(0)()

                def unit_scores(oc, unit):
                    c = oc // 4
                    u = oc % 2
                    qt_keys = [("QT", u, i) for i in range(len(qtl))]
                    v = ev[0] % NEV
                    ev[0] += 1
                    bSh = [sbank(), sbank()]
                    if unit[0] == "blk":
                        m = unit[1]
                        tb = 158 + 128 * m
                        off = tb - qa
                        kprev = c * KTW + (tb - 128 - 30)
                        kown = c * KTW + (tb - 30)
                        for hd in range(2):
                            pr0, pr1 = hd * 64, hd * 64 + 64
                            for kh, kcol in enumerate((kprev, kown)):
                                P.add("pe", C("matmul", banks[bSh[hd]][:, kh * 128: kh * 128 + 128],
                                              lhsT=KT[pr0:pr1, kcol:kcol + 128], rhs=QT[u][pr0:pr1, off:off + 128], start=True, stop=True),
                                      reads=[("KT", c, 0), ("KT", c, 1), ("KT", c, 2)] + qt_keys, writes=[("ps", bSh[hd])])
                        for hd in range(2):
                            P.add("act", C("activation", out=PT[v][:, hd * 256:(hd + 1) * 256], in_=banks[bSh[hd]][:, 0:256], func=AF.Exp, scale=0.125),
                                  reads=[("ps", bSh[hd])], writes=[("PT", v)])
                        M4 = MASK40 if m == 0 else MASK4
                        P.add("dve", C("tensor_tensor", out=EB[v][:, :], in0=PT[v][:, :], in1=M4[:, :], op=ALU.mult),
                              reads=[("PT", v), "MASK4"], writes=[("EB", v)])
                    else:
                        for hd in range(2):
                            pr0, pr1 = hd * 64, hd * 64 + 64
                            P.add("pe", C("matmul", banks[bSh[hd]][:, 0:32], lhsT=IDENT[:, :], rhs=MASKNS[:, 0:32], start=True, stop=False),
                                  reads=["IDENT", "MASKNS"], writes=[("ps", bSh[hd])])
                            for sq_ in range(4):
                                off = (T_SMP - qa) + 8 * sq_
                                kcache = c * KTW + NTOK + sq_ * 128
                                P.add("pe", C("matmul", banks[bSh[hd]][:, sq_ * 8: sq_ * 8 + 8],
                                              lhsT=KT[pr0:pr1, kcache:kcache + 128], rhs=QT[u][pr0:pr1, off:off + 8], start=False, stop=(sq_ == 3)),
                                      reads=["KTC"] + qt_keys, writes=[("ps", bSh[hd])])
                            P.add("pe", C("matmul", banks[bSh[hd]][0:8, 32:64], lhsT=IDENT[0:8, 0:8], rhs=MASKNS[0:8, 32:64], start=True, stop=False),
                                  reads=["IDENT", "MASKNS"], writes=[("ps", bSh[hd])])
                            for sq_ in range(4):
                                off = (T_SMP - qa) + 8 * sq_
                                knew = c * KTW + (T_SMP + 8 * sq_ - 30)
                                P.add("pe", C("matmul", banks[bSh[hd]][0:8, 32 + sq_ * 8: 40 + sq_ * 8],
                                              lhsT=KT[pr0:pr1, knew:knew + 8], rhs=QT[u][pr0:pr1, off:off + 8], start=False, stop=(sq_ == 3)),
                                      reads=[("KT", c, 2)] + qt_keys, writes=[("ps", bSh[hd])])
                        for hd in range(2):
                            P.add("act", C("activation", out=EB[v][:, hd * 64: hd * 64 + 32], in_=banks[bSh[hd]][:, 0:32], func=AF.Exp, scale=0.125),
                                  reads=[("ps", bSh[hd])], writes=[("EB", v)])
                            P.add("act", C("activation", out=EB[v][0:8, hd * 64 + 32: hd * 64 + 64], in_=banks[bSh[hd]][0:8, 32:64], func=AF.Exp, scale=0.125),
                                  reads=[("ps", bSh[hd])], writes=[("EB", v)])
                    return v

                def unit_pv(oc, unit, v):
                    c = oc // 4
                    kva, kvb = 2 * c, 2 * c + 1
                    bO = obank()
                    if unit[0] == "blk":
                        m = unit[1]
                        off = 158 + 128 * m - qa
                        w = 128
                        for hd, kv in enumerate((kva, kvb)):
                            pr0, pr1 = hd * 64, hd * 64 + 64
                            for kh, vblk in enumerate((m, m + 1)):
                                P.add("pe", C("matmul", banks[bO][pr0:pr1, 0:128], lhsT=VB[:, vblk * 512 + kv * 64: vblk * 512 + kv * 64 + 64],
                                              rhs=EB[v][:, hd * 256 + kh * 128: hd * 256 + kh * 128 + 128], start=(kh == 0), stop=(kh == 1)),
                                      reads=[("VB", vblk), ("EB", v)], writes=[("ps", bO)])
                            for kh in range(2):
                                P.add("pe", C("matmul", banks[bO][pr0:pr1, 128:256], lhsT=ONE1[:, 0:64],
                                              rhs=EB[v][:, hd * 256 + kh * 128: hd * 256 + kh * 128 + 128], start=(kh == 0), stop=(kh == 1)),
                                      reads=["ONE1", ("EB", v)], writes=[("ps", bO)])
                    else:
                        off = T_SMP - qa
                        w = 32
                        for hd, kv in enumerate((kva, kvb)):
                            pr0, pr1 = hd * 64, hd * 64 + 64
                            for sq_ in range(4):
                                rc = EB[v][:, hd * 64 + sq_ * 8: hd * 64 + sq_ * 8 + 8]
                                rn = EB[v][0:8, hd * 64 + 32 + sq_ * 8: hd * 64 + 40 + sq_ * 8]
                                P.add("pe", C("matmul", banks[bO][pr0:pr1, sq_ * 8: sq_ * 8 + 8],
                                              lhsT=VB[:, (9 + sq_) * 512 + kv * 64: (9 + sq_) * 512 + kv * 64 + 64], rhs=rc, start=True, stop=False),
                                      reads=["VBC", ("EB", v)], writes=[("ps", bO)])
                                P.add("pe", C("matmul", banks[bO][pr0:pr1, sq_ * 8: sq_ * 8 + 8],
                                              lhsT=VN[0:8, sq_ * 512 + kv * 64: sq_ * 512 + kv * 64 + 64], rhs=rn, start=False, stop=True),
                                      reads=[("VN", sq_), ("EB", v)], writes=[("ps", bO)])
                                P.add("pe", C("matmul", banks[bO][pr0:pr1, 128 + sq_ * 8: 136 + sq_ * 8], lhsT=ONE1[:, 0:64], rhs=rc, start=True, stop=False),
                                      reads=["ONE1", ("EB", v)], writes=[("ps", bO)])
                                P.add("pe", C("matmul", banks[bO][pr0:pr1, 128 + sq_ * 8: 136 + sq_ * 8], lhsT=ONE1[0:8, 0:64], rhs=rn, start=False, stop=True),
                                      reads=["ONE1", ("EB", v)], writes=[("ps", bO)])
                    P.add("act", C("activation", out=DEN[v][:, 0:w], in_=banks[bO][:, 128:128 + w], func=AF.Ln, bias=ES[:, oc:oc + 1]),
                          reads=[("ps", bO), "ES"], writes=[("DEN", v)])
                    P.add("act", C("activation", out=DEN[v][:, 0:w], in_=DEN[v][:, 0:w], func=AF.Exp, scale=-1.0),
                          reads=[("DEN", v)], writes=[("DEN", v)])
                    P.add("dve", C("tensor_tensor", out=oT(oc, off, w), in0=banks[bO][:, 0:w], in1=DEN[v][:, 0:w], op=ALU.mult),
                          reads=[("ps", bO), ("DEN", v)], writes=oT_keys(oc, off, w))

                units = [("blk", m) for m in blocks] + ([("smp", 0)] if smp else [])
                seq = [(oc, un) for oc in range(16) for un in units]
                sctr = [0]
                octr = [0]

                def sbank():
                    b_ = sctr[0] % 4
                    sctr[0] += 1
                    return b_

                def obank():
                    b_ = 4 + octr[0] % 2
                    octr[0] += 1
                    return b_

                stage_q(0)
                run_deferred()
                pend = []
                for idx, (oc, un) in enumerate(seq):
                    if un is units[0] and oc + 1 < 16:
                        stage_q(oc + 1)
                    v = unit_scores(oc, un)
                    pend.append((oc, un, v))
                    if len(pend) > 2:
                        unit_pv(*pend.pop(0))
                    run_deferred()
                while pend:
                    unit_pv(*pend.pop(0))
                otl = [(t0 - qa, n) for (t0, n) in qtl]
                for oc in range(16):
                    s = get_tile(("wo", oc))
                    bks = proj_group(s, 0, 128, KC,
                                     lambda k, o0, n: oT(k, o0, n), lambda k, o0, n: oT_keys(k, o0, n), otl)
                    issue_next()
                    for i, (o0, n) in enumerate(otl):
                        t0 = qa + o0
                        P.add("dve", C("tensor_tensor",
                            out=Xap(oc, t0, t0 + n), in0=banks[bks[i]][:, :n], in1=Xap(oc, t0, t0 + n), op=ALU.add),
                            reads=[("ps", bks[i])] + seg_keys("X", oc, t0, t0 + n), writes=seg_keys("X", oc, t0, t0 + n))

            checkpoint(6)
            attn_pass(0)
            checkpoint(7)
            attn_pass(1)
            checkpoint(8)
            P.fence()

            mlp(1, 158, TE, V_NMLP1)
            checkpoint(9)

            rms_stats(158, TE)
            yc = [0]
            for (t0, n) in tiles(158, TE):
                for kc in range(KC):
                    yb = yc[0] % 2
                    yc[0] += 1
                    P.add("dve", C("scalar_tensor_tensor",
                        out=YST[yb][:, :n], in0=Xap(kc, t0, t0 + n), scalar=vcol(V_FIN + kc),
                        in1=TR0[:, t0:t0 + n], op0=ALU.mult, op1=ALU.mult),
                        reads=seg_keys("X", kc, t0, t0 + n) + seg_keys("RS", 0, t0, t0 + n) + ["VEC"], writes=[("YST", yb)])
                    outs.append(P.add("sp", C("dma_start",
                        out=yT[:, kc, t0 - 158: t0 - 158 + n], in_=YST[yb][:, :n]), reads=[("YST", yb)], dma="yst%d" % yb))

            assert wstate["consumed"] == len(order), (wstate, len(order))
        except _Stop:
            pass
        P.finalize(outs)
        P.emit(nc)
    return nc


def _tile_kc(W, c0):
    return np.ascontiguousarray(W[:, c0:c0 + 128].reshape(16, 128, 128).transpose(1, 0, 2).reshape(128, 2048))


def _col(v):
    return np.ascontiguousarray(np.asarray(v, np.float32).reshape(16, 128).T)


_NC_CACHE = {}


def prepare(x_prompt, x_sample, state_conv, cache_k, cache_v, norm_mix, w_pw1, b_pw1, w_dw, b_dw,
           conv_ln_g, conv_ln_b, w_pw2, b_pw2, kv_norm, w_k, w_v, w_q, w_o, sinks, norm_mlp,
           w_up, w_down, final_norm):
    f = lambda a: np.asarray(a, dtype=np.float32)
    x_prompt, x_sample, state_conv, cache_k, cache_v = map(f, (x_prompt, x_sample, state_conv, cache_k, cache_v))
    w_pw1, w_pw2, w_k, w_v, w_q, w_o, w_up, w_down = map(f, (w_pw1, w_pw2, w_k, w_v, w_q, w_o, w_up, w_down))
    norm_mix, norm_mlp, b_pw1, w_dw, b_dw = map(f, (norm_mix, norm_mlp, b_pw1, w_dw, b_dw))
    conv_ln_g, conv_ln_b, b_pw2, kv_norm, sinks, final_norm = map(f, (conv_ln_g, conv_ln_b, b_pw2, kv_norm, sinks, final_norm))

    tidx = unique_tiles()

    qperm = np.zeros(2048, np.int64)
    head_of = np.zeros((16, 2), np.int64)
    for oc in range(16):
        c, i = oc // 4, oc % 4
        for hd in range(2):
            h = 8 * c + 4 * hd + i
            head_of[oc, hd] = h
            qperm[oc * 128 + hd * 64: oc * 128 + hd * 64 + 64] = h * 64 + np.arange(64)
    wq_p = w_q[0][:, qperm]
    wo_p = w_o[0][qperm, :]

    wst = np.empty((len(tidx), 128, 2048), np.float32)
    for nm, ti in tidx.items():
        kind = nm[0]
        if kind == "pw1a":
            wst[ti] = _tile_kc(w_pw1[0], nm[1] * 128)
        elif kind == "pw1g":
            wst[ti] = _tile_kc(w_pw1[0], 2048 + nm[1] * 128)
        elif kind == "pw2":
            wst[ti] = _tile_kc(w_pw2[0], nm[1] * 128)
        elif kind == "up":
            wst[ti] = _tile_kc(w_up[nm[1]], nm[2] * 128)
        elif kind == "down":
            l, g, ocp = nm[1], nm[2], nm[3]
            wst[ti] = w_down[l][g * 1024:(g + 1) * 1024, ocp * 256:(ocp + 1) * 256].reshape(8, 128, 256).transpose(1, 0, 2).reshape(128, 2048)
        elif kind == "wk":
            wst[ti] = _tile_kc(w_k, nm[1] * 128)
        elif kind == "wv":
            c = nm[1]
            wst[ti] = w_v[c * 512:(c + 1) * 512, :].reshape(4, 128, 512).transpose(1, 0, 2).reshape(128, 2048)
        elif kind == "wq":
            wst[ti] = _tile_kc(wq_p, nm[1] * 128)
        elif kind == "wo":
            wst[ti] = _tile_kc(wo_p, nm[1] * 128)
        else:
            raise AssertionError(nm)

    vec = np.zeros((128, NV), np.float32)
    vec[:, V_NMIX0:V_NMIX0 + 16] = _col(norm_mix[0])
    vec[:, V_NMLP0:V_NMLP0 + 16] = _col(norm_mlp[0])
    vec[:, V_KVN:V_KVN + 16] = _col(kv_norm)
    vec[:, V_NMIX1:V_NMIX1 + 16] = _col(norm_mix[1])
    vec[:, V_NMLP1:V_NMLP1 + 16] = _col(norm_mlp[1])
    vec[:, V_FIN:V_FIN + 16] = _col(final_norm)
    vec[:, V_BA:V_BA + 16] = _col(b_pw1[0][:2048])
    vec[:, V_BG:V_BG + 16] = _col(b_pw1[0][2048:])
    vec[:, V_BDW:V_BDW + 16] = _col(b_dw[0])
    vec[:, V_LNG:V_LNG + 16] = _col(conv_ln_g[0])
    vec[:, V_LNB:V_LNB + 16] = _col(conv_ln_b[0])
    vec[:, V_BPW2:V_BPW2 + 16] = _col(b_pw2[0])
    for oc in range(16):
        vec[0:64, V_SINK + oc] = sinks[0][head_of[oc, 0]]
        vec[64:128, V_SINK + oc] = sinks[0][head_of[oc, 1]]
    vec[:, V_WDW:] = w_dw[0].T.reshape(16, 128, CW).transpose(1, 0, 2).reshape(128, 16 * CW)

    mask_prev = (np.arange(128)[:, None] > np.arange(128)[None, :]).astype(np.float32)
    mask_own = (np.arange(128)[:, None] <= np.arange(128)[None, :]).astype(np.float32)
    m4 = np.concatenate([mask_prev, mask_own, mask_prev, mask_own], axis=1)
    m4_first = m4.copy()
    m4_first[:, 0:128] = 0.0
    m4_first[:, 256:384] = 0.0
    ms = np.zeros((128, 32), np.float32)
    ms[:, 0:8] = mask_prev[:, 0:8]
    ms[:, 8:16] = mask_prev[:, 0:8]
    ms[0:8, 16:24] = mask_own[0:8, 0:8]
    ms[0:8, 24:32] = mask_own[0:8, 0:8]
    NEG = np.float32(-240000.0)
    mn = np.concatenate([(1.0 - mask_prev) * NEG, (1.0 - mask_own) * NEG], axis=1).astype(np.float32)
    mn_first = mn.copy()
    mn_first[:, 0:128] = NEG
    mns = np.zeros((128, 64), np.float32)
    for s_ in range(4):
        mns[:, s_ * 8:(s_ + 1) * 8] = (1.0 - mask_prev[:, 0:8]) * NEG
        mns[0:8, 32 + s_ * 8: 40 + s_ * 8] = (1.0 - mask_own[0:8, 0:8]) * NEG
    eye = np.eye(128, dtype=np.float32)
    pm = np.zeros((128, 128), np.float32)
    sgn = np.zeros(128, np.float32)
    fidx = np.full(128, -1, np.int64)
    for m_ in range(128):
        d = m_ % 64
        if d < 8:
            pm[m_ + 8, m_] = 1.0
            sgn[m_] = -1.0
            fidx[m_] = d
        elif d < 16:
            pm[m_ - 8, m_] = 1.0
            sgn[m_] = 1.0
            fidx[m_] = d - 8
    inv = (np.float32(500000.0) ** (-np.arange(8, dtype=np.float32) / np.float32(8))).astype(np.float32)

    in_maps = []
    for core in range(NCORE):
        b, j = core // 4, core % 4
        xe = np.zeros((TE, D), np.float32)
        if j > 0:
            xe[0:158] = x_prompt[b, 1024 * j - 158: 1024 * j]
        xe[158:1182] = x_prompt[b, 1024 * j: 1024 * (j + 1)]
        xe[1182:1214] = x_sample[4 * core: 4 * core + 4].reshape(32, D)
        xTc = np.ascontiguousarray(xe.T.reshape(16, 128, TE).transpose(1, 0, 2))
        sc = state_conv[0, 4 * core: 4 * core + 4]
        ctxTc = np.ascontiguousarray(sc.transpose(2, 0, 1).reshape(16, 128, 4, 30).transpose(1, 0, 2, 3))
        ck = cache_k[4 * core: 4 * core + 4].reshape(4, 128, 512)
        cv = cache_v[4 * core: 4 * core + 4].reshape(4, 128, 512)
        ckTc = np.ascontiguousarray(ck.transpose(2, 0, 1).reshape(4, 128, 4 * 128).transpose(1, 0, 2))
        pos = np.zeros(NTOK, np.float32)
        pos[0:1152] = (1024 * j - 128) + np.arange(1152)
        pos[1152:1184] = 16384 + (np.arange(32) % 8)
        tabc = np.zeros((128, 2 * NTOK), np.float32)
        tabc[:, 0:NTOK] = 1.0
        for m_ in range(128):
            if fidx[m_] >= 0:
                ang = (pos * inv[fidx[m_]]).astype(np.float32)
                tabc[m_, 0:NTOK] = np.cos(ang).astype(np.float32)
                tabc[m_, NTOK:] = sgn[m_] * np.sin(ang).astype(np.float32)
        v = vec.copy()
        v[:, V_FLAG] = 0.0 if j == 0 else 1.0
        in_maps.append({
            "xT": xTc, "wst": wst, "ctxT": ctxTc, "ckT": ckTc,
            "sc_raw": np.ascontiguousarray(sc), "ck_raw": np.ascontiguousarray(ck), "cv_raw": np.ascontiguousarray(cv),
            "vecs": v, "tab": tabc, "mask4": m4, "masks": ms, "perm": pm,
            "mask40": (m4_first if j == 0 else m4), "ident": eye, "maskns": mns,
        })

    return in_maps


def kernel(**inputs):
    in_maps = prepare(**inputs)
    if "nc" not in _NC_CACHE:
        _NC_CACHE["nc"] = build()
    nc = _NC_CACHE["nc"]
    res = run_bass_kernel_spmd(nc, in_maps, core_ids=list(range(NCORE)))
    return assemble(res.results)


def assemble(R_):

    y_prompt = np.zeros((2, 4096, D), np.float32)
    y_sample = np.zeros((32, 8, D), np.float32)
    conv_prompt = np.zeros((1, 2, 30, D), np.float32)
    conv_sample = np.zeros((1, 32, 30, D), np.float32)
    win_k_prompt = np.zeros((2, 128, 8, 64), np.float32)
    win_v_prompt = np.zeros((2, 128, 8, 64), np.float32)
    win_k_sample = np.zeros((32, 128, 8, 64), np.float32)
    win_v_sample = np.zeros((32, 128, 8, 64), np.float32)
    for core in range(NCORE):
        b, j = core // 4, core % 4
        r = R_[core]
        y = np.transpose(r["yT"], (2, 1, 0)).reshape(NQ, D)
        y_prompt[b, 1024 * j: 1024 * (j + 1)] = y[0:1024]
        y_sample[4 * core: 4 * core + 4] = y[1024:1056].reshape(4, 8, D)
        uo = r["uT_out"].reshape(128, 16, 62)
        ut = np.transpose(uo, (2, 1, 0)).reshape(62, D)
        kt = np.transpose(r["kT_out"].reshape(128, 4, 160), (2, 1, 0)).reshape(160, 512)
        if j == 3:
            conv_prompt[0, b] = ut[0:30]
            win_k_prompt[b] = kt[0:128].reshape(128, 8, 64)
            win_v_prompt[b] = r["v_out"].reshape(128, 8, 64)
        for s in range(4):
            q = 4 * core + s
            conv_sample[0, q, 0:22] = r["conv_old"][s]
            conv_sample[0, q, 22:30] = ut[30 + 8 * s: 30 + 8 * s + 8]
            win_k_sample[q, 0:120] = r["k_old"][s].reshape(120, 8, 64)
            win_k_sample[q, 120:128] = kt[128 + 8 * s: 128 + 8 * s + 8].reshape(8, 8, 64)
            win_v_sample[q, 0:120] = r["v_old"][s].reshape(120, 8, 64)
            win_v_sample[q, 120:128] = r["v_new"][s].reshape(8, 8, 64)
    return (y_prompt, y_sample, conv_prompt, conv_sample, win_k_prompt, win_v_prompt, win_k_sample, win_v_sample)
```

```python
import contextlib
import os
import numpy as np
import concourse.bass as bass
import concourse.mybir as mybir
from concourse.bass_utils import run_bass_kernel_spmd

F32 = mybir.dt.float32
BF16 = mybir.dt.bfloat16
AF = mybir.ActivationFunctionType
ALU = mybir.AluOpType

ENGS = ("pe", "act", "dve", "pool", "sp")


def C(name, *a, **k):
    return (name, a, k)


class Op:
    __slots__ = ("eng", "fn", "deps", "signal", "sem", "val", "pos", "dma")

    def __init__(self, eng, fn, dma=None):
        self.eng = eng
        self.fn = fn
        self.deps = []
        self.signal = False
        self.sem = None
        self.val = 0
        self.pos = 0
        self.dma = dma


class _Stop(Exception):
    pass


STOP = float(os.environ.get("MK_STOP", "99"))


def checkpoint(k):
    if STOP <= k:
        raise _Stop()


class Prog:
    SAFE_DIST = 10 ** 9

    def __init__(self):
        self.streams = {e: [] for e in ENGS}
        self.last_w = {}
        self.readers = {}
        self.last_op = {}
        self.pending_fence = {e: [] for e in ENGS}
        self.dma_pending = []

    def add(self, eng, fn, reads=(), writes=(), dma=None):
        op = Op(eng, fn, dma=dma)
        if dma is not None:
            op.signal = True
        deps = set()
        for r in reads:
            w = self.last_w.get(r)
            if w is not None:
                deps.add(w)
        for wkey in writes:
            w = self.last_w.get(wkey)
            if w is not None:
                deps.add(w)
            for rd in self.readers.get(wkey, ()):
                deps.add(rd)
        for f in self.pending_fence[eng]:
            deps.add(f)
        self.pending_fence[eng] = []
        op.pos = len(self.streams[eng])
        keep = []
        latest = {}
        for d in deps:
            if d.dma is None and d.eng == eng:
                if eng == "pe" or eng == "sp":
                    continue
                if op.pos - d.pos >= self.SAFE_DIST:
                    continue
            if d.dma is None:
                cur = latest.get(d.eng)
                if cur is None or d.pos > cur.pos:
                    latest[d.eng] = d
            else:
                keep.append(d)
        keep.extend(latest.values())
        op.deps = keep
        for d in keep:
            d.signal = True
        self.streams[eng].append(op)
        if dma is None:
            self.last_op[eng] = op
        elif not dma.startswith("slot"):
            self.dma_pending.append(op)
        for r in reads:
            self.readers.setdefault(r, []).append(op)
        for wkey in writes:
            self.last_w[wkey] = op
            self.readers[wkey] = []
        return op

    def fence(self):
        lasts = [o for o in self.last_op.values()]
        dmas = self.dma_pending
        self.dma_pending = []
        for e in ENGS:
            self.pending_fence[e] = [o for o in lasts if o.eng != e] + dmas

    def finalize(self, final_ops):
        for o in final_ops:
            o.signal = True
        counters = {}
        for e in ENGS:
            for op in self.streams[e]:
                if not op.signal:
                    continue
                if op.dma is not None:
                    key = ("dma", op.dma)
                    counters[key] = counters.get(key, 0) + 16
                else:
                    key = ("eng", e)
                    counters[key] = counters.get(key, 0) + 1
                op.sem = key
                op.val = counters[key]
        self.sem_keys = list(counters.keys())
        self.final_ops = final_ops

    def emit(self, nc):
        with contextlib.ExitStack() as st:
            sems = {}
            for i, k in enumerate(self.sem_keys):
                sems[k] = st.enter_context(nc.semaphore("s%d" % i))
            block = st.enter_context(nc.Block())
            prog = self

            def run(ename, e, is_last=False):
                waited = {}
                for op in prog.streams[ename]:
                    for d in op.deps:
                        if waited.get(d.sem, 0) < d.val:
                            e.wait_ge(sems[d.sem], d.val)
                            waited[d.sem] = d.val
                    name_, a_, k_ = op.fn
                    ins = getattr(e, name_)(*a_, **k_)
                    if op.signal:
                        ins.then_inc(sems[op.sem], 16 if op.dma is not None else 1)
                if is_last:
                    for d in prog.final_ops:
                        if waited.get(d.sem, 0) < d.val:
                            e.wait_ge(sems[d.sem], d.val)
                            waited[d.sem] = d.val

            @block.sync
            def _(e):
                run("sp", e, is_last=True)

            @block.scalar
            def _(e):
                run("act", e)

            @block.vector
            def _(e):
                run("dve", e)

            @block.gpsimd
            def _(e):
                run("pool", e)

            @block.tensor
            def _(e):
                run("pe", e)


NCORE = 8
D = 2048
KC = 16
TE = 1214
T_PRE, T_HALO, T_MAIN, T_SMP = 0, 30, 158, 1182
NTOK = TE - 30
NQ = TE - 158
SEGS = [0, 30, 158, 592, 622, 670, 1214]
NS = 4
CW = 31
NDV = 8
EPS = 1e-6
UW = 742

V_NMIX0, V_NMLP0, V_KVN, V_NMIX1, V_NMLP1, V_FIN = 0, 16, 32, 48, 64, 80
V_BA, V_BG, V_BDW, V_LNG, V_LNB, V_BPW2, V_SINK = 96, 112, 128, 144, 160, 176, 192
V_FLAG = 208
V_WDW = 212
NV = V_WDW + 16 * CW


def seg_keys(name, kc, a, b):
    return [(name, kc, i) for i in range(len(SEGS) - 1) if SEGS[i] < b and SEGS[i + 1] > a]


def tiles(a, b):
    out = []
    t = a
    while t < b:
        n = min(512, b - t)
        out.append((t, n))
        t += n
    return out


def weight_order():
    order = []
    for _pass in range(2):
        for oc in range(16):
            order.append(("pw1a", oc))
            order.append(("pw1g", oc))
        for oc in range(16):
            order.append(("pw2", oc))
    for g in range(8):
        for fc in range(8):
            order.append(("up", 0, g * 8 + fc))
        for ocp in range(8):
            order.append(("down", 0, g, ocp))
    for c in range(4):
        order.append(("wk", c))
    for _grp in range(2):
        for c in range(4):
            order.append(("wv", c))
    for _pass in range(2):
        for oc in range(16):
            order.append(("wq", oc))
        for oc in range(16):
            order.append(("wo", oc))
    for g in range(8):
        for fc in range(8):
            order.append(("up", 1, g * 8 + fc))
        for ocp in range(8):
            order.append(("down", 1, g, ocp))
    return order


def unique_tiles():
    seen = {}
    for nm in weight_order():
        if nm not in seen:
            seen[nm] = len(seen)
    return seen


def build():
    nc = bass.Bass("TRN2", target_bir_lowering=False)
    tidx = unique_tiles()
    NT = len(tidx)
    order = weight_order()

    def din(name, shape):
        return nc.dram_tensor(name, shape, F32, kind="ExternalInput").ap()

    def dout(name, shape):
        return nc.dram_tensor(name, shape, F32, kind="ExternalOutput").ap()

    xT = din("xT", [128, KC, TE])
    wst = din("wst", [NT, 128, 2048])
    ctxT = din("ctxT", [128, KC, 4, 30])
    ckT = din("ckT", [128, 4, 512])
    sc_raw = din("sc_raw", [4, 30, D])
    ck_raw = din("ck_raw", [4, 128, 512])
    cv_raw = din("cv_raw", [4, 128, 512])
    vecs = din("vecs", [128, NV])
    tab = din("tab", [128, 2 * NTOK])
    mask4 = din("mask4", [128, 512])
    masks = din("masks", [128, 32])
    mask40 = din("mask40", [128, 512])
    perm = din("perm", [128, 128])
    ident = din("ident", [128, 128])
    maskns = din("maskns", [128, 64])

    yT = dout("yT", [128, KC, NQ])
    uT_out = dout("uT_out", [128, KC * 62])
    conv_old = dout("conv_old", [4, 22, D])
    kT_out = dout("kT_out", [128, 4 * 160])
    v_out = dout("v_out", [128, 512])
    v_new = dout("v_new", [4, 8, 512])
    k_old = dout("k_old", [4, 120, 512])
    v_old = dout("v_old", [4, 120, 512])

    P = Prog()
    outs = []
    with contextlib.ExitStack() as st:
        def sb(name, cols, dt):
            return st.enter_context(nc.sbuf_tensor(name, [128, cols], dt))

        X = sb("X", KC * TE, F32)
        H = sb("H", KC * TE, BF16)
        RING = sb("RING", NS * 2048, BF16)
        VEC = sb("VEC", NV, F32)
        TAB = sb("TAB", 2 * NTOK, F32)
        TR0 = sb("TR0", TE, F32)
        TR1 = sb("TR1", TE, F32)
        SQ = sb("SQ", 4 * 512, BF16)
        ONESB = sb("ONESB", 128, BF16)
        ONE1 = sb("ONE1", 64, BF16)
        PERMF = sb("PERMF", 128, F32)
        PERMB = sb("PERMB", 128, BF16)
        MASK4 = sb("MASK4", 512, BF16)
        MASKS = sb("MASKS", 32, BF16)
        MASK40 = sb("MASK40", 512, BF16)
        ES = sb("ES", 16, F32)
        EPSC = sb("EPSC", 1, F32)
        IDENT = sb("IDENT", 128, BF16)
        MASKNS = sb("MASKNS", 64, BF16)
        RBYTES = 44288
        R = sb("R", RBYTES // 4, F32)
        banks = [st.enter_context(nc.psum_tensor("ps%d" % i, [128, 512], F32)) for i in range(8)]

        def carve(off, ncols, dt):
            sz = 4 if dt == F32 else 2
            assert off % 4 == 0 and (ncols * sz) % 4 == 0
            assert off + ncols * sz <= RBYTES, (off, ncols, sz)
            v = R[:, off // 4: off // 4 + (ncols * sz) // 4]
            return v if dt == F32 else v.bitcast(BF16)

        o = 0
        SIG = [carve(o + i * 2488, 622, F32) for i in range(2)]; o += 2 * 2488
        U = [carve(o + i * UW * 4, UW, F32) for i in range(2)]; o += 2 * UW * 4
        ACCD = [carve(o + i * 2368, 592, F32) for i in range(2)]; o += 2 * 2368
        UB = [carve(o + i * 1488, 744, BF16) for i in range(2)]; o += 2 * 1488
        NDG = 32
        DG = carve(o, NDG * 128, BF16); o += NDG * 256
        UC = carve(o, 16 * 30, F32); o += 16 * 30 * 4
        LNT = [carve(o + i * 2368, 592, F32) for i in range(2)]; o += 2 * 2368
        USAVE = carve(o, 16 * 62, F32); o += 16 * 62 * 4
        assert o <= RBYTES, o
        o = 0
        HID = carve(o, 8 * NTOK, BF16); o += 8 * NTOK * 2
        RT = [carve(o + i * 2048, 512, F32) for i in range(3)]; o += 3 * 2048
        YST = [carve(o + i * 2048, 512, F32) for i in range(2)]; o += 2 * 2048
        assert o <= RBYTES, o
        KTW = NTOK + 512
        o = 0
        KT = carve(o, 4 * KTW, BF16); o += 4 * KTW * 2
        VB = carve(o, 13 * 512, BF16); o += 13 * 512 * 2
        VN = carve(o, 4 * 512, BF16); o += 4 * 512 * 2
        o_alias = o
        QB = [carve(o + i * 1088, 544, BF16) for i in range(2)]; o += 2 * 1088
        QT = [carve(o + i * 1088, 544, BF16) for i in range(2)]; o += 2 * 1088
        KOUT = carve(o_alias, 4 * 160, F32)
        VST = [carve(o_alias + 2560 + i * 1024, 256, F32) for i in range(2)]
        assert o_alias + 2560 + 2048 <= o + 256
        o = max(o, o_alias + 2560 + 2048)
        NEV = 3
        EB = [carve(o + i * 1024, 512, BF16) for i in range(NEV)]; o += NEV * 1024
        PT = [carve(o + i * 1024, 512, BF16) for i in range(NEV)]; o += NEV * 1024
        DEN = [carve(o + i * 512, 128, F32) for i in range(NEV)]; o += NEV * 512
        assert o <= RBYTES, o

        def Xap(kc, a, b):
            return X[:, kc * TE + a: kc * TE + b]

        def Hap(kc, a, b):
            return H[:, kc * TE + a: kc * TE + b]

        def vcol(c):
            return VEC[:, c:c + 1]

        bank_ctr = [0]
        dgc = [0]

        def bank():
            b = bank_ctr[0] % 8
            bank_ctr[0] += 1
            return b

        wstate = {"issued": 0, "consumed": 0}

        def issue_next():
            n = wstate["issued"]
            if n >= len(order):
                return
            s = n % NS
            ti = tidx[order[n]]
            P.add("pool", C("dma_start", out=RING[:, s * 2048:(s + 1) * 2048], in_=wst[ti]),
                  writes=[("slot", s)], dma="slot%d" % s)
            wstate["issued"] = n + 1

        def get_tile(name):
            n = wstate["consumed"]
            assert order[n] == name, (order[n], name)
            s = n % NS
            wstate["consumed"] = n + 1
            return s

        def slot_ap(s, a, b):
            return RING[:, s * 2048 + a: s * 2048 + b]

        P.add("sp", C("dma_start", out=VEC[:], in_=vecs[:]), writes=["VEC"], dma="vec")
        for (ta, tb_) in ((0, 622), (622, TE)):
            for k0 in range(0, KC, 8):
                wk = []
                for kc in range(k0, k0 + 8):
                    wk += seg_keys("X", kc, ta, tb_)
                P.add("sp", C("dma_start",
                    out=X[:, k0 * TE:(k0 + 8) * TE].rearrange("p (k t) -> p k t", t=TE)[:, :, ta:tb_], in_=xT[:, k0:k0 + 8, ta:tb_]),
                    writes=wk, dma="xl%d_%d" % (k0, ta))
        P.add("sp", C("dma_start", out=TAB[:], in_=tab[:]), writes=["TAB"], dma="tab")
        P.add("sp", C("dma_start", out=PERMF[:], in_=perm[:]), writes=["PERMF"], dma="permf")
        P.add("pool", C("dma_start", out=PERMB[:], in_=perm[:]), writes=["PERMB"], dma="permb")
        P.add("pool", C("dma_start", out=MASK4[:], in_=mask4[:]), writes=["MASK4"], dma="mask4")
        P.add("pool", C("dma_start", out=IDENT[:], in_=ident[:]), writes=["IDENT"], dma="ident")
        P.add("pool", C("dma_start", out=MASKNS[:], in_=maskns[:]), writes=["MASKNS"], dma="maskns")
        P.add("pool", C("dma_start", out=MASKS[:], in_=masks[:]), writes=["MASKS"], dma="masks")
        P.add("pool", C("dma_start", out=MASK40[:], in_=mask40[:]), writes=["MASK4"], dma="mask40")
        for _ in range(NS):
            issue_next()
        P.add("dve", C("memset", ONESB[:], 1.0 / D), writes=["ONESB"])
        P.add("dve", C("memset", ONE1[:], 1.0), writes=["ONE1"])
        P.add("dve", C("memset", EPSC[:], EPS), writes=["EPSC"])
        P.add("act", C("activation", out=ES[:], in_=VEC[:, V_SINK:V_SINK + 16], func=AF.Exp),
              reads=["VEC"], writes=["ES"])
        outs.append(P.add("sp", C("dma_start", out=conv_old[:], in_=sc_raw[:, 8:30, :]), dma="d2d0"))
        outs.append(P.add("sp", C("dma_start", out=k_old[:], in_=ck_raw[:, 8:128, :]), dma="d2d1"))
        outs.append(P.add("sp", C("dma_start", out=v_old[:], in_=cv_raw[:, 8:128, :]), dma="d2d2"))

        def rstd_from_bank(b, n, rs_ap, rs_keys):
            P.add("act", C("activation", out=rs_ap, in_=banks[b][:, :n], func=AF.Ln, bias=EPSC[:, 0:1]),
                  reads=[("ps", b), "EPSC"], writes=rs_keys)
            P.add("act", C("activation", out=rs_ap, in_=rs_ap, func=AF.Exp, scale=-0.5), reads=rs_keys, writes=rs_keys)

        def rms_stats(a, b):
            for (t0, n) in tiles(a, b):
                bk = bank()
                for kc in range(KC):
                    sq = kc % 4
                    if kc % 2 == 0:
                        P.add("act", C("activation",
                            out=SQ[:, sq * 512: sq * 512 + n], in_=Xap(kc, t0, t0 + n), func=AF.Square),
                            reads=seg_keys("X", kc, t0, t0 + n), writes=[("SQ", sq)])
                    else:
                        P.add("dve", C("tensor_tensor",
                            out=SQ[:, sq * 512: sq * 512 + n], in0=Xap(kc, t0, t0 + n), in1=Xap(kc, t0, t0 + n), op=ALU.mult),
                            reads=seg_keys("X", kc, t0, t0 + n), writes=[("SQ", sq)])
                    P.add("pe", C("matmul",
                        banks[bk][:, :n], lhsT=ONESB[:], rhs=SQ[:, sq * 512: sq * 512 + n],
                        start=(kc == 0), stop=(kc == KC - 1)),
                        reads=[("SQ", sq), "ONESB"], writes=[("ps", bk)])
                rstd_from_bank(bk, n, TR0[:, t0:t0 + n], seg_keys("RS", 0, t0, t0 + n))

        def rms_to_h(a, b, gcol):
            rms_stats(a, b)
            for kc in range(KC):
                for (t0, n) in tiles(a, b):
                    P.add("dve", C("scalar_tensor_tensor",
                        out=Hap(kc, t0, t0 + n), in0=Xap(kc, t0, t0 + n), scalar=vcol(gcol + kc),
                        in1=TR0[:, t0:t0 + n], op0=ALU.mult, op1=ALU.mult),
                        reads=seg_keys("X", kc, t0, t0 + n) + seg_keys("RS", 0, t0, t0 + n) + ["VEC"],
                        writes=seg_keys("H", kc, t0, t0 + n))

        def proj_group(slot, lhs_off, lhs_stride, nk, rhs_fn, rhs_keys_fn, tl):
            bks = [bank() for _ in tl]
            for k in range(nk):
                for i, (t0, n) in enumerate(tl):
                    P.add("pe", C("matmul",
                        banks[bks[i]][:, :n], lhsT=slot_ap(slot, lhs_off + k * lhs_stride, lhs_off + k * lhs_stride + 128),
                        rhs=rhs_fn(k, t0, n), start=(k == 0), stop=(k == nk - 1)),
                        reads=[("slot", slot)] + rhs_keys_fn(k, t0, n), writes=[("ps", bks[i])])
            return bks

        def h_rhs(k, t0, n):
            return Hap(k, t0, t0 + n)

        def h_keys(k, t0, n):
            return seg_keys("H", k, t0, t0 + n)

        try:
            checkpoint(0)
            def conv_pass(pa):
                if pa == 0:
                    ga, gb, oa, ob = 0, 622, 30, 622
                    cbase = 622
                else:
                    ga, gb, oa, ob = 622, 1214, 622, 1214
                    cbase = 0
                rms_to_h(ga, gb, V_NMIX0)
                checkpoint(1)
                tl = tiles(ga, gb)

                def cap(oc, a, b):
                    return Hap(oc, cbase + a, cbase + b)

                def ckeys(oc, a, b):
                    return seg_keys("H", oc, cbase + a, cbase + b)

                UWID = 622 if pa == 0 else UW

                def stage1(oc):
                    u = oc % 2
                    sA = get_tile(("pw1a", oc))
                    bA = proj_group(sA, 0, 128, KC, h_rhs, h_keys, tl)
                    sG = get_tile(("pw1g", oc))
                    bG = proj_group(sG, 0, 128, KC, h_rhs, h_keys, tl)
                    issue_next()
                    issue_next()
                    if pa == 1:
                        P.add("act", C("activation", out=U[u][:, 0:30], in_=UC[:, oc * 30:(oc + 1) * 30], func=AF.Copy),
                              reads=[("UC", oc)], writes=[("UX", u)])
                        P.add("sp", C("dma_start",
                            out=U[u][:, 590:742].rearrange("p (s c) -> p s c", c=38)[:, :, 0:30], in_=ctxT[:, oc, :, :]),
                            writes=[("UX", u)], dma="ctx%d" % u)
                    for i, (t0, n) in enumerate(tl):
                        g0 = t0 - ga
                        P.add("act", C("activation",
                            out=SIG[u][:, g0:g0 + n], in_=banks[bG[i]][:, :n], func=AF.Sigmoid, bias=vcol(V_BG + oc)),
                            reads=[("ps", bG[i]), "VEC"], writes=[("SIG", u)])
                        npr = min(t0 + n, T_SMP) - t0
                        if npr > 0:
                            uc0 = t0 if pa == 0 else 30 + (t0 - 622)
                            P.add("dve", C("scalar_tensor_tensor",
                                out=U[u][:, uc0:uc0 + npr], in0=banks[bA[i]][:, :npr], scalar=vcol(V_BA + oc),
                                in1=SIG[u][:, g0:g0 + npr], op0=ALU.add, op1=ALU.mult),
                                reads=[("ps", bA[i]), ("SIG", u), "VEC"], writes=[("U", u)])
                        if npr < n:
                            assert n - npr == 32
                            P.add("dve", C("scalar_tensor_tensor",
                                out=U[u][:, 590:742].rearrange("p (s c) -> p s c", c=38)[:, :, 30:38],
                                in0=banks[bA[i]][:, npr:npr + 32].rearrange("p (s c) -> p s c", c=8), scalar=vcol(V_BA + oc),
                                in1=SIG[u][:, g0 + npr:g0 + npr + 32].rearrange("p (s c) -> p s c", c=8),
                                op0=ALU.add, op1=ALU.mult),
                                reads=[("ps", bA[i]), ("SIG", u), "VEC"], writes=[("U", u)])
                    if pa == 0:
                        P.add("dve", C("tensor_single_scalar", out=U[u][:, 0:158], in_=U[u][:, 0:158],
                                       scalar=vcol(V_FLAG), op=ALU.mult),
                              reads=[("U", u), "VEC"], writes=[("U", u)])
                        P.add("act", C("activation", out=UC[:, oc * 30:(oc + 1) * 30], in_=U[u][:, 592:622], func=AF.Copy),
                              reads=[("U", u)], writes=[("UC", oc)])
                    else:
                        P.add("act", C("activation", out=USAVE[:, oc * 62: oc * 62 + 30], in_=U[u][:, 560:590], func=AF.Copy),
                              reads=[("U", u)], writes=["USAVE"])
                        P.add("act", C("activation",
                            out=USAVE[:, oc * 62 + 30: oc * 62 + 62].rearrange("p (s c) -> p s c", c=8),
                            in_=U[u][:, 590:742].rearrange("p (s c) -> p s c", c=38)[:, :, 30:38], func=AF.Copy),
                            reads=[("U", u)], writes=["USAVE"])
                    if NDV < CW:
                        P.add("act", C("activation", out=UB[u][:, 0:UWID], in_=U[u][:, 0:UWID], func=AF.Copy),
                              reads=[("U", u), ("UX", u)], writes=[("UB", u)])

                def stage2a(oc):
                    dgs = []
                    for j in range(NDV, CW):
                        dg = dgc[0] % NDG
                        dgc[0] += 1
                        P.add("act", C("activation", out=DG[:, dg * 128:(dg + 1) * 128], in_=IDENT[:, :], func=AF.Copy,
                                       scale=vcol(V_WDW + oc * CW + j)),
                              reads=["IDENT", "VEC"], writes=[("DG", dg)])
                        dgs.append(dg)
                    return dgs

                def stage2(oc, dgs):
                    u = oc % 2
                    if pa == 0:
                        parts = [(lambda acc: acc[:, 0:512], lambda X_, j: X_[:, j:j + 512], lambda: cap(oc, 0, 512), lambda bk: banks[bk][:, 0:512]),
                                 (lambda acc: acc[:, 512:592], lambda X_, j: X_[:, 512 + j:592 + j], lambda: cap(oc, 512, 592), lambda bk: banks[bk][:, 0:80])]
                    else:
                        parts = [(lambda acc: acc[:, 0:512], lambda X_, j: X_[:, j:j + 512], lambda: cap(oc, 0, 512), lambda bk: banks[bk][:, 0:512]),
                                 (lambda acc: acc[:, 512:560], lambda X_, j: X_[:, 512 + j:560 + j], lambda: cap(oc, 512, 560), lambda bk: banks[bk][:, 0:48]),
                                 (lambda acc: acc[:, 560:592].rearrange("p (s c) -> p s c", c=8),
                                  lambda X_, j: X_[:, 590:742].rearrange("p (s c) -> p s c", c=38)[:, :, j:j + 8],
                                  lambda: cap(oc, 560, 592).rearrange("p (s c) -> p s c", c=8),
                                  lambda bk: banks[bk][:, 0:32].rearrange("p (s c) -> p s c", c=8))]
                    akey = ("ACC", "dve", u)
                    cb = []
                    if NDV < CW:
                        cb = [bank() for _ in parts]
                        for j in range(NDV, CW):
                            dg = dgs[j - NDV]
                            for pi, (accv, xv, cv, pv) in enumerate(parts):
                                P.add("pe", C("matmul", pv(cb[pi]), lhsT=DG[:, dg * 128:(dg + 1) * 128], rhs=xv(UB[u], j),
                                              start=(j == NDV), stop=(j == CW - 1)),
                                      reads=[("DG", dg), ("UB", u)], writes=[("ps", cb[pi])])
                    for j in range(NDV):
                        wcol = vcol(V_WDW + oc * CW + j)
                        for pi, (accv, xv, cv, pv) in enumerate(parts):
                            if j == 0:
                                P.add("dve", C("tensor_scalar", out=accv(ACCD[u]), in0=xv(U[u], j), scalar1=wcol,
                                               scalar2=vcol(V_BDW + oc), op0=ALU.mult, op1=ALU.add),
                                      reads=[("U", u), ("UX", u), "VEC"], writes=[akey])
                            else:
                                P.add("dve", C("scalar_tensor_tensor", out=accv(ACCD[u]), in0=xv(U[u], j), scalar=wcol,
                                               in1=accv(ACCD[u]), op0=ALU.mult, op1=ALU.add),
                                      reads=[("U", u), ("UX", u), "VEC", akey], writes=[akey])
                    for pi, (accv, xv, cv, pv) in enumerate(parts):
                        if NDV == 0:
                            P.add("dve", C("tensor_single_scalar", out=cv(), in_=pv(cb[pi]), scalar=vcol(V_BDW + oc), op=ALU.add),
                                  reads=[("ps", cb[pi]), "VEC"], writes=ckeys(oc, 0, 592))
                        elif NDV == CW:
                            P.add("dve", C("tensor_copy", out=cv(), in_=accv(ACCD[u])), reads=[akey], writes=ckeys(oc, 0, 592))
                        else:
                            P.add("dve", C("tensor_tensor", out=cv(), in0=pv(cb[pi]), in1=accv(ACCD[u]), op=ALU.add),
                                  reads=[("ps", cb[pi]), akey], writes=ckeys(oc, 0, 592))

                stage1(0)
                for oc in range(16):
                    dgs = stage2a(oc)
                    if oc + 1 < 16:
                        stage1(oc + 1)
                    stage2(oc, dgs)
                ltl = tiles(0, 592)
                bmu = [bank() for _ in ltl]
                bvar = [bank() for _ in ltl]
                for oc in range(16):
                    for i, (c0, n) in enumerate(ltl):
                        sq = (2 * oc + i) % 4
                        P.add("act", C("activation",
                            out=SQ[:, sq * 512: sq * 512 + n], in_=cap(oc, c0, c0 + n), func=AF.Square),
                            reads=ckeys(oc, c0, c0 + n), writes=[("SQ", sq)])
                        P.add("pe", C("matmul",
                            banks[bmu[i]][:, :n], lhsT=ONESB[:], rhs=cap(oc, c0, c0 + n), start=(oc == 0), stop=(oc == 15)),
                            reads=ckeys(oc, c0, c0 + n) + ["ONESB"], writes=[("ps", bmu[i])])
                        P.add("pe", C("matmul",
                            banks[bvar[i]][:, :n], lhsT=ONESB[:], rhs=SQ[:, sq * 512: sq * 512 + n], start=(oc == 0), stop=(oc == 15)),
                            reads=[("SQ", sq), "ONESB"], writes=[("ps", bvar[i])])
                for i, (c0, n) in enumerate(ltl):
                    P.add("act", C("activation", out=TR1[:, c0:c0 + n], in_=banks[bmu[i]][:, :n], func=AF.Copy),
                          reads=[("ps", bmu[i])], writes=[("MU", i)])
                    P.add("dve", C("tensor_tensor", out=LNT[0][:, c0:c0 + n], in0=TR1[:, c0:c0 + n], in1=TR1[:, c0:c0 + n], op=ALU.mult),
                          reads=[("MU", i)], writes=[("LNT", 0)])
                    P.add("dve", C("tensor_tensor", out=LNT[0][:, c0:c0 + n], in0=banks[bvar[i]][:, :n], in1=LNT[0][:, c0:c0 + n], op=ALU.subtract),
                          reads=[("ps", bvar[i]), ("LNT", 0)], writes=[("LNT", 0)])
                    P.add("act", C("activation", out=TR0[:, c0:c0 + n], in_=LNT[0][:, c0:c0 + n], func=AF.Ln, bias=EPSC[:, 0:1]),
                          reads=[("LNT", 0), "EPSC"], writes=seg_keys("RS", 0, c0, c0 + n))
                    P.add("act", C("activation", out=TR0[:, c0:c0 + n], in_=TR0[:, c0:c0 + n], func=AF.Exp, scale=-0.5),
                          reads=seg_keys("RS", 0, c0, c0 + n), writes=seg_keys("RS", 0, c0, c0 + n))
                rs_keys = seg_keys("RS", 0, 0, 592)
                for oc in range(16):
                    a = oc % 2
                    P.add("dve", C("tensor_tensor", out=LNT[a][:, 0:592], in0=cap(oc, 0, 592), in1=TR1[:, 0:592], op=ALU.subtract),
                          reads=ckeys(oc, 0, 592) + [("MU", 0), ("MU", 1)], writes=[("LNT", a)])
                    P.add("dve", C("tensor_tensor", out=LNT[a][:, 0:592], in0=LNT[a][:, 0:592], in1=TR0[:, 0:592], op=ALU.mult),
                          reads=[("LNT", a)] + rs_keys, writes=[("LNT", a)])
                    P.add("act", C("activation", out=Hap(oc, oa, ob), in_=LNT[a][:, 0:592], func=AF.Silu,
                                                                   bias=vcol(V_LNB + oc), scale=vcol(V_LNG + oc)),
                          reads=[("LNT", a), "VEC"], writes=seg_keys("H", oc, oa, ob))
                otl = tiles(oa, ob)
                for oc in range(16):
                    s2 = get_tile(("pw2", oc))
                    bks = proj_group(s2, 0, 128, KC, h_rhs, h_keys, otl)
                    issue_next()
                    for i, (t0, n) in enumerate(otl):
                        P.add("dve", C("scalar_tensor_tensor",
                            out=Xap(oc, t0, t0 + n), in0=banks[bks[i]][:, :n], scalar=vcol(V_BPW2 + oc),
                            in1=Xap(oc, t0, t0 + n), op0=ALU.add, op1=ALU.add),
                            reads=[("ps", bks[i]), "VEC"] + seg_keys("X", oc, t0, t0 + n), writes=seg_keys("X", oc, t0, t0 + n))

            conv_pass(0)
            checkpoint(2)
            conv_pass(1)
            checkpoint(3)
            outs.append(P.add("sp", C("dma_start", out=uT_out[:], in_=USAVE[:]), reads=["USAVE"], dma="usave"))
            P.fence()

            def mlp(l, a, b, gcol):
                rms_to_h(a, b, gcol)
                tl = tiles(a, b)
                for g in range(8):
                    for fc in range(8):
                        s = get_tile(("up", l, g * 8 + fc))
                        bks = proj_group(s, 0, 128, KC, h_rhs, h_keys, tl)
                        issue_next()
                        for i, (t0, n) in enumerate(tl):
                            P.add("act", C("activation", out=RT[i][:, :n], in_=banks[bks[i]][:, :n], func=AF.Relu),
                                  reads=[("ps", bks[i])], writes=[("RT", i)])
                        for i, (t0, n) in enumerate(tl):
                            P.add("act", C("activation",
                                out=HID[:, fc * NTOK + (t0 - a): fc * NTOK + (t0 - a) + n], in_=RT[i][:, :n], func=AF.Square),
                                reads=[("RT", i)], writes=[("HID", fc, i)])
                    for ocp in range(8):
                        s = get_tile(("down", l, g, ocp))
                        for o2 in range(2):
                            oc = 2 * ocp + o2
                            bks = proj_group(s, o2 * 128, 256, 8,
                                             lambda k, t0, n: HID[:, k * NTOK + (t0 - a): k * NTOK + (t0 - a) + n],
                                             lambda k, t0, n: [("HID", k, i) for i in range(3)], tl)
                            for i, (t0, n) in enumerate(tl):
                                P.add("dve", C("tensor_tensor",
                                    out=Xap(oc, t0, t0 + n), in0=banks[bks[i]][:, :n], in1=Xap(oc, t0, t0 + n), op=ALU.add),
                                    reads=[("ps", bks[i])] + seg_keys("X", oc, t0, t0 + n), writes=seg_keys("X", oc, t0, t0 + n))
                        issue_next()

            mlp(0, 30, TE, V_NMLP0)
            P.fence()
            checkpoint(4)

            P.add("pool", C("dma_start",
                out=KT[:].rearrange("p (c w) -> p c w", w=KTW)[:, :, NTOK:NTOK + 512],
                in_=ckT[:]), writes=["KTC"], dma="ktc")
            P.add("pool", C("dma_start",
                out=VB[:, 9 * 512:13 * 512].rearrange("p (s f) -> p s f", f=512),
                in_=cv_raw[:].rearrange("s k f -> k s f")), writes=["VBC"], dma="vbc")
            rms_to_h(30, TE, V_KVN)
            ktl = tiles(30, TE)
            CT = lambda t0, n: TAB[:, t0 - 30: t0 - 30 + n]
            ST = lambda t0, n: TAB[:, NTOK + t0 - 30: NTOK + t0 - 30 + n]
            def kproj(c):
                s = get_tile(("wk", c))
                bks = proj_group(s, 0, 128, KC, h_rhs, h_keys, ktl)
                issue_next()
                return bks

            def kpost(c, bks):
                kfs = [TR1[:, t0 - 30: t0 - 30 + n] for (t0, n) in ktl]
                tvs = [TR0[:, t0 - 30: t0 - 30 + n] for (t0, n) in ktl]
                tks = [seg_keys("RS", 0, t0 - 30, t0 - 30 + n) for (t0, n) in ktl]
                for i, (t0, n) in enumerate(ktl):
                    P.add("act", C("activation", out=kfs[i], in_=banks[bks[i]][:, :n], func=AF.Copy),
                          reads=[("ps", bks[i])], writes=[("KF", i)])
                for i, (t0, n) in enumerate(ktl):
                    P.add("pe", C("matmul", banks[bks[i]][:, :n], lhsT=PERMF[:], rhs=kfs[i], start=True, stop=True),
                          reads=[("KF", i), "PERMF"], writes=[("ps", bks[i])])
                for i, (t0, n) in enumerate(ktl):
                    P.add("dve", C("tensor_tensor", out=tvs[i], in0=banks[bks[i]][:, :n], in1=ST(t0, n), op=ALU.mult),
                          reads=[("ps", bks[i]), "TAB"], writes=tks[i])
                for i, (t0, n) in enumerate(ktl):
                    P.add("dve", C("tensor_tensor", out=kfs[i], in0=kfs[i], in1=CT(t0, n), op=ALU.mult),
                          reads=[("KF", i), "TAB"], writes=[("KF", i)])
                for i, (t0, n) in enumerate(ktl):
                    P.add("dve", C("tensor_tensor", out=kfs[i], in0=kfs[i], in1=tvs[i], op=ALU.add),
                          reads=[("KF", i)] + tks[i], writes=[("KF", i)])
                for i, (t0, n) in enumerate(ktl):
                    P.add("act", C("activation",
                        out=KT[:, c * KTW + t0 - 30: c * KTW + t0 - 30 + n], in_=kfs[i], func=AF.Copy),
                        reads=[("KF", i)], writes=[("KT", c, i)])
                    if i == 2:
                        assert t0 == 1054 and n == 160
                        P.add("act", C("activation", out=KOUT[:, c * 160:(c + 1) * 160], in_=kfs[i], func=AF.Copy),
                              reads=[("KF", i)], writes=["KOUT"])

            kprev = None
            for c in range(4):
                bks_ = kproj(c)
                if kprev is not None:
                    kpost(*kprev)
                kprev = (c, bks_)
            kpost(*kprev)
            outs.append(P.add("sp", C("dma_start", out=kT_out[:], in_=KOUT[:]), reads=["KOUT"], dma="kout"))
            checkpoint(5)
            vgroups = [[("blk", bi) for bi in range(6)], [("blk", bi) for bi in range(6, 9)] + [("smp", s_) for s_ in range(4)]]
            vst_ctr = [0]
            for grp in vgroups:
                gb_ = [bank() for _ in grp]
                for c in range(4):
                    s = get_tile(("wv", c))
                    for gi, (kind, idx) in enumerate(grp):
                        for kk_ in range(4):
                            kc = c * 4 + kk_
                            if kind == "blk":
                                t0 = 30 + 128 * idx
                                P.add("pe", C("matmul",
                                    banks[gb_[gi]][:, :], lhsT=Hap(kc, t0, t0 + 128), rhs=slot_ap(s, kk_ * 512, kk_ * 512 + 512),
                                    start=(kc == 0), stop=(kc == KC - 1)),
                                    reads=[("slot", s)] + seg_keys("H", kc, t0, t0 + 128), writes=[("ps", gb_[gi])])
                            else:
                                t0 = T_SMP + 8 * idx
                                P.add("pe", C("matmul",
                                    banks[gb_[gi]][0:8, :], lhsT=Hap(kc, t0, t0 + 8), rhs=slot_ap(s, kk_ * 512, kk_ * 512 + 512),
                                    start=(kc == 0), stop=(kc == KC - 1)),
                                    reads=[("slot", s)] + seg_keys("H", kc, t0, t0 + 8), writes=[("ps", gb_[gi])])
                    issue_next()
                for gi, (kind, idx) in enumerate(grp):
                    bkk = gb_[gi]
                    if kind == "blk":
                        P.add("act", C("activation", out=VB[:, idx * 512:(idx + 1) * 512], in_=banks[bkk][:, :], func=AF.Copy),
                              reads=[("ps", bkk)], writes=[("VB", idx)])
                        if idx == 8:
                            for hh in range(2):
                                P.add("act", C("activation", out=VST[hh][:, :], in_=banks[bkk][:, hh * 256:(hh + 1) * 256], func=AF.Copy),
                                      reads=[("ps", bkk)], writes=[("VST", hh)])
                                outs.append(P.add("sp", C("dma_start", out=v_out[:, hh * 256:(hh + 1) * 256], in_=VST[hh][:, :]),
                                                  reads=[("VST", hh)], dma="vst%d" % hh))
                    else:
                        P.add("act", C("activation", out=VN[0:8, idx * 512:(idx + 1) * 512], in_=banks[bkk][0:8, :], func=AF.Copy),
                              reads=[("ps", bkk)], writes=[("VN", idx)])
                        for hh in range(2):
                            P.add("act", C("activation", out=VST[hh][0:8, :], in_=banks[bkk][0:8, hh * 256:(hh + 1) * 256], func=AF.Copy),
                                  reads=[("ps", bkk)], writes=[("VST", hh)])
                            outs.append(P.add("sp", C("dma_start", out=v_new[idx, :, hh * 256:(hh + 1) * 256], in_=VST[hh][0:8, :]),
                                              reads=[("VST", hh)], dma="vst%d" % hh))

            def attn_pass(pa):
                if pa == 0:
                    qa, qb, obase = 158, 670, 670
                    blocks = [0, 1, 2, 3]
                    smp = []
                else:
                    qa, qb, obase = 670, 1214, 30
                    blocks = [4, 5, 6, 7]
                    smp = [0, 1, 2, 3]
                nq = qb - qa
                rms_to_h(qa, qb, V_NMIX1)
                qtl = tiles(qa, qb)

                def oT(oc, a, n):
                    return Hap(oc, obase + a, obase + a + n)

                def oT_keys(oc, a, n):
                    return seg_keys("H", oc, obase + a, obase + a + n)

                ev = [0]

                deferred = []

                def stage_q(oc):
                    u = oc % 2
                    s = get_tile(("wq", oc))
                    bks = [6, 7][:len(qtl)]
                    for k in range(KC):
                        for i, (t0, n) in enumerate(qtl):
                            P.add("pe", C("matmul", banks[bks[i]][:, :n], lhsT=slot_ap(s, k * 128, k * 128 + 128),
                                          rhs=Hap(k, t0, t0 + n), start=(k == 0), stop=(k == KC - 1)),
                                  reads=[("slot", s)] + seg_keys("H", k, t0, t0 + n), writes=[("ps", bks[i])])
                    issue_next()
                    for i, (t0, n) in enumerate(qtl):
                        off = t0 - qa
                        P.add("act", C("activation", out=QB[u][:, off:off + n], in_=banks[bks[i]][:, :n], func=AF.Copy),
                              reads=[("ps", bks[i])], writes=[("QB", u, i), "KOUT", ("VST", 0), ("VST", 1)])

                    def rope():
                        for i, (t0, n) in enumerate(qtl):
                            off = t0 - qa
                            b2 = bks[i]
                            P.add("pe", C("matmul", banks[b2][:, :n], lhsT=PERMB[:], rhs=QB[u][:, off:off + n], start=True, stop=True),
                                  reads=[("QB", u, i), "PERMB"], writes=[("ps", b2)])
                            t1 = TR1[:, off:off + n]
                            t2 = TR1[:, 600 + off:600 + off + n]
                            P.add("dve", C("tensor_tensor", out=t1, in0=banks[b2][:, :n], in1=ST(t0, n), op=ALU.mult),
                                  reads=[("ps", b2), "TAB"], writes=[("QTMP", 0, i)])
                            P.add("dve", C("tensor_tensor", out=t2, in0=QB[u][:, off:off + n], in1=CT(t0, n), op=ALU.mult),
                                  reads=[("QB", u, i), "TAB"], writes=[("QTMP", 1, i)])
                            P.add("dve", C("tensor_tensor", out=QT[u][:, off:off + n], in0=t1, in1=t2, op=ALU.add),
                                  reads=[("QTMP", 0, i), ("QTMP", 1, i)], writes=[("QT", u, i)])
                    deferred.append(rope)

                def run_deferred():
                    while deferred:
                        deferred.pop(0)()

                def unit_scores(oc, unit):
                    c = oc // 4
                    u = oc % 2
                    qt_keys = [("QT", u, i) for i in range(len(qtl))]
                    v = ev[0] % NEV
                    ev[0] += 1
                    bSh = [sbank(), sbank()]
                    if unit[0] == "blk":
                        m = unit[1]
                        tb = 158 + 128 * m
                        off = tb - qa
                        kprev = c * KTW + (tb - 128 - 30)
                        kown = c * KTW + (tb - 30)
                        for hd in range(2):
                            pr0, pr1 = hd * 64, hd * 64 + 64
                            for kh, kcol in enumerate((kprev, kown)):
                                P.add("pe", C("matmul", banks[bSh[hd]][:, kh * 128: kh * 128 + 128],
                                              lhsT=KT[pr0:pr1, kcol:kcol + 128], rhs=QT[u][pr0:pr1, off:off + 128], start=True, stop=True),
                                      reads=[("KT", c, 0), ("KT", c, 1), ("KT", c, 2)] + qt_keys, writes=[("ps", bSh[hd])])
                        for hd in range(2):
                            P.add("act", C("activation", out=PT[v][:, hd * 256:(hd + 1) * 256], in_=banks[bSh[hd]][:, 0:256], func=AF.Exp, scale=0.125),
                                  reads=[("ps", bSh[hd])], writes=[("PT", v)])
                        M4 = MASK40 if m == 0 else MASK4
                        P.add("dve", C("tensor_tensor", out=EB[v][:, :], in0=PT[v][:, :], in1=M4[:, :], op=ALU.mult),
                              reads=[("PT", v), "MASK4"], writes=[("EB", v)])
                    else:
                        for hd in range(2):
                            pr0, pr1 = hd * 64, hd * 64 + 64
                            P.add("pe", C("matmul", banks[bSh[hd]][:, 0:32], lhsT=IDENT[:, :], rhs=MASKNS[:, 0:32], start=True, stop=False),
                                  reads=["IDENT", "MASKNS"], writes=[("ps", bSh[hd])])
                            for sq_ in range(4):
                                off = (T_SMP - qa) + 8 * sq_
                                kcache = c * KTW + NTOK + sq_ * 128
                                P.add("pe", C("matmul", banks[bSh[hd]][:, sq_ * 8: sq_ * 8 + 8],
                                              lhsT=KT[pr0:pr1, kcache:kcache + 128], rhs=QT[u][pr0:pr1, off:off + 8], start=False, stop=(sq_ == 3)),
                                      reads=["KTC"] + qt_keys, writes=[("ps", bSh[hd])])
                            P.add("pe", C("matmul", banks[bSh[hd]][0:8, 32:64], lhsT=IDENT[0:8, 0:8], rhs=MASKNS[0:8, 32:64], start=True, stop=False),
                                  reads=["IDENT", "MASKNS"], writes=[("ps", bSh[hd])])
                            for sq_ in range(4):
                                off = (T_SMP - qa) + 8 * sq_
                                knew = c * KTW + (T_SMP + 8 * sq_ - 30)
                                P.add("pe", C("matmul", banks[bSh[hd]][0:8, 32 + sq_ * 8: 40 + sq_ * 8],
                                              lhsT=KT[pr0:pr1, knew:knew + 8], rhs=QT[u][pr0:pr1, off:off + 8], start=False, stop=(sq_ == 3)),
                                      reads=[("KT", c, 2)] + qt_keys, writes=[("ps", bSh[hd])])
                        for hd in range(2):
                            P.add("act", C("activation", out=EB[v][:, hd * 64: hd * 64 + 32], in_=banks[bSh[hd]][:, 0:32], func=AF.Exp, scale=0.125),
                                  reads=[("ps", bSh[hd])], writes=[("EB", v)])
                            P.add("act", C("activation", out=EB[v][0:8, hd * 64 + 32: hd * 64 + 64], in_=banks[bSh[hd]][0:8, 32:64], func=AF.Exp, scale=0.125),
                                  reads=[("ps", bSh[hd])], writes=[("EB", v)])
                    return v

                def unit_pv(oc, unit, v):
                    c = oc // 4
                    kva, kvb = 2 * c, 2 * c + 1
                    bO = obank()
                    if unit[0] == "blk":
                        m = unit[1]
                        off = 158 + 128 * m - qa
                        w = 128
                        for hd, kv in enumerate((kva, kvb)):
                            pr0, pr1 = hd * 64, hd * 64 + 64
                            for kh, vblk in enumerate((m, m + 1)):
                                P.add("pe", C("matmul", banks[bO][pr0:pr1, 0:128], lhsT=VB[:, vblk * 512 + kv * 64: vblk * 512 + kv * 64 + 64],
                                              rhs=EB[v][:, hd * 256 + kh * 128: hd * 256 + kh * 128 + 128], start=(kh == 0), stop=(kh == 1)),
                                      reads=[("VB", vblk), ("EB", v)], writes=[("ps", bO)])
                            for kh in range(2):
                                P.add("pe", C("matmul", banks[bO][pr0:pr1, 128:256], lhsT=ONE1[:, 0:64],
                                              rhs=EB[v][:, hd * 256 + kh * 128: hd * 256 + kh * 128 + 128], start=(kh == 0), stop=(kh == 1)),
                                      reads=["ONE1", ("EB", v)], writes=[("ps", bO)])
                    else:
                        off = T_SMP - qa
                        w = 32
                        for hd, kv in enumerate((kva, kvb)):
                            pr0, pr1 = hd * 64, hd * 64 + 64
                            for sq_ in range(4):
                                rc = EB[v][:, hd * 64 + sq_ * 8: hd * 64 + sq_ * 8 + 8]
                                rn = EB[v][0:8, hd * 64 + 32 + sq_ * 8: hd * 64 + 40 + sq_ * 8]
                                P.add("pe", C("matmul", banks[bO][pr0:pr1, sq_ * 8: sq_ * 8 + 8],
                                              lhsT=VB[:, (9 + sq_) * 512 + kv * 64: (9 + sq_) * 512 + kv * 64 + 64], rhs=rc, start=True, stop=False),
                                      reads=["VBC", ("EB", v)], writes=[("ps", bO)])
                                P.add("pe", C("matmul", banks[bO][pr0:pr1, sq_ * 8: sq_ * 8 + 8],
                                              lhsT=VN[0:8, sq_ * 512 + kv * 64: sq_ * 512 + kv * 64 + 64], rhs=rn, start=False, stop=True),
                                      reads=[("VN", sq_), ("EB", v)], writes=[("ps", bO)])
                                P.add("pe", C("matmul", banks[bO][pr0:pr1, 128 + sq_ * 8: 136 + sq_ * 8], lhsT=ONE1[:, 0:64], rhs=rc, start=True, stop=False),
                                      reads=["ONE1", ("EB", v)], writes=[("ps", bO)])
                                P.add("pe", C("matmul", banks[bO][pr0:pr1, 128 + sq_ * 8: 136 + sq_ * 8], lhsT=ONE1[0:8, 0:64], rhs=rn, start=False, stop=True),
                                      reads=["ONE1", ("EB", v)], writes=[("ps", bO)])
                    P.add("act", C("activation", out=DEN[v][:, 0:w], in_=banks[bO][:, 128:128 + w], func=AF.Ln, bias=ES[:, oc:oc + 1]),
                          reads=[("ps", bO), "ES"], writes=[("DEN", v)])
                    P.add("act", C("activation", out=DEN[v][:, 0:w], in_=DEN[v][:, 0:w], func=AF.Exp, scale=-1.0),
                          reads=[("DEN", v)], writes=[("DEN", v)])
                    P.add("dve", C("tensor_tensor", out=oT(oc, off, w), in0=banks[bO][:, 0:w], in1=DEN[v][:, 0:w], op=ALU.mult),
                          reads=[("ps", bO), ("DEN", v)], writes=oT_keys(oc, off, w))

                units = [("blk", m) for m in blocks] + ([("smp", 0)] if smp else [])
                seq = [(oc, un) for oc in range(16) for un in units]
                sctr = [0]
                octr = [0]

                def sbank():
                    b_ = sctr[0] % 4
                    sctr[0] += 1
                    return b_

                def obank():
                    b_ = 4 + octr[0] % 2
                    octr[0] += 1
                    return b_

                stage_q(0)
                run_deferred()
                pend = []
                for idx, (oc, un) in enumerate(seq):
                    if un is units[0] and oc + 1 < 16:
                        stage_q(oc + 1)
                    v = unit_scores(oc, un)
                    pend.append((oc, un, v))
                    if len(pend) > 2:
                        unit_pv(*pend.pop(0))
                    run_deferred()
                while pend:
                    unit_pv(*pend.pop(0))
                otl = [(t0 - qa, n) for (t0, n) in qtl]
                for oc in range(16):
                    s = get_tile(("wo", oc))
                    bks = proj_group(s, 0, 128, KC,
                                     lambda k, o0, n: oT(k, o0, n), lambda k, o0, n: oT_keys(k, o0, n), otl)
                    issue_next()
                    for i, (o0, n) in enumerate(otl):
                        t0 = qa + o0
                        P.add("dve", C("tensor_tensor",
                            out=Xap(oc, t0, t0 + n), in0=banks[bks[i]][:, :n], in1=Xap(oc, t0, t0 + n), op=ALU.add),
                            reads=[("ps", bks[i])] + seg_keys("X", oc, t0, t0 + n), writes=seg_keys("X", oc, t0, t0 + n))

            checkpoint(6)
            attn_pass(0)
            checkpoint(7)
            attn_pass(1)
            checkpoint(8)
            P.fence()

            mlp(1, 158, TE, V_NMLP1)
            checkpoint(9)

            rms_stats(158, TE)
            yc = [0]
            for (t0, n) in tiles(158, TE):
                for kc in range(KC):
                    yb = yc[0] % 2
                    yc[0] += 1
                    P.add("dve", C("scalar_tensor_tensor",
                        out=YST[yb][:, :n], in0=Xap(kc, t0, t0 + n), scalar=vcol(V_FIN + kc),
                        in1=TR0[:, t0:t0 + n], op0=ALU.mult, op1=ALU.mult),
                        reads=seg_keys("X", kc, t0, t0 + n) + seg_keys("RS", 0, t0, t0 + n) + ["VEC"], writes=[("YST", yb)])
                    outs.append(P.add("sp", C("dma_start",
                        out=yT[:, kc, t0 - 158: t0 - 158 + n], in_=YST[yb][:, :n]), reads=[("YST", yb)], dma="yst%d" % yb))

            assert wstate["consumed"] == len(order), (wstate, len(order))
        except _Stop:
            pass
        P.finalize(outs)
        P.emit(nc)
    return nc


def _tile_kc(W, c0):
    return np.ascontiguousarray(W[:, c0:c0 + 128].reshape(16, 128, 128).transpose(1, 0, 2).reshape(128, 2048))


def _col(v):
    return np.ascontiguousarray(np.asarray(v, np.float32).reshape(16, 128).T)


_NC_CACHE = {}


def prepare(x_prompt, x_sample, state_conv, cache_k, cache_v, norm_mix, w_pw1, b_pw1, w_dw, b_dw,
           conv_ln_g, conv_ln_b, w_pw2, b_pw2, kv_norm, w_k, w_v, w_q, w_o, sinks, norm_mlp,
           w_up, w_down, final_norm):
    f = lambda a: np.asarray(a, dtype=np.float32)
    x_prompt, x_sample, state_conv, cache_k, cache_v = map(f, (x_prompt, x_sample, state_conv, cache_k, cache_v))
    w_pw1, w_pw2, w_k, w_v, w_q, w_o, w_up, w_down = map(f, (w_pw1, w_pw2, w_k, w_v, w_q, w_o, w_up, w_down))
    norm_mix, norm_mlp, b_pw1, w_dw, b_dw = map(f, (norm_mix, norm_mlp, b_pw1, w_dw, b_dw))
    conv_ln_g, conv_ln_b, b_pw2, kv_norm, sinks, final_norm = map(f, (conv_ln_g, conv_ln_b, b_pw2, kv_norm, sinks, final_norm))

    tidx = unique_tiles()

    qperm = np.zeros(2048, np.int64)
    head_of = np.zeros((16, 2), np.int64)
    for oc in range(16):
        c, i = oc // 4, oc % 4
        for hd in range(2):
            h = 8 * c + 4 * hd + i
            head_of[oc, hd] = h
            qperm[oc * 128 + hd * 64: oc * 128 + hd * 64 + 64] = h * 64 + np.arange(64)
    wq_p = w_q[0][:, qperm]
    wo_p = w_o[0][qperm, :]

    wst = np.empty((len(tidx), 128, 2048), np.float32)
    for nm, ti in tidx.items():
        kind = nm[0]
        if kind == "pw1a":
            wst[ti] = _tile_kc(w_pw1[0], nm[1] * 128)
        elif kind == "pw1g":
            wst[ti] = _tile_kc(w_pw1[0], 2048 + nm[1] * 128)
        elif kind == "pw2":
            wst[ti] = _tile_kc(w_pw2[0], nm[1] * 128)
        elif kind == "up":
            wst[ti] = _tile_kc(w_up[nm[1]], nm[2] * 128)
        elif kind == "down":
            l, g, ocp = nm[1], nm[2], nm[3]
            wst[ti] = w_down[l][g * 1024:(g + 1) * 1024, ocp * 256:(ocp + 1) * 256].reshape(8, 128, 256).transpose(1, 0, 2).reshape(128, 2048)
        elif kind == "wk":
            wst[ti] = _tile_kc(w_k, nm[1] * 128)
        elif kind == "wv":
            c = nm[1]
            wst[ti] = w_v[c * 512:(c + 1) * 512, :].reshape(4, 128, 512).transpose(1, 0, 2).reshape(128, 2048)
        elif kind == "wq":
            wst[ti] = _tile_kc(wq_p, nm[1] * 128)
        elif kind == "wo":
            wst[ti] = _tile_kc(wo_p, nm[1] * 128)
        else:
            raise AssertionError(nm)

    vec = np.zeros((128, NV), np.float32)
    vec[:, V_NMIX0:V_NMIX0 + 16] = _col(norm_mix[0])
    vec[:, V_NMLP0:V_NMLP0 + 16] = _col(norm_mlp[0])
    vec[:, V_KVN:V_KVN + 16] = _col(kv_norm)
    vec[:, V_NMIX1:V_NMIX1 + 16] = _col(norm_mix[1])
    vec[:, V_NMLP1:V_NMLP1 + 16] = _col(norm_mlp[1])
    vec[:, V_FIN:V_FIN + 16] = _col(final_norm)
    vec[:, V_BA:V_BA + 16] = _col(b_pw1[0][:2048])
    vec[:, V_BG:V_BG + 16] = _col(b_pw1[0][2048:])
    vec[:, V_BDW:V_BDW + 16] = _col(b_dw[0])
    vec[:, V_LNG:V_LNG + 16] = _col(conv_ln_g[0])
    vec[:, V_LNB:V_LNB + 16] = _col(conv_ln_b[0])
    vec[:, V_BPW2:V_BPW2 + 16] = _col(b_pw2[0])
    for oc in range(16):
        vec[0:64, V_SINK + oc] = sinks[0][head_of[oc, 0]]
        vec[64:128, V_SINK + oc] = sinks[0][head_of[oc, 1]]
    vec[:, V_WDW:] = w_dw[0].T.reshape(16, 128, CW).transpose(1, 0, 2).reshape(128, 16 * CW)

    mask_prev = (np.arange(128)[:, None] > np.arange(128)[None, :]).astype(np.float32)
    mask_own = (np.arange(128)[:, None] <= np.arange(128)[None, :]).astype(np.float32)
    m4 = np.concatenate([mask_prev, mask_own, mask_prev, mask_own], axis=1)
    m4_first = m4.copy()
    m4_first[:, 0:128] = 0.0
    m4_first[:, 256:384] = 0.0
    ms = np.zeros((128, 32), np.float32)
    ms[:, 0:8] = mask_prev[:, 0:8]
    ms[:, 8:16] = mask_prev[:, 0:8]
    ms[0:8, 16:24] = mask_own[0:8, 0:8]
    ms[0:8, 24:32] = mask_own[0:8, 0:8]
    NEG = np.float32(-240000.0)
    mn = np.concatenate([(1.0 - mask_prev) * NEG, (1.0 - mask_own) * NEG], axis=1).astype(np.float32)
    mn_first = mn.copy()
    mn_first[:, 0:128] = NEG
    mns = np.zeros((128, 64), np.float32)
    for s_ in range(4):
        mns[:, s_ * 8:(s_ + 1) * 8] = (1.0 - mask_prev[:, 0:8]) * NEG
        mns[0:8, 32 + s_ * 8: 40 + s_ * 8] = (1.0 - mask_own[0:8, 0:8]) * NEG
    eye = np.eye(128, dtype=np.float32)
    pm = np.zeros((128, 128), np.float32)
    sgn = np.zeros(128, np.float32)
    fidx = np.full(128, -1, np.int64)
    for m_ in range(128):
        d = m_ % 64
        if d < 8:
            pm[m_ + 8, m_] = 1.0
            sgn[m_] = -1.0
            fidx[m_] = d
        elif d < 16:
            pm[m_ - 8, m_] = 1.0
            sgn[m_] = 1.0
            fidx[m_] = d - 8
    inv = (np.float32(500000.0) ** (-np.arange(8, dtype=np.float32) / np.float32(8))).astype(np.float32)

    in_maps = []
    for core in range(NCORE):
        b, j = core // 4, core % 4
        xe = np.zeros((TE, D), np.float32)
        if j > 0:
            xe[0:158] = x_prompt[b, 1024 * j - 158: 1024 * j]
        xe[158:1182] = x_prompt[b, 1024 * j: 1024 * (j + 1)]
        xe[1182:1214] = x_sample[4 * core: 4 * core + 4].reshape(32, D)
        xTc = np.ascontiguousarray(xe.T.reshape(16, 128, TE).transpose(1, 0, 2))
        sc = state_conv[0, 4 * core: 4 * core + 4]
        ctxTc = np.ascontiguousarray(sc.transpose(2, 0, 1).reshape(16, 128, 4, 30).transpose(1, 0, 2, 3))
        ck = cache_k[4 * core: 4 * core + 4].reshape(4, 128, 512)
        cv = cache_v[4 * core: 4 * core + 4].reshape(4, 128, 512)
        ckTc = np.ascontiguousarray(ck.transpose(2, 0, 1).reshape(4, 128, 4 * 128).transpose(1, 0, 2))
        pos = np.zeros(NTOK, np.float32)
        pos[0:1152] = (1024 * j - 128) + np.arange(1152)
        pos[1152:1184] = 16384 + (np.arange(32) % 8)
        tabc = np.zeros((128, 2 * NTOK), np.float32)
        tabc[:, 0:NTOK] = 1.0
        for m_ in range(128):
            if fidx[m_] >= 0:
                ang = (pos * inv[fidx[m_]]).astype(np.float32)
                tabc[m_, 0:NTOK] = np.cos(ang).astype(np.float32)
                tabc[m_, NTOK:] = sgn[m_] * np.sin(ang).astype(np.float32)
        v = vec.copy()
        v[:, V_FLAG] = 0.0 if j == 0 else 1.0
        in_maps.append({
            "xT": xTc, "wst": wst, "ctxT": ctxTc, "ckT": ckTc,
            "sc_raw": np.ascontiguousarray(sc), "ck_raw": np.ascontiguousarray(ck), "cv_raw": np.ascontiguousarray(cv),
            "vecs": v, "tab": tabc, "mask4": m4, "masks": ms, "perm": pm,
            "mask40": (m4_first if j == 0 else m4), "ident": eye, "maskns": mns,
        })

    return in_maps


def kernel(**inputs):
    in_maps = prepare(**inputs)
    if "nc" not in _NC_CACHE:
        _NC_CACHE["nc"] = build()
    nc = _NC_CACHE["nc"]
    res = run_bass_kernel_spmd(nc, in_maps, core_ids=list(range(NCORE)))
    return assemble(res.results)


def assemble(R_):

    y_prompt = np.zeros((2, 4096, D), np.float32)
    y_sample = np.zeros((32, 8, D), np.float32)
    conv_prompt = np.zeros((1, 2, 30, D), np.float32)
    conv_sample = np.zeros((1, 32, 30, D), np.float32)
    win_k_prompt = np.zeros((2, 128, 8, 64), np.float32)
    win_v_prompt = np.zeros((2, 128, 8, 64), np.float32)
    win_k_sample = np.zeros((32, 128, 8, 64), np.float32)
    win_v_sample = np.zeros((32, 128, 8, 64), np.float32)
    for core in range(NCORE):
        b, j = core // 4, core % 4
        r = R_[core]
        y = np.transpose(r["yT"], (2, 1, 0)).reshape(NQ, D)
        y_prompt[b, 1024 * j: 1024 * (j + 1)] = y[0:1024]
        y_sample[4 * core: 4 * core + 4] = y[1024:1056].reshape(4, 8, D)
        uo = r["uT_out"].reshape(128, 16, 62)
        ut = np.transpose(uo, (2, 1, 0)).reshape(62, D)
        kt = np.transpose(r["kT_out"].reshape(128, 4, 160), (2, 1, 0)).reshape(160, 512)
        if j == 3:
            conv_prompt[0, b] = ut[0:30]
            win_k_prompt[b] = kt[0:128].reshape(128, 8, 64)
            win_v_prompt[b] = r["v_out"].reshape(128, 8, 64)
        for s in range(4):
            q = 4 * core + s
            conv_sample[0, q, 0:22] = r["conv_old"][s]
            conv_sample[0, q, 22:30] = ut[30 + 8 * s: 30 + 8 * s + 8]
            win_k_sample[q, 0:120] = r["k_old"][s].reshape(120, 8, 64)
            win_k_sample[q, 120:128] = kt[128 + 8 * s: 128 + 8 * s + 8].reshape(8, 8, 64)
            win_v_sample[q, 0:120] = r["v_old"][s].reshape(120, 8, 64)
            win_v_sample[q, 120:128] = r["v_new"][s].reshape(8, 8, 64)
    return (y_prompt, y_sample, conv_prompt, conv_sample, win_k_prompt, win_v_prompt, win_k_sample, win_v_sample)
```

```python
import contextlib
import os
import numpy as np
import concourse.bass as bass
import concourse.mybir as mybir
from concourse.bass_utils import run_bass_kernel_spmd

F32 = mybir.dt.float32
BF16 = mybir.dt.bfloat16
AF = mybir.ActivationFunctionType
ALU = mybir.AluOpType

ENGS = ("pe", "act", "dve", "pool", "sp")


def C(name, *a, **k):
    return (name, a, k)


class Op:
    __slots__ = ("eng", "fn", "deps", "signal", "sem", "val", "pos", "dma")

    def __init__(self, eng, fn, dma=None):
        self.eng = eng
        self.fn = fn
        self.deps = []
        self.signal = False
        self.sem = None
        self.val = 0
        self.pos = 0
        self.dma = dma


class _Stop(Exception):
    pass


STOP = float(os.environ.get("MK_STOP", "99"))


def checkpoint(k):
    if STOP <= k:
        raise _Stop()


class Prog:
    SAFE_DIST = 10 ** 9

    def __init__(self):
        self.streams = {e: [] for e in ENGS}
        self.last_w = {}
        self.readers = {}
        self.last_op = {}
        self.pending_fence = {e: [] for e in ENGS}
        self.dma_pending = []

    def add(self, eng, fn, reads=(), writes=(), dma=None):
        op = Op(eng, fn, dma=dma)
        if dma is not None:
            op.signal = True
        deps = set()
        for r in reads:
            w = self.last_w.get(r)
            if w is not None:
                deps.add(w)
        for wkey in writes:
            w = self.last_w.get(wkey)
            if w is not None:
                deps.add(w)
            for rd in self.readers.get(wkey, ()):
                deps.add(rd)
        for f in self.pending_fence[eng]:
            deps.add(f)
        self.pending_fence[eng] = []
        op.pos = len(self.streams[eng])
        keep = []
        latest = {}
        for d in deps:
            if d.dma is None and d.eng == eng:
                if eng == "pe" or eng == "sp":
                    continue
                if op.pos - d.pos >= self.SAFE_DIST:
                    continue
            if d.dma is None:
                cur = latest.get(d.eng)
                if cur is None or d.pos > cur.pos:
                    latest[d.eng] = d
            else:
                keep.append(d)
        keep.extend(latest.values())
        op.deps = keep
        for d in keep:
            d.signal = True
        self.streams[eng].append(op)
        if dma is None:
            self.last_op[eng] = op
        elif not dma.startswith("slot"):
            self.dma_pending.append(op)
        for r in reads:
            self.readers.setdefault(r, []).append(op)
        for wkey in writes:
            self.last_w[wkey] = op
            self.readers[wkey] = []
        return op

    def fence(self):
        lasts = [o for o in self.last_op.values()]
        dmas = self.dma_pending
        self.dma_pending = []
        for e in ENGS:
            self.pending_fence[e] = [o for o in lasts if o.eng != e] + dmas

    def finalize(self, final_ops):
        for o in final_ops:
            o.signal = True
        counters = {}
        for e in ENGS:
            for op in self.streams[e]:
                if not op.signal:
                    continue
                if op.dma is not None:
                    key = ("dma", op.dma)
                    counters[key] = counters.get(key, 0) + 16
                else:
                    key = ("eng", e)
                    counters[key] = counters.get(key, 0) + 1
                op.sem = key
                op.val = counters[key]
        self.sem_keys = list(counters.keys())
        self.final_ops = final_ops

    def emit(self, nc):
        with contextlib.ExitStack() as st:
            sems = {}
            for i, k in enumerate(self.sem_keys):
                sems[k] = st.enter_context(nc.semaphore("s%d" % i))
            block = st.enter_context(nc.Block())
            prog = self

            def run(ename, e, is_last=False):
                waited = {}
                for op in prog.streams[ename]:
                    for d in op.deps:
                        if waited.get(d.sem, 0) < d.val:
                            e.wait_ge(sems[d.sem], d.val)
                            waited[d.sem] = d.val
                    name_, a_, k_ = op.fn
                    ins = getattr(e, name_)(*a_, **k_)
                    if op.signal:
                        ins.then_inc(sems[op.sem], 16 if op.dma is not None else 1)
                if is_last:
                    for d in prog.final_ops:
                        if waited.get(d.sem, 0) < d.val:
                            e.wait_ge(sems[d.sem], d.val)
                            waited[d.sem] = d.val

            @block.sync
            def _(e):
                run("sp", e, is_last=True)

            @block.scalar
            def _(e):
                run("act", e)

            @block.vector
            def _(e):
                run("dve", e)

            @block.gpsimd
            def _(e):
                run("pool", e)

            @block.tensor
            def _(e):
                run("pe", e)


NCORE = 8
D = 2048
KC = 16
TE = 1214
T_PRE, T_HALO, T_MAIN, T_SMP = 0, 30, 158, 1182
NTOK = TE - 30
NQ = TE - 158
SEGS = [0, 30, 158, 592, 622, 670, 1214]
NS = 4
CW = 31
NDV = 8
EPS = 1e-6
UW = 742

V_NMIX0, V_NMLP0, V_KVN, V_NMIX1, V_NMLP1, V_FIN = 0, 16, 32, 48, 64, 80
V_BA, V_BG, V_BDW, V_LNG, V_LNB, V_BPW2, V_SINK = 96, 112, 128, 144, 160, 176, 192
V_FLAG = 208
V_WDW = 212
NV = V_WDW + 16 * CW


def seg_keys(name, kc, a, b):
    return [(name, kc, i) for i in range(len(SEGS) - 1) if SEGS[i] < b and SEGS[i + 1] > a]


def tiles(a, b):
    out = []
    t = a
    while t < b:
        n = min(512, b - t)
        out.append((t, n))
        t += n
    return out


def weight_order():
    order = []
    for _pass in range(2):
        for oc in range(16):
            order.append(("pw1a", oc))
            order.append(("pw1g", oc))
        for oc in range(16):
            order.append(("pw2", oc))
    for g in range(8):
        for fc in range(8):
            order.append(("up", 0, g * 8 + fc))
        for ocp in range(8):
            order.append(("down", 0, g, ocp))
    for c in range(4):
        order.append(("wk", c))
    for _grp in range(2):
        for c in range(4):
            order.append(("wv", c))
    for _pass in range(2):
        for oc in range(16):
            order.append(("wq", oc))
        for oc in range(16):
            order.append(("wo", oc))
    for g in range(8):
        for fc in range(8):
            order.append(("up", 1, g * 8 + fc))
        for ocp in range(8):
            order.append(("down", 1, g, ocp))
    return order


def unique_tiles():
    seen = {}
    for nm in weight_order():
        if nm not in seen:
            seen[nm] = len(seen)
    return seen


def build():
    nc = bass.Bass("TRN2", target_bir_lowering=False)
    tidx = unique_tiles()
    NT = len(tidx)
    order = weight_order()

    def din(name, shape):
        return nc.dram_tensor(name, shape, F32, kind="ExternalInput").ap()

    def dout(name, shape):
        return nc.dram_tensor(name, shape, F32, kind="ExternalOutput").ap()

    xT = din("xT", [128, KC, TE])
    wst = din("wst", [NT, 128, 2048])
    ctxT = din("ctxT", [128, KC, 4, 30])
    ckT = din("ckT", [128, 4, 512])
    sc_raw = din("sc_raw", [4, 30, D])
    ck_raw = din("ck_raw", [4, 128, 512])
    cv_raw = din("cv_raw", [4, 128, 512])
    vecs = din("vecs", [128, NV])
    tab = din("tab", [128, 2 * NTOK])
    mask4 = din("mask4", [128, 512])
    masks = din("masks", [128, 32])
    mask40 = din("mask40", [128, 512])
    perm = din("perm", [128, 128])
    ident = din("ident", [128, 128])
    maskns = din("maskns", [128, 64])

    yT = dout("yT", [128, KC, NQ])
    uT_out = dout("uT_out", [128, KC * 62])
    conv_old = dout("conv_old", [4, 22, D])
    kT_out = dout("kT_out", [128, 4 * 160])
    v_out = dout("v_out", [128, 512])
    v_new = dout("v_new", [4, 8, 512])
    k_old = dout("k_old", [4, 120, 512])
    v_old = dout("v_old", [4, 120, 512])

    P = Prog()
    outs = []
    with contextlib.ExitStack() as st:
        def sb(name, cols, dt):
            return st.enter_context(nc.sbuf_tensor(name, [128, cols], dt))

        X = sb("X", KC * TE, F32)
        H = sb("H", KC * TE, BF16)
        RING = sb("RING", NS * 2048, BF16)
        VEC = sb("VEC", NV, F32)
        TAB = sb("TAB", 2 * NTOK, F32)
        TR0 = sb("TR0", TE, F32)
        TR1 = sb("TR1", TE, F32)
        SQ = sb("SQ", 4 * 512, BF16)
        ONESB = sb("ONESB", 128, BF16)
        ONE1 = sb("ONE1", 64, BF16)
        PERMF = sb("PERMF", 128, F32)
        PERMB = sb("PERMB", 128, BF16)
        MASK4 = sb("MASK4", 512, BF16)
        MASKS = sb("MASKS", 32, BF16)
        MASK40 = sb("MASK40", 512, BF16)
        ES = sb("ES", 16, F32)
        EPSC = sb("EPSC", 1, F32)
        IDENT = sb("IDENT", 128, BF16)
        MASKNS = sb("MASKNS", 64, BF16)
        RBYTES = 44288
        R = sb("R", RBYTES // 4, F32)
        banks = [st.enter_context(nc.psum_tensor("ps%d" % i, [128, 512], F32)) for i in range(8)]

        def carve(off, ncols, dt):
            sz = 4 if dt == F32 else 2
            assert off % 4 == 0 and (ncols * sz) % 4 == 0
            assert off + ncols * sz <= RBYTES, (off, ncols, sz)
            v = R[:, off // 4: off // 4 + (ncols * sz) // 4]
            return v if dt == F32 else v.bitcast(BF16)

        o = 0
        SIG = [carve(o + i * 2488, 622, F32) for i in range(2)]; o += 2 * 2488
        U = [carve(o + i * UW * 4, UW, F32) for i in range(2)]; o += 2 * UW * 4
        ACCD = [carve(o + i * 2368, 592, F32) for i in range(2)]; o += 2 * 2368
        UB = [carve(o + i * 1488, 744, BF16) for i in range(2)]; o += 2 * 1488
        NDG = 32
        DG = carve(o, NDG * 128, BF16); o += NDG * 256
        UC = carve(o, 16 * 30, F32); o += 16 * 30 * 4
        LNT = [carve(o + i * 2368, 592, F32) for i in range(2)]; o += 2 * 2368
        USAVE = carve(o, 16 * 62, F32); o += 16 * 62 * 4
        assert o <= RBYTES, o
        o = 0
        HID = carve(o, 8 * NTOK, BF16); o += 8 * NTOK * 2
        RT = [carve(o + i * 2048, 512, F32) for i in range(3)]; o += 3 * 2048
        NYST = 6
        YST = [carve(o + i * 2048, 512, F32) for i in range(NYST)]; o += NYST * 2048
        assert o <= RBYTES, o
        KTW = NTOK + 512
        o = 0
        KT = carve(o, 4 * KTW, BF16); o += 4 * KTW * 2
        VB = carve(o, 13 * 512, BF16); o += 13 * 512 * 2
        VN = carve(o, 4 * 512, BF16); o += 4 * 512 * 2
        o_alias = o
        QB = [carve(o + i * 1088, 544, BF16) for i in range(2)]; o += 2 * 1088
        QT = [carve(o + i * 1088, 544, BF16) for i in range(2)]; o += 2 * 1088
        KOUT = carve(o_alias, 4 * 160, F32)
        VST = [carve(o_alias + 2560 + i * 1024, 256, F32) for i in range(2)]
        assert o_alias + 2560 + 2048 <= o + 256
        o = max(o, o_alias + 2560 + 2048)
        NEV = 3
        EB = [carve(o + i * 1024, 512, BF16) for i in range(NEV)]; o += NEV * 1024
        PT = [carve(o + i * 1024, 512, BF16) for i in range(NEV)]; o += NEV * 1024
        DEN = [carve(o + i * 512, 128, F32) for i in range(NEV)]; o += NEV * 512
        assert o <= RBYTES, o

        def Xap(kc, a, b):
            return X[:, kc * TE + a: kc * TE + b]

        def Hap(kc, a, b):
            return H[:, kc * TE + a: kc * TE + b]

        def vcol(c):
            return VEC[:, c:c + 1]

        bank_ctr = [0]
        dgc = [0]

        def bank():
            b = bank_ctr[0] % 8
            bank_ctr[0] += 1
            return b

        wstate = {"issued": 0, "consumed": 0}

        def issue_next():
            n = wstate["issued"]
            if n >= len(order):
                return
            s = n % NS
            ti = tidx[order[n]]
            P.add("pool", C("dma_start", out=RING[:, s * 2048:(s + 1) * 2048], in_=wst[ti]),
                  writes=[("slot", s)], dma="slot%d" % s)
            wstate["issued"] = n + 1

        def get_tile(name):
            n = wstate["consumed"]
            assert order[n] == name, (order[n], name)
            s = n % NS
            wstate["consumed"] = n + 1
            return s

        def slot_ap(s, a, b):
            return RING[:, s * 2048 + a: s * 2048 + b]

        P.add("sp", C("dma_start", out=VEC[:], in_=vecs[:]), writes=["VEC"], dma="vec")
        for (ta, tb_) in ((0, 622), (622, TE)):
            for k0 in range(0, KC, 8):
                wk = []
                for kc in range(k0, k0 + 8):
                    wk += seg_keys("X", kc, ta, tb_)
                P.add("sp", C("dma_start",
                    out=X[:, k0 * TE:(k0 + 8) * TE].rearrange("p (k t) -> p k t", t=TE)[:, :, ta:tb_], in_=xT[:, k0:k0 + 8, ta:tb_]),
                    writes=wk, dma="xl%d_%d" % (k0, ta))
        P.add("sp", C("dma_start", out=TAB[:], in_=tab[:]), writes=["TAB"], dma="tab")
        P.add("sp", C("dma_start", out=PERMF[:], in_=perm[:]), writes=["PERMF"], dma="permf")
        P.add("pool", C("dma_start", out=PERMB[:], in_=perm[:]), writes=["PERMB"], dma="permb")
        P.add("pool", C("dma_start", out=MASK4[:], in_=mask4[:]), writes=["MASK4"], dma="mask4")
        P.add("pool", C("dma_start", out=IDENT[:], in_=ident[:]), writes=["IDENT"], dma="ident")
        P.add("pool", C("dma_start", out=MASKNS[:], in_=maskns[:]), writes=["MASKNS"], dma="maskns")
        P.add("pool", C("dma_start", out=MASKS[:], in_=masks[:]), writes=["MASKS"], dma="masks")
        P.add("pool", C("dma_start", out=MASK40[:], in_=mask40[:]), writes=["MASK4"], dma="mask40")
        for _ in range(NS):
            issue_next()
        P.add("dve", C("memset", ONESB[:], 1.0 / D), writes=["ONESB"])
        P.add("dve", C("memset", ONE1[:], 1.0), writes=["ONE1"])
        P.add("dve", C("memset", EPSC[:], EPS), writes=["EPSC"])
        P.add("act", C("activation", out=ES[:], in_=VEC[:, V_SINK:V_SINK + 16], func=AF.Exp),
              reads=["VEC"], writes=["ES"])
        outs.append(P.add("sp", C("dma_start", out=conv_old[:], in_=sc_raw[:, 8:30, :]), dma="d2d0"))
        outs.append(P.add("sp", C("dma_start", out=k_old[:], in_=ck_raw[:, 8:128, :]), dma="d2d1"))
        outs.append(P.add("sp", C("dma_start", out=v_old[:], in_=cv_raw[:, 8:128, :]), dma="d2d2"))

        def rstd_from_bank(b, n, rs_ap, rs_keys):
            P.add("act", C("activation", out=rs_ap, in_=banks[b][:, :n], func=AF.Ln, bias=EPSC[:, 0:1]),
                  reads=[("ps", b), "EPSC"], writes=rs_keys)
            P.add("act", C("activation", out=rs_ap, in_=rs_ap, func=AF.Exp, scale=-0.5), reads=rs_keys, writes=rs_keys)

        def rms_stats(a, b):
            for (t0, n) in tiles(a, b):
                bk = bank()
                for kc in range(KC):
                    sq = kc % 4
                    if kc % 2 == 0:
                        P.add("act", C("activation",
                            out=SQ[:, sq * 512: sq * 512 + n], in_=Xap(kc, t0, t0 + n), func=AF.Square),
                            reads=seg_keys("X", kc, t0, t0 + n), writes=[("SQ", sq)])
                    else:
                        P.add("dve", C("tensor_tensor",
                            out=SQ[:, sq * 512: sq * 512 + n], in0=Xap(kc, t0, t0 + n), in1=Xap(kc, t0, t0 + n), op=ALU.mult),
                            reads=seg_keys("X", kc, t0, t0 + n), writes=[("SQ", sq)])
                    P.add("pe", C("matmul",
                        banks[bk][:, :n], lhsT=ONESB[:], rhs=SQ[:, sq * 512: sq * 512 + n],
                        start=(kc == 0), stop=(kc == KC - 1)),
                        reads=[("SQ", sq), "ONESB"], writes=[("ps", bk)])
                rstd_from_bank(bk, n, TR0[:, t0:t0 + n], seg_keys("RS", 0, t0, t0 + n))

        def rms_to_h(a, b, gcol):
            rms_stats(a, b)
            for kc in range(KC):
                for (t0, n) in tiles(a, b):
                    P.add("dve", C("scalar_tensor_tensor",
                        out=Hap(kc, t0, t0 + n), in0=Xap(kc, t0, t0 + n), scalar=vcol(gcol + kc),
                        in1=TR0[:, t0:t0 + n], op0=ALU.mult, op1=ALU.mult),
                        reads=seg_keys("X", kc, t0, t0 + n) + seg_keys("RS", 0, t0, t0 + n) + ["VEC"],
                        writes=seg_keys("H", kc, t0, t0 + n))

        def proj_group(slot, lhs_off, lhs_stride, nk, rhs_fn, rhs_keys_fn, tl):
            bks = [bank() for _ in tl]
            for k in range(nk):
                for i, (t0, n) in enumerate(tl):
                    P.add("pe", C("matmul",
                        banks[bks[i]][:, :n], lhsT=slot_ap(slot, lhs_off + k * lhs_stride, lhs_off + k * lhs_stride + 128),
                        rhs=rhs_fn(k, t0, n), start=(k == 0), stop=(k == nk - 1)),
                        reads=[("slot", slot)] + rhs_keys_fn(k, t0, n), writes=[("ps", bks[i])])
            return bks

        def h_rhs(k, t0, n):
            return Hap(k, t0, t0 + n)

        def h_keys(k, t0, n):
            return seg_keys("H", k, t0, t0 + n)

        try:
            checkpoint(0)
            def conv_pass(pa):
                if pa == 0:
                    ga, gb, oa, ob = 0, 622, 30, 622
                    cbase = 622
                else:
                    ga, gb, oa, ob = 622, 1214, 622, 1214
                    cbase = 0
                rms_to_h(ga, gb, V_NMIX0)
                checkpoint(1)
                tl = tiles(ga, gb)

                def cap(oc, a, b):
                    return Hap(oc, cbase + a, cbase + b)

                def ckeys(oc, a, b):
                    return seg_keys("H", oc, cbase + a, cbase + b)

                UWID = 622 if pa == 0 else UW

                def stage1(oc):
                    u = oc % 2
                    sA = get_tile(("pw1a", oc))
                    bA = proj_group(sA, 0, 128, KC, h_rhs, h_keys, tl)
                    sG = get_tile(("pw1g", oc))
                    bG = proj_group(sG, 0, 128, KC, h_rhs, h_keys, tl)
                    issue_next()
                    issue_next()
                    if pa == 1:
                        P.add("act", C("activation", out=U[u][:, 0:30], in_=UC[:, oc * 30:(oc + 1) * 30], func=AF.Copy),
                              reads=[("UC", oc)], writes=[("UX", u)])
                        P.add("sp", C("dma_start",
                            out=U[u][:, 590:742].rearrange("p (s c) -> p s c", c=38)[:, :, 0:30], in_=ctxT[:, oc, :, :]),
                            writes=[("UX", u)], dma="ctx%d" % u)
                    for i, (t0, n) in enumerate(tl):
                        g0 = t0 - ga
                        P.add("act", C("activation",
                            out=SIG[u][:, g0:g0 + n], in_=banks[bG[i]][:, :n], func=AF.Sigmoid, bias=vcol(V_BG + oc)),
                            reads=[("ps", bG[i]), "VEC"], writes=[("SIG", u)])
                        npr = min(t0 + n, T_SMP) - t0
                        if npr > 0:
                            uc0 = t0 if pa == 0 else 30 + (t0 - 622)
                            P.add("dve", C("scalar_tensor_tensor",
                                out=U[u][:, uc0:uc0 + npr], in0=banks[bA[i]][:, :npr], scalar=vcol(V_BA + oc),
                                in1=SIG[u][:, g0:g0 + npr], op0=ALU.add, op1=ALU.mult),
                                reads=[("ps", bA[i]), ("SIG", u), "VEC"], writes=[("U", u)])
                        if npr < n:
                            assert n - npr == 32
                            P.add("dve", C("scalar_tensor_tensor",
                                out=U[u][:, 590:742].rearrange("p (s c) -> p s c", c=38)[:, :, 30:38],
                                in0=banks[bA[i]][:, npr:npr + 32].rearrange("p (s c) -> p s c", c=8), scalar=vcol(V_BA + oc),
                                in1=SIG[u][:, g0 + npr:g0 + npr + 32].rearrange("p (s c) -> p s c", c=8),
                                op0=ALU.add, op1=ALU.mult),
                                reads=[("ps", bA[i]), ("SIG", u), "VEC"], writes=[("U", u)])
                    if pa == 0:
                        P.add("dve", C("tensor_single_scalar", out=U[u][:, 0:158], in_=U[u][:, 0:158],
                                       scalar=vcol(V_FLAG), op=ALU.mult),
                              reads=[("U", u), "VEC"], writes=[("U", u)])
                        P.add("act", C("activation", out=UC[:, oc * 30:(oc + 1) * 30], in_=U[u][:, 592:622], func=AF.Copy),
                              reads=[("U", u)], writes=[("UC", oc)])
                    else:
                        P.add("act", C("activation", out=USAVE[:, oc * 62: oc * 62 + 30], in_=U[u][:, 560:590], func=AF.Copy),
                              reads=[("U", u)], writes=["USAVE"])
                        P.add("act", C("activation",
                            out=USAVE[:, oc * 62 + 30: oc * 62 + 62].rearrange("p (s c) -> p s c", c=8),
                            in_=U[u][:, 590:742].rearrange("p (s c) -> p s c", c=38)[:, :, 30:38], func=AF.Copy),
                            reads=[("U", u)], writes=["USAVE"])
                    if NDV < CW:
                        P.add("act", C("activation", out=UB[u][:, 0:UWID], in_=U[u][:, 0:UWID], func=AF.Copy),
                              reads=[("U", u), ("UX", u)], writes=[("UB", u)])

                def stage2a(oc):
                    dgs = []
                    for j in range(NDV, CW):
                        dg = dgc[0] % NDG
                        dgc[0] += 1
                        P.add("act", C("activation", out=DG[:, dg * 128:(dg + 1) * 128], in_=IDENT[:, :], func=AF.Copy,
                                       scale=vcol(V_WDW + oc * CW + j)),
                              reads=["IDENT", "VEC"], writes=[("DG", dg)])
                        dgs.append(dg)
                    return dgs

                def stage2(oc, dgs):
                    u = oc % 2
                    if pa == 0:
                        parts = [(lambda acc: acc[:, 0:512], lambda X_, j: X_[:, j:j + 512], lambda: cap(oc, 0, 512), lambda bk: banks[bk][:, 0:512]),
                                 (lambda acc: acc[:, 512:592], lambda X_, j: X_[:, 512 + j:592 + j], lambda: cap(oc, 512, 592), lambda bk: banks[bk][:, 0:80])]
                    else:
                        parts = [(lambda acc: acc[:, 0:512], lambda X_, j: X_[:, j:j + 512], lambda: cap(oc, 0, 512), lambda bk: banks[bk][:, 0:512]),
                                 (lambda acc: acc[:, 512:560], lambda X_, j: X_[:, 512 + j:560 + j], lambda: cap(oc, 512, 560), lambda bk: banks[bk][:, 0:48]),
                                 (lambda acc: acc[:, 560:592].rearrange("p (s c) -> p s c", c=8),
                                  lambda X_, j: X_[:, 590:742].rearrange("p (s c) -> p s c", c=38)[:, :, j:j + 8],
                                  lambda: cap(oc, 560, 592).rearrange("p (s c) -> p s c", c=8),
                                  lambda bk: banks[bk][:, 0:32].rearrange("p (s c) -> p s c", c=8))]
                    akey = ("ACC", "dve", u)
                    cb = []
                    if NDV < CW:
                        cb = [bank() for _ in parts]
                        for j in range(NDV, CW):
                            dg = dgs[j - NDV]
                            for pi, (accv, xv, cv, pv) in enumerate(parts):
                                P.add("pe", C("matmul", pv(cb[pi]), lhsT=DG[:, dg * 128:(dg + 1) * 128], rhs=xv(UB[u], j),
                                              start=(j == NDV), stop=(j == CW - 1)),
                                      reads=[("DG", dg), ("UB", u)], writes=[("ps", cb[pi])])
                    for j in range(NDV):
                        wcol = vcol(V_WDW + oc * CW + j)
                        for pi, (accv, xv, cv, pv) in enumerate(parts):
                            if j == 0:
                                P.add("dve", C("tensor_scalar", out=accv(ACCD[u]), in0=xv(U[u], j), scalar1=wcol,
                                               scalar2=vcol(V_BDW + oc), op0=ALU.mult, op1=ALU.add),
                                      reads=[("U", u), ("UX", u), "VEC"], writes=[akey])
                            else:
                                P.add("dve", C("scalar_tensor_tensor", out=accv(ACCD[u]), in0=xv(U[u], j), scalar=wcol,
                                               in1=accv(ACCD[u]), op0=ALU.mult, op1=ALU.add),
                                      reads=[("U", u), ("UX", u), "VEC", akey], writes=[akey])
                    for pi, (accv, xv, cv, pv) in enumerate(parts):
                        if NDV == 0:
                            P.add("dve", C("tensor_single_scalar", out=cv(), in_=pv(cb[pi]), scalar=vcol(V_BDW + oc), op=ALU.add),
                                  reads=[("ps", cb[pi]), "VEC"], writes=ckeys(oc, 0, 592))
                        elif NDV == CW:
                            P.add("dve", C("tensor_copy", out=cv(), in_=accv(ACCD[u])), reads=[akey], writes=ckeys(oc, 0, 592))
                        else:
                            P.add("dve", C("tensor_tensor", out=cv(), in0=pv(cb[pi]), in1=accv(ACCD[u]), op=ALU.add),
                                  reads=[("ps", cb[pi]), akey], writes=ckeys(oc, 0, 592))

                stage1(0)
                for oc in range(16):
                    dgs = stage2a(oc)
                    if oc + 1 < 16:
                        stage1(oc + 1)
                    stage2(oc, dgs)
                ltl = tiles(0, 592)
                bmu = [bank() for _ in ltl]
                bvar = [bank() for _ in ltl]
                for oc in range(16):
                    for i, (c0, n) in enumerate(ltl):
                        sq = (2 * oc + i) % 4
                        P.add("act", C("activation",
                            out=SQ[:, sq * 512: sq * 512 + n], in_=cap(oc, c0, c0 + n), func=AF.Square),
                            reads=ckeys(oc, c0, c0 + n), writes=[("SQ", sq)])
                        P.add("pe", C("matmul",
                            banks[bmu[i]][:, :n], lhsT=ONESB[:], rhs=cap(oc, c0, c0 + n), start=(oc == 0), stop=(oc == 15)),
                            reads=ckeys(oc, c0, c0 + n) + ["ONESB"], writes=[("ps", bmu[i])])
                        P.add("pe", C("matmul",
                            banks[bvar[i]][:, :n], lhsT=ONESB[:], rhs=SQ[:, sq * 512: sq * 512 + n], start=(oc == 0), stop=(oc == 15)),
                            reads=[("SQ", sq), "ONESB"], writes=[("ps", bvar[i])])
                for i, (c0, n) in enumerate(ltl):
                    P.add("act", C("activation", out=TR1[:, c0:c0 + n], in_=banks[bmu[i]][:, :n], func=AF.Copy),
                          reads=[("ps", bmu[i])], writes=[("MU", i)])
                    P.add("dve", C("tensor_tensor", out=LNT[0][:, c0:c0 + n], in0=TR1[:, c0:c0 + n], in1=TR1[:, c0:c0 + n], op=ALU.mult),
                          reads=[("MU", i)], writes=[("LNT", 0)])
                    P.add("dve", C("tensor_tensor", out=LNT[0][:, c0:c0 + n], in0=banks[bvar[i]][:, :n], in1=LNT[0][:, c0:c0 + n], op=ALU.subtract),
                          reads=[("ps", bvar[i]), ("LNT", 0)], writes=[("LNT", 0)])
                    P.add("act", C("activation", out=TR0[:, c0:c0 + n], in_=LNT[0][:, c0:c0 + n], func=AF.Ln, bias=EPSC[:, 0:1]),
                          reads=[("LNT", 0), "EPSC"], writes=seg_keys("RS", 0, c0, c0 + n))
                    P.add("act", C("activation", out=TR0[:, c0:c0 + n], in_=TR0[:, c0:c0 + n], func=AF.Exp, scale=-0.5),
                          reads=seg_keys("RS", 0, c0, c0 + n), writes=seg_keys("RS", 0, c0, c0 + n))
                rs_keys = seg_keys("RS", 0, 0, 592)
                for oc in range(16):
                    a = oc % 2
                    P.add("dve", C("tensor_tensor", out=LNT[a][:, 0:592], in0=cap(oc, 0, 592), in1=TR1[:, 0:592], op=ALU.subtract),
                          reads=ckeys(oc, 0, 592) + [("MU", 0), ("MU", 1)], writes=[("LNT", a)])
                    P.add("dve", C("tensor_tensor", out=LNT[a][:, 0:592], in0=LNT[a][:, 0:592], in1=TR0[:, 0:592], op=ALU.mult),
                          reads=[("LNT", a)] + rs_keys, writes=[("LNT", a)])
                    P.add("act", C("activation", out=Hap(oc, oa, ob), in_=LNT[a][:, 0:592], func=AF.Silu,
                                                                   bias=vcol(V_LNB + oc), scale=vcol(V_LNG + oc)),
                          reads=[("LNT", a), "VEC"], writes=seg_keys("H", oc, oa, ob))
                otl = tiles(oa, ob)
                for oc in range(16):
                    s2 = get_tile(("pw2", oc))
                    bks = proj_group(s2, 0, 128, KC, h_rhs, h_keys, otl)
                    issue_next()
                    for i, (t0, n) in enumerate(otl):
                        P.add("dve", C("scalar_tensor_tensor",
                            out=Xap(oc, t0, t0 + n), in0=banks[bks[i]][:, :n], scalar=vcol(V_BPW2 + oc),
                            in1=Xap(oc, t0, t0 + n), op0=ALU.add, op1=ALU.add),
                            reads=[("ps", bks[i]), "VEC"] + seg_keys("X", oc, t0, t0 + n), writes=seg_keys("X", oc, t0, t0 + n))

            conv_pass(0)
            checkpoint(2)
            conv_pass(1)
            checkpoint(3)
            outs.append(P.add("sp", C("dma_start", out=uT_out[:], in_=USAVE[:]), reads=["USAVE"], dma="usave"))
            P.fence()

            def mlp(l, a, b, gcol):
                rms_to_h(a, b, gcol)
                tl = tiles(a, b)
                for g in range(8):
                    for fc in range(8):
                        s = get_tile(("up", l, g * 8 + fc))
                        bks = proj_group(s, 0, 128, KC, h_rhs, h_keys, tl)
                        issue_next()
                        for i, (t0, n) in enumerate(tl):
                            P.add("act", C("activation", out=RT[i][:, :n], in_=banks[bks[i]][:, :n], func=AF.Relu),
                                  reads=[("ps", bks[i])], writes=[("RT", i)])
                        for i, (t0, n) in enumerate(tl):
                            P.add("act", C("activation",
                                out=HID[:, fc * NTOK + (t0 - a): fc * NTOK + (t0 - a) + n], in_=RT[i][:, :n], func=AF.Square),
                                reads=[("RT", i)], writes=[("HID", fc, i)])
                    for ocp in range(8):
                        s = get_tile(("down", l, g, ocp))
                        for o2 in range(2):
                            oc = 2 * ocp + o2
                            bks = proj_group(s, o2 * 128, 256, 8,
                                             lambda k, t0, n: HID[:, k * NTOK + (t0 - a): k * NTOK + (t0 - a) + n],
                                             lambda k, t0, n: [("HID", k, i) for i in range(3)], tl)
                            for i, (t0, n) in enumerate(tl):
                                P.add("dve", C("tensor_tensor",
                                    out=Xap(oc, t0, t0 + n), in0=banks[bks[i]][:, :n], in1=Xap(oc, t0, t0 + n), op=ALU.add),
                                    reads=[("ps", bks[i])] + seg_keys("X", oc, t0, t0 + n), writes=seg_keys("X", oc, t0, t0 + n))
                        issue_next()

            mlp(0, 30, TE, V_NMLP0)
            P.fence()
            checkpoint(4)

            P.add("pool", C("dma_start",
                out=KT[:].rearrange("p (c w) -> p c w", w=KTW)[:, :, NTOK:NTOK + 512],
                in_=ckT[:]), writes=["KTC"], dma="ktc")
            P.add("pool", C("dma_start",
                out=VB[:, 9 * 512:13 * 512].rearrange("p (s f) -> p s f", f=512),
                in_=cv_raw[:].rearrange("s k f -> k s f")), writes=["VBC"], dma="vbc")
            rms_to_h(30, TE, V_KVN)
            ktl = tiles(30, TE)
            CT = lambda t0, n: TAB[:, t0 - 30: t0 - 30 + n]
            ST = lambda t0, n: TAB[:, NTOK + t0 - 30: NTOK + t0 - 30 + n]
            def kproj(c):
                s = get_tile(("wk", c))
                bks = proj_group(s, 0, 128, KC, h_rhs, h_keys, ktl)
                issue_next()
                return bks

            def kpost(c, bks):
                kfs = [TR1[:, t0 - 30: t0 - 30 + n] for (t0, n) in ktl]
                tvs = [TR0[:, t0 - 30: t0 - 30 + n] for (t0, n) in ktl]
                tks = [seg_keys("RS", 0, t0 - 30, t0 - 30 + n) for (t0, n) in ktl]
                for i, (t0, n) in enumerate(ktl):
                    P.add("act", C("activation", out=kfs[i], in_=banks[bks[i]][:, :n], func=AF.Copy),
                          reads=[("ps", bks[i])], writes=[("KF", i)])
                for i, (t0, n) in enumerate(ktl):
                    P.add("pe", C("matmul", banks[bks[i]][:, :n], lhsT=PERMF[:], rhs=kfs[i], start=True, stop=True),
                          reads=[("KF", i), "PERMF"], writes=[("ps", bks[i])])
                for i, (t0, n) in enumerate(ktl):
                    P.add("dve", C("tensor_tensor", out=tvs[i], in0=banks[bks[i]][:, :n], in1=ST(t0, n), op=ALU.mult),
                          reads=[("ps", bks[i]), "TAB"], writes=tks[i])
                for i, (t0, n) in enumerate(ktl):
                    P.add("dve", C("tensor_tensor", out=kfs[i], in0=kfs[i], in1=CT(t0, n), op=ALU.mult),
                          reads=[("KF", i), "TAB"], writes=[("KF", i)])
                for i, (t0, n) in enumerate(ktl):
                    P.add("dve", C("tensor_tensor", out=kfs[i], in0=kfs[i], in1=tvs[i], op=ALU.add),
                          reads=[("KF", i)] + tks[i], writes=[("KF", i)])
                for i, (t0, n) in enumerate(ktl):
                    P.add("act", C("activation",
                        out=KT[:, c * KTW + t0 - 30: c * KTW + t0 - 30 + n], in_=kfs[i], func=AF.Copy),
                        reads=[("KF", i)], writes=[("KT", c, i)])
                    if i == 2:
                        assert t0 == 1054 and n == 160
                        P.add("act", C("activation", out=KOUT[:, c * 160:(c + 1) * 160], in_=kfs[i], func=AF.Copy),
                              reads=[("KF", i)], writes=["KOUT"])

            kprev = None
            for c in range(4):
                bks_ = kproj(c)
                if kprev is not None:
                    kpost(*kprev)
                kprev = (c, bks_)
            kpost(*kprev)
            outs.append(P.add("sp", C("dma_start", out=kT_out[:], in_=KOUT[:]), reads=["KOUT"], dma="kout"))
            checkpoint(5)
            vgroups = [[("blk", bi) for bi in range(6)], [("blk", bi) for bi in range(6, 9)] + [("smp", s_) for s_ in range(4)]]
            vst_ctr = [0]
            for grp in vgroups:
                gb_ = [bank() for _ in grp]
                for c in range(4):
                    s = get_tile(("wv", c))
                    for gi, (kind, idx) in enumerate(grp):
                        for kk_ in range(4):
                            kc = c * 4 + kk_
                            if kind == "blk":
                                t0 = 30 + 128 * idx
                                P.add("pe", C("matmul",
                                    banks[gb_[gi]][:, :], lhsT=Hap(kc, t0, t0 + 128), rhs=slot_ap(s, kk_ * 512, kk_ * 512 + 512),
                                    start=(kc == 0), stop=(kc == KC - 1)),
                                    reads=[("slot", s)] + seg_keys("H", kc, t0, t0 + 128), writes=[("ps", gb_[gi])])
                            else:
                                t0 = T_SMP + 8 * idx
                                P.add("pe", C("matmul",
                                    banks[gb_[gi]][0:8, :], lhsT=Hap(kc, t0, t0 + 8), rhs=slot_ap(s, kk_ * 512, kk_ * 512 + 512),
                                    start=(kc == 0), stop=(kc == KC - 1)),
                                    reads=[("slot", s)] + seg_keys("H", kc, t0, t0 + 8), writes=[("ps", gb_[gi])])
                    issue_next()
                for gi, (kind, idx) in enumerate(grp):
                    bkk = gb_[gi]
                    if kind == "blk":
                        P.add("act", C("activation", out=VB[:, idx * 512:(idx + 1) * 512], in_=banks[bkk][:, :], func=AF.Copy),
                              reads=[("ps", bkk)], writes=[("VB", idx)])
                        if idx == 8:
                            for hh in range(2):
                                P.add("act", C("activation", out=VST[hh][:, :], in_=banks[bkk][:, hh * 256:(hh + 1) * 256], func=AF.Copy),
                                      reads=[("ps", bkk)], writes=[("VST", hh)])
                                outs.append(P.add("sp", C("dma_start", out=v_out[:, hh * 256:(hh + 1) * 256], in_=VST[hh][:, :]),
                                                  reads=[("VST", hh)], dma="vst%d" % hh))
                    else:
                        P.add("act", C("activation", out=VN[0:8, idx * 512:(idx + 1) * 512], in_=banks[bkk][0:8, :], func=AF.Copy),
                              reads=[("ps", bkk)], writes=[("VN", idx)])
                        for hh in range(2):
                            P.add("act", C("activation", out=VST[hh][0:8, :], in_=banks[bkk][0:8, hh * 256:(hh + 1) * 256], func=AF.Copy),
                                  reads=[("ps", bkk)], writes=[("VST", hh)])
                            outs.append(P.add("sp", C("dma_start", out=v_new[idx, :, hh * 256:(hh + 1) * 256], in_=VST[hh][0:8, :]),
                                              reads=[("VST", hh)], dma="vst%d" % hh))

            def attn_pass(pa):
                if pa == 0:
                    qa, qb, obase = 158, 670, 670
                    blocks = [0, 1, 2, 3]
                    smp = []
                else:
                    qa, qb, obase = 670, 1214, 30
                    blocks = [4, 5, 6, 7]
                    smp = [0, 1, 2, 3]
                nq = qb - qa
                rms_to_h(qa, qb, V_NMIX1)
                qtl = tiles(qa, qb)

                def oT(oc, a, n):
                    return Hap(oc, obase + a, obase + a + n)

                def oT_keys(oc, a, n):
                    return seg_keys("H", oc, obase + a, obase + a + n)

                ev = [0]

                deferred = []

                def stage_q(oc):
                    u = oc % 2
                    s = get_tile(("wq", oc))
                    bks = [6, 7][:len(qtl)]
                    for k in range(KC):
                        for i, (t0, n) in enumerate(qtl):
                            P.add("pe", C("matmul", banks[bks[i]][:, :n], lhsT=slot_ap(s, k * 128, k * 128 + 128),
                                          rhs=Hap(k, t0, t0 + n), start=(k == 0), stop=(k == KC - 1)),
                                  reads=[("slot", s)] + seg_keys("H", k, t0, t0 + n), writes=[("ps", bks[i])])
                    issue_next()
                    for i, (t0, n) in enumerate(qtl):
                        off = t0 - qa
                        P.add("act", C("activation", out=QB[u][:, off:off + n], in_=banks[bks[i]][:, :n], func=AF.Copy),
                              reads=[("ps", bks[i])], writes=[("QB", u, i), "KOUT", ("VST", 0), ("VST", 1)])

                    def rope():
                        for i, (t0, n) in enumerate(qtl):
                            off = t0 - qa
                            b2 = bks[i]
                            P.add("pe", C("matmul", banks[b2][:, :n], lhsT=PERMB[:], rhs=QB[u][:, off:off + n], start=True, stop=True),
                                  reads=[("QB", u, i), "PERMB"], writes=[("ps", b2)])
                            t1 = TR1[:, off:off + n]
                            t2 = TR1[:, 600 + off:600 + off + n]
                            P.add("dve", C("tensor_tensor", out=t1, in0=banks[b2][:, :n], in1=ST(t0, n), op=ALU.mult),
                                  reads=[("ps", b2), "TAB"], writes=[("QTMP", 0, i)])
                            P.add("dve", C("tensor_tensor", out=t2, in0=QB[u][:, off:off + n], in1=CT(t0, n), op=ALU.mult),
                                  reads=[("QB", u, i), "TAB"], writes=[("QTMP", 1, i)])
                            P.add("dve", C("tensor_tensor", out=QT[u][:, off:off + n], in0=t1, in1=t2, op=ALU.add),
                                  reads=[("QTMP", 0, i), ("QTMP", 1, i)], writes=[("QT", u, i)])
                    deferred.append(rope)

                def run_deferred():
                    while deferred:
                        deferred.pop(0)()

                def unit_scores(oc, unit):
                    c = oc // 4
                    u = oc % 2
                    qt_keys = [("QT", u, i) for i in range(len(qtl))]
                    v = ev[0] % NEV
                    ev[0] += 1
                    bSh = [sbank(), sbank()]
                    if unit[0] == "blk":
                        m = unit[1]
                        tb = 158 + 128 * m
                        off = tb - qa
                        kprev = c * KTW + (tb - 128 - 30)
                        kown = c * KTW + (tb - 30)
                        for hd in range(2):
                            pr0, pr1 = hd * 64, hd * 64 + 64
                            for kh, kcol in enumerate((kprev, kown)):
                                P.add("pe", C("matmul", banks[bSh[hd]][:, kh * 128: kh * 128 + 128],
                                              lhsT=KT[pr0:pr1, kcol:kcol + 128], rhs=QT[u][pr0:pr1, off:off + 128], start=True, stop=True),
                                      reads=[("KT", c, 0), ("KT", c, 1), ("KT", c, 2)] + qt_keys, writes=[("ps", bSh[hd])])
                        for hd in range(2):
                            P.add("act", C("activation", out=PT[v][:, hd * 256:(hd + 1) * 256], in_=banks[bSh[hd]][:, 0:256], func=AF.Exp, scale=0.125),
                                  reads=[("ps", bSh[hd])], writes=[("PT", v)])
                        M4 = MASK40 if m == 0 else MASK4
                        P.add("dve", C("tensor_tensor", out=EB[v][:, :], in0=PT[v][:, :], in1=M4[:, :], op=ALU.mult),
                              reads=[("PT", v), "MASK4"], writes=[("EB", v)])
                    else:
                        for hd in range(2):
                            pr0, pr1 = hd * 64, hd * 64 + 64
                            P.add("pe", C("matmul", banks[bSh[hd]][:, 0:32], lhsT=IDENT[:, :], rhs=MASKNS[:, 0:32], start=True, stop=False),
                                  reads=["IDENT", "MASKNS"], writes=[("ps", bSh[hd])])
                            for sq_ in range(4):
                                off = (T_SMP - qa) + 8 * sq_
                                kcache = c * KTW + NTOK + sq_ * 128
                                P.add("pe", C("matmul", banks[bSh[hd]][:, sq_ * 8: sq_ * 8 + 8],
                                              lhsT=KT[pr0:pr1, kcache:kcache + 128], rhs=QT[u][pr0:pr1, off:off + 8], start=False, stop=(sq_ == 3)),
                                      reads=["KTC"] + qt_keys, writes=[("ps", bSh[hd])])
                            P.add("pe", C("matmul", banks[bSh[hd]][0:8, 32:64], lhsT=IDENT[0:8, 0:8], rhs=MASKNS[0:8, 32:64], start=True, stop=False),
                                  reads=["IDENT", "MASKNS"], writes=[("ps", bSh[hd])])
                            for sq_ in range(4):
                                off = (T_SMP - qa) + 8 * sq_
                                knew = c * KTW + (T_SMP + 8 * sq_ - 30)
                                P.add("pe", C("matmul", banks[bSh[hd]][0:8, 32 + sq_ * 8: 40 + sq_ * 8],
                                              lhsT=KT[pr0:pr1, knew:knew + 8], rhs=QT[u][pr0:pr1, off:off + 8], start=False, stop=(sq_ == 3)),
                                      reads=[("KT", c, 2)] + qt_keys, writes=[("ps", bSh[hd])])
                        for hd in range(2):
                            P.add("act", C("activation", out=EB[v][:, hd * 64: hd * 64 + 32], in_=banks[bSh[hd]][:, 0:32], func=AF.Exp, scale=0.125),
                                  reads=[("ps", bSh[hd])], writes=[("EB", v)])
                            P.add("act", C("activation", out=EB[v][0:8, hd * 64 + 32: hd * 64 + 64], in_=banks[bSh[hd]][0:8, 32:64], func=AF.Exp, scale=0.125),
                                  reads=[("ps", bSh[hd])], writes=[("EB", v)])
                    return v

                def unit_pv(oc, unit, v):
                    c = oc // 4
                    kva, kvb = 2 * c, 2 * c + 1
                    bO = obank()
                    if unit[0] == "blk":
                        m = unit[1]
                        off = 158 + 128 * m - qa
                        w = 128
                        for hd, kv in enumerate((kva, kvb)):
                            pr0, pr1 = hd * 64, hd * 64 + 64
                            for kh, vblk in enumerate((m, m + 1)):
                                P.add("pe", C("matmul", banks[bO][pr0:pr1, 0:128], lhsT=VB[:, vblk * 512 + kv * 64: vblk * 512 + kv * 64 + 64],
                                              rhs=EB[v][:, hd * 256 + kh * 128: hd * 256 + kh * 128 + 128], start=(kh == 0), stop=(kh == 1)),
                                      reads=[("VB", vblk), ("EB", v)], writes=[("ps", bO)])
                            for kh in range(2):
                                P.add("pe", C("matmul", banks[bO][pr0:pr1, 128:256], lhsT=ONE1[:, 0:64],
                                              rhs=EB[v][:, hd * 256 + kh * 128: hd * 256 + kh * 128 + 128], start=(kh == 0), stop=(kh == 1)),
                                      reads=["ONE1", ("EB", v)], writes=[("ps", bO)])
                    else:
                        off = T_SMP - qa
                        w = 32
                        for hd, kv in enumerate((kva, kvb)):
                            pr0, pr1 = hd * 64, hd * 64 + 64
                            for sq_ in range(4):
                                rc = EB[v][:, hd * 64 + sq_ * 8: hd * 64 + sq_ * 8 + 8]
                                rn = EB[v][0:8, hd * 64 + 32 + sq_ * 8: hd * 64 + 40 + sq_ * 8]
                                P.add("pe", C("matmul", banks[bO][pr0:pr1, sq_ * 8: sq_ * 8 + 8],
                                              lhsT=VB[:, (9 + sq_) * 512 + kv * 64: (9 + sq_) * 512 + kv * 64 + 64], rhs=rc, start=True, stop=False),
                                      reads=["VBC", ("EB", v)], writes=[("ps", bO)])
                                P.add("pe", C("matmul", banks[bO][pr0:pr1, sq_ * 8: sq_ * 8 + 8],
                                              lhsT=VN[0:8, sq_ * 512 + kv * 64: sq_ * 512 + kv * 64 + 64], rhs=rn, start=False, stop=True),
                                      reads=[("VN", sq_), ("EB", v)], writes=[("ps", bO)])
                                P.add("pe", C("matmul", banks[bO][pr0:pr1, 128 + sq_ * 8: 136 + sq_ * 8], lhsT=ONE1[:, 0:64], rhs=rc, start=True, stop=False),
                                      reads=["ONE1", ("EB", v)], writes=[("ps", bO)])
                                P.add("pe", C("matmul", banks[bO][pr0:pr1, 128 + sq_ * 8: 136 + sq_ * 8], lhsT=ONE1[0:8, 0:64], rhs=rn, start=False, stop=True),
                                      reads=["ONE1", ("EB", v)], writes=[("ps", bO)])
                    P.add("act", C("activation", out=DEN[v][:, 0:w], in_=banks[bO][:, 128:128 + w], func=AF.Ln, bias=ES[:, oc:oc + 1]),
                          reads=[("ps", bO), "ES"], writes=[("DEN", v)])
                    P.add("act", C("activation", out=DEN[v][:, 0:w], in_=DEN[v][:, 0:w], func=AF.Exp, scale=-1.0),
                          reads=[("DEN", v)], writes=[("DEN", v)])
                    P.add("dve", C("tensor_tensor", out=oT(oc, off, w), in0=banks[bO][:, 0:w], in1=DEN[v][:, 0:w], op=ALU.mult),
                          reads=[("ps", bO), ("DEN", v)], writes=oT_keys(oc, off, w))

                units = [("blk", m) for m in blocks] + ([("smp", 0)] if smp else [])
                seq = [(oc, un) for oc in range(16) for un in units]
                sctr = [0]
                octr = [0]

                def sbank():
                    b_ = sctr[0] % 4
                    sctr[0] += 1
                    return b_

                def obank():
                    b_ = 4 + octr[0] % 2
                    octr[0] += 1
                    return b_

                stage_q(0)
                run_deferred()
                pend = []
                for idx, (oc, un) in enumerate(seq):
                    if un is units[0] and oc + 1 < 16:
                        stage_q(oc + 1)
                    v = unit_scores(oc, un)
                    pend.append((oc, un, v))
                    if len(pend) > 2:
                        unit_pv(*pend.pop(0))
                    run_deferred()
                while pend:
                    unit_pv(*pend.pop(0))
                otl = [(t0 - qa, n) for (t0, n) in qtl]
                for oc in range(16):
                    s = get_tile(("wo", oc))
                    bks = proj_group(s, 0, 128, KC,
                                     lambda k, o0, n: oT(k, o0, n), lambda k, o0, n: oT_keys(k, o0, n), otl)
                    issue_next()
                    for i, (o0, n) in enumerate(otl):
                        t0 = qa + o0
                        P.add("dve", C("tensor_tensor",
                            out=Xap(oc, t0, t0 + n), in0=banks[bks[i]][:, :n], in1=Xap(oc, t0, t0 + n), op=ALU.add),
                            reads=[("ps", bks[i])] + seg_keys("X", oc, t0, t0 + n), writes=seg_keys("X", oc, t0, t0 + n))

            checkpoint(6)
            attn_pass(0)
            checkpoint(7)
            attn_pass(1)
            checkpoint(8)
            P.fence()

            mlp(1, 158, TE, V_NMLP1)
            checkpoint(9)

            rms_stats(158, TE)
            yc = [0]
            for (t0, n) in tiles(158, TE):
                for kc in range(KC):
                    yb = yc[0] % NYST
                    yc[0] += 1
                    P.add("dve", C("scalar_tensor_tensor",
                        out=YST[yb][:, :n], in0=Xap(kc, t0, t0 + n), scalar=vcol(V_FIN + kc),
                        in1=TR0[:, t0:t0 + n], op0=ALU.mult, op1=ALU.mult),
                        reads=seg_keys("X", kc, t0, t0 + n) + seg_keys("RS", 0, t0, t0 + n) + ["VEC"], writes=[("YST", yb)])
                    outs.append(P.add("sp", C("dma_start",
                        out=yT[:, kc, t0 - 158: t0 - 158 + n], in_=YST[yb][:, :n]), reads=[("YST", yb)], dma="yst%d" % yb))

            assert wstate["consumed"] == len(order), (wstate, len(order))
        except _Stop:
            pass
        P.finalize(outs)
        P.emit(nc)
    return nc


def _tile_kc(W, c0):
    return np.ascontiguousarray(W[:, c0:c0 + 128].reshape(16, 128, 128).transpose(1, 0, 2).reshape(128, 2048))


def _col(v):
    return np.ascontiguousarray(np.asarray(v, np.float32).reshape(16, 128).T)


_NC_CACHE = {}


def prepare(x_prompt, x_sample, state_conv, cache_k, cache_v, norm_mix, w_pw1, b_pw1, w_dw, b_dw,
           conv_ln_g, conv_ln_b, w_pw2, b_pw2, kv_norm, w_k, w_v, w_q, w_o, sinks, norm_mlp,
           w_up, w_down, final_norm):
    f = lambda a: np.asarray(a, dtype=np.float32)
    x_prompt, x_sample, state_conv, cache_k, cache_v = map(f, (x_prompt, x_sample, state_conv, cache_k, cache_v))
    w_pw1, w_pw2, w_k, w_v, w_q, w_o, w_up, w_down = map(f, (w_pw1, w_pw2, w_k, w_v, w_q, w_o, w_up, w_down))
    norm_mix, norm_mlp, b_pw1, w_dw, b_dw = map(f, (norm_mix, norm_mlp, b_pw1, w_dw, b_dw))
    conv_ln_g, conv_ln_b, b_pw2, kv_norm, sinks, final_norm = map(f, (conv_ln_g, conv_ln_b, b_pw2, kv_norm, sinks, final_norm))

    tidx = unique_tiles()

    qperm = np.zeros(2048, np.int64)
    head_of = np.zeros((16, 2), np.int64)
    for oc in range(16):
        c, i = oc // 4, oc % 4
        for hd in range(2):
            h = 8 * c + 4 * hd + i
            head_of[oc, hd] = h
            qperm[oc * 128 + hd * 64: oc * 128 + hd * 64 + 64] = h * 64 + np.arange(64)
    wq_p = w_q[0][:, qperm]
    wo_p = w_o[0][qperm, :]

    wst = np.empty((len(tidx), 128, 2048), np.float32)
    for nm, ti in tidx.items():
        kind = nm[0]
        if kind == "pw1a":
            wst[ti] = _tile_kc(w_pw1[0], nm[1] * 128)
        elif kind == "pw1g":
            wst[ti] = _tile_kc(w_pw1[0], 2048 + nm[1] * 128)
        elif kind == "pw2":
            wst[ti] = _tile_kc(w_pw2[0], nm[1] * 128)
        elif kind == "up":
            wst[ti] = _tile_kc(w_up[nm[1]], nm[2] * 128)
        elif kind == "down":
            l, g, ocp = nm[1], nm[2], nm[3]
            wst[ti] = w_down[l][g * 1024:(g + 1) * 1024, ocp * 256:(ocp + 1) * 256].reshape(8, 128, 256).transpose(1, 0, 2).reshape(128, 2048)
        elif kind == "wk":
            wst[ti] = _tile_kc(w_k, nm[1] * 128)
        elif kind == "wv":
            c = nm[1]
            wst[ti] = w_v[c * 512:(c + 1) * 512, :].reshape(4, 128, 512).transpose(1, 0, 2).reshape(128, 2048)
        elif kind == "wq":
            wst[ti] = _tile_kc(wq_p, nm[1] * 128)
        elif kind == "wo":
            wst[ti] = _tile_kc(wo_p, nm[1] * 128)
        else:
            raise AssertionError(nm)

    vec = np.zeros((128, NV), np.float32)
    vec[:, V_NMIX0:V_NMIX0 + 16] = _col(norm_mix[0])
    vec[:, V_NMLP0:V_NMLP0 + 16] = _col(norm_mlp[0])
    vec[:, V_KVN:V_KVN + 16] = _col(kv_norm)
    vec[:, V_NMIX1:V_NMIX1 + 16] = _col(norm_mix[1])
    vec[:, V_NMLP1:V_NMLP1 + 16] = _col(norm_mlp[1])
    vec[:, V_FIN:V_FIN + 16] = _col(final_norm)
    vec[:, V_BA:V_BA + 16] = _col(b_pw1[0][:2048])
    vec[:, V_BG:V_BG + 16] = _col(b_pw1[0][2048:])
    vec[:, V_BDW:V_BDW + 16] = _col(b_dw[0])
    vec[:, V_LNG:V_LNG + 16] = _col(conv_ln_g[0])
    vec[:, V_LNB:V_LNB + 16] = _col(conv_ln_b[0])
    vec[:, V_BPW2:V_BPW2 + 16] = _col(b_pw2[0])
    for oc in range(16):
        vec[0:64, V_SINK + oc] = sinks[0][head_of[oc, 0]]
        vec[64:128, V_SINK + oc] = sinks[0][head_of[oc, 1]]
    vec[:, V_WDW:] = w_dw[0].T.reshape(16, 128, CW).transpose(1, 0, 2).reshape(128, 16 * CW)

    mask_prev = (np.arange(128)[:, None] > np.arange(128)[None, :]).astype(np.float32)
    mask_own = (np.arange(128)[:, None] <= np.arange(128)[None, :]).astype(np.float32)
    m4 = np.concatenate([mask_prev, mask_own, mask_prev, mask_own], axis=1)
    m4_first = m4.copy()
    m4_first[:, 0:128] = 0.0
    m4_first[:, 256:384] = 0.0
    ms = np.zeros((128, 32), np.float32)
    ms[:, 0:8] = mask_prev[:, 0:8]
    ms[:, 8:16] = mask_prev[:, 0:8]
    ms[0:8, 16:24] = mask_own[0:8, 0:8]
    ms[0:8, 24:32] = mask_own[0:8, 0:8]
    NEG = np.float32(-240000.0)
    mn = np.concatenate([(1.0 - mask_prev) * NEG, (1.0 - mask_own) * NEG], axis=1).astype(np.float32)
    mn_first = mn.copy()
    mn_first[:, 0:128] = NEG
    mns = np.zeros((128, 64), np.float32)
    for s_ in range(4):
        mns[:, s_ * 8:(s_ + 1) * 8] = (1.0 - mask_prev[:, 0:8]) * NEG
        mns[0:8, 32 + s_ * 8: 40 + s_ * 8] = (1.0 - mask_own[0:8, 0:8]) * NEG
    eye = np.eye(128, dtype=np.float32)
    pm = np.zeros((128, 128), np.float32)
    sgn = np.zeros(128, np.float32)
    fidx = np.full(128, -1, np.int64)
    for m_ in range(128):
        d = m_ % 64
        if d < 8:
            pm[m_ + 8, m_] = 1.0
            sgn[m_] = -1.0
            fidx[m_] = d
        elif d < 16:
            pm[m_ - 8, m_] = 1.0
            sgn[m_] = 1.0
            fidx[m_] = d - 8
    inv = (np.float32(500000.0) ** (-np.arange(8, dtype=np.float32) / np.float32(8))).astype(np.float32)

    in_maps = []
    for core in range(NCORE):
        b, j = core // 4, core % 4
        xe = np.zeros((TE, D), np.float32)
        if j > 0:
            xe[0:158] = x_prompt[b, 1024 * j - 158: 1024 * j]
        xe[158:1182] = x_prompt[b, 1024 * j: 1024 * (j + 1)]
        xe[1182:1214] = x_sample[4 * core: 4 * core + 4].reshape(32, D)
        xTc = np.ascontiguousarray(xe.T.reshape(16, 128, TE).transpose(1, 0, 2))
        sc = state_conv[0, 4 * core: 4 * core + 4]
        ctxTc = np.ascontiguousarray(sc.transpose(2, 0, 1).reshape(16, 128, 4, 30).transpose(1, 0, 2, 3))
        ck = cache_k[4 * core: 4 * core + 4].reshape(4, 128, 512)
        cv = cache_v[4 * core: 4 * core + 4].reshape(4, 128, 512)
        ckTc = np.ascontiguousarray(ck.transpose(2, 0, 1).reshape(4, 128, 4 * 128).transpose(1, 0, 2))
        pos = np.zeros(NTOK, np.float32)
        pos[0:1152] = (1024 * j - 128) + np.arange(1152)
        pos[1152:1184] = 16384 + (np.arange(32) % 8)
        tabc = np.zeros((128, 2 * NTOK), np.float32)
        tabc[:, 0:NTOK] = 1.0
        for m_ in range(128):
            if fidx[m_] >= 0:
                ang = (pos * inv[fidx[m_]]).astype(np.float32)
                tabc[m_, 0:NTOK] = np.cos(ang).astype(np.float32)
                tabc[m_, NTOK:] = sgn[m_] * np.sin(ang).astype(np.float32)
        v = vec.copy()
        v[:, V_FLAG] = 0.0 if j == 0 else 1.0
        in_maps.append({
            "xT": xTc, "wst": wst, "ctxT": ctxTc, "ckT": ckTc,
            "sc_raw": np.ascontiguousarray(sc), "ck_raw": np.ascontiguousarray(ck), "cv_raw": np.ascontiguousarray(cv),
            "vecs": v, "tab": tabc, "mask4": m4, "masks": ms, "perm": pm,
            "mask40": (m4_first if j == 0 else m4), "ident": eye, "maskns": mns,
        })

    return in_maps


def kernel(**inputs):
    in_maps = prepare(**inputs)
    if "nc" not in _NC_CACHE:
        _NC_CACHE["nc"] = build()
    nc = _NC_CACHE["nc"]
    res = run_bass_kernel_spmd(nc, in_maps, core_ids=list(range(NCORE)))
    return assemble(res.results)


def assemble(R_):

    y_prompt = np.zeros((2, 4096, D), np.float32)
    y_sample = np.zeros((32, 8, D), np.float32)
    conv_prompt = np.zeros((1, 2, 30, D), np.float32)
    conv_sample = np.zeros((1, 32, 30, D), np.float32)
    win_k_prompt = np.zeros((2, 128, 8, 64), np.float32)
    win_v_prompt = np.zeros((2, 128, 8, 64), np.float32)
    win_k_sample = np.zeros((32, 128, 8, 64), np.float32)
    win_v_sample = np.zeros((32, 128, 8, 64), np.float32)
    for core in range(NCORE):
        b, j = core // 4, core % 4
        r = R_[core]
        y = np.transpose(r["yT"], (2, 1, 0)).reshape(NQ, D)
        y_prompt[b, 1024 * j: 1024 * (j + 1)] = y[0:1024]
        y_sample[4 * core: 4 * core + 4] = y[1024:1056].reshape(4, 8, D)
        uo = r["uT_out"].reshape(128, 16, 62)
        ut = np.transpose(uo, (2, 1, 0)).reshape(62, D)
        kt = np.transpose(r["kT_out"].reshape(128, 4, 160), (2, 1, 0)).reshape(160, 512)
        if j == 3:
            conv_prompt[0, b] = ut[0:30]
            win_k_prompt[b] = kt[0:128].reshape(128, 8, 64)
            win_v_prompt[b] = r["v_out"].reshape(128, 8, 64)
        for s in range(4):
            q = 4 * core + s
            conv_sample[0, q, 0:22] = r["conv_old"][s]
            conv_sample[0, q, 22:30] = ut[30 + 8 * s: 30 + 8 * s + 8]
            win_k_sample[q, 0:120] = r["k_old"][s].reshape(120, 8, 64)
            win_k_sample[q, 120:128] = kt[128 + 8 * s: 128 + 8 * s + 8].reshape(8, 8, 64)
            win_v_sample[q, 0:120] = r["v_old"][s].reshape(120, 8, 64)
            win_v_sample[q, 120:128] = r["v_new"][s].reshape(8, 8, 64)
    return (y_prompt, y_sample, conv_prompt, conv_sample, win_k_prompt, win_v_prompt, win_k_sample, win_v_sample)
```

```python
import contextlib
import os
import numpy as np
import concourse.bass as bass
import concourse.mybir as mybir
from concourse.bass_utils import run_bass_kernel_spmd

F32 = mybir.dt.float32
BF16 = mybir.dt.bfloat16
AF = mybir.ActivationFunctionType
ALU = mybir.AluOpType

ENGS = ("pe", "act", "dve", "pool", "sp")


def C(name, *a, **k):
    return (name, a, k)


class Op:
    __slots__ = ("eng", "fn", "deps", "signal", "sem", "val", "pos", "dma")

    def __init__(self, eng, fn, dma=None):
        self.eng = eng
        self.fn = fn
        self.deps = []
        self.signal = False
        self.sem = None
        self.val = 0
        self.pos = 0
        self.dma = dma


class _Stop(Exception):
    pass


STOP = float(os.environ.get("MK_STOP", "99"))


def checkpoint(k):
    if STOP <= k:
        raise _Stop()


class Prog:
    SAFE_DIST = 10 ** 9

    def __init__(self):
        self.streams = {e: [] for e in ENGS}
        self.last_w = {}
        self.readers = {}
        self.last_op = {}
        self.pending_fence = {e: [] for e in ENGS}
        self.dma_pending = []

    def add(self, eng, fn, reads=(), writes=(), dma=None):
        op = Op(eng, fn, dma=dma)
        if dma is not None:
            op.signal = True
        deps = set()
        for r in reads:
            w = self.last_w.get(r)
            if w is not None:
                deps.add(w)
        for wkey in writes:
            w = self.last_w.get(wkey)
            if w is not None:
                deps.add(w)
            for rd in self.readers.get(wkey, ()):
                deps.add(rd)
        for f in self.pending_fence[eng]:
            deps.add(f)
        self.pending_fence[eng] = []
        op.pos = len(self.streams[eng])
        keep = []
        latest = {}
        for d in deps:
            if d.dma is None and d.eng == eng:
                if eng == "pe" or eng == "sp":
                    continue
                if op.pos - d.pos >= self.SAFE_DIST:
                    continue
            if d.dma is None:
                cur = latest.get(d.eng)
                if cur is None or d.pos > cur.pos:
                    latest[d.eng] = d
            else:
                keep.append(d)
        keep.extend(latest.values())
        op.deps = keep
        for d in keep:
            d.signal = True
        self.streams[eng].append(op)
        if dma is None:
            self.last_op[eng] = op
        elif not dma.startswith("slot"):
            self.dma_pending.append(op)
        for r in reads:
            self.readers.setdefault(r, []).append(op)
        for wkey in writes:
            self.last_w[wkey] = op
            self.readers[wkey] = []
        return op

    def fence(self):
        lasts = [o for o in self.last_op.values()]
        dmas = self.dma_pending
        self.dma_pending = []
        for e in ENGS:
            self.pending_fence[e] = [o for o in lasts if o.eng != e] + dmas

    def finalize(self, final_ops):
        for o in final_ops:
            o.signal = True
        counters = {}
        for e in ENGS:
            for op in self.streams[e]:
                if not op.signal:
                    continue
                if op.dma is not None:
                    key = ("dma", op.dma)
                    counters[key] = counters.get(key, 0) + 16
                else:
                    key = ("eng", e)
                    counters[key] = counters.get(key, 0) + 1
                op.sem = key
                op.val = counters[key]
        self.sem_keys = list(counters.keys())
        self.final_ops = final_ops

    def emit(self, nc):
        with contextlib.ExitStack() as st:
            sems = {}
            for i, k in enumerate(self.sem_keys):
                sems[k] = st.enter_context(nc.semaphore("s%d" % i))
            block = st.enter_context(nc.Block())
            prog = self

            def run(ename, e, is_last=False):
                waited = {}
                for op in prog.streams[ename]:
                    for d in op.deps:
                        if waited.get(d.sem, 0) < d.val:
                            e.wait_ge(sems[d.sem], d.val)
                            waited[d.sem] = d.val
                    name_, a_, k_ = op.fn
                    ins = getattr(e, name_)(*a_, **k_)
                    if op.signal:
                        ins.then_inc(sems[op.sem], 16 if op.dma is not None else 1)
                if is_last:
                    for d in prog.final_ops:
                        if waited.get(d.sem, 0) < d.val:
                            e.wait_ge(sems[d.sem], d.val)
                            waited[d.sem] = d.val

            @block.sync
            def _(e):
                run("sp", e, is_last=True)

            @block.scalar
            def _(e):
                run("act", e)

            @block.vector
            def _(e):
                run("dve", e)

            @block.gpsimd
            def _(e):
                run("pool", e)

            @block.tensor
            def _(e):
                run("pe", e)


NCORE = 8
D = 2048
KC = 16
TE = 1214
T_PRE, T_HALO, T_MAIN, T_SMP = 0, 30, 158, 1182
NTOK = TE - 30
NQ = TE - 158
SEGS = [0, 30, 158, 592, 622, 670, 1214]
NS = 4
CW = 31
NDV = 8
EPS = 1e-6
UW = 742

V_NMIX0, V_NMLP0, V_KVN, V_NMIX1, V_NMLP1, V_FIN = 0, 16, 32, 48, 64, 80
V_BA, V_BG, V_BDW, V_LNG, V_LNB, V_BPW2, V_SINK = 96, 112, 128, 144, 160, 176, 192
V_FLAG = 208
V_WDW = 212
NV = V_WDW + 16 * CW


def seg_keys(name, kc, a, b):
    return [(name, kc, i) for i in range(len(SEGS) - 1) if SEGS[i] < b and SEGS[i + 1] > a]


def tiles(a, b):
    out = []
    t = a
    while t < b:
        n = min(512, b - t)
        out.append((t, n))
        t += n
    return out


def weight_order():
    order = []
    for _pass in range(2):
        for oc in range(16):
            order.append(("pw1a", oc))
            order.append(("pw1g", oc))
        for oc in range(16):
            order.append(("pw2", oc))
    for g in range(8):
        for fc in range(8):
            order.append(("up", 0, g * 8 + fc))
        for ocp in range(8):
            order.append(("down", 0, g, ocp))
    for c in range(4):
        order.append(("wk", c))
    for _grp in range(2):
        for c in range(4):
            order.append(("wv", c))
    for _pass in range(2):
        for oc in range(16):
            order.append(("wq", oc))
        for oc in range(16):
            order.append(("wo", oc))
    for g in range(8):
        for fc in range(8):
            order.append(("up", 1, g * 8 + fc))
        for ocp in range(8):
            order.append(("down", 1, g, ocp))
    return order


def unique_tiles():
    seen = {}
    for nm in weight_order():
        if nm not in seen:
            seen[nm] = len(seen)
    return seen


def build():
    nc = bass.Bass("TRN2", target_bir_lowering=False)
    tidx = unique_tiles()
    NT = len(tidx)
    order = weight_order()

    def din(name, shape):
        return nc.dram_tensor(name, shape, F32, kind="ExternalInput").ap()

    def dout(name, shape):
        return nc.dram_tensor(name, shape, F32, kind="ExternalOutput").ap()

    xT = din("xT", [128, KC, TE])
    wst = din("wst", [NT, 128, 2048])
    ctxT = din("ctxT", [128, KC, 4, 30])
    ckT = din("ckT", [128, 4, 512])
    sc_raw = din("sc_raw", [4, 30, D])
    ck_raw = din("ck_raw", [4, 128, 512])
    cv_raw = din("cv_raw", [4, 128, 512])
    vecs = din("vecs", [128, NV])
    tab = din("tab", [128, 2 * NTOK])
    mask4 = din("mask4", [128, 512])
    masks = din("masks", [128, 32])
    mask40 = din("mask40", [128, 512])
    perm = din("perm", [128, 128])
    ident = din("ident", [128, 128])
    maskns = din("maskns", [128, 64])

    yT = dout("yT", [128, KC, NQ])
    uT_out = dout("uT_out", [128, KC * 62])
    conv_old = dout("conv_old", [4, 22, D])
    kT_out = dout("kT_out", [128, 4 * 160])
    v_out = dout("v_out", [128, 512])
    v_new = dout("v_new", [4, 8, 512])
    k_old = dout("k_old", [4, 120, 512])
    v_old = dout("v_old", [4, 120, 512])

    P = Prog()
    outs = []
    with contextlib.ExitStack() as st:
        def sb(name, cols, dt):
            return st.enter_context(nc.sbuf_tensor(name, [128, cols], dt))

        X = sb("X", KC * TE, F32)
        H = sb("H", KC * TE, BF16)
        RING = sb("RING", NS * 2048, BF16)
        VEC = sb("VEC", NV, F32)
        TAB = sb("TAB", 2 * NTOK, F32)
        TR0 = sb("TR0", TE, F32)
        TR1 = sb("TR1", TE, F32)
        SQ = sb("SQ", 4 * 512, BF16)
        ONESB = sb("ONESB", 128, BF16)
        ONE1 = sb("ONE1", 64, BF16)
        PERMF = sb("PERMF", 128, F32)
        PERMB = sb("PERMB", 128, BF16)
        MASK4 = sb("MASK4", 512, BF16)
        MASKS = sb("MASKS", 32, BF16)
        MASK40 = sb("MASK40", 512, BF16)
        ES = sb("ES", 16, F32)
        EPSC = sb("EPSC", 1, F32)
        IDENT = sb("IDENT", 128, BF16)
        MASKNS = sb("MASKNS", 64, BF16)
        RBYTES = 44288
        R = sb("R", RBYTES // 4, F32)
        banks = [st.enter_context(nc.psum_tensor("ps%d" % i, [128, 512], F32)) for i in range(8)]

        def carve(off, ncols, dt):
            sz = 4 if dt == F32 else 2
            assert off % 4 == 0 and (ncols * sz) % 4 == 0
            assert off + ncols * sz <= RBYTES, (off, ncols, sz)
            v = R[:, off // 4: off // 4 + (ncols * sz) // 4]
            return v if dt == F32 else v.bitcast(BF16)

        o = 0
        SIG = [carve(o + i * 2488, 622, F32) for i in range(2)]; o += 2 * 2488
        U = [carve(o + i * UW * 4, UW, F32) for i in range(2)]; o += 2 * UW * 4
        ACCD = [carve(o + i * 2368, 592, F32) for i in range(2)]; o += 2 * 2368
        UB = [carve(o + i * 1488, 744, BF16) for i in range(2)]; o += 2 * 1488
        NDG = 32
        DG = carve(o, NDG * 128, BF16); o += NDG * 256
        UC = carve(o, 16 * 30, F32); o += 16 * 30 * 4
        LNT = [carve(o + i * 2368, 592, F32) for i in range(2)]; o += 2 * 2368
        USAVE = carve(o, 16 * 62, F32); o += 16 * 62 * 4
        assert o <= RBYTES, o
        o = 0
        HID = carve(o, 8 * NTOK, BF16); o += 8 * NTOK * 2
        RT = [carve(o + i * 2048, 512, F32) for i in range(3)]; o += 3 * 2048
        NYST = 6
        YST = [carve(o + i * 2048, 512, F32) for i in range(NYST)]; o += NYST * 2048
        assert o <= RBYTES, o
        KTW = NTOK + 512
        o = 0
        KT = carve(o, 4 * KTW, BF16); o += 4 * KTW * 2
        VB = carve(o, 13 * 512, BF16); o += 13 * 512 * 2
        VN = carve(o, 4 * 512, BF16); o += 4 * 512 * 2
        o_alias = o
        QB = [carve(o + i * 1088, 544, BF16) for i in range(2)]; o += 2 * 1088
        QT = [carve(o + i * 1088, 544, BF16) for i in range(2)]; o += 2 * 1088
        KOUT = carve(o_alias, 4 * 160, F32)
        VST = [carve(o_alias + 2560 + i * 1024, 256, F32) for i in range(2)]
        assert o_alias + 2560 + 2048 <= o + 256
        o = max(o, o_alias + 2560 + 2048)
        NEV = 3
        EB = [carve(o + i * 1024, 512, BF16) for i in range(NEV)]; o += NEV * 1024
        PT = [carve(o + i * 1024, 512, BF16) for i in range(NEV)]; o += NEV * 1024
        DEN = [carve(o + i * 512, 128, F32) for i in range(NEV)]; o += NEV * 512
        assert o <= RBYTES, o

        def Xap(kc, a, b):
            return X[:, kc * TE + a: kc * TE + b]

        def Hap(kc, a, b):
            return H[:, kc * TE + a: kc * TE + b]

        def vcol(c):
            return VEC[:, c:c + 1]

        bank_ctr = [0]
        dgc = [0]

        def bank():
            b = bank_ctr[0] % 8
            bank_ctr[0] += 1
            return b

        wstate = {"issued": 0, "consumed": 0}

        def issue_next(extra_reads=()):
            n = wstate["issued"]
            if n >= len(order):
                return
            s = n % NS
            ti = tidx[order[n]]
            P.add("pool", C("dma_start", out=RING[:, s * 2048:(s + 1) * 2048], in_=wst[ti]),
                  reads=list(extra_reads), writes=[("slot", s)], dma="slot%d" % s)
            wstate["issued"] = n + 1

        def get_tile(name):
            n = wstate["consumed"]
            assert order[n] == name, (order[n], name)
            s = n % NS
            wstate["consumed"] = n + 1
            return s

        def slot_ap(s, a, b):
            return RING[:, s * 2048 + a: s * 2048 + b]

        P.add("sp", C("dma_start", out=VEC[:], in_=vecs[:]), writes=["VEC"], dma="vec")
        for (ta, tb_) in ((0, 622), (622, TE)):
            for k0 in range(0, KC, 8):
                wk = []
                for kc in range(k0, k0 + 8):
                    wk += seg_keys("X", kc, ta, tb_)
                P.add("sp", C("dma_start",
                    out=X[:, k0 * TE:(k0 + 8) * TE].rearrange("p (k t) -> p k t", t=TE)[:, :, ta:tb_], in_=xT[:, k0:k0 + 8, ta:tb_]),
                    writes=wk, dma="xl%d_%d" % (k0, ta))
        P.add("sp", C("dma_start", out=TAB[:], in_=tab[:]), writes=["TAB"], dma="tab")
        P.add("sp", C("dma_start", out=PERMF[:], in_=perm[:]), writes=["PERMF"], dma="permf")
        P.add("pool", C("dma_start", out=PERMB[:], in_=perm[:]), writes=["PERMB"], dma="permb")
        P.add("pool", C("dma_start", out=MASK4[:], in_=mask4[:]), writes=["MASK4"], dma="mask4")
        P.add("pool", C("dma_start", out=IDENT[:], in_=ident[:]), writes=["IDENT"], dma="ident")
        P.add("pool", C("dma_start", out=MASKNS[:], in_=maskns[:]), writes=["MASKNS"], dma="maskns")
        P.add("pool", C("dma_start", out=MASKS[:], in_=masks[:]), writes=["MASKS"], dma="masks")
        P.add("pool", C("dma_start", out=MASK40[:], in_=mask40[:]), writes=["MASK4"], dma="mask40")
        xfirst = [k_ for kc in range(KC) for k_ in seg_keys("X", kc, 0, 622)]
        for _ in range(NS):
            issue_next(extra_reads=xfirst)
        P.add("dve", C("memset", ONESB[:], 1.0 / D), writes=["ONESB"])
        P.add("dve", C("memset", ONE1[:], 1.0), writes=["ONE1"])
        P.add("dve", C("memset", EPSC[:], EPS), writes=["EPSC"])
        P.add("act", C("activation", out=ES[:], in_=VEC[:, V_SINK:V_SINK + 16], func=AF.Exp),
              reads=["VEC"], writes=["ES"])
        outs.append(P.add("sp", C("dma_start", out=conv_old[:], in_=sc_raw[:, 8:30, :]), dma="d2d0"))
        outs.append(P.add("sp", C("dma_start", out=k_old[:], in_=ck_raw[:, 8:128, :]), dma="d2d1"))
        outs.append(P.add("sp", C("dma_start", out=v_old[:], in_=cv_raw[:, 8:128, :]), dma="d2d2"))

        def rstd_from_bank(b, n, rs_ap, rs_keys):
            P.add("act", C("activation", out=rs_ap, in_=banks[b][:, :n], func=AF.Ln, bias=EPSC[:, 0:1]),
                  reads=[("ps", b), "EPSC"], writes=rs_keys)
            P.add("act", C("activation", out=rs_ap, in_=rs_ap, func=AF.Exp, scale=-0.5), reads=rs_keys, writes=rs_keys)

        def rms_stats(a, b):
            for (t0, n) in tiles(a, b):
                bk = bank()
                for kc in range(KC):
                    sq = kc % 4
                    if kc % 2 == 0:
                        P.add("act", C("activation",
                            out=SQ[:, sq * 512: sq * 512 + n], in_=Xap(kc, t0, t0 + n), func=AF.Square),
                            reads=seg_keys("X", kc, t0, t0 + n), writes=[("SQ", sq)])
                    else:
                        P.add("dve", C("tensor_tensor",
                            out=SQ[:, sq * 512: sq * 512 + n], in0=Xap(kc, t0, t0 + n), in1=Xap(kc, t0, t0 + n), op=ALU.mult),
                            reads=seg_keys("X", kc, t0, t0 + n), writes=[("SQ", sq)])
                    P.add("pe", C("matmul",
                        banks[bk][:, :n], lhsT=ONESB[:], rhs=SQ[:, sq * 512: sq * 512 + n],
                        start=(kc == 0), stop=(kc == KC - 1)),
                        reads=[("SQ", sq), "ONESB"], writes=[("ps", bk)])
                rstd_from_bank(bk, n, TR0[:, t0:t0 + n], seg_keys("RS", 0, t0, t0 + n))

        def rms_to_h(a, b, gcol):
            rms_stats(a, b)
            for kc in range(KC):
                for (t0, n) in tiles(a, b):
                    P.add("dve", C("scalar_tensor_tensor",
                        out=Hap(kc, t0, t0 + n), in0=Xap(kc, t0, t0 + n), scalar=vcol(gcol + kc),
                        in1=TR0[:, t0:t0 + n], op0=ALU.mult, op1=ALU.mult),
                        reads=seg_keys("X", kc, t0, t0 + n) + seg_keys("RS", 0, t0, t0 + n) + ["VEC"],
                        writes=seg_keys("H", kc, t0, t0 + n))

        def proj_group(slot, lhs_off, lhs_stride, nk, rhs_fn, rhs_keys_fn, tl):
            bks = [bank() for _ in tl]
            for k in range(nk):
                for i, (t0, n) in enumerate(tl):
                    P.add("pe", C("matmul",
                        banks[bks[i]][:, :n], lhsT=slot_ap(slot, lhs_off + k * lhs_stride, lhs_off + k * lhs_stride + 128),
                        rhs=rhs_fn(k, t0, n), start=(k == 0), stop=(k == nk - 1)),
                        reads=[("slot", slot)] + rhs_keys_fn(k, t0, n), writes=[("ps", bks[i])])
            return bks

        def h_rhs(k, t0, n):
            return Hap(k, t0, t0 + n)

        def h_keys(k, t0, n):
            return seg_keys("H", k, t0, t0 + n)

        try:
            checkpoint(0)
            def conv_pass(pa):
                if pa == 0:
                    ga, gb, oa, ob = 0, 622, 30, 622
                    cbase = 622
                else:
                    ga, gb, oa, ob = 622, 1214, 622, 1214
                    cbase = 0
                rms_to_h(ga, gb, V_NMIX0)
                checkpoint(1)
                tl = tiles(ga, gb)

                def cap(oc, a, b):
                    return Hap(oc, cbase + a, cbase + b)

                def ckeys(oc, a, b):
                    return seg_keys("H", oc, cbase + a, cbase + b)

                UWID = 622 if pa == 0 else UW

                def stage1(oc):
                    u = oc % 2
                    sA = get_tile(("pw1a", oc))
                    bA = proj_group(sA, 0, 128, KC, h_rhs, h_keys, tl)
                    sG = get_tile(("pw1g", oc))
                    bG = proj_group(sG, 0, 128, KC, h_rhs, h_keys, tl)
                    issue_next()
                    issue_next()
                    if pa == 1:
                        P.add("act", C("activation", out=U[u][:, 0:30], in_=UC[:, oc * 30:(oc + 1) * 30], func=AF.Copy),
                              reads=[("UC", oc)], writes=[("UX", u)])
                        P.add("sp", C("dma_start",
                            out=U[u][:, 590:742].rearrange("p (s c) -> p s c", c=38)[:, :, 0:30], in_=ctxT[:, oc, :, :]),
                            writes=[("UX", u)], dma="ctx%d" % u)
                    for i, (t0, n) in enumerate(tl):
                        g0 = t0 - ga
                        P.add("act", C("activation",
                            out=SIG[u][:, g0:g0 + n], in_=banks[bG[i]][:, :n], func=AF.Sigmoid, bias=vcol(V_BG + oc)),
                            reads=[("ps", bG[i]), "VEC"], writes=[("SIG", u)])
                        npr = min(t0 + n, T_SMP) - t0
                        if npr > 0:
                            uc0 = t0 if pa == 0 else 30 + (t0 - 622)
                            P.add("dve", C("scalar_tensor_tensor",
                                out=U[u][:, uc0:uc0 + npr], in0=banks[bA[i]][:, :npr], scalar=vcol(V_BA + oc),
                                in1=SIG[u][:, g0:g0 + npr], op0=ALU.add, op1=ALU.mult),
                                reads=[("ps", bA[i]), ("SIG", u), "VEC"], writes=[("U", u)])
                        if npr < n:
                            assert n - npr == 32
                            P.add("dve", C("scalar_tensor_tensor",
                                out=U[u][:, 590:742].rearrange("p (s c) -> p s c", c=38)[:, :, 30:38],
                                in0=banks[bA[i]][:, npr:npr + 32].rearrange("p (s c) -> p s c", c=8), scalar=vcol(V_BA + oc),
                                in1=SIG[u][:, g0 + npr:g0 + npr + 32].rearrange("p (s c) -> p s c", c=8),
                                op0=ALU.add, op1=ALU.mult),
                                reads=[("ps", bA[i]), ("SIG", u), "VEC"], writes=[("U", u)])
                    if pa == 0:
                        P.add("dve", C("tensor_single_scalar", out=U[u][:, 0:158], in_=U[u][:, 0:158],
                                       scalar=vcol(V_FLAG), op=ALU.mult),
                              reads=[("U", u), "VEC"], writes=[("U", u)])
                        P.add("act", C("activation", out=UC[:, oc * 30:(oc + 1) * 30], in_=U[u][:, 592:622], func=AF.Copy),
                              reads=[("U", u)], writes=[("UC", oc)])
                    else:
                        P.add("act", C("activation", out=USAVE[:, oc * 62: oc * 62 + 30], in_=U[u][:, 560:590], func=AF.Copy),
                              reads=[("U", u)], writes=["USAVE"])
                        P.add("act", C("activation",
                            out=USAVE[:, oc * 62 + 30: oc * 62 + 62].rearrange("p (s c) -> p s c", c=8),
                            in_=U[u][:, 590:742].rearrange("p (s c) -> p s c", c=38)[:, :, 30:38], func=AF.Copy),
                            reads=[("U", u)], writes=["USAVE"])
                    if NDV < CW:
                        P.add("act", C("activation", out=UB[u][:, 0:UWID], in_=U[u][:, 0:UWID], func=AF.Copy),
                              reads=[("U", u), ("UX", u)], writes=[("UB", u)])

                def stage2a(oc):
                    dgs = []
                    for j in range(NDV, CW):
                        dg = dgc[0] % NDG
                        dgc[0] += 1
                        P.add("act", C("activation", out=DG[:, dg * 128:(dg + 1) * 128], in_=IDENT[:, :], func=AF.Copy,
                                       scale=vcol(V_WDW + oc * CW + j)),
                              reads=["IDENT", "VEC"], writes=[("DG", dg)])
                        dgs.append(dg)
                    return dgs

                def stage2(oc, dgs):
                    u = oc % 2
                    if pa == 0:
                        parts = [(lambda acc: acc[:, 0:512], lambda X_, j: X_[:, j:j + 512], lambda: cap(oc, 0, 512), lambda bk: banks[bk][:, 0:512]),
                                 (lambda acc: acc[:, 512:592], lambda X_, j: X_[:, 512 + j:592 + j], lambda: cap(oc, 512, 592), lambda bk: banks[bk][:, 0:80])]
                    else:
                        parts = [(lambda acc: acc[:, 0:512], lambda X_, j: X_[:, j:j + 512], lambda: cap(oc, 0, 512), lambda bk: banks[bk][:, 0:512]),
                                 (lambda acc: acc[:, 512:560], lambda X_, j: X_[:, 512 + j:560 + j], lambda: cap(oc, 512, 560), lambda bk: banks[bk][:, 0:48]),
                                 (lambda acc: acc[:, 560:592].rearrange("p (s c) -> p s c", c=8),
                                  lambda X_, j: X_[:, 590:742].rearrange("p (s c) -> p s c", c=38)[:, :, j:j + 8],
                                  lambda: cap(oc, 560, 592).rearrange("p (s c) -> p s c", c=8),
                                  lambda bk: banks[bk][:, 0:32].rearrange("p (s c) -> p s c", c=8))]
                    akey = ("ACC", "dve", u)
                    cb = []
                    if NDV < CW:
                        cb = [bank() for _ in parts]
                        for j in range(NDV, CW):
                            dg = dgs[j - NDV]
                            for pi, (accv, xv, cv, pv) in enumerate(parts):
                                P.add("pe", C("matmul", pv(cb[pi]), lhsT=DG[:, dg * 128:(dg + 1) * 128], rhs=xv(UB[u], j),
                                              start=(j == NDV), stop=(j == CW - 1)),
                                      reads=[("DG", dg), ("UB", u)], writes=[("ps", cb[pi])])
                    for j in range(NDV):
                        wcol = vcol(V_WDW + oc * CW + j)
                        for pi, (accv, xv, cv, pv) in enumerate(parts):
                            if j == 0:
                                P.add("dve", C("tensor_scalar", out=accv(ACCD[u]), in0=xv(U[u], j), scalar1=wcol,
                                               scalar2=vcol(V_BDW + oc), op0=ALU.mult, op1=ALU.add),
                                      reads=[("U", u), ("UX", u), "VEC"], writes=[akey])
                            else:
                                P.add("dve", C("scalar_tensor_tensor", out=accv(ACCD[u]), in0=xv(U[u], j), scalar=wcol,
                                               in1=accv(ACCD[u]), op0=ALU.mult, op1=ALU.add),
                                      reads=[("U", u), ("UX", u), "VEC", akey], writes=[akey])
                    for pi, (accv, xv, cv, pv) in enumerate(parts):
                        if NDV == 0:
                            P.add("dve", C("tensor_single_scalar", out=cv(), in_=pv(cb[pi]), scalar=vcol(V_BDW + oc), op=ALU.add),
                                  reads=[("ps", cb[pi]), "VEC"], writes=ckeys(oc, 0, 592))
                        elif NDV == CW:
                            P.add("dve", C("tensor_copy", out=cv(), in_=accv(ACCD[u])), reads=[akey], writes=ckeys(oc, 0, 592))
                        else:
                            P.add("dve", C("tensor_tensor", out=cv(), in0=pv(cb[pi]), in1=accv(ACCD[u]), op=ALU.add),
                                  reads=[("ps", cb[pi]), akey], writes=ckeys(oc, 0, 592))

                stage1(0)
                for oc in range(16):
                    dgs = stage2a(oc)
                    if oc + 1 < 16:
                        stage1(oc + 1)
                    stage2(oc, dgs)
                ltl = tiles(0, 592)
                bmu = [bank() for _ in ltl]
                bvar = [bank() for _ in ltl]
                for oc in range(16):
                    for i, (c0, n) in enumerate(ltl):
                        sq = (2 * oc + i) % 4
                        P.add("act", C("activation",
                            out=SQ[:, sq * 512: sq * 512 + n], in_=cap(oc, c0, c0 + n), func=AF.Square),
                            reads=ckeys(oc, c0, c0 + n), writes=[("SQ", sq)])
                        P.add("pe", C("matmul",
                            banks[bmu[i]][:, :n], lhsT=ONESB[:], rhs=cap(oc, c0, c0 + n), start=(oc == 0), stop=(oc == 15)),
                            reads=ckeys(oc, c0, c0 + n) + ["ONESB"], writes=[("ps", bmu[i])])
                        P.add("pe", C("matmul",
                            banks[bvar[i]][:, :n], lhsT=ONESB[:], rhs=SQ[:, sq * 512: sq * 512 + n], start=(oc == 0), stop=(oc == 15)),
                            reads=[("SQ", sq), "ONESB"], writes=[("ps", bvar[i])])
                for i, (c0, n) in enumerate(ltl):
                    P.add("act", C("activation", out=TR1[:, c0:c0 + n], in_=banks[bmu[i]][:, :n], func=AF.Copy),
                          reads=[("ps", bmu[i])], writes=[("MU", i)])
                    P.add("dve", C("tensor_tensor", out=LNT[0][:, c0:c0 + n], in0=TR1[:, c0:c0 + n], in1=TR1[:, c0:c0 + n], op=ALU.mult),
                          reads=[("MU", i)], writes=[("LNT", 0)])
                    P.add("dve", C("tensor_tensor", out=LNT[0][:, c0:c0 + n], in0=banks[bvar[i]][:, :n], in1=LNT[0][:, c0:c0 + n], op=ALU.subtract),
                          reads=[("ps", bvar[i]), ("LNT", 0)], writes=[("LNT", 0)])
                    P.add("act", C("activation", out=TR0[:, c0:c0 + n], in_=LNT[0][:, c0:c0 + n], func=AF.Ln, bias=EPSC[:, 0:1]),
                          reads=[("LNT", 0), "EPSC"], writes=seg_keys("RS", 0, c0, c0 + n))
                    P.add("act", C("activation", out=TR0[:, c0:c0 + n], in_=TR0[:, c0:c0 + n], func=AF.Exp, scale=-0.5),
                          reads=seg_keys("RS", 0, c0, c0 + n), writes=seg_keys("RS", 0, c0, c0 + n))
                rs_keys = seg_keys("RS", 0, 0, 592)
                for oc in range(16):
                    a = oc % 2
                    P.add("dve", C("tensor_tensor", out=LNT[a][:, 0:592], in0=cap(oc, 0, 592), in1=TR1[:, 0:592], op=ALU.subtract),
                          reads=ckeys(oc, 0, 592) + [("MU", 0), ("MU", 1)], writes=[("LNT", a)])
                    P.add("dve", C("tensor_tensor", out=LNT[a][:, 0:592], in0=LNT[a][:, 0:592], in1=TR0[:, 0:592], op=ALU.mult),
                          reads=[("LNT", a)] + rs_keys, writes=[("LNT", a)])
                    P.add("act", C("activation", out=Hap(oc, oa, ob), in_=LNT[a][:, 0:592], func=AF.Silu,
                                                                   bias=vcol(V_LNB + oc), scale=vcol(V_LNG + oc)),
                          reads=[("LNT", a), "VEC"], writes=seg_keys("H", oc, oa, ob))
                otl = tiles(oa, ob)
                for oc in range(16):
                    s2 = get_tile(("pw2", oc))
                    bks = proj_group(s2, 0, 128, KC, h_rhs, h_keys, otl)
                    issue_next()
                    for i, (t0, n) in enumerate(otl):
                        P.add("dve", C("scalar_tensor_tensor",
                            out=Xap(oc, t0, t0 + n), in0=banks[bks[i]][:, :n], scalar=vcol(V_BPW2 + oc),
                            in1=Xap(oc, t0, t0 + n), op0=ALU.add, op1=ALU.add),
                            reads=[("ps", bks[i]), "VEC"] + seg_keys("X", oc, t0, t0 + n), writes=seg_keys("X", oc, t0, t0 + n))

            conv_pass(0)
            checkpoint(2)
            conv_pass(1)
            checkpoint(3)
            outs.append(P.add("sp", C("dma_start", out=uT_out[:], in_=USAVE[:]), reads=["USAVE"], dma="usave"))
            P.fence()

            def mlp(l, a, b, gcol):
                rms_to_h(a, b, gcol)
                tl = tiles(a, b)
                for g in range(8):
                    for fc in range(8):
                        s = get_tile(("up", l, g * 8 + fc))
                        bks = proj_group(s, 0, 128, KC, h_rhs, h_keys, tl)
                        issue_next()
                        for i, (t0, n) in enumerate(tl):
                            P.add("act", C("activation", out=RT[i][:, :n], in_=banks[bks[i]][:, :n], func=AF.Relu),
                                  reads=[("ps", bks[i])], writes=[("RT", i)])
                        for i, (t0, n) in enumerate(tl):
                            P.add("act", C("activation",
                                out=HID[:, fc * NTOK + (t0 - a): fc * NTOK + (t0 - a) + n], in_=RT[i][:, :n], func=AF.Square),
                                reads=[("RT", i)], writes=[("HID", fc, i)])
                    for ocp in range(8):
                        s = get_tile(("down", l, g, ocp))
                        for o2 in range(2):
                            oc = 2 * ocp + o2
                            bks = proj_group(s, o2 * 128, 256, 8,
                                             lambda k, t0, n: HID[:, k * NTOK + (t0 - a): k * NTOK + (t0 - a) + n],
                                             lambda k, t0, n: [("HID", k, i) for i in range(3)], tl)
                            for i, (t0, n) in enumerate(tl):
                                P.add("dve", C("tensor_tensor",
                                    out=Xap(oc, t0, t0 + n), in0=banks[bks[i]][:, :n], in1=Xap(oc, t0, t0 + n), op=ALU.add),
                                    reads=[("ps", bks[i])] + seg_keys("X", oc, t0, t0 + n), writes=seg_keys("X", oc, t0, t0 + n))
                        issue_next()

            mlp(0, 30, TE, V_NMLP0)
            P.fence()
            checkpoint(4)

            P.add("pool", C("dma_start",
                out=KT[:].rearrange("p (c w) -> p c w", w=KTW)[:, :, NTOK:NTOK + 512],
                in_=ckT[:]), writes=["KTC"], dma="ktc")
            P.add("pool", C("dma_start",
                out=VB[:, 9 * 512:13 * 512].rearrange("p (s f) -> p s f", f=512),
                in_=cv_raw[:].rearrange("s k f -> k s f")), writes=["VBC"], dma="vbc")
            rms_to_h(30, TE, V_KVN)
            ktl = tiles(30, TE)
            CT = lambda t0, n: TAB[:, t0 - 30: t0 - 30 + n]
            ST = lambda t0, n: TAB[:, NTOK + t0 - 30: NTOK + t0 - 30 + n]
            def kproj(c):
                s = get_tile(("wk", c))
                bks = proj_group(s, 0, 128, KC, h_rhs, h_keys, ktl)
                issue_next()
                return bks

            def kpost(c, bks):
                kfs = [TR1[:, t0 - 30: t0 - 30 + n] for (t0, n) in ktl]
                tvs = [TR0[:, t0 - 30: t0 - 30 + n] for (t0, n) in ktl]
                tks = [seg_keys("RS", 0, t0 - 30, t0 - 30 + n) for (t0, n) in ktl]
                for i, (t0, n) in enumerate(ktl):
                    P.add("act", C("activation", out=kfs[i], in_=banks[bks[i]][:, :n], func=AF.Copy),
                          reads=[("ps", bks[i])], writes=[("KF", i)])
                for i, (t0, n) in enumerate(ktl):
                    P.add("pe", C("matmul", banks[bks[i]][:, :n], lhsT=PERMF[:], rhs=kfs[i], start=True, stop=True),
                          reads=[("KF", i), "PERMF"], writes=[("ps", bks[i])])
                for i, (t0, n) in enumerate(ktl):
                    P.add("dve", C("tensor_tensor", out=tvs[i], in0=banks[bks[i]][:, :n], in1=ST(t0, n), op=ALU.mult),
                          reads=[("ps", bks[i]), "TAB"], writes=tks[i])
                for i, (t0, n) in enumerate(ktl):
                    P.add("dve", C("tensor_tensor", out=kfs[i], in0=kfs[i], in1=CT(t0, n), op=ALU.mult),
                          reads=[("KF", i), "TAB"], writes=[("KF", i)])
                for i, (t0, n) in enumerate(ktl):
                    P.add("dve", C("tensor_tensor", out=kfs[i], in0=kfs[i], in1=tvs[i], op=ALU.add),
                          reads=[("KF", i)] + tks[i], writes=[("KF", i)])
                for i, (t0, n) in enumerate(ktl):
                    P.add("act", C("activation",
                        out=KT[:, c * KTW + t0 - 30: c * KTW + t0 - 30 + n], in_=kfs[i], func=AF.Copy),
                        reads=[("KF", i)], writes=[("KT", c, i)])
                    if i == 2:
                        assert t0 == 1054 and n == 160
                        P.add("act", C("activation", out=KOUT[:, c * 160:(c + 1) * 160], in_=kfs[i], func=AF.Copy),
                              reads=[("KF", i)], writes=["KOUT"])

            kprev = None
            for c in range(4):
                bks_ = kproj(c)
                if kprev is not None:
                    kpost(*kprev)
                kprev = (c, bks_)
            kpost(*kprev)
            outs.append(P.add("sp", C("dma_start", out=kT_out[:], in_=KOUT[:]), reads=["KOUT"], dma="kout"))
            checkpoint(5)
            vgroups = [[("blk", bi) for bi in range(6)], [("blk", bi) for bi in range(6, 9)] + [("smp", s_) for s_ in range(4)]]
            vst_ctr = [0]
            for grp in vgroups:
                gb_ = [bank() for _ in grp]
                for c in range(4):
                    s = get_tile(("wv", c))
                    for gi, (kind, idx) in enumerate(grp):
                        for kk_ in range(4):
                            kc = c * 4 + kk_
                            if kind == "blk":
                                t0 = 30 + 128 * idx
                                P.add("pe", C("matmul",
                                    banks[gb_[gi]][:, :], lhsT=Hap(kc, t0, t0 + 128), rhs=slot_ap(s, kk_ * 512, kk_ * 512 + 512),
                                    start=(kc == 0), stop=(kc == KC - 1)),
                                    reads=[("slot", s)] + seg_keys("H", kc, t0, t0 + 128), writes=[("ps", gb_[gi])])
                            else:
                                t0 = T_SMP + 8 * idx
                                P.add("pe", C("matmul",
                                    banks[gb_[gi]][0:8, :], lhsT=Hap(kc, t0, t0 + 8), rhs=slot_ap(s, kk_ * 512, kk_ * 512 + 512),
                                    start=(kc == 0), stop=(kc == KC - 1)),
                                    reads=[("slot", s)] + seg_keys("H", kc, t0, t0 + 8), writes=[("ps", gb_[gi])])
                    issue_next()
                for gi, (kind, idx) in enumerate(grp):
                    bkk = gb_[gi]
                    if kind == "blk":
                        P.add("act", C("activation", out=VB[:, idx * 512:(idx + 1) * 512], in_=banks[bkk][:, :], func=AF.Copy),
                              reads=[("ps", bkk)], writes=[("VB", idx)])
                        if idx == 8:
                            for hh in range(2):
                                P.add("act", C("activation", out=VST[hh][:, :], in_=banks[bkk][:, hh * 256:(hh + 1) * 256], func=AF.Copy),
                                      reads=[("ps", bkk)], writes=[("VST", hh)])
                                outs.append(P.add("sp", C("dma_start", out=v_out[:, hh * 256:(hh + 1) * 256], in_=VST[hh][:, :]),
                                                  reads=[("VST", hh)], dma="vst%d" % hh))
                    else:
                        P.add("act", C("activation", out=VN[0:8, idx * 512:(idx + 1) * 512], in_=banks[bkk][0:8, :], func=AF.Copy),
                              reads=[("ps", bkk)], writes=[("VN", idx)])
                        for hh in range(2):
                            P.add("act", C("activation", out=VST[hh][0:8, :], in_=banks[bkk][0:8, hh * 256:(hh + 1) * 256], func=AF.Copy),
                                  reads=[("ps", bkk)], writes=[("VST", hh)])
                            outs.append(P.add("sp", C("dma_start", out=v_new[idx, :, hh * 256:(hh + 1) * 256], in_=VST[hh][0:8, :]),
                                              reads=[("VST", hh)], dma="vst%d" % hh))

            def attn_pass(pa):
                if pa == 0:
                    qa, qb, obase = 158, 670, 670
                    blocks = [0, 1, 2, 3]
                    smp = []
                else:
                    qa, qb, obase = 670, 1214, 30
                    blocks = [4, 5, 6, 7]
                    smp = [0, 1, 2, 3]
                nq = qb - qa
                rms_to_h(qa, qb, V_NMIX1)
                qtl = tiles(qa, qb)

                def oT(oc, a, n):
                    return Hap(oc, obase + a, obase + a + n)

                def oT_keys(oc, a, n):
                    return seg_keys("H", oc, obase + a, obase + a + n)

                ev = [0]

                deferred = []

                def stage_q(oc):
                    u = oc % 2
                    s = get_tile(("wq", oc))
                    bks = [6, 7][:len(qtl)]
                    for k in range(KC):
                        for i, (t0, n) in enumerate(qtl):
                            P.add("pe", C("matmul", banks[bks[i]][:, :n], lhsT=slot_ap(s, k * 128, k * 128 + 128),
                                          rhs=Hap(k, t0, t0 + n), start=(k == 0), stop=(k == KC - 1)),
                                  reads=[("slot", s)] + seg_keys("H", k, t0, t0 + n), writes=[("ps", bks[i])])
                    issue_next()
                    for i, (t0, n) in enumerate(qtl):
                        off = t0 - qa
                        P.add("act", C("activation", out=QB[u][:, off:off + n], in_=banks[bks[i]][:, :n], func=AF.Copy),
                              reads=[("ps", bks[i])], writes=[("QB", u, i), "KOUT", ("VST", 0), ("VST", 1)])

                    def rope():
                        for i, (t0, n) in enumerate(qtl):
                            off = t0 - qa
                            b2 = bks[i]
                            P.add("pe", C("matmul", banks[b2][:, :n], lhsT=PERMB[:], rhs=QB[u][:, off:off + n], start=True, stop=True),
                                  reads=[("QB", u, i), "PERMB"], writes=[("ps", b2)])
                            t1 = TR1[:, off:off + n]
                            t2 = TR1[:, 600 + off:600 + off + n]
                            P.add("dve", C("tensor_tensor", out=t1, in0=banks[b2][:, :n], in1=ST(t0, n), op=ALU.mult),
                                  reads=[("ps", b2), "TAB"], writes=[("QTMP", 0, i)])
                            P.add("dve", C("tensor_tensor", out=t2, in0=QB[u][:, off:off + n], in1=CT(t0, n), op=ALU.mult),
                                  reads=[("QB", u, i), "TAB"], writes=[("QTMP", 1, i)])
                            P.add("dve", C("tensor_tensor", out=QT[u][:, off:off + n], in0=t1, in1=t2, op=ALU.add),
                                  reads=[("QTMP", 0, i), ("QTMP", 1, i)], writes=[("QT", u, i)])
                    deferred.append(rope)

                def run_deferred():
                    while deferred:
                        deferred.pop(0)()

                def unit_scores(oc, unit):
                    c = oc // 4
                    u = oc % 2
                    qt_keys = [("QT", u, i) for i in range(len(qtl))]
                    v = ev[0] % NEV
                    ev[0] += 1
                    bSh = [sbank(), sbank()]
                    if unit[0] == "blk":
                        m = unit[1]
                        tb = 158 + 128 * m
                        off = tb - qa
                        kprev = c * KTW + (tb - 128 - 30)
                        kown = c * KTW + (tb - 30)
                        for hd in range(2):
                            pr0, pr1 = hd * 64, hd * 64 + 64
                            for kh, kcol in enumerate((kprev, kown)):
                                P.add("pe", C("matmul", banks[bSh[hd]][:, kh * 128: kh * 128 + 128],
                                              lhsT=KT[pr0:pr1, kcol:kcol + 128], rhs=QT[u][pr0:pr1, off:off + 128], start=True, stop=True),
                                      reads=[("KT", c, 0), ("KT", c, 1), ("KT", c, 2)] + qt_keys, writes=[("ps", bSh[hd])])
                        for hd in range(2):
                            P.add("act", C("activation", out=PT[v][:, hd * 256:(hd + 1) * 256], in_=banks[bSh[hd]][:, 0:256], func=AF.Exp, scale=0.125),
                                  reads=[("ps", bSh[hd])], writes=[("PT", v)])
                        M4 = MASK40 if m == 0 else MASK4
                        P.add("dve", C("tensor_tensor", out=EB[v][:, :], in0=PT[v][:, :], in1=M4[:, :], op=ALU.mult),
                              reads=[("PT", v), "MASK4"], writes=[("EB", v)])
                    else:
                        for hd in range(2):
                            pr0, pr1 = hd * 64, hd * 64 + 64
                            P.add("pe", C("matmul", banks[bSh[hd]][:, 0:32], lhsT=IDENT[:, :], rhs=MASKNS[:, 0:32], start=True, stop=False),
                                  reads=["IDENT", "MASKNS"], writes=[("ps", bSh[hd])])
                            for sq_ in range(4):
                                off = (T_SMP - qa) + 8 * sq_
                                kcache = c * KTW + NTOK + sq_ * 128
                                P.add("pe", C("matmul", banks[bSh[hd]][:, sq_ * 8: sq_ * 8 + 8],
                                              lhsT=KT[pr0:pr1, kcache:kcache + 128], rhs=QT[u][pr0:pr1, off:off + 8], start=False, stop=(sq_ == 3)),
                                      reads=["KTC"] + qt_keys, writes=[("ps", bSh[hd])])
                            P.add("pe", C("matmul", banks[bSh[hd]][0:8, 32:64], lhsT=IDENT[0:8, 0:8], rhs=MASKNS[0:8, 32:64], start=True, stop=False),
                                  reads=["IDENT", "MASKNS"], writes=[("ps", bSh[hd])])
                            for sq_ in range(4):
                                off = (T_SMP - qa) + 8 * sq_
                                knew = c * KTW + (T_SMP + 8 * sq_ - 30)
                                P.add("pe", C("matmul", banks[bSh[hd]][0:8, 32 + sq_ * 8: 40 + sq_ * 8],
                                              lhsT=KT[pr0:pr1, knew:knew + 8], rhs=QT[u][pr0:pr1, off:off + 8], start=False, stop=(sq_ == 3)),
                                      reads=[("KT", c, 2)] + qt_keys, writes=[("ps", bSh[hd])])
                        for hd in range(2):
                            P.add("act", C("activation", out=EB[v][:, hd * 64: hd * 64 + 32], in_=banks[bSh[hd]][:, 0:32], func=AF.Exp, scale=0.125),
                                  reads=[("ps", bSh[hd])], writes=[("EB", v)])
                            P.add("act", C("activation", out=EB[v][0:8, hd * 64 + 32: hd * 64 + 64], in_=banks[bSh[hd]][0:8, 32:64], func=AF.Exp, scale=0.125),
                                  reads=[("ps", bSh[hd])], writes=[("EB", v)])
                    return v

                def unit_pv(oc, unit, v):
                    c = oc // 4
                    kva, kvb = 2 * c, 2 * c + 1
                    bO = obank()
                    if unit[0] == "blk":
                        m = unit[1]
                        off = 158 + 128 * m - qa
                        w = 128
                        for hd, kv in enumerate((kva, kvb)):
                            pr0, pr1 = hd * 64, hd * 64 + 64
                            for kh, vblk in enumerate((m, m + 1)):
                                P.add("pe", C("matmul", banks[bO][pr0:pr1, 0:128], lhsT=VB[:, vblk * 512 + kv * 64: vblk * 512 + kv * 64 + 64],
                                              rhs=EB[v][:, hd * 256 + kh * 128: hd * 256 + kh * 128 + 128], start=(kh == 0), stop=(kh == 1)),
                                      reads=[("VB", vblk), ("EB", v)], writes=[("ps", bO)])
                            for kh in range(2):
                                P.add("pe", C("matmul", banks[bO][pr0:pr1, 128:256], lhsT=ONE1[:, 0:64],
                                              rhs=EB[v][:, hd * 256 + kh * 128: hd * 256 + kh * 128 + 128], start=(kh == 0), stop=(kh == 1)),
                                      reads=["ONE1", ("EB", v)], writes=[("ps", bO)])
                    else:
                        off = T_SMP - qa
                        w = 32
                        for hd, kv in enumerate((kva, kvb)):
                            pr0, pr1 = hd * 64, hd * 64 + 64
                            for sq_ in range(4):
                                rc = EB[v][:, hd * 64 + sq_ * 8: hd * 64 + sq_ * 8 + 8]
                                rn = EB[v][0:8, hd * 64 + 32 + sq_ * 8: hd * 64 + 40 + sq_ * 8]
                                P.add("pe", C("matmul", banks[bO][pr0:pr1, sq_ * 8: sq_ * 8 + 8],
                                              lhsT=VB[:, (9 + sq_) * 512 + kv * 64: (9 + sq_) * 512 + kv * 64 + 64], rhs=rc, start=True, stop=False),
                                      reads=["VBC", ("EB", v)], writes=[("ps", bO)])
                                P.add("pe", C("matmul", banks[bO][pr0:pr1, sq_ * 8: sq_ * 8 + 8],
                                              lhsT=VN[0:8, sq_ * 512 + kv * 64: sq_ * 512 + kv * 64 + 64], rhs=rn, start=False, stop=True),
                                      reads=[("VN", sq_), ("EB", v)], writes=[("ps", bO)])
                                P.add("pe", C("matmul", banks[bO][pr0:pr1, 128 + sq_ * 8: 136 + sq_ * 8], lhsT=ONE1[:, 0:64], rhs=rc, start=True, stop=False),
                                      reads=["ONE1", ("EB", v)], writes=[("ps", bO)])
                                P.add("pe", C("matmul", banks[bO][pr0:pr1, 128 + sq_ * 8: 136 + sq_ * 8], lhsT=ONE1[0:8, 0:64], rhs=rn, start=False, stop=True),
                                      reads=["ONE1", ("EB", v)], writes=[("ps", bO)])
                    P.add("act", C("activation", out=DEN[v][:, 0:w], in_=banks[bO][:, 128:128 + w], func=AF.Ln, bias=ES[:, oc:oc + 1]),
                          reads=[("ps", bO), "ES"], writes=[("DEN", v)])
                    P.add("act", C("activation", out=DEN[v][:, 0:w], in_=DEN[v][:, 0:w], func=AF.Exp, scale=-1.0),
                          reads=[("DEN", v)], writes=[("DEN", v)])
                    P.add("dve", C("tensor_tensor", out=oT(oc, off, w), in0=banks[bO][:, 0:w], in1=DEN[v][:, 0:w], op=ALU.mult),
                          reads=[("ps", bO), ("DEN", v)], writes=oT_keys(oc, off, w))

                units = [("blk", m) for m in blocks] + ([("smp", 0)] if smp else [])
                seq = [(oc, un) for oc in range(16) for un in units]
                sctr = [0]
                octr = [0]

                def sbank():
                    b_ = sctr[0] % 4
                    sctr[0] += 1
                    return b_

                def obank():
                    b_ = 4 + octr[0] % 2
                    octr[0] += 1
                    return b_

                stage_q(0)
                run_deferred()
                pend = []
                for idx, (oc, un) in enumerate(seq):
                    if un is units[0] and oc + 1 < 16:
                        stage_q(oc + 1)
                    v = unit_scores(oc, un)
                    pend.append((oc, un, v))
                    if len(pend) > 2:
                        unit_pv(*pend.pop(0))
                    run_deferred()
                while pend:
                    unit_pv(*pend.pop(0))
                otl = [(t0 - qa, n) for (t0, n) in qtl]
                for oc in range(16):
                    s = get_tile(("wo", oc))
                    bks = proj_group(s, 0, 128, KC,
                                     lambda k, o0, n: oT(k, o0, n), lambda k, o0, n: oT_keys(k, o0, n), otl)
                    issue_next()
                    for i, (o0, n) in enumerate(otl):
                        t0 = qa + o0
                        P.add("dve", C("tensor_tensor",
                            out=Xap(oc, t0, t0 + n), in0=banks[bks[i]][:, :n], in1=Xap(oc, t0, t0 + n), op=ALU.add),
                            reads=[("ps", bks[i])] + seg_keys("X", oc, t0, t0 + n), writes=seg_keys("X", oc, t0, t0 + n))

            checkpoint(6)
            attn_pass(0)
            checkpoint(7)
            attn_pass(1)
            checkpoint(8)
            P.fence()

            mlp(1, 158, TE, V_NMLP1)
            checkpoint(9)

            rms_stats(158, TE)
            yc = [0]
            for (t0, n) in tiles(158, TE):
                for kc in range(KC):
                    yb = yc[0] % NYST
                    yc[0] += 1
                    P.add("dve", C("scalar_tensor_tensor",
                        out=YST[yb][:, :n], in0=Xap(kc, t0, t0 + n), scalar=vcol(V_FIN + kc),
                        in1=TR0[:, t0:t0 + n], op0=ALU.mult, op1=ALU.mult),
                        reads=seg_keys("X", kc, t0, t0 + n) + seg_keys("RS", 0, t0, t0 + n) + ["VEC"], writes=[("YST", yb)])
                    outs.append(P.add("sp", C("dma_start",
                        out=yT[:, kc, t0 - 158: t0 - 158 + n], in_=YST[yb][:, :n]), reads=[("YST", yb)], dma="yst%d" % yb))

            assert wstate["consumed"] == len(order), (wstate, len(order))
        except _Stop:
            pass
        P.finalize(outs)
        P.emit(nc)
    return nc


def _tile_kc(W, c0):
    return np.ascontiguousarray(W[:, c0:c0 + 128].reshape(16, 128, 128).transpose(1, 0, 2).reshape(128, 2048))


def _col(v):
    return np.ascontiguousarray(np.asarray(v, np.float32).reshape(16, 128).T)


_NC_CACHE = {}


def prepare(x_prompt, x_sample, state_conv, cache_k, cache_v, norm_mix, w_pw1, b_pw1, w_dw, b_dw,
           conv_ln_g, conv_ln_b, w_pw2, b_pw2, kv_norm, w_k, w_v, w_q, w_o, sinks, norm_mlp,
           w_up, w_down, final_norm):
    f = lambda a: np.asarray(a, dtype=np.float32)
    x_prompt, x_sample, state_conv, cache_k, cache_v = map(f, (x_prompt, x_sample, state_conv, cache_k, cache_v))
    w_pw1, w_pw2, w_k, w_v, w_q, w_o, w_up, w_down = map(f, (w_pw1, w_pw2, w_k, w_v, w_q, w_o, w_up, w_down))
    norm_mix, norm_mlp, b_pw1, w_dw, b_dw = map(f, (norm_mix, norm_mlp, b_pw1, w_dw, b_dw))
    conv_ln_g, conv_ln_b, b_pw2, kv_norm, sinks, final_norm = map(f, (conv_ln_g, conv_ln_b, b_pw2, kv_norm, sinks, final_norm))

    tidx = unique_tiles()

    qperm = np.zeros(2048, np.int64)
    head_of = np.zeros((16, 2), np.int64)
    for oc in range(16):
        c, i = oc // 4, oc % 4
        for hd in range(2):
            h = 8 * c + 4 * hd + i
            head_of[oc, hd] = h
            qperm[oc * 128 + hd * 64: oc * 128 + hd * 64 + 64] = h * 64 + np.arange(64)
    wq_p = w_q[0][:, qperm]
    wo_p = w_o[0][qperm, :]

    wst = np.empty((len(tidx), 128, 2048), np.float32)
    for nm, ti in tidx.items():
        kind = nm[0]
        if kind == "pw1a":
            wst[ti] = _tile_kc(w_pw1[0], nm[1] * 128)
        elif kind == "pw1g":
            wst[ti] = _tile_kc(w_pw1[0], 2048 + nm[1] * 128)
        elif kind == "pw2":
            wst[ti] = _tile_kc(w_pw2[0], nm[1] * 128)
        elif kind == "up":
            wst[ti] = _tile_kc(w_up[nm[1]], nm[2] * 128)
        elif kind == "down":
            l, g, ocp = nm[1], nm[2], nm[3]
            wst[ti] = w_down[l][g * 1024:(g + 1) * 1024, ocp * 256:(ocp + 1) * 256].reshape(8, 128, 256).transpose(1, 0, 2).reshape(128, 2048)
        elif kind == "wk":
            wst[ti] = _tile_kc(w_k, nm[1] * 128)
        elif kind == "wv":
            c = nm[1]
            wst[ti] = w_v[c * 512:(c + 1) * 512, :].reshape(4, 128, 512).transpose(1, 0, 2).reshape(128, 2048)
        elif kind == "wq":
            wst[ti] = _tile_kc(wq_p, nm[1] * 128)
        elif kind == "wo":
            wst[ti] = _tile_kc(wo_p, nm[1] * 128)
        else:
            raise AssertionError(nm)

    vec = np.zeros((128, NV), np.float32)
    vec[:, V_NMIX0:V_NMIX0 + 16] = _col(norm_mix[0])
    vec[:, V_NMLP0:V_NMLP0 + 16] = _col(norm_mlp[0])
    vec[:, V_KVN:V_KVN + 16] = _col(kv_norm)
    vec[:, V_NMIX1:V_NMIX1 + 16] = _col(norm_mix[1])
    vec[:, V_NMLP1:V_NMLP1 + 16] = _col(norm_mlp[1])
    vec[:, V_FIN:V_FIN + 16] = _col(final_norm)
    vec[:, V_BA:V_BA + 16] = _col(b_pw1[0][:2048])
    vec[:, V_BG:V_BG + 16] = _col(b_pw1[0][2048:])
    vec[:, V_BDW:V_BDW + 16] = _col(b_dw[0])
    vec[:, V_LNG:V_LNG + 16] = _col(conv_ln_g[0])
    vec[:, V_LNB:V_LNB + 16] = _col(conv_ln_b[0])
    vec[:, V_BPW2:V_BPW2 + 16] = _col(b_pw2[0])
    for oc in range(16):
        vec[0:64, V_SINK + oc] = sinks[0][head_of[oc, 0]]
        vec[64:128, V_SINK + oc] = sinks[0][head_of[oc, 1]]
    vec[:, V_WDW:] = w_dw[0].T.reshape(16, 128, CW).transpose(1, 0, 2).reshape(128, 16 * CW)

    mask_prev = (np.arange(128)[:, None] > np.arange(128)[None, :]).astype(np.float32)
    mask_own = (np.arange(128)[:, None] <= np.arange(128)[None, :]).astype(np.float32)
    m4 = np.concatenate([mask_prev, mask_own, mask_prev, mask_own], axis=1)
    m4_first = m4.copy()
    m4_first[:, 0:128] = 0.0
    m4_first[:, 256:384] = 0.0
    ms = np.zeros((128, 32), np.float32)
    ms[:, 0:8] = mask_prev[:, 0:8]
    ms[:, 8:16] = mask_prev[:, 0:8]
    ms[0:8, 16:24] = mask_own[0:8, 0:8]
    ms[0:8, 24:32] = mask_own[0:8, 0:8]
    NEG = np.float32(-240000.0)
    mn = np.concatenate([(1.0 - mask_prev) * NEG, (1.0 - mask_own) * NEG], axis=1).astype(np.float32)
    mn_first = mn.copy()
    mn_first[:, 0:128] = NEG
    mns = np.zeros((128, 64), np.float32)
    for s_ in range(4):
        mns[:, s_ * 8:(s_ + 1) * 8] = (1.0 - mask_prev[:, 0:8]) * NEG
        mns[0:8, 32 + s_ * 8: 40 + s_ * 8] = (1.0 - mask_own[0:8, 0:8]) * NEG
    eye = np.eye(128, dtype=np.float32)
    pm = np.zeros((128, 128), np.float32)
    sgn = np.zeros(128, np.float32)
    fidx = np.full(128, -1, np.int64)
    for m_ in range(128):
        d = m_ % 64
        if d < 8:
            pm[m_ + 8, m_] = 1.0
            sgn[m_] = -1.0
            fidx[m_] = d
        elif d < 16:
            pm[m_ - 8, m_] = 1.0
            sgn[m_] = 1.0
            fidx[m_] = d - 8
    inv = (np.float32(500000.0) ** (-np.arange(8, dtype=np.float32) / np.float32(8))).astype(np.float32)

    in_maps = []
    for core in range(NCORE):
        b, j = core // 4, core % 4
        xe = np.zeros((TE, D), np.float32)
        if j > 0:
            xe[0:158] = x_prompt[b, 1024 * j - 158: 1024 * j]
        xe[158:1182] = x_prompt[b, 1024 * j: 1024 * (j + 1)]
        xe[1182:1214] = x_sample[4 * core: 4 * core + 4].reshape(32, D)
        xTc = np.ascontiguousarray(xe.T.reshape(16, 128, TE).transpose(1, 0, 2))
        sc = state_conv[0, 4 * core: 4 * core + 4]
        ctxTc = np.ascontiguousarray(sc.transpose(2, 0, 1).reshape(16, 128, 4, 30).transpose(1, 0, 2, 3))
        ck = cache_k[4 * core: 4 * core + 4].reshape(4, 128, 512)
        cv = cache_v[4 * core: 4 * core + 4].reshape(4, 128, 512)
        ckTc = np.ascontiguousarray(ck.transpose(2, 0, 1).reshape(4, 128, 4 * 128).transpose(1, 0, 2))
        pos = np.zeros(NTOK, np.float32)
        pos[0:1152] = (1024 * j - 128) + np.arange(1152)
        pos[1152:1184] = 16384 + (np.arange(32) % 8)
        tabc = np.zeros((128, 2 * NTOK), np.float32)
        tabc[:, 0:NTOK] = 1.0
        for m_ in range(128):
            if fidx[m_] >= 0:
                ang = (pos * inv[fidx[m_]]).astype(np.float32)
                tabc[m_, 0:NTOK] = np.cos(ang).astype(np.float32)
                tabc[m_, NTOK:] = sgn[m_] * np.sin(ang).astype(np.float32)
        v = vec.copy()
        v[:, V_FLAG] = 0.0 if j == 0 else 1.0
        in_maps.append({
            "xT": xTc, "wst": wst, "ctxT": ctxTc, "ckT": ckTc,
            "sc_raw": np.ascontiguousarray(sc), "ck_raw": np.ascontiguousarray(ck), "cv_raw": np.ascontiguousarray(cv),
            "vecs": v, "tab": tabc, "mask4": m4, "masks": ms, "perm": pm,
            "mask40": (m4_first if j == 0 else m4), "ident": eye, "maskns": mns,
        })

    return in_maps


def kernel(**inputs):
    in_maps = prepare(**inputs)
    if "nc" not in _NC_CACHE:
        _NC_CACHE["nc"] = build()
    nc = _NC_CACHE["nc"]
    res = run_bass_kernel_spmd(nc, in_maps, core_ids=list(range(NCORE)))
    return assemble(res.results)


def assemble(R_):

    y_prompt = np.zeros((2, 4096, D), np.float32)
    y_sample = np.zeros((32, 8, D), np.float32)
    conv_prompt = np.zeros((1, 2, 30, D), np.float32)
    conv_sample = np.zeros((1, 32, 30, D), np.float32)
    win_k_prompt = np.zeros((2, 128, 8, 64), np.float32)
    win_v_prompt = np.zeros((2, 128, 8, 64), np.float32)
    win_k_sample = np.zeros((32, 128, 8, 64), np.float32)
    win_v_sample = np.zeros((32, 128, 8, 64), np.float32)
    for core in range(NCORE):
        b, j = core // 4, core % 4
        r = R_[core]
        y = np.transpose(r["yT"], (2, 1, 0)).reshape(NQ, D)
        y_prompt[b, 1024 * j: 1024 * (j + 1)] = y[0:1024]
        y_sample[4 * core: 4 * core + 4] = y[1024:1056].reshape(4, 8, D)
        uo = r["uT_out"].reshape(128, 16, 62)
        ut = np.transpose(uo, (2, 1, 0)).reshape(62, D)
        kt = np.transpose(r["kT_out"].reshape(128, 4, 160), (2, 1, 0)).reshape(160, 512)
        if j == 3:
            conv_prompt[0, b] = ut[0:30]
            win_k_prompt[b] = kt[0:128].reshape(128, 8, 64)
            win_v_prompt[b] = r["v_out"].reshape(128, 8, 64)
        for s in range(4):
            q = 4 * core + s
            conv_sample[0, q, 0:22] = r["conv_old"][s]
            conv_sample[0, q, 22:30] = ut[30 + 8 * s: 30 + 8 * s + 8]
            win_k_sample[q, 0:120] = r["k_old"][s].reshape(120, 8, 64)
            win_k_sample[q, 120:128] = kt[128 + 8 * s: 128 + 8 * s + 8].reshape(8, 8, 64)
            win_v_sample[q, 0:120] = r["v_old"][s].reshape(120, 8, 64)
            win_v_sample[q, 120:128] = r["v_new"][s].reshape(8, 8, 64)
    return (y_prompt, y_sample, conv_prompt, conv_sample, win_k_prompt, win_v_prompt, win_k_sample, win_v_sample)
```
